# Optimizing a Trainium2 kernel written in Bass

```python
import math
import jax, jax.numpy as jnp
from jax import lax
import numpy as np

D_MODEL = 1024
BATCH = 8
SEQ = 4096
DEPTH = 2

N_META = 16
D_SSD = D_MODEL
SSD_HEAD_DIM = 64
SSD_HEADS = D_SSD // SSD_HEAD_DIM
SSD_GROUPS = 4
SSD_STATE = 128
SSD_CONV_K = 4
SSD_CHUNK = 128
SSD_CONV_DIM = D_SSD + 2 * SSD_GROUPS * SSD_STATE
GLA_HEADS = 4
GLA_DK = D_MODEL // 2
GLA_DV = D_MODEL
GLA_HEAD_K = GLA_DK // GLA_HEADS
GLA_HEAD_V = GLA_DV // GLA_HEADS
GLA_GATE_RANK = 16
GLA_GATE_TEMP = 16.0
GLA_CHUNK = 64
N_BRANCH = 2
D_BRANCH = D_MODEL
IN_WIDTHS = (
    D_SSD,
    SSD_CONV_DIM,
    SSD_HEADS,
    GLA_DK,
    GLA_DK,
    GLA_DV,
    GLA_DV,
    GLA_GATE_RANK,
    N_BRANCH * D_MODEL,
)
N_IN = sum(IN_WIDTHS)
EPS = 1e-6

kernel_name = "hybrid_ssd_gla_gated_merge"


def rmsnorm(x, w):
    xf = x.astype(jnp.float32)
    y = xf * lax.rsqrt(jnp.mean(xf * xf, axis=-1, keepdims=True) + EPS)
    return (y * w).astype(x.dtype)


def split_cols(t, widths):
    out, start = [], 0
    for w in widths:
        out.append(t[..., start:start + w])
        start += w
    return out


def causal_depthwise_conv(x, w, b):
    k = w.shape[0]
    out = lax.conv_general_dilated(
        x, w[:, None, :].astype(x.dtype), window_strides=(1,), padding=[(k - 1, 0)],
        dimension_numbers=('NWC', 'WIO', 'NWC'), feature_group_count=x.shape[-1])
    return out + b


def ssd_chunked(x, dt, a, bmat, cmat, chunk, state0):
    b, l, h, p = x.shape
    n = bmat.shape[-1]
    nc = l // chunk
    xc = x.reshape(b, nc, chunk, h, p)
    bc = bmat.reshape(b, nc, chunk, h, n)
    cc = cmat.reshape(b, nc, chunk, h, n)
    dtc = dt.reshape(b, nc, chunk, h)
    da = (dt * a.astype(jnp.float32)).reshape(b, nc, chunk, h).transpose(0, 3, 1, 2)
    da_cum = jnp.cumsum(da, axis=-1)
    causal = jnp.tril(jnp.ones((chunk, chunk), dtype=bool))
    seg = da_cum[..., :, None] - da_cum[..., None, :]
    decay = jnp.exp(jnp.where(causal, seg, -jnp.inf))
    scores = jnp.einsum('bzthn,bzshn->bhzts', cc, bc)
    y_diag = jnp.einsum('bhzts,bzsh,bzshp->bzthp', scores * decay, dtc, xc)
    decay_to_end = jnp.exp(da_cum[..., -1:] - da_cum)
    states = jnp.einsum('bzshn,bhzs,bzsh,bzshp->bzhpn', bc, decay_to_end, dtc, xc)
    chunk_decay = jnp.exp(da_cum[..., -1])

    def step(s, inp):
        st, dec = inp
        return dec[..., None, None] * s + st, s

    final, prev = lax.scan(step, state0,
                           (states.transpose(1, 0, 2, 3, 4), chunk_decay.transpose(2, 0, 1)))
    prev = prev.transpose(1, 0, 2, 3, 4)
    y_off = jnp.einsum('bzthn,bzhpn,bhzt->bzthp', cc, prev, jnp.exp(da_cum))
    return (y_diag + y_off).reshape(b, l, h, p), final


def gla_chunked(q, k, v, log_a, chunk, state0):
    b, l, h, dk = q.shape
    nc = l // chunk

    def to_chunks(t):
        return t.reshape(b, nc, chunk, h, t.shape[-1]).transpose(1, 0, 3, 2, 4)

    qc, kc, vc = to_chunks(q), to_chunks(k), to_chunks(v)
    gc = jnp.cumsum(to_chunks(log_a).astype(jnp.float32), axis=3)
    causal = jnp.tril(jnp.ones((chunk, chunk), dtype=bool))[:, :, None]

    def step(s, inp):
        qi, ki, vi, gi = inp
        o_inter = jnp.einsum('bhtk,bhkv->bhtv', qi * jnp.exp(gi), s)
        rel = gi[:, :, :, None, :] - gi[:, :, None, :, :]
        rel = jnp.exp(jnp.where(causal, rel, -jnp.inf))
        att = jnp.einsum('bhtk,bhsk,bhtsk->bhts', qi, ki, rel)
        o_intra = jnp.einsum('bhts,bhsv->bhtv', att, vi)
        g_last = gi[:, :, -1]
        k_dec = ki * jnp.exp(g_last[:, :, None, :] - gi)
        s_new = jnp.exp(g_last)[..., None] * s + jnp.einsum('bhsk,bhsv->bhkv', k_dec, vi)
        return s_new, o_inter + o_intra

    final, o = lax.scan(step, state0, (qc, kc, vc, gc))
    return o.transpose(1, 0, 3, 2, 4).reshape(b, l, h, vc.shape[-1]), final


def hybrid_layer(hres, norm_w, w_in, conv_w, conv_b, dt_bias, a_log, d_skip, ssd_norm_w,
                 gla_gate_w2, gla_gate_b, gla_norm_w, w_branch, w_out):
    b, l, _ = hres.shape
    m = N_META
    u = rmsnorm(hres, norm_w)
    proj = jnp.einsum('bld,dn->bln', u, w_in)
    z, xbc, dt_raw, q, k, v, g_gla, a_lr, merge_g = split_cols(proj, IN_WIDTHS)

    xbc = jax.nn.silu(causal_depthwise_conv(xbc, conv_w, conv_b))
    xs, bm, cm = split_cols(xbc, (D_SSD, SSD_GROUPS * SSD_STATE, SSD_GROUPS * SSD_STATE))
    xs = xs.reshape(b, l, SSD_HEADS, SSD_HEAD_DIM)
    rep = SSD_HEADS // SSD_GROUPS
    bm = jnp.repeat(bm.reshape(b, l, SSD_GROUPS, SSD_STATE), rep, axis=2)
    cm = jnp.repeat(cm.reshape(b, l, SSD_GROUPS, SSD_STATE), rep, axis=2)
    dt = jax.nn.softplus((dt_raw + dt_bias).astype(jnp.float32))
    a = -jnp.exp(a_log.astype(jnp.float32))
    s0 = jnp.zeros((b, SSD_HEADS, SSD_HEAD_DIM, SSD_STATE), jnp.float32)
    y_m, s_m = ssd_chunked(xs[:, :m], dt[:, :m], a, bm[:, :m], cm[:, :m], N_META, s0)
    y_r, _ = ssd_chunked(xs[:, m:], dt[:, m:], a, bm[:, m:], cm[:, m:], SSD_CHUNK, s_m)
    y = jnp.concatenate([y_m, y_r], axis=1) + d_skip[:, None] * xs
    y = y.reshape(b, l, D_SSD) * jax.nn.silu(z)
    y_ssd = rmsnorm(y.reshape(b, l, SSD_GROUPS, D_SSD // SSD_GROUPS),
                    ssd_norm_w.reshape(SSD_GROUPS, -1)).reshape(b, l, D_SSD)

    q = q.reshape(b, l, GLA_HEADS, GLA_HEAD_K) * (GLA_HEAD_K ** -0.5)
    k = k.reshape(b, l, GLA_HEADS, GLA_HEAD_K)
    v = v.reshape(b, l, GLA_HEADS, GLA_HEAD_V)
    gate_logit = jnp.einsum('blr,rk->blk', a_lr, gla_gate_w2) + gla_gate_b
    log_a = (jax.nn.log_sigmoid(gate_logit.astype(jnp.float32)) / GLA_GATE_TEMP)
    log_a = log_a.reshape(b, l, GLA_HEADS, GLA_HEAD_K)
    st0 = jnp.zeros((b, GLA_HEADS, GLA_HEAD_K, GLA_HEAD_V), jnp.float32)
    o_m, st_m = gla_chunked(q[:, :m], k[:, :m], v[:, :m], log_a[:, :m], N_META, st0)
    o_r, _ = gla_chunked(q[:, m:], k[:, m:], v[:, m:], log_a[:, m:], GLA_CHUNK, st_m)
    o = rmsnorm(jnp.concatenate([o_m, o_r], axis=1), gla_norm_w)
    y_gla = o.reshape(b, l, GLA_DV) * jax.nn.silu(g_gla)

    branches = jnp.stack([y_ssd.astype(y_gla.dtype), y_gla], axis=2)
    branch_proj = jnp.einsum('blnc,ncd->blnd', branches, w_branch)
    gates = jax.nn.sigmoid(merge_g.reshape(b, l, N_BRANCH, D_MODEL))
    merged = jnp.sum(gates * branch_proj, axis=2)
    return hres + jnp.einsum('bld,de->ble', merged, w_out)


def setup_inputs(seed: int = 0) -> dict:
    key = jax.random.key(seed)
    ks = jax.random.split(key, 16)
    f32 = jnp.float32
    nrm = jax.random.normal
    x = nrm(ks[0], (BATCH, SEQ, D_MODEL), f32)
    meta_tokens = nrm(ks[1], (N_META, D_MODEL), f32)
    norm_w = 1.0 + 0.02 * nrm(ks[2], (DEPTH, D_MODEL), f32)
    w_in = nrm(ks[3], (DEPTH, D_MODEL, N_IN), f32) * D_MODEL ** -0.5
    conv_w = nrm(ks[4], (DEPTH, SSD_CONV_K, SSD_CONV_DIM), f32) * SSD_CONV_K ** -0.5
    conv_b = 0.02 * nrm(ks[5], (DEPTH, SSD_CONV_DIM), f32)
    dt_init = jnp.exp(jax.random.uniform(ks[6], (DEPTH, SSD_HEADS), f32,
                                         minval=math.log(1e-3), maxval=math.log(1e-1)))
    dt_bias = dt_init + jnp.log(-jnp.expm1(-dt_init))
    a_log = jnp.log(jax.random.uniform(ks[7], (DEPTH, SSD_HEADS), f32, minval=1.0, maxval=16.0))
    d_skip = 1.0 + 0.1 * nrm(ks[8], (DEPTH, SSD_HEADS), f32)
    ssd_norm_w = 1.0 + 0.02 * nrm(ks[9], (DEPTH, D_SSD), f32)
    gla_gate_w2 = nrm(ks[10], (DEPTH, GLA_GATE_RANK, GLA_DK), f32) * GLA_GATE_RANK ** -0.5
    gla_gate_b = 0.5 * nrm(ks[11], (DEPTH, GLA_DK), f32)
    gla_norm_w = 1.0 + 0.02 * nrm(ks[12], (DEPTH, GLA_HEAD_V), f32)
    w_branch = nrm(ks[13], (DEPTH, N_BRANCH, D_BRANCH, D_MODEL), f32) * D_BRANCH ** -0.5
    w_out = nrm(ks[14], (DEPTH, D_MODEL, D_MODEL), f32) * D_MODEL ** -0.5
    final_norm_w = 1.0 + 0.02 * nrm(ks[15], (D_MODEL,), f32)
    return {"x": x, "meta_tokens": meta_tokens, "norm_w": norm_w, "w_in": w_in,
            "conv_w": conv_w, "conv_b": conv_b, "dt_bias": dt_bias, "a_log": a_log,
            "d_skip": d_skip, "ssd_norm_w": ssd_norm_w, "gla_gate_w2": gla_gate_w2,
            "gla_gate_b": gla_gate_b, "gla_norm_w": gla_norm_w, "w_branch": w_branch,
            "w_out": w_out, "final_norm_w": final_norm_w}


def reference(x, meta_tokens, norm_w, w_in, conv_w, conv_b, dt_bias, a_log, d_skip, ssd_norm_w,
              gla_gate_w2, gla_gate_b, gla_norm_w, w_branch, w_out, final_norm_w):
    b = x.shape[0]
    meta = jnp.broadcast_to(meta_tokens.astype(x.dtype)[None], (b, N_META, D_MODEL))
    h = jnp.concatenate([meta, x], axis=1)
    for i in range(DEPTH):
        h = hybrid_layer(h, norm_w[i], w_in[i], conv_w[i], conv_b[i], dt_bias[i], a_log[i],
                         d_skip[i], ssd_norm_w[i], gla_gate_w2[i], gla_gate_b[i], gla_norm_w[i],
                         w_branch[i], w_out[i])
    return rmsnorm(h, final_norm_w)[:, N_META:]
```

```python
import math
import numpy as np
import concourse.bass as bass
import concourse.mybir as mybir
from concourse.bass_utils import run_bass_kernel_spmd
from contextlib import ExitStack

F32 = mybir.dt.float32
BF16 = mybir.dt.bfloat16
AF = mybir.ActivationFunctionType
ALU = mybir.AluOpType

D = 1024
NIN = 8224
EPS = 1e-6
C_Z, C_XBC, C_DT, C_Q, C_K, C_V, C_G, C_ALR, C_M = 0, 1024, 3072, 3088, 3600, 4112, 5136, 6160, 6176


class Op:
    __slots__ = ("eng", "fn", "deps", "sig", "count", "sem", "is_dma", "inc")

    def __init__(self, eng, fn, is_dma=False):
        self.eng = eng
        self.fn = fn
        self.deps = set()
        self.sig = False
        self.count = 0
        self.sem = None
        self.is_dma = is_dma
        self.inc = 1


class Sched:
    ENGS = ("pe", "act", "dve", "pool", "sp")

    def __init__(self, nc, stack):
        self.nc = nc
        self.stack = stack
        self.ops = []
        self.res = {}
        self.engsem = {e: stack.enter_context(nc.semaphore("s_" + e)) for e in self.ENGS}
        self.dma_counts = {}
        self.alias = {}

    def new_sem(self, name):
        return self.stack.enter_context(self.nc.semaphore(name))

    def uniq(self, key):
        lst = self.alias.setdefault(key, [])
        k = ("__u", key, len(lst))
        lst.append(k)
        return k

    def add(self, eng, fn, reads=(), writes=(), dma_sem=None, raw_keys=False):
        op = Op(eng, fn, is_dma=dma_sem is not None)
        if not raw_keys:
            reads = [rr for r in reads for rr in self.alias.get(r, [r])]
            writes = [ww for w in writes for ww in self.alias.get(w, [w])]
        if dma_sem is not None:
            op.sem = dma_sem
            op.sig = True
            op.inc = 16
            c = self.dma_counts.get(id(dma_sem), 0) + 16
            self.dma_counts[id(dma_sem)] = c
            op.count = c
        for r in reads:
            st = self.res.get(r)
            if st is None:
                st = [None, []]
                self.res[r] = st
            if st[0] is not None:
                op.deps.add(st[0])
            st[1].append(op)
        for w in writes:
            st = self.res.get(w)
            if st is None:
                st = [None, []]
                self.res[w] = st
            if st[0] is not None:
                op.deps.add(st[0])
            last = {}
            for rd in st[1]:
                if rd is op:
                    continue
                if rd.is_dma:
                    op.deps.add(rd)
                else:
                    last[rd.eng] = rd
            for rd in last.values():
                op.deps.add(rd)
            st[0] = op
            st[1] = []
        op.deps.discard(op)
        self.ops.append(op)
        return op

    def finalize(self):
        for op in self.ops:
            if op.eng == "pe" and not op.is_dma:
                op.deps = {d for d in op.deps if not (d.eng == "pe" and not d.is_dma)}
            for d in op.deps:
                d.sig = True
        cnt = {e: 0 for e in self.ENGS}
        for op in self.ops:
            if op.is_dma:
                continue
            if op.sig:
                cnt[op.eng] += 1
                op.count = cnt[op.eng]
                op.sem = self.engsem[op.eng]

    def emit(self, block):
        self.finalize()
        by_eng = {e: [o for o in self.ops if o.eng == e] for e in self.ENGS}

        def run(engine, ops):
            waited = {}
            for op in ops:
                need = {}
                for d in op.deps:
                    k = id(d.sem)
                    if need.get(k, (None, 0))[1] < d.count:
                        need[k] = (d.sem, d.count)
                for k, (sem, c) in need.items():
                    if waited.get(k, 0) < c:
                        engine.wait_ge(sem, c)
                        waited[k] = c
                if op.fn is None:
                    continue
                ins = op.fn(engine)
                if op.sig:
                    ins.then_inc(op.sem, op.inc)

        @block.tensor
        def _(e):
            run(e, by_eng["pe"])

        @block.scalar
        def _(e):
            run(e, by_eng["act"])

        @block.vector
        def _(e):
            run(e, by_eng["dve"])

        @block.gpsimd
        def _(e):
            run(e, by_eng["pool"])

        @block.sync
        def _(e):
            run(e, by_eng["sp"])


class Ring:
    def __init__(self, items, name):
        self.items = items
        self.name = name
        self.i = 0

    def get(self):
        j = self.i % len(self.items)
        self.i += 1
        return self.items[j], (self.name, j)


def build(NT=33, NCH=3, NL=2, dbg=False):
    assert NT % NCH == 0
    N = NCH * 128
    ntiles = NT // NCH
    TR = (NT - 1) * 128
    nc = bass.Bass("TRN2", target_bir_lowering=False)

    def din(name, shape):
        return nc.dram_tensor(name, list(shape), F32, kind="ExternalInput").ap()

    x_d = din("x", [TR, D])
    meta_d = din("meta", [16, D])
    win_d = din("w_in", [2, D, NIN])
    wbr_d = din("w_br", [2, 2, D, D])
    wout_d = din("w_out", [2, D, D])
    normw_d = din("normw", [128, 16])
    cw_d = din("cw", [128, 128])
    cb_d = din("cb", [1, 2 * 2048])
    dtb_d = din("dtb", [2, 16])
    alog_d = din("alog", [2, 16])
    dskip_d = din("dskip", [2, 16])
    ssdnw_d = din("ssdnw", [128, 16])
    w2_d = din("w2", [16, 2 * 512])
    gb_d = din("gb", [1, 2 * 512])
    gnw_d = din("gnw", [128, 4])
    fnw_d = din("fnw", [1, D])
    ident_d = din("ident", [128, 128])
    tri_d = din("tri", [128, 128])
    strict_d = din("strict", [128, 128])
    mask0_d = din("mask0", [128, 1])
    sel_d = din("sel", [128, 3 * 128])
    out_d = nc.dram_tensor("out", [TR, D], F32, kind="ExternalOutput").ap()

    with ExitStack() as st:
        S = Sched(nc, st)

        def sb(name, shape, dt=F32):
            return st.enter_context(nc.sbuf_tensor("s_" + name, list(shape), dt))

        def ps(name, shape, dt=F32):
            return st.enter_context(nc.psum_tensor(name, list(shape), dt))

        A = S.add
        h = sb("h", [128, NCH, D])
        sqj = sb("sqj", [128, D], BF16)
        ubs = [sb("ub%d" % i, [128, D], BF16) for i in range(2)]
        UB = Ring(ubs, "ub")
        uT = sb("uT", [128, 8, N], BF16)
        NWB = 3
        wbufs = [sb("wb%d" % i, [128, 8, 512], BF16) for i in range(NWB)]
        wsems = [S.new_sem("wsem%d" % i) for i in range(NWB)]
        wmisc = sb("wmisc", [128, 2, 8, 32], BF16)
        NDG = 4
        dgs = [sb("dg%d" % i, [128, 4, 128], BF16) for i in range(NDG)]
        DG = Ring(dgs, "dg")
        XW = N + 4
        xbcT = sb("xbcT", [128, 16, XW], BF16)
        hist = sb("hist", [128, 2, 16, 4], BF16)
        x2 = sb("x2", [128, NCH, D], BF16)
        B2tok = sb("B2tok", [128, NCH, 512], BF16)
        BCT = sb("BCT", [128, 8, N], BF16)
        zs = sb("zs", [128, NCH, D], BF16)
        qkT = sb("qkT", [128, 8, N], BF16)
        vtok = sb("vtok", [128, NCH, D], BF16)
        sgT = sb("sgT", [128, 8, N], BF16)
        alrT = sb("alrT", [128, N], BF16)
        ths = [sb("th%d" % i, [128, 512], BF16) for i in range(2)]
        TH = Ring(ths, "th")
        gws = [sb("gw%d" % i, [128, 512], BF16) for i in range(2)]
        GW = Ring(gws, "gw")
        ssq = sb("ssq", [128, NCH])
        rs = sb("rs", [128, NCH])
        dtr = sb("dtr", [128, NCH, 16])
        dte_x = sb("dte_x", [128, NCH, 16])
        dt_t = sb("dt_t", [128, NCH, 16])
        da_t = sb("da_t", [128, NCH, 16])
        ecumh = sb("ecumh", [128, NCH, 16])
        dte = sb("dte", [128, NCH, 16])
        cdb = sb("cdb", [128, NCH, 16])
        wq = sb("wq", [128, NCH, 16])
        dth = sb("dth", [128, NCH, 16])
        sq4 = sb("sq4", [128, 4])
        r4 = sb("r4", [128, 4])
        sq4g = sb("sq4g", [128, 4])
        r4g = sb("r4g", [128, 4])
        egl = sb("egl", [128, 4])
        At = sb("At", [128, 8, 128])
        Lm = sb("Lm", [128, 16, 128], BF16)
        MT = sb("MT", [128, 16, 128], BF16)
        scm = sb("scm", [128, 4, 128], BF16)
        xdt = sb("xdt", [128, D], BF16)
        xw = sb("xw", [128, D], BF16)
        xd = sb("xd", [128, D], BF16)
        t1 = sb("t1", [128, D])
        yn = sb("yn", [128, D], BF16)
        stmp = sb("stmp", [128, D])
        e1 = t1[:, 0:512]
        l1 = t1[:, 512:1024]
        eg = stmp[:, 0:512]
        emg = stmp[:, 512:1024]
        on = yn
        qtl = sb("qtl", [128, 4, 128], BF16)
        ktl = sb("ktl", [128, 4, 128], BF16)
        ktok = sb("ktok", [128, 512], BF16)
        attm = sb("attm", [128, 4, 128], BF16)
        gth = [sb("gth%d" % i, [128, 4, N], BF16) for i in range(2)]
        t0m = sb("t0m", [128, 4, N])
        t1m = sb("t1m", [128, N])
        Sssd = [sb("Sssd%d" % l, [128, D]) for l in range(2)]
        Sbf = [sb("Sbf%d" % l, [128, D], BF16) for l in range(2)]
        Sgla = [sb("Sgla%d" % l, [128, D]) for l in range(2)]
        Gbf = [sb("Gbf%d" % l, [128, D], BF16) for l in range(2)]
        identf = sb("identf", [128, 128])
        identb = sb("identb", [128, 128], BF16)
        tri = sb("tri", [128, 128])
        tri16 = sb("tri16", [128, 128])
        strict = sb("strict", [128, 128])
        maskb = sb("maskb", [128, 128], BF16)
        maskq = sb("maskq", [128, 128], BF16)
        onesf = sb("onesf", [128, 128])
        sel = sb("sel", [128, 3, 128], BF16)
        selN = sb("selN", [128, 3, N], BF16)
        rows = sb("rows", [128, 2048], BF16)
        gb8 = sb("gb8", [8, 128])
        gb8h = sb("gb8h", [8, 128], BF16)
        gb8hf = sb("gb8hf", [8, 128])
        gb8l = sb("gb8l", [8, 128], BF16)
        mhalf = sb("mhalf", [128, 1])
        mask0 = sb("mask0", [128, 1])
        normw = sb("normw", [128, 16])
        cw = sb("cw", [128, 128])
        dtb_bc = sb("dtb_bc", [128, 2, 16])
        a_bc = sb("a_bc", [128, 2, 16])
        dskh_bc = sb("dskh_bc", [128, 2, 16])
        ssdnw = sb("ssdnw", [128, 16])
        w2b = sb("w2b", [128, 1024], BF16)
        gnwh = sb("gnwh", [128, 4])
        fnw_bc = sb("fnw_bc", [128, D])
        onorm = [sb("onorm%d" % i, [128, D]) for i in range(1)]
        ON = Ring(onorm, "onorm")
        pbs = [ps("pb%d" % i, [128, 512]) for i in range(6)]
        PB = Ring(pbs, "pb")
        pts = [ps("pt%d" % i, [128, 1024], BF16) for i in range(2)]
        PT = Ring(pts, "pt")

        csem = S.new_sem("csem")

        csemp = S.new_sem("csemp")
        cops = []
        copsp = []

        def cload(dst, src, key, eng="sp"):
            if eng == "sp":
                cops.append(A(eng, lambda e: e.dma_start(out=dst, in_=src), writes=[S.uniq(key)], dma_sem=csem, raw_keys=True))
            else:
                copsp.append(A(eng, lambda e: e.dma_start(out=dst, in_=src), writes=[S.uniq(key)], dma_sem=csemp, raw_keys=True))

        cload(identf[:], ident_d, "identf")
        cload(tri[:], tri_d, "tri")
        cload(strict[:], strict_d, "strict")
        cload(mask0[:], mask0_d, "mask0")
        cload(normw[:], normw_d, "normw")
        cload(cw[:], cw_d, "cw")
        cload(ssdnw[:], ssdnw_d, "ssdnw")
        cload(gnwh[:], gnw_d, "gnwh")
        cload(gb8[:], gb_d.rearrange("o (a b) -> (o a) b", a=8), "gb8")
        cload(fnw_bc[:], fnw_d[0:1, :].partition_broadcast(128), "fnw")
        for l in range(2):
            cload(dtb_bc[:, l, :], dtb_d[l:l + 1, :].partition_broadcast(128), "dtb")
            cload(a_bc[:, l, :], alog_d[l:l + 1, :].partition_broadcast(128), "a_bc")
            cload(dskh_bc[:, l, :], dskip_d[l:l + 1, :].partition_broadcast(128), "dskh")
        A("pool", lambda e: e.memset(rows[:], 0.0), writes=["rows0"])
        A("pool", lambda e: e.memset(alrT[:], 0.0), writes=["alrT"])
        A("pool", lambda e: e.memset(w2b[:], 0.0), writes=["w2b0"])
        copsp.append(A("pool", lambda e: e.dma_start(out=rows[0:1, :], in_=cb_d[:, 0:2048]), reads=["rows0"], writes=[S.uniq("rows")], dma_sem=csemp, raw_keys=True))
        copsp.append(A("pool", lambda e: e.dma_start(out=rows[1:2, :], in_=cb_d[:, 2048:4096]), reads=["rows0"], writes=[S.uniq("rows")], dma_sem=csemp, raw_keys=True))
        cload(sel[:].rearrange("p r m -> p (r m)"), sel_d, "sel", eng="pool")
        copsp.append(A("pool", lambda e: e.dma_start(out=w2b[0:16, :], in_=w2_d), reads=["w2b0"], writes=[S.uniq("w2b")], dma_sem=csemp, raw_keys=True))
        for l in range(2):
            cload(wmisc[:, l, :, 0:16], win_d[l, :, C_DT:C_DT + 16].rearrange("(k p) n -> p k n", p=128), "wmisc", eng="pool")
            cload(wmisc[:, l, :, 16:32], win_d[l, :, C_ALR:C_ALR + 16].rearrange("(k p) n -> p k n", p=128), "wmisc", eng="pool")

        for o in cops:
            o.count = 16 * len(cops)
        for o in copsp:
            o.count = 16 * len(copsp)
        A("dve", lambda e: e.tensor_copy(out=identb[:], in_=identf[:]), reads=["identf"], writes=["identb"])
        A("dve", lambda e: e.tensor_copy(out=maskb[:], in_=tri[:]), reads=["tri"], writes=["maskb"])
        A("dve", lambda e: e.tensor_scalar(out=maskq[:], in0=tri[:], scalar1=0.25, scalar2=None, op0=ALU.mult), reads=["tri"], writes=["maskq"])
        A("dve", lambda e: e.tensor_scalar(out=tri16[:], in0=tri[:], scalar1=-1.0 / 16.0, scalar2=None, op0=ALU.mult), reads=["tri"], writes=["tri16"])
        A("pool", lambda e: e.memset(onesf[:], 1.0), writes=["onesf"])
        A("pool", lambda e: e.memset(mhalf[:], -0.5), writes=["mhalf"])
        for r in range(3):
            A("pool", lambda e, r=r: e.tensor_copy(out=selN[:, r, :], in_=bc(sel[:, r, 0:1], [128, N])), reads=["sel"], writes=["selN"])
        A("pool", lambda e: e.memset(hist[:], 0.0), writes=["hist"])
        A("pool", lambda e: e.memset(xbcT[:], 0.0), writes=[("xbcT", b) for b in range(16)])
        for l in range(2):
            A("pool", lambda e, l=l: e.memset(Sssd[l][:], 0.0), writes=[("Sssd", l)])
            A("pool", lambda e, l=l: e.memset(Sbf[l][:], 0.0), writes=[("Sbf", l)])
            A("pool", lambda e, l=l: e.memset(Sgla[l][:], 0.0), writes=[("Sgla", l)])
            A("pool", lambda e, l=l: e.memset(Gbf[l][:], 0.0), writes=[("Gbf", l)])
        A("act", lambda e: e.activation(out=a_bc[:], in_=a_bc[:], func=AF.Exp), reads=["a_bc"], writes=["a_bc"])
        A("dve", lambda e: e.tensor_scalar(out=a_bc[:], in0=a_bc[:], scalar1=-1.0, scalar2=None, op0=ALU.mult), reads=["a_bc"], writes=["a_bc"])
        A("dve", lambda e: e.tensor_scalar(out=dskh_bc[:], in0=dskh_bc[:], scalar1=0.5, scalar2=None, op0=ALU.mult), reads=["dskh"], writes=["dskh"])
        A("dve", lambda e: e.tensor_scalar(out=gnwh[:], in0=gnwh[:], scalar1=0.5, scalar2=None, op0=ALU.mult), reads=["gnwh"], writes=["gnwh"])
        A("dve", lambda e: e.tensor_copy(out=gb8h[:], in_=gb8[:]), reads=["gb8"], writes=["gb8h"])
        A("dve", lambda e: e.tensor_copy(out=gb8hf[:], in_=gb8h[:]), reads=["gb8h"], writes=["gb8hf"])
        A("dve", lambda e: e.tensor_tensor(out=gb8l[:], in0=gb8[:], in1=gb8hf[:], op=ALU.subtract), reads=["gb8", "gb8hf"], writes=["gb8l"])
        csem2 = S.new_sem("csem2")
        cops2 = []
        for a8 in range(8):
            cops2.append(A("sp", lambda e, a8=a8: e.dma_start(out=rows[2:3, a8 * 128:(a8 + 1) * 128], in_=gb8h[a8:a8 + 1, :]), reads=["gb8h", "rows0"], writes=[S.uniq("rows")], dma_sem=csem2))
            cops2.append(A("sp", lambda e, a8=a8: e.dma_start(out=rows[2:3, 1024 + a8 * 128:1024 + (a8 + 1) * 128], in_=gb8l[a8:a8 + 1, :]), reads=["gb8l", "rows0"], writes=[S.uniq("rows")], dma_sem=csem2))
        for o in cops2:
            o.count = 16 * len(cops2)

        def wsrc(l, kind, idx):
            if kind == "in":
                src = win_d[l, :, idx:idx + 512]
            elif kind == "br":
                b, j = idx
                src = wbr_d[l, b, :, j * 512:(j + 1) * 512]
            else:
                src = wout_d[l, :, idx * 512:(idx + 1) * 512]
            return src.rearrange("(k p) n -> p k n", p=128)

        def layer_groups(l):
            g = []
            for i in range(4):
                g.append((l, "in", C_XBC + 512 * i))
            for i in range(2):
                g.append((l, "in", C_Z + 512 * i))
            g.append((l, "in", C_Q))
            g.append((l, "in", C_K))
            for i in range(2):
                g.append((l, "in", C_V + 512 * i))
            for i in range(2):
                g.append((l, "in", C_G + 512 * i))
            for j in range(2):
                for b in range(2):
                    g.append((l, "in", C_M + b * 1024 + j * 512))
                    g.append((l, "br", (b, j)))
            for j in range(2):
                g.append((l, "out", j))
            return g

        wlist = []
        for ti in range(ntiles):
            for l in range(NL):
                wlist += layer_groups(l)
        wstate = {"issued": 0, "used": 0}
        PF = 2

        GP = len(layer_groups(0)) * NL
        wscr = nc.dram_tensor("wscr", [GP, 128, 8 * 512], BF16, kind="Internal").ap()
        wsemh = [S.new_sem("wsemh%d" % i) for i in range(NWB)]
        wsemst = [S.new_sem("wsemst%d" % i) for i in range(NWB)]

        def w_issue_upto(n):
            while wstate["issued"] < min(n, len(wlist)):
                i = wstate["issued"]
                l, kind, idx = wlist[i]
                slot = i % NWB
                if i < GP:
                    src = wsrc(l, kind, idx)
                    A("pool", lambda e, slot=slot, src=src: e.dma_start(out=wbufs[slot][:], in_=src),
                      writes=[("wb", slot)], dma_sem=wsems[slot])
                    if ntiles > 1:
                        A("sp", lambda e, slot=slot, i=i: e.dma_start(out=wscr[i], in_=wbufs[slot][:].rearrange("p k n -> p (k n)")),
                          reads=[("wb", slot)], writes=[("wscr", i)], dma_sem=wsemst[slot])
                else:
                    g = i % GP
                    A("sp", lambda e, slot=slot, g=g: e.dma_start(out=wbufs[slot][:].rearrange("p k n -> p (k n)"), in_=wscr[g]),
                      reads=[("wscr", g)], writes=[("wb", slot)], dma_sem=wsemh[slot])
                wstate["issued"] += 1

        def next_w(expect):
            i = wstate["used"]
            assert wlist[i] == expect, (wlist[i], expect)
            w_issue_upto(i + 1 + PF)
            wstate["used"] += 1
            return wbufs[i % NWB], ("wb", i % NWB)

        def bc(ap, shape):
            return ap.to_broadcast(list(shape))

        HALF_LN = math.log(0.5)
        QS_LN = math.log(128.0 ** -0.5)

        def silu2_evac(bank_ap, out_ap, key, wkey, ncols):
            th, tkey = TH.get()
            A("act", lambda e: e.activation(out=th[:, 0:ncols], in_=bank_ap, func=AF.Tanh, scale=0.5), reads=[key], writes=[tkey])
            A("dve", lambda e: e.scalar_tensor_tensor(out=out_ap, in0=th[:, 0:ncols], scalar=1.0, in1=bank_ap, op0=ALU.add, op1=ALU.mult),
              reads=[tkey, key], writes=[wkey])

        uT_all = [("uT", c) for c in range(NCH)]
        out_keys = []
        osem = S.new_sem("osem")
        xsems = [S.new_sem("xsem%d" % i) for i in range(NCH)]

        def tile_layer(ti, l):
            first = ti == 0
            last_layer = l == NL - 1
            for c in range(NCH):
                A("act", lambda e, c=c: e.activation(out=sqj[:], in_=h[:, c, :], func=AF.Square, accum_out=ssq[:, c:c + 1]),
                  reads=[("h", c)], writes=[("ssq", c), "sqj"])
                A("dve", lambda e, c=c: e.tensor_scalar(out=rs[:, c:c + 1], in0=ssq[:, c:c + 1], scalar1=1.0 / D, scalar2=EPS, op0=ALU.mult, op1=ALU.add),
                  reads=[("ssq", c)], writes=[("rs", c)])
                A("pool", lambda e, c=c: e.tensor_tensor(out=rs[:, c:c + 1], in0=rs[:, c:c + 1], in1=mhalf[:], op=ALU.pow),
                  reads=[("rs", c), "mhalf"], writes=[("rs", c)])
                ub, ukey = UB.get()
                A("pool", lambda e, c=c, ub=ub: e.tensor_scalar(out=ub[:], in0=h[:, c, :], scalar1=rs[:, c:c + 1], scalar2=None, op0=ALU.mult),
                  reads=[("h", c), ("rs", c)], writes=[ukey])
                bank, key = PT.get()
                for k in range(8):
                    A("pe", lambda e, k=k, ub=ub, bank=bank: e.transpose(out=bank[:, k * 128:(k + 1) * 128], in_=ub[:, k * 128:(k + 1) * 128], identity=identb[:]),
                      reads=[ukey, "identb"], writes=[key])
                A("dve", lambda e, c=c, bank=bank: e.tensor_tensor(out=uT[:, :, c * 128:(c + 1) * 128], in0=bank[:, :].rearrange("p (k t) -> p k t", k=8),
                                                                    in1=bc(normw[:, l * 8:(l + 1) * 8].unsqueeze(2), [128, 8, 128]), op=ALU.mult),
                  reads=[key, "normw"], writes=[("uT", c)])
            for c in range(NCH):
                bank, key = PB.get()
                for k in range(8):
                    A("pe", lambda e, k=k, c=c, bank=bank: e.matmul(bank[:, 0:16], lhsT=uT[:, k, c * 128:(c + 1) * 128], rhs=wmisc[:, l, k, 0:16], start=(k == 0), stop=(k == 7)),
                      reads=[("uT", c), "wmisc"], writes=[key])
                A("dve", lambda e, c=c, bank=bank: e.tensor_tensor(out=dtr[:, c, :], in0=bank[:, 0:16], in1=dtb_bc[:, l, :], op=ALU.add),
                  reads=[key, "dtb"], writes=[("dtr", c)])
                A("act", lambda e, c=c: e.activation(out=dte_x[:, c, :], in_=dtr[:, c, :], func=AF.Exp), reads=[("dtr", c)], writes=[("dte_x", c)])
                A("act", lambda e, c=c: e.activation(out=dt_t[:, c, :], in_=dte_x[:, c, :], func=AF.Ln, bias=1.0), reads=[("dte_x", c)], writes=[("dt", c)])
                if first and c == 0:
                    A("dve", lambda e, c=c: e.tensor_scalar(out=dt_t[:, c, :], in0=dt_t[:, c, :], scalar1=mask0[:, 0:1], scalar2=None, op0=ALU.mult),
                      reads=[("dt", c), "mask0"], writes=[("dt", c)])
                A("dve", lambda e, c=c: e.tensor_tensor(out=da_t[:, c, :], in0=dt_t[:, c, :], in1=a_bc[:, l, :], op=ALU.mult),
                  reads=[("dt", c), "a_bc"], writes=[("da", c)])
                bank2, key2 = PB.get()
                A("pe", lambda e, c=c, bank2=bank2: e.matmul(bank2[:, 0:16], lhsT=tri[:], rhs=da_t[:, c, :], start=True, stop=True), reads=[("da", c), "tri"], writes=[key2])
                A("pe", lambda e, c=c, bank2=bank2: e.matmul(bank2[:, 16:32], lhsT=onesf[:], rhs=da_t[:, c, :], start=True, stop=True), reads=[("da", c), "onesf"], writes=[key2])
                A("pe", lambda e, c=c, bank2=bank2: e.matmul(bank2[:, 32:48], lhsT=strict[:], rhs=da_t[:, c, :], start=True, stop=True), reads=[("da", c), "strict"], writes=[key2])
                A("act", lambda e, c=c, bank2=bank2: e.activation(out=ecumh[:, c, :], in_=bank2[:, 0:16], func=AF.Exp, bias=HALF_LN), reads=[key2], writes=[("ecumh", c)])
                A("act", lambda e, c=c, bank2=bank2: e.activation(out=cdb[:, c, :], in_=bank2[:, 16:32], func=AF.Exp), reads=[key2], writes=[("cdb", c)])
                A("act", lambda e, c=c, bank2=bank2: e.activation(out=dte[:, c, :], in_=bank2[:, 32:48], func=AF.Exp), reads=[key2], writes=[("dte", c)])
                A("dve", lambda e, c=c: e.scalar_tensor_tensor(out=wq[:, c, :], in0=dt_t[:, c, :], scalar=0.25, in1=dte[:, c, :], op0=ALU.mult, op1=ALU.mult),
                  reads=[("dt", c), ("dte", c)], writes=[("wq", c)])
                A("dve", lambda e, c=c: e.tensor_scalar(out=dth[:, c, :], in0=dt_t[:, c, :], scalar1=0.5, scalar2=None, op0=ALU.mult),
                  reads=[("dt", c)], writes=[("dth", c)])
            bank, key = PB.get()
            for k in range(8):
                A("pe", lambda e, k=k, bank=bank: e.matmul(bank[0:16, 0:N], lhsT=wmisc[:, l, k, 16:32], rhs=uT[:, k, :], start=(k == 0), stop=(k == 7)),
                  reads=uT_all + ["wmisc"], writes=[key])
            A("act", lambda e, bank=bank: e.activation(out=alrT[0:16, :], in_=bank[0:16, 0:N], func=AF.Copy), reads=[key], writes=["alrT"])
            for gi in range(4):
                wb, wkey = next_w((l, "in", C_XBC + 512 * gi))
                for j in range(4):
                    blk = gi * 4 + j
                    bank, key = PB.get()
                    for k in range(8):
                        A("pe", lambda e, k=k, j=j, wb=wb, bank=bank: e.matmul(bank[:, 0:N], lhsT=wb[:, k, j * 128:(j + 1) * 128], rhs=uT[:, k, :], start=(k == 0), stop=(k == 7)),
                          reads=uT_all + [wkey], writes=[key])
                    A("act", lambda e, blk=blk, bank=bank: e.activation(out=xbcT[:, blk, 4:4 + N], in_=bank[:, 0:N], func=AF.Copy), reads=[key], writes=[("xbcT", blk)])
            allx = [("xbcT", b) for b in range(16)]
            A("pool", lambda e: e.tensor_copy(out=xbcT[:, :, 1:4], in_=hist[:, l, :, 1:4]), reads=[("hist", l)], writes=allx)
            A("pool", lambda e: e.tensor_copy(out=hist[:, l, :, 1:4], in_=xbcT[:, :, N + 1:N + 4]), reads=allx, writes=[("hist", l)])
            dgof = {}

            def gen_diag(blk):
                dg, dkey = DG.get()
                for tap in range(4):
                    col = (l * 16 + blk) * 4 + tap
                    A("pool", lambda e, dg=dg, tap=tap, col=col: e.tensor_scalar(out=dg[:, tap, :], in0=identb[:], scalar1=cw[:, col:col + 1], scalar2=None, op0=ALU.mult),
                      reads=["identb", "cw"], writes=[dkey])
                dgof[blk] = (dg, dkey)

            def conv_tok(c, blks, out_ap, wkey):
                bank, key = PB.get()
                for j, blk in enumerate(blks):
                    dg, dkey = dgof[blk]
                    o = bank[:, j * 128:(j + 1) * 128]
                    A("pe", lambda e, o=o, blk=blk: e.matmul(o, lhsT=sel[:, l, :], rhs=rows[:, blk * 128:(blk + 1) * 128], start=True, stop=False),
                      reads=["sel", "rows"], writes=[key])
                    for tap in range(4):
                        A("pe", lambda e, o=o, blk=blk, tap=tap, dg=dg, c=c: e.matmul(o, lhsT=xbcT[:, blk, c * 128 + 1 + tap:c * 128 + 1 + tap + 128], rhs=dg[:, tap, :], start=False, stop=(tap == 3)),
                          reads=[("xbcT", blk), dkey], writes=[key])
                silu2_evac(bank[:, 0:128 * len(blks)], out_ap, key, wkey, 128 * len(blks))

            for half in range(2):
                blks = [half * 4 + j for j in range(4)]
                for b in blks:
                    gen_diag(b)
                for c in range(NCH):
                    conv_tok(c, blks, x2[:, c, half * 512:(half + 1) * 512], ("x2", c))
            for blk in range(8, 16):
                if blk % 4 == 0:
                    for b4 in range(blk, blk + 4):
                        gen_diag(b4)
                    if blk == 8:
                        for c in range(NCH):
                            conv_tok(c, [8, 9, 10, 11], B2tok[:, c, :], ("B2tok", c))
                dg, dkey = dgof[blk]
                bank, key = PB.get()
                A("pe", lambda e, bank=bank, blk=blk: e.matmul(bank[:, 0:N], lhsT=rows[:, blk * 128:(blk + 1) * 128], rhs=selN[:, l, :], start=True, stop=False),
                  reads=["selN", "rows"], writes=[key])
                for tap in range(4):
                    A("pe", lambda e, bank=bank, blk=blk, tap=tap, dg=dg: e.matmul(bank[:, 0:N], lhsT=dg[:, tap, :], rhs=xbcT[:, blk, 1 + tap:1 + tap + N], start=False, stop=(tap == 3)),
                      reads=[("xbcT", blk), dkey], writes=[key])
                silu2_evac(bank[:, 0:N], BCT[:, blk - 8, :], key, ("BCT", blk - 8), N)
            for gi in range(2):
                wb, wkey = next_w((l, "in", C_Z + 512 * gi))
                for c in range(NCH):
                    bank, key = PB.get()
                    for k in range(8):
                        A("pe", lambda e, k=k, c=c, wb=wb, bank=bank: e.matmul(bank[:, :], lhsT=uT[:, k, c * 128:(c + 1) * 128], rhs=wb[:, k, :], start=(k == 0), stop=(k == 7)),
                          reads=[("uT", c), wkey], writes=[key])
                    silu2_evac(bank[:, :], zs[:, c, gi * 512:(gi + 1) * 512], key, ("zs", c), 512)
            for qk in range(2):
                wb, wkey = next_w((l, "in", C_Q if qk == 0 else C_K))
                for j in range(4):
                    bank, key = PB.get()
                    for k in range(8):
                        A("pe", lambda e, k=k, j=j, wb=wb, bank=bank: e.matmul(bank[:, 0:N], lhsT=wb[:, k, j * 128:(j + 1) * 128], rhs=uT[:, k, :], start=(k == 0), stop=(k == 7)),
                          reads=uT_all + [wkey], writes=[key])
                    A("act", lambda e, bank=bank, qk=qk, j=j: e.activation(out=qkT[:, qk * 4 + j, :], in_=bank[:, 0:N], func=AF.Copy), reads=[key], writes=[("qkT", qk * 4 + j)])
            for gi in range(2):
                wb, wkey = next_w((l, "in", C_V + 512 * gi))
                for c in range(NCH):
                    bank, key = PB.get()
                    for k in range(8):
                        A("pe", lambda e, k=k, c=c, wb=wb, bank=bank: e.matmul(bank[:, :], lhsT=uT[:, k, c * 128:(c + 1) * 128], rhs=wb[:, k, :], start=(k == 0), stop=(k == 7)),
                          reads=[("uT", c), wkey], writes=[key])
                    A("act", lambda e, bank=bank, c=c, gi=gi: e.activation(out=vtok[:, c, gi * 512:(gi + 1) * 512], in_=bank[:, :], func=AF.Copy), reads=[key], writes=[("vtok", c)])
            for gi in range(2):
                wb, wkey = next_w((l, "in", C_G + 512 * gi))
                for j in range(4):
                    blk = gi * 4 + j
                    bank, key = PB.get()
                    for k in range(8):
                        A("pe", lambda e, k=k, j=j, wb=wb, bank=bank: e.matmul(bank[:, 0:N], lhsT=wb[:, k, j * 128:(j + 1) * 128], rhs=uT[:, k, :], start=(k == 0), stop=(k == 7)),
                          reads=uT_all + [wkey], writes=[key])
                    gw, gkey = GW.get()
                    th, tkey = TH.get()
                    A("act", lambda e, bank=bank, gw=gw, blk=blk: e.activation(out=gw[:, 0:N], in_=bank[:, 0:N], func=AF.Copy, scale=gnwh[:, l * 2 + (blk % 2):l * 2 + (blk % 2) + 1]),
                      reads=[key, "gnwh"], writes=[gkey])
                    A("act", lambda e, bank=bank, th=th: e.activation(out=th[:, 0:N], in_=bank[:, 0:N], func=AF.Tanh, scale=0.5), reads=[key], writes=[tkey])
                    A("dve", lambda e, gw=gw, th=th, blk=blk: e.scalar_tensor_tensor(out=sgT[:, blk, :], in0=th[:, 0:N], scalar=1.0, in1=gw[:, 0:N], op0=ALU.add, op1=ALU.mult),
                      reads=[gkey, tkey], writes=[("sgT", blk)])
            for c in range(NCH):
                cs = slice(c * 128, (c + 1) * 128)
                for hv in range(2):
                    for hh in range(8):
                        hd = hv * 8 + hh
                        A("pool", lambda e, hh=hh, hd=hd, c=c: e.tensor_scalar(out=At[:, hh, :], in0=strict[:], scalar1=da_t[:, c, hd:hd + 1], scalar2=None, op0=ALU.mult),
                          reads=["strict", ("da", c)], writes=[("At", hh)])
                    for qd in range(2):
                        bank, key = PB.get()
                        for hq in range(4):
                            hh = qd * 4 + hq
                            A("pe", lambda e, bank=bank, hq=hq, hh=hh: e.matmul(bank[:, hq * 128:(hq + 1) * 128], lhsT=At[:, hh, :], rhs=tri[:], start=True, stop=True),
                              reads=[("At", hh), "tri"], writes=[key])
                        h0 = hv * 8 + qd * 4
                        A("act", lambda e, bank=bank, h0=h0: e.activation(out=Lm[:, h0:h0 + 4, :], in_=bank[:, :].rearrange("p (a t) -> p a t", a=4), func=AF.Exp),
                          reads=[key], writes=[("Lm", h0 // 4)])
                bank, key = PB.get()
                for g in range(4):
                    A("pe", lambda e, bank=bank, g=g, cs=cs: e.matmul(bank[:, g * 128:(g + 1) * 128], lhsT=BCT[:, g, cs], rhs=BCT[:, 4 + g, cs], start=True, stop=True),
                      reads=[("BCT", g), ("BCT", 4 + g)], writes=[key])
                A("dve", lambda e, bank=bank: e.tensor_tensor(out=scm[:], in0=bank[:, :].rearrange("p (g t) -> p g t", g=4), in1=bc(maskq[:, :].unsqueeze(1), [128, 4, 128]), op=ALU.mult),
                  reads=[key, "maskq"], writes=["scm"])
                A("dve", lambda e: e.tensor_tensor(out=MT[:].rearrange("p (g a) t -> p g a t", g=4), in0=Lm[:].rearrange("p (g a) t -> p g a t", g=4),
                                                   in1=bc(scm[:].unsqueeze(2), [128, 4, 4, 128]), op=ALU.mult),
                  reads=["scm"] + [("Lm", i) for i in range(4)], writes=["MT"])
                x3 = x2[:, c, :].rearrange("p (a d) -> p a d", a=16)
                A("pool", lambda e, x3=x3, c=c: e.tensor_tensor(out=xdt[:].rearrange("p (a d) -> p a d", a=16), in0=x3, in1=bc(dth[:, c, :].unsqueeze(2), [128, 16, 64]), op=ALU.mult),
                  reads=[("x2", c), ("dth", c)], writes=["xdt"])
                A("pool", lambda e, x3=x3, c=c: e.tensor_tensor(out=xw[:].rearrange("p (a d) -> p a d", a=16), in0=x3, in1=bc(wq[:, c, :].unsqueeze(2), [128, 16, 64]), op=ALU.mult),
                  reads=[("x2", c), ("wq", c)], writes=["xw"])
                A("pool", lambda e, x3=x3: e.tensor_tensor(out=xd[:].rearrange("p (a d) -> p a d", a=16), in0=x3, in1=bc(dskh_bc[:, l, :].unsqueeze(2), [128, 16, 64]), op=ALU.mult),
                  reads=[("x2", c), "dskh"], writes=["xd"])
                ydk = []
                for hb in range(2):
                    bank, key = PB.get()
                    ydk.append((bank, key))
                    A("pe", lambda e, bank=bank, hb=hb: e.matmul(bank[:, :], lhsT=identb[:], rhs=xd[:, hb * 512:(hb + 1) * 512], start=True, stop=False),
                      reads=["identb", "xd"], writes=[key])
                    for h8 in range(8):
                        hd = hb * 8 + h8
                        A("pe", lambda e, bank=bank, h8=h8, hd=hd: e.matmul(bank[:, h8 * 64:(h8 + 1) * 64], lhsT=MT[:, hd, :], rhs=xdt[:, hd * 64:(hd + 1) * 64], start=False, stop=(h8 == 7)),
                          reads=["MT", "xdt"], writes=[key])
                for hb in range(2):
                    bank, key = PB.get()
                    for gg in range(2):
                        g = hb * 2 + gg
                        A("pe", lambda e, bank=bank, gg=gg, g=g, cs=cs: e.matmul(bank[:, gg * 256:(gg + 1) * 256], lhsT=BCT[:, 4 + g, cs], rhs=Sbf[l][:, g * 256:(g + 1) * 256], start=True, stop=True),
                          reads=[("BCT", 4 + g), ("Sbf", l)], writes=[key])
                    hs = slice(hb * 512, (hb + 1) * 512)
                    A("dve", lambda e, bank=bank, hb=hb, hs=hs, c=c: e.tensor_tensor(out=t1[:, hs].rearrange("p (a d) -> p a d", a=8), in0=bank[:, :].rearrange("p (a d) -> p a d", a=8),
                                                                                  in1=bc(ecumh[:, c, hb * 8:(hb + 1) * 8].unsqueeze(2), [128, 8, 64]), op=ALU.mult),
                      reads=[key, ("ecumh", c)], writes=[("t1", hb)])
                    ybank, ykey = ydk[hb]
                    A("dve", lambda e, ybank=ybank, hs=hs: e.tensor_tensor(out=t1[:, hs], in0=t1[:, hs], in1=ybank[:, :], op=ALU.add),
                      reads=[ykey, ("t1", hb)], writes=[("t1", hb)])
                    A("pool", lambda e, hs=hs, c=c: e.tensor_tensor(out=t1[:, hs], in0=t1[:, hs], in1=zs[:, c, hs], op=ALU.mult),
                      reads=[("t1", hb), ("zs", c)], writes=[("t1", hb)])
                for g in range(4):
                    A("act", lambda e, g=g: e.activation(out=sqj[:, 0:256], in_=t1[:, g * 256:(g + 1) * 256], func=AF.Square, accum_out=sq4[:, g:g + 1]),
                      reads=[("t1", g // 2)], writes=[("sq4", g), "sqj"])
                sq4k = [("sq4", g) for g in range(4)]
                A("dve", lambda e: e.tensor_scalar(out=r4[:], in0=sq4[:], scalar1=1.0 / 256, scalar2=4 * EPS, op0=ALU.mult, op1=ALU.add), reads=sq4k, writes=["r4"])
                A("pool", lambda e: e.tensor_tensor(out=r4[:], in0=r4[:], in1=bc(mhalf[:, 0:1], [128, 4]), op=ALU.pow), reads=["r4", "mhalf"], writes=["r4"])
                A("pool", lambda e: e.tensor_tensor(out=yn[:].rearrange("p (g d) -> p g d", g=4), in0=t1[:].rearrange("p (g d) -> p g d", g=4),
                                                    in1=bc(r4[:, :].unsqueeze(2), [128, 4, 256]), op=ALU.mult),
                  reads=["r4", ("t1", 0), ("t1", 1)], writes=[("yn", 0), ("yn", 1)])
                bank, key = PT.get()
                for k in range(8):
                    A("pe", lambda e, k=k, bank=bank: e.transpose(out=bank[:, k * 128:(k + 1) * 128], in_=yn[:, k * 128:(k + 1) * 128], identity=identb[:]),
                      reads=[("yn", 0), ("yn", 1), "identb"], writes=[key])
                A("dve", lambda e, bank=bank, c=c: e.tensor_tensor(out=xbcT[:, 0:8, 4 + c * 128:4 + (c + 1) * 128], in0=bank[:, :].rearrange("p (k t) -> p k t", k=8),
                                                                    in1=bc(ssdnw[:, l * 8:(l + 1) * 8].unsqueeze(2), [128, 8, 128]), op=ALU.mult),
                  reads=[key, "ssdnw"], writes=[("xbcT", b) for b in range(8)])
                A("pool", lambda e, c=c: e.tensor_tensor(out=stmp[:].rearrange("p (a d) -> p a d", a=16), in0=Sssd[l][:].rearrange("p (a d) -> p a d", a=16),
                                                         in1=bc(cdb[:, c, :].unsqueeze(2), [128, 16, 64]), op=ALU.mult),
                  reads=[("Sssd", l), ("cdb", c)], writes=["stmp", "stmp2"])
                for hb in range(2):
                    bank, key = PB.get()
                    for gg in range(2):
                        g = hb * 2 + gg
                        A("pe", lambda e, bank=bank, gg=gg, g=g, c=c: e.matmul(bank[:, gg * 256:(gg + 1) * 256], lhsT=B2tok[:, c, g * 128:(g + 1) * 128], rhs=xw[:, g * 256:(g + 1) * 256], start=True, stop=True),
                          reads=[("B2tok", c), "xw"], writes=[key])
                    hs = slice(hb * 512, (hb + 1) * 512)
                    A("dve", lambda e, bank=bank, hs=hs: e.tensor_tensor(out=Sssd[l][:, hs], in0=stmp[:, hs], in1=bank[:, :], op=ALU.add),
                      reads=[key, "stmp" if hb == 0 else "stmp2"], writes=[("Sssd", l)])
                A("act", lambda e: e.activation(out=Sbf[l][:], in_=Sssd[l][:], func=AF.Copy), reads=[("Sssd", l)], writes=[("Sbf", l)])
            for c in range(NCH):
                cs = slice(c * 128, (c + 1) * 128)
                bank, key = PB.get()
                A("pe", lambda e, bank=bank, cs=cs: e.matmul(bank[:, :], lhsT=alrT[:, cs], rhs=w2b[:, l * 512:(l + 1) * 512], start=True, stop=False),
                  reads=["alrT", "w2b"], writes=[key])
                A("pe", lambda e, bank=bank: e.matmul(bank[:, :], lhsT=sel[:, 2, :], rhs=rows[:, l * 512:(l + 1) * 512], start=False, stop=False),
                  reads=["sel", "rows"], writes=[key])
                A("pe", lambda e, bank=bank: e.matmul(bank[:, :], lhsT=sel[:, 2, :], rhs=rows[:, 1024 + l * 512:1024 + (l + 1) * 512], start=False, stop=True),
                  reads=["sel", "rows"], writes=[key])
                A("act", lambda e, bank=bank: e.activation(out=e1, in_=bank[:, :], func=AF.Exp, scale=-1.0), reads=[key], writes=[("t1", 0)])
                A("act", lambda e: e.activation(out=l1, in_=e1, func=AF.Ln, bias=1.0), reads=[("t1", 0)], writes=[("t1", 1)])
                bankg, keyg = PB.get()
                for j in range(4):
                    A("pe", lambda e, bankg=bankg, j=j: e.matmul(bankg[:, j * 128:(j + 1) * 128], lhsT=l1[:, j * 128:(j + 1) * 128], rhs=tri16[:], start=True, stop=True),
                      reads=[("t1", 1), "tri16"], writes=[keyg])
                A("act", lambda e, bankg=bankg: e.activation(out=eg, in_=bankg[:, :], func=AF.Exp, bias=QS_LN), reads=[keyg], writes=["stmp"])
                A("act", lambda e, bankg=bankg: e.activation(out=emg, in_=bankg[:, :], func=AF.Exp, scale=-1.0), reads=[keyg], writes=["stmp2"])
                A("act", lambda e, bankg=bankg: e.activation(out=egl[:, :], in_=bankg[:, :].rearrange("p (j t) -> p j t", j=4)[:, :, 127], func=AF.Exp), reads=[keyg], writes=["egl"])
                A("dve", lambda e, cs=cs: e.tensor_tensor(out=qtl[:], in0=qkT[:, 0:4, cs], in1=eg.rearrange("p (j t) -> p j t", j=4), op=ALU.mult),
                  reads=["stmp"] + [("qkT", j) for j in range(4)], writes=["qtl"])
                A("dve", lambda e, cs=cs: e.tensor_tensor(out=ktl[:], in0=qkT[:, 4:8, cs], in1=emg.rearrange("p (j t) -> p j t", j=4), op=ALU.mult),
                  reads=["stmp2"] + [("qkT", 4 + j) for j in range(4)], writes=["ktl"])
                bank, key = PB.get()
                for hd in range(4):
                    A("pe", lambda e, bank=bank, hd=hd: e.matmul(bank[:, hd * 128:(hd + 1) * 128], lhsT=ktl[:, hd, :], rhs=qtl[:, hd, :], start=True, stop=True),
                      reads=["ktl", "qtl"], writes=[key])
                A("dve", lambda e, bank=bank: e.tensor_tensor(out=attm[:], in0=bank[:, :].rearrange("p (g t) -> p g t", g=4), in1=bc(maskb[:, :].unsqueeze(1), [128, 4, 128]), op=ALU.mult),
                  reads=[key, "maskb"], writes=["attm"])
                bankt, keyt = PT.get()
                for hd in range(4):
                    A("pe", lambda e, bankt=bankt, hd=hd: e.transpose(out=bankt[:, hd * 128:(hd + 1) * 128], in_=ktl[:, hd, :], identity=identb[:]),
                      reads=["ktl", "identb"], writes=[keyt])
                A("act", lambda e, bankt=bankt: e.activation(out=ktok[:], in_=bankt[:, 0:512], func=AF.Copy), reads=[keyt], writes=["ktok"])
                obk = []
                for hb in range(2):
                    bank, key = PB.get()
                    obk.append((bank, key))
                    for hh in range(2):
                        hd = hb * 2 + hh
                        o = bank[:, hh * 256:(hh + 1) * 256]
                        A("pe", lambda e, o=o, hd=hd, c=c: e.matmul(o, lhsT=attm[:, hd, :], rhs=vtok[:, c, hd * 256:(hd + 1) * 256], start=True, stop=False),
                          reads=["attm", ("vtok", c)], writes=[key])
                        A("pe", lambda e, o=o, hd=hd: e.matmul(o, lhsT=qtl[:, hd, :], rhs=Gbf[l][:, hd * 256:(hd + 1) * 256], start=False, stop=True),
                          reads=["qtl", ("Gbf", l)], writes=[key])
                        A("act", lambda e, o=o, hd=hd: e.activation(out=sqj[:, 0:256], in_=o, func=AF.Square, accum_out=sq4g[:, hd:hd + 1]), reads=[key], writes=[("sq4g", hd), "sqj"])
                A("dve", lambda e: e.tensor_scalar(out=r4g[:], in0=sq4g[:], scalar1=1.0 / 256, scalar2=EPS, op0=ALU.mult, op1=ALU.add), reads=[("sq4g", i) for i in range(4)], writes=["r4g"])
                A("pool", lambda e: e.tensor_tensor(out=r4g[:], in0=r4g[:], in1=bc(mhalf[:, 0:1], [128, 4]), op=ALU.pow), reads=["r4g", "mhalf"], writes=["r4g"])
                for hb in range(2):
                    bank, key = obk[hb]
                    A("dve", lambda e, bank=bank, hb=hb: e.tensor_tensor(out=on[:, hb * 512:(hb + 1) * 512].rearrange("p (a d) -> p a d", a=2), in0=bank[:, :].rearrange("p (a d) -> p a d", a=2),
                                                                          in1=bc(r4g[:, hb * 2:(hb + 1) * 2].unsqueeze(2), [128, 2, 256]), op=ALU.mult),
                      reads=[key, "r4g"], writes=[("yn", hb)])
                bank, key = PT.get()
                for k in range(8):
                    A("pe", lambda e, k=k, bank=bank: e.transpose(out=bank[:, k * 128:(k + 1) * 128], in_=on[:, k * 128:(k + 1) * 128], identity=identb[:]),
                      reads=[("yn", 0), ("yn", 1), "identb"], writes=[key])
                A("dve", lambda e, bank=bank, cs=cs, c=c: e.tensor_tensor(out=xbcT[:, 8:16, 4 + c * 128:4 + (c + 1) * 128], in0=bank[:, :].rearrange("p (k t) -> p k t", k=8),
                                                                         in1=sgT[:, :, cs], op=ALU.mult),
                  reads=[key] + [("sgT", b) for b in range(8)], writes=[("xbcT", 8 + b) for b in range(8)])
                for hb in range(2):
                    bank, key = PB.get()
                    for hh in range(2):
                        hd = hb * 2 + hh
                        A("pe", lambda e, bank=bank, hh=hh, hd=hd, c=c: e.matmul(bank[:, hh * 256:(hh + 1) * 256], lhsT=ktok[:, hd * 128:(hd + 1) * 128], rhs=vtok[:, c, hd * 256:(hd + 1) * 256], start=True, stop=True),
                          reads=["ktok", ("vtok", c)], writes=[key])
                    hs = slice(hb * 512, (hb + 1) * 512)
                    A("dve", lambda e, bank=bank, hs=hs: e.tensor_tensor(out=stmp[:, hs], in0=Sgla[l][:, hs], in1=bank[:, :], op=ALU.add),
                      reads=[key, ("Sgla", l)], writes=["stmp" if hb == 0 else "stmp2"])
                for hd in range(4):
                    A("act", lambda e, hd=hd: e.activation(out=Sgla[l][:, hd * 256:(hd + 1) * 256], in_=stmp[:, hd * 256:(hd + 1) * 256], func=AF.Copy, scale=egl[:, hd:hd + 1]),
                      reads=["stmp" if hd < 2 else "stmp2", "egl"], writes=[("Sgla", l)])
                A("pool", lambda e: e.tensor_copy(out=Gbf[l][:], in_=Sgla[l][:]), reads=[("Sgla", l)], writes=[("Gbf", l)])
            for j in range(2):
                for b in range(2):
                    wg, wgkey = next_w((l, "in", C_M + b * 1024 + j * 512))
                    for jj in range(4):
                        bank, key = PB.get()
                        for k in range(8):
                            A("pe", lambda e, k=k, jj=jj, wg=wg, bank=bank: e.matmul(bank[:, 0:N], lhsT=wg[:, k, jj * 128:(jj + 1) * 128], rhs=uT[:, k, :], start=(k == 0), stop=(k == 7)),
                              reads=uT_all + [wgkey], writes=[key])
                        A("act", lambda e, bank=bank, b=b, jj=jj: e.activation(out=gth[b][:, jj, :], in_=bank[:, 0:N], func=AF.Tanh, scale=0.5), reads=[key], writes=[("gth", b, jj)])
                    wbr, wbkey = next_w((l, "br", (b, j)))
                    for jj in range(4):
                        bank, key = PB.get()
                        for k in range(8):
                            A("pe", lambda e, k=k, jj=jj, wbr=wbr, bank=bank, b=b: e.matmul(bank[:, 0:N], lhsT=wbr[:, k, jj * 128:(jj + 1) * 128], rhs=xbcT[:, b * 8 + k, 4:4 + N], start=(k == 0), stop=(k == 7)),
                              reads=[("xbcT", b * 8 + k) for k in range(8)] + [wbkey], writes=[key])
                        if b == 0:
                            A("dve", lambda e, bank=bank, jj=jj: e.scalar_tensor_tensor(out=t0m[:, jj, :], in0=gth[0][:, jj, :], scalar=1.0, in1=bank[:, 0:N], op0=ALU.add, op1=ALU.mult),
                              reads=[key, ("gth", 0, jj)], writes=[("t0m", jj)])
                        else:
                            A("dve", lambda e, bank=bank, jj=jj: e.scalar_tensor_tensor(out=t1m[:, :], in0=gth[1][:, jj, :], scalar=1.0, in1=bank[:, 0:N], op0=ALU.add, op1=ALU.mult),
                              reads=[key, ("gth", 1, jj)], writes=["t1m"])
                            A("pool", lambda e, jj=jj, j=j: e.tensor_tensor(out=qkT[:, j * 4 + jj, :], in0=t0m[:, jj, :], in1=t1m[:, :], op=ALU.add),
                              reads=["t1m", ("t0m", jj)], writes=[("qkT", j * 4 + jj)])
            for j2 in range(2):
                wo, wokey = next_w((l, "out", j2))
                for c in range(NCH):
                    bank, key = PB.get()
                    for k in range(8):
                        A("pe", lambda e, k=k, c=c, wo=wo, bank=bank: e.matmul(bank[:, :], lhsT=qkT[:, k, c * 128:(c + 1) * 128], rhs=wo[:, k, :], start=(k == 0), stop=(k == 7)),
                          reads=[("qkT", k) for k in range(8)] + [wokey], writes=[key])
                    A("dve", lambda e, bank=bank, c=c, j2=j2: e.scalar_tensor_tensor(out=h[:, c, j2 * 512:(j2 + 1) * 512], in0=bank[:, :], scalar=0.5, in1=h[:, c, j2 * 512:(j2 + 1) * 512], op0=ALU.mult, op1=ALU.add),
                      reads=[key, ("h", c)], writes=[("h", c)])
            if first and not last_layer:
                A("pool", lambda e: e.memset(h[0:112, 0, :], 0.0), writes=[("h", 0)])

        for ti in range(ntiles):
            for c in range(NCH):
                gc = ti * NCH + c
                if gc == 0:
                    A("pool", lambda e: e.memset(h[:, 0, :], 0.0), writes=[("h", 0)])
                    A("sp", lambda e: e.dma_start(out=h[112:128, 0, :], in_=meta_d), writes=[("h", 0)], dma_sem=xsems[0])
                else:
                    r0 = (gc - 1) * 128
                    A("sp", lambda e, c=c, r0=r0: e.dma_start(out=h[:, c, :], in_=x_d[r0:r0 + 128, :]), writes=[("h", c)], dma_sem=xsems[c])
            for l in range(NL):
                tile_layer(ti, l)
            for c in range(NCH):
                gc = ti * NCH + c
                if gc == 0:
                    continue
                r0 = (gc - 1) * 128
                A("act", lambda e, c=c: e.activation(out=sqj[:], in_=h[:, c, :], func=AF.Square, accum_out=ssq[:, c:c + 1]), reads=[("h", c)], writes=[("ssq", c), "sqj"])
                A("dve", lambda e, c=c: e.tensor_scalar(out=rs[:, c:c + 1], in0=ssq[:, c:c + 1], scalar1=1.0 / D, scalar2=EPS, op0=ALU.mult, op1=ALU.add),
                  reads=[("ssq", c)], writes=[("rs", c)])
                A("pool", lambda e, c=c: e.tensor_tensor(out=rs[:, c:c + 1], in0=rs[:, c:c + 1], in1=mhalf[:], op=ALU.pow), reads=[("rs", c), "mhalf"], writes=[("rs", c)])
                ob, okey = ON.get()
                A("dve", lambda e, c=c, ob=ob: e.scalar_tensor_tensor(out=ob[:], in0=h[:, c, :], scalar=rs[:, c:c + 1], in1=fnw_bc[:], op0=ALU.mult, op1=ALU.mult),
                  reads=[("h", c), ("rs", c), "fnw"], writes=[okey])
                ok = ("out", gc)
                out_keys.append(ok)
                A("sp", lambda e, ob=ob, r0=r0: e.dma_start(out=out_d[r0:r0 + 128, :], in_=ob[:]), reads=[okey], writes=[ok], dma_sem=osem)
        A("sp", None, reads=out_keys)
        assert wstate["used"] == len(wlist)
        with nc.Block() as block:
            S.emit(block)
    return nc


def host_consts():
    i = np.arange(128)
    tri = (i[:, None] <= i[None, :]).astype(np.float32)
    strict = (i[:, None] > i[None, :]).astype(np.float32)
    mask0 = (i >= 112).astype(np.float32)[:, None]
    sel = np.zeros((128, 3, 128), np.float32)
    for r in range(3):
        sel[r, r, :] = 1.0
    return {"ident": np.eye(128, dtype=np.float32), "tri": tri, "strict": strict, "mask0": mask0, "sel": sel.reshape(128, 384)}


def make_in_maps(inputs, ncores, TR):
    f = lambda a: np.ascontiguousarray(np.asarray(a, dtype=np.float32))
    common = dict(host_consts())
    common["meta"] = f(inputs["meta_tokens"])
    common["w_in"] = f(inputs["w_in"])
    common["w_br"] = f(inputs["w_branch"])
    common["w_out"] = f(inputs["w_out"])
    common["normw"] = f(np.asarray(inputs["norm_w"]).reshape(2, 8, 128).transpose(2, 0, 1).reshape(128, 16))
    common["cw"] = f(np.asarray(inputs["conv_w"]).reshape(2, 4, 16, 128).transpose(3, 0, 2, 1).reshape(128, 128))
    common["cb"] = f(np.asarray(inputs["conv_b"]).reshape(1, 4096))
    common["dtb"] = f(inputs["dt_bias"])
    common["alog"] = f(inputs["a_log"])
    common["dskip"] = f(inputs["d_skip"])
    common["ssdnw"] = f(np.asarray(inputs["ssd_norm_w"]).reshape(2, 8, 128).transpose(2, 0, 1).reshape(128, 16))
    common["w2"] = f(np.asarray(inputs["gla_gate_w2"]).transpose(1, 0, 2).reshape(16, 1024))
    common["gb"] = f(np.asarray(inputs["gla_gate_b"]).reshape(1, 1024))
    common["gnw"] = f(np.asarray(inputs["gla_norm_w"]).reshape(2, 2, 128).transpose(2, 0, 1).reshape(128, 4))
    common["fnw"] = f(np.asarray(inputs["final_norm_w"]).reshape(1, 1024))
    x = np.asarray(inputs["x"], dtype=np.float32)
    maps = []
    for b in range(ncores):
        m = dict(common)
        m["x"] = np.ascontiguousarray(x[b, :TR])
        maps.append(m)
    return maps


_NC_CACHE = {}


def kernel(**inputs):
    x = np.asarray(inputs["x"])
    B, T, _ = x.shape
    NT = T // 128 + 1
    key = (NT,)
    if key not in _NC_CACHE:
        _NC_CACHE[key] = build(NT=NT, NCH=3, NL=2)
    nc = _NC_CACHE[key]
    maps = make_in_maps(inputs, B, T)
    res = run_bass_kernel_spmd(nc, maps, core_ids=list(range(B)))
    out = np.stack([np.asarray(r["out"], dtype=np.float32) for r in res.results], axis=0)
    return out
```

```python
import math
import numpy as np
import concourse.bass as bass
import concourse.mybir as mybir
from concourse.bass_utils import run_bass_kernel_spmd
from contextlib import ExitStack

F32 = mybir.dt.float32
BF16 = mybir.dt.bfloat16
AF = mybir.ActivationFunctionType
ALU = mybir.AluOpType

D = 1024
NIN = 8224
EPS = 1e-6
C_Z, C_XBC, C_DT, C_Q, C_K, C_V, C_G, C_ALR, C_M = 0, 1024, 3072, 3088, 3600, 4112, 5136, 6160, 6176


class Op:
    __slots__ = ("eng", "fn", "deps", "sig", "count", "sem", "is_dma", "inc")

    def __init__(self, eng, fn, is_dma=False):
        self.eng = eng
        self.fn = fn
        self.deps = set()
        self.sig = False
        self.count = 0
        self.sem = None
        self.is_dma = is_dma
        self.inc = 1


class Sched:
    ENGS = ("pe", "act", "dve", "pool", "sp")

    def __init__(self, nc, stack):
        self.nc = nc
        self.stack = stack
        self.ops = []
        self.res = {}
        self.engsem = {e: stack.enter_context(nc.semaphore("s_" + e)) for e in self.ENGS}
        self.dma_counts = {}
        self.alias = {}

    def new_sem(self, name):
        return self.stack.enter_context(self.nc.semaphore(name))

    def uniq(self, key):
        lst = self.alias.setdefault(key, [])
        k = ("__u", key, len(lst))
        lst.append(k)
        return k

    def add(self, eng, fn, reads=(), writes=(), dma_sem=None, raw_keys=False):
        op = Op(eng, fn, is_dma=dma_sem is not None)
        if not raw_keys:
            reads = [rr for r in reads for rr in self.alias.get(r, [r])]
            writes = [ww for w in writes for ww in self.alias.get(w, [w])]
        if dma_sem is not None:
            op.sem = dma_sem
            op.sig = True
            op.inc = 16
            c = self.dma_counts.get(id(dma_sem), 0) + 16
            self.dma_counts[id(dma_sem)] = c
            op.count = c
        for r in reads:
            st = self.res.get(r)
            if st is None:
                st = [None, []]
                self.res[r] = st
            if st[0] is not None:
                op.deps.add(st[0])
            st[1].append(op)
        for w in writes:
            st = self.res.get(w)
            if st is None:
                st = [None, []]
                self.res[w] = st
            if st[0] is not None:
                op.deps.add(st[0])
            last = {}
            for rd in st[1]:
                if rd is op:
                    continue
                if rd.is_dma:
                    op.deps.add(rd)
                else:
                    last[rd.eng] = rd
            for rd in last.values():
                op.deps.add(rd)
            st[0] = op
            st[1] = []
        op.deps.discard(op)
        self.ops.append(op)
        return op

    def finalize(self):
        for op in self.ops:
            if op.eng == "pe" and not op.is_dma:
                op.deps = {d for d in op.deps if not (d.eng == "pe" and not d.is_dma)}
            for d in op.deps:
                d.sig = True
        cnt = {e: 0 for e in self.ENGS}
        for op in self.ops:
            if op.is_dma:
                continue
            if op.sig:
                cnt[op.eng] += 1
                op.count = cnt[op.eng]
                op.sem = self.engsem[op.eng]

    def emit(self, block):
        self.finalize()
        by_eng = {e: [o for o in self.ops if o.eng == e] for e in self.ENGS}

        def run(engine, ops):
            waited = {}
            for op in ops:
                need = {}
                for d in op.deps:
                    k = id(d.sem)
                    if need.get(k, (None, 0))[1] < d.count:
                        need[k] = (d.sem, d.count)
                for k, (sem, c) in need.items():
                    if waited.get(k, 0) < c:
                        engine.wait_ge(sem, c)
                        waited[k] = c
                if op.fn is None:
                    continue
                ins = op.fn(engine)
                if op.sig:
                    ins.then_inc(op.sem, op.inc)

        @block.tensor
        def _(e):
            run(e, by_eng["pe"])

        @block.scalar
        def _(e):
            run(e, by_eng["act"])

        @block.vector
        def _(e):
            run(e, by_eng["dve"])

        @block.gpsimd
        def _(e):
            run(e, by_eng["pool"])

        @block.sync
        def _(e):
            run(e, by_eng["sp"])


class Ring:
    def __init__(self, items, name):
        self.items = items
        self.name = name
        self.i = 0

    def get(self):
        j = self.i % len(self.items)
        self.i += 1
        return self.items[j], (self.name, j)


def build(NT=33, NCH=3, NL=2, dbg=False):
    assert NT % NCH == 0
    N = NCH * 128
    ntiles = NT // NCH
    TR = (NT - 1) * 128
    nc = bass.Bass("TRN2", target_bir_lowering=False)

    def din(name, shape):
        return nc.dram_tensor(name, list(shape), F32, kind="ExternalInput").ap()

    x_d = din("x", [TR, D])
    meta_d = din("meta", [16, D])
    win_d = din("w_in", [2, D, NIN])
    wbr_d = din("w_br", [2, 2, D, D])
    wout_d = din("w_out", [2, D, D])
    normw_d = din("normw", [128, 16])
    cw_d = din("cw", [128, 128])
    cb_d = din("cb", [1, 2 * 2048])
    dtb_d = din("dtb", [2, 16])
    alog_d = din("alog", [2, 16])
    dskip_d = din("dskip", [2, 16])
    ssdnw_d = din("ssdnw", [128, 16])
    w2_d = din("w2", [16, 2 * 512])
    gb_d = din("gb", [1, 2 * 512])
    gnw_d = din("gnw", [128, 4])
    fnw_d = din("fnw", [1, D])
    ident_d = din("ident", [128, 128])
    tri_d = din("tri", [128, 128])
    strict_d = din("strict", [128, 128])
    mask0_d = din("mask0", [128, 1])
    sel_d = din("sel", [128, 3 * 128])
    out_d = nc.dram_tensor("out", [TR, D], F32, kind="ExternalOutput").ap()

    with ExitStack() as st:
        S = Sched(nc, st)

        def sb(name, shape, dt=F32):
            return st.enter_context(nc.sbuf_tensor("s_" + name, list(shape), dt))

        def ps(name, shape, dt=F32):
            return st.enter_context(nc.psum_tensor(name, list(shape), dt))

        A = S.add
        h = sb("h", [128, NCH, D])
        sqj = sb("sqj", [128, D], BF16)
        ubs = [sb("ub%d" % i, [128, D], BF16) for i in range(2)]
        UB = Ring(ubs, "ub")
        uT = sb("uT", [128, 8, N], BF16)
        NWB = 3
        wbufs = [sb("wb%d" % i, [128, 8, 512], BF16) for i in range(NWB)]
        wsems = [S.new_sem("wsem%d" % i) for i in range(NWB)]
        wmisc = sb("wmisc", [128, 2, 8, 32], BF16)
        NDG = 4
        dgs = [sb("dg%d" % i, [128, 4, 128], BF16) for i in range(NDG)]
        DG = Ring(dgs, "dg")
        XW = N + 4
        xbcT = sb("xbcT", [128, 16, XW], BF16)
        hist = sb("hist", [128, 2, 16, 4], BF16)
        x2 = sb("x2", [128, NCH, D], BF16)
        B2tok = sb("B2tok", [128, NCH, 512], BF16)
        BCT = sb("BCT", [128, 8, N], BF16)
        zs = sb("zs", [128, NCH, D], BF16)
        qkT = sb("qkT", [128, 8, N], BF16)
        vtok = sb("vtok", [128, NCH, D], BF16)
        sgT = sb("sgT", [128, 8, N], BF16)
        alrT = sb("alrT", [128, N], BF16)
        ths = [sb("th%d" % i, [128, 512], BF16) for i in range(2)]
        TH = Ring(ths, "th")
        gws = [sb("gw%d" % i, [128, 512], BF16) for i in range(2)]
        GW = Ring(gws, "gw")
        ssq = sb("ssq", [128, NCH])
        rs = sb("rs", [128, NCH])
        dtr = sb("dtr", [128, NCH, 16])
        dte_x = sb("dte_x", [128, NCH, 16])
        dt_t = sb("dt_t", [128, NCH, 16])
        da_t = sb("da_t", [128, NCH, 16])
        ecumh = sb("ecumh", [128, NCH, 16])
        dte = sb("dte", [128, NCH, 16])
        cdb = sb("cdb", [128, NCH, 16])
        wq = sb("wq", [128, NCH, 16])
        dth = sb("dth", [128, NCH, 16])
        sq4 = sb("sq4", [128, 4])
        r4 = sb("r4", [128, 4])
        sq4g = sb("sq4g", [128, 4])
        r4g = sb("r4g", [128, 4])
        egl = sb("egl", [128, 4])
        At = sb("At", [128, 8, 128])
        Lm = sb("Lm", [128, 16, 128], BF16)
        MT = sb("MT", [128, 16, 128], BF16)
        scm = sb("scm", [128, 4, 128], BF16)
        xdt = sb("xdt", [128, D], BF16)
        xw = sb("xw", [128, D], BF16)
        xd = sb("xd", [128, D], BF16)
        t1 = sb("t1", [128, D])
        yn = sb("yn", [128, D], BF16)
        stmp = sb("stmp", [128, D])
        e1 = t1[:, 0:512]
        l1 = t1[:, 512:1024]
        eg = stmp[:, 0:512]
        emg = stmp[:, 512:1024]
        on = yn
        qtl = sb("qtl", [128, 4, 128], BF16)
        ktl = sb("ktl", [128, 4, 128], BF16)
        ktok = sb("ktok", [128, 512], BF16)
        attm = sb("attm", [128, 4, 128], BF16)
        gth = [sb("gth%d" % i, [128, 4, N], BF16) for i in range(2)]
        t0m = sb("t0m", [128, 4, N])
        t1ms = [sb("t1m%d" % i, [128, N]) for i in range(2)]
        T1M = Ring(t1ms, "t1m")
        Sssd = [sb("Sssd%d" % l, [128, D]) for l in range(2)]
        Sbf = [sb("Sbf%d" % l, [128, D], BF16) for l in range(2)]
        Sgla = [sb("Sgla%d" % l, [128, D]) for l in range(2)]
        Gbf = [sb("Gbf%d" % l, [128, D], BF16) for l in range(2)]
        identf = sb("identf", [128, 128])
        identb = sb("identb", [128, 128], BF16)
        tri = sb("tri", [128, 128])
        tri16 = sb("tri16", [128, 128])
        strict = sb("strict", [128, 128])
        maskb = sb("maskb", [128, 128], BF16)
        maskq = sb("maskq", [128, 128], BF16)
        onesf = sb("onesf", [128, 128])
        sel = sb("sel", [128, 3, 128], BF16)
        selN = sb("selN", [128, 3, N], BF16)
        rows = sb("rows", [128, 2048], BF16)
        gb8 = sb("gb8", [8, 128])
        gb8h = sb("gb8h", [8, 128], BF16)
        gb8hf = sb("gb8hf", [8, 128])
        gb8l = sb("gb8l", [8, 128], BF16)
        mhalf = sb("mhalf", [128, 1])
        mask0 = sb("mask0", [128, 1])
        normw = sb("normw", [128, 16])
        cw = sb("cw", [128, 128])
        dtb_bc = sb("dtb_bc", [128, 2, 16])
        a_bc = sb("a_bc", [128, 2, 16])
        dskh_bc = sb("dskh_bc", [128, 2, 16])
        ssdnw = sb("ssdnw", [128, 16])
        w2b = sb("w2b", [128, 1024], BF16)
        gnwh = sb("gnwh", [128, 4])
        fnw_bc = sb("fnw_bc", [128, D])
        onorm = [sb("onorm%d" % i, [128, D]) for i in range(1)]
        ON = Ring(onorm, "onorm")
        pbs = [ps("pb%d" % i, [128, 512]) for i in range(6)]
        PB = Ring(pbs, "pb")
        pts = [ps("pt%d" % i, [128, 1024], BF16) for i in range(2)]
        PT = Ring(pts, "pt")

        csem = S.new_sem("csem")

        csemp = S.new_sem("csemp")
        cops = []
        copsp = []

        def cload(dst, src, key, eng="sp"):
            if eng == "sp":
                cops.append(A(eng, lambda e: e.dma_start(out=dst, in_=src), writes=[S.uniq(key)], dma_sem=csem, raw_keys=True))
            else:
                copsp.append(A(eng, lambda e: e.dma_start(out=dst, in_=src), writes=[S.uniq(key)], dma_sem=csemp, raw_keys=True))

        cload(identf[:], ident_d, "identf")
        cload(tri[:], tri_d, "tri")
        cload(strict[:], strict_d, "strict")
        cload(mask0[:], mask0_d, "mask0")
        cload(normw[:], normw_d, "normw")
        cload(cw[:], cw_d, "cw")
        cload(ssdnw[:], ssdnw_d, "ssdnw")
        cload(gnwh[:], gnw_d, "gnwh")
        cload(gb8[:], gb_d.rearrange("o (a b) -> (o a) b", a=8), "gb8")
        cload(fnw_bc[:], fnw_d[0:1, :].partition_broadcast(128), "fnw")
        for l in range(2):
            cload(dtb_bc[:, l, :], dtb_d[l:l + 1, :].partition_broadcast(128), "dtb")
            cload(a_bc[:, l, :], alog_d[l:l + 1, :].partition_broadcast(128), "a_bc")
            cload(dskh_bc[:, l, :], dskip_d[l:l + 1, :].partition_broadcast(128), "dskh")
        A("pool", lambda e: e.memset(rows[:], 0.0), writes=["rows0"])
        A("pool", lambda e: e.memset(alrT[:], 0.0), writes=["alrT"])
        A("pool", lambda e: e.memset(w2b[:], 0.0), writes=["w2b0"])
        copsp.append(A("pool", lambda e: e.dma_start(out=rows[0:1, :], in_=cb_d[:, 0:2048]), reads=["rows0"], writes=[S.uniq("rows")], dma_sem=csemp, raw_keys=True))
        copsp.append(A("pool", lambda e: e.dma_start(out=rows[1:2, :], in_=cb_d[:, 2048:4096]), reads=["rows0"], writes=[S.uniq("rows")], dma_sem=csemp, raw_keys=True))
        cload(sel[:].rearrange("p r m -> p (r m)"), sel_d, "sel", eng="pool")
        copsp.append(A("pool", lambda e: e.dma_start(out=w2b[0:16, :], in_=w2_d), reads=["w2b0"], writes=[S.uniq("w2b")], dma_sem=csemp, raw_keys=True))
        for l in range(2):
            cload(wmisc[:, l, :, 0:16], win_d[l, :, C_DT:C_DT + 16].rearrange("(k p) n -> p k n", p=128), "wmisc", eng="pool")
            cload(wmisc[:, l, :, 16:32], win_d[l, :, C_ALR:C_ALR + 16].rearrange("(k p) n -> p k n", p=128), "wmisc", eng="pool")

        for o in cops:
            o.count = 16 * len(cops)
        for o in copsp:
            o.count = 16 * len(copsp)
        A("dve", lambda e: e.tensor_copy(out=identb[:], in_=identf[:]), reads=["identf"], writes=["identb"])
        A("dve", lambda e: e.tensor_copy(out=maskb[:], in_=tri[:]), reads=["tri"], writes=["maskb"])
        A("dve", lambda e: e.tensor_scalar(out=maskq[:], in0=tri[:], scalar1=0.25, scalar2=None, op0=ALU.mult), reads=["tri"], writes=["maskq"])
        A("dve", lambda e: e.tensor_scalar(out=tri16[:], in0=tri[:], scalar1=-1.0 / 16.0, scalar2=None, op0=ALU.mult), reads=["tri"], writes=["tri16"])
        A("pool", lambda e: e.memset(onesf[:], 1.0), writes=["onesf"])
        A("pool", lambda e: e.memset(mhalf[:], -0.5), writes=["mhalf"])
        for r in range(3):
            A("pool", lambda e, r=r: e.tensor_copy(out=selN[:, r, :], in_=bc(sel[:, r, 0:1], [128, N])), reads=["sel"], writes=["selN"])
        A("pool", lambda e: e.memset(hist[:], 0.0), writes=["hist"])
        A("pool", lambda e: e.memset(xbcT[:], 0.0), writes=[("xbcT", b) for b in range(16)])
        for l in range(2):
            A("pool", lambda e, l=l: e.memset(Sssd[l][:], 0.0), writes=[("Sssd", l)])
            A("pool", lambda e, l=l: e.memset(Sbf[l][:], 0.0), writes=[("Sbf", l)])
            A("pool", lambda e, l=l: e.memset(Sgla[l][:], 0.0), writes=[("Sgla", l)])
            A("pool", lambda e, l=l: e.memset(Gbf[l][:], 0.0), writes=[("Gbf", l)])
        A("act", lambda e: e.activation(out=a_bc[:], in_=a_bc[:], func=AF.Exp), reads=["a_bc"], writes=["a_bc"])
        A("dve", lambda e: e.tensor_scalar(out=a_bc[:], in0=a_bc[:], scalar1=-1.0, scalar2=None, op0=ALU.mult), reads=["a_bc"], writes=["a_bc"])
        A("dve", lambda e: e.tensor_scalar(out=dskh_bc[:], in0=dskh_bc[:], scalar1=0.5, scalar2=None, op0=ALU.mult), reads=["dskh"], writes=["dskh"])
        A("dve", lambda e: e.tensor_scalar(out=gnwh[:], in0=gnwh[:], scalar1=0.5, scalar2=None, op0=ALU.mult), reads=["gnwh"], writes=["gnwh"])
        A("dve", lambda e: e.tensor_copy(out=gb8h[:], in_=gb8[:]), reads=["gb8"], writes=["gb8h"])
        A("dve", lambda e: e.tensor_copy(out=gb8hf[:], in_=gb8h[:]), reads=["gb8h"], writes=["gb8hf"])
        A("dve", lambda e: e.tensor_tensor(out=gb8l[:], in0=gb8[:], in1=gb8hf[:], op=ALU.subtract), reads=["gb8", "gb8hf"], writes=["gb8l"])
        csem2 = S.new_sem("csem2")
        cops2 = []
        for a8 in range(8):
            cops2.append(A("sp", lambda e, a8=a8: e.dma_start(out=rows[2:3, a8 * 128:(a8 + 1) * 128], in_=gb8h[a8:a8 + 1, :]), reads=["gb8h", "rows0"], writes=[S.uniq("rows")], dma_sem=csem2))
            cops2.append(A("sp", lambda e, a8=a8: e.dma_start(out=rows[2:3, 1024 + a8 * 128:1024 + (a8 + 1) * 128], in_=gb8l[a8:a8 + 1, :]), reads=["gb8l", "rows0"], writes=[S.uniq("rows")], dma_sem=csem2))
        for o in cops2:
            o.count = 16 * len(cops2)

        def wsrc(l, kind, idx):
            if kind == "in":
                src = win_d[l, :, idx:idx + 512]
            elif kind == "br":
                b, j = idx
                src = wbr_d[l, b, :, j * 512:(j + 1) * 512]
            else:
                src = wout_d[l, :, idx * 512:(idx + 1) * 512]
            return src.rearrange("(k p) n -> p k n", p=128)

        def layer_groups(l):
            g = []
            for i in range(4):
                g.append((l, "in", C_XBC + 512 * i))
            for i in range(2):
                g.append((l, "in", C_Z + 512 * i))
            g.append((l, "in", C_Q))
            g.append((l, "in", C_K))
            for i in range(2):
                g.append((l, "in", C_V + 512 * i))
            for i in range(2):
                g.append((l, "in", C_G + 512 * i))
            for j in range(2):
                for b in range(2):
                    g.append((l, "in", C_M + b * 1024 + j * 512))
                    g.append((l, "br", (b, j)))
            for j in range(2):
                g.append((l, "out", j))
            return g

        wlist = []
        for ti in range(ntiles):
            for l in range(NL):
                wlist += layer_groups(l)
        wstate = {"issued": 0, "used": 0}
        PF = 2

        GP = len(layer_groups(0)) * NL
        wscr = nc.dram_tensor("wscr", [GP, 128, 8 * 512], BF16, kind="Internal").ap()
        wsemh = [S.new_sem("wsemh%d" % i) for i in range(NWB)]
        wsemst = [S.new_sem("wsemst%d" % i) for i in range(NWB)]

        def w_issue_upto(n):
            while wstate["issued"] < min(n, len(wlist)):
                i = wstate["issued"]
                l, kind, idx = wlist[i]
                slot = i % NWB
                if i < GP:
                    src = wsrc(l, kind, idx)
                    A("pool", lambda e, slot=slot, src=src: e.dma_start(out=wbufs[slot][:], in_=src),
                      writes=[("wb", slot)], dma_sem=wsems[slot])
                    if ntiles > 1:
                        A("sp", lambda e, slot=slot, i=i: e.dma_start(out=wscr[i], in_=wbufs[slot][:].rearrange("p k n -> p (k n)")),
                          reads=[("wb", slot)], writes=[("wscr", i)], dma_sem=wsemst[slot])
                else:
                    g = i % GP
                    A("sp", lambda e, slot=slot, g=g: e.dma_start(out=wbufs[slot][:].rearrange("p k n -> p (k n)"), in_=wscr[g]),
                      reads=[("wscr", g)], writes=[("wb", slot)], dma_sem=wsemh[slot])
                wstate["issued"] += 1

        def next_w(expect):
            i = wstate["used"]
            assert wlist[i] == expect, (wlist[i], expect)
            w_issue_upto(i + 1 + PF)
            wstate["used"] += 1
            return wbufs[i % NWB], ("wb", i % NWB)

        def bc(ap, shape):
            return ap.to_broadcast(list(shape))

        HALF_LN = math.log(0.5)
        QS_LN = math.log(128.0 ** -0.5)

        def silu2_evac(bank_ap, out_ap, key, wkey, ncols):
            th, tkey = TH.get()
            A("act", lambda e: e.activation(out=th[:, 0:ncols], in_=bank_ap, func=AF.Tanh, scale=0.5), reads=[key], writes=[tkey])
            A("dve", lambda e: e.scalar_tensor_tensor(out=out_ap, in0=th[:, 0:ncols], scalar=1.0, in1=bank_ap, op0=ALU.add, op1=ALU.mult),
              reads=[tkey, key], writes=[wkey])

        uT_all = [("uT", c) for c in range(NCH)]
        out_keys = []
        osem = S.new_sem("osem")
        xsems = [S.new_sem("xsem%d" % i) for i in range(NCH)]

        def tile_layer(ti, l):
            first = ti == 0
            last_layer = l == NL - 1
            for c in range(NCH):
                A("act", lambda e, c=c: e.activation(out=sqj[:], in_=h[:, c, :], func=AF.Square, accum_out=ssq[:, c:c + 1]),
                  reads=[("h", c)], writes=[("ssq", c), "sqj"])
                A("dve", lambda e, c=c: e.tensor_scalar(out=rs[:, c:c + 1], in0=ssq[:, c:c + 1], scalar1=1.0 / D, scalar2=EPS, op0=ALU.mult, op1=ALU.add),
                  reads=[("ssq", c)], writes=[("rs", c)])
                A("pool", lambda e, c=c: e.tensor_tensor(out=rs[:, c:c + 1], in0=rs[:, c:c + 1], in1=mhalf[:], op=ALU.pow),
                  reads=[("rs", c), "mhalf"], writes=[("rs", c)])
                ub, ukey = UB.get()
                A("dve", lambda e, c=c, ub=ub: e.tensor_scalar(out=ub[:], in0=h[:, c, :], scalar1=rs[:, c:c + 1], scalar2=None, op0=ALU.mult),
                  reads=[("h", c), ("rs", c)], writes=[ukey])
                bank, key = PT.get()
                for k in range(8):
                    A("pe", lambda e, k=k, ub=ub, bank=bank: e.transpose(out=bank[:, k * 128:(k + 1) * 128], in_=ub[:, k * 128:(k + 1) * 128], identity=identb[:]),
                      reads=[ukey, "identb"], writes=[key])
                A("dve", lambda e, c=c, bank=bank: e.tensor_tensor(out=uT[:, :, c * 128:(c + 1) * 128], in0=bank[:, :].rearrange("p (k t) -> p k t", k=8),
                                                                    in1=bc(normw[:, l * 8:(l + 1) * 8].unsqueeze(2), [128, 8, 128]), op=ALU.mult),
                  reads=[key, "normw"], writes=[("uT", c)])
            for c in range(NCH):
                bank, key = PB.get()
                for k in range(8):
                    A("pe", lambda e, k=k, c=c, bank=bank: e.matmul(bank[:, 0:16], lhsT=uT[:, k, c * 128:(c + 1) * 128], rhs=wmisc[:, l, k, 0:16], start=(k == 0), stop=(k == 7)),
                      reads=[("uT", c), "wmisc"], writes=[key])
                A("dve", lambda e, c=c, bank=bank: e.tensor_tensor(out=dtr[:, c, :], in0=bank[:, 0:16], in1=dtb_bc[:, l, :], op=ALU.add),
                  reads=[key, "dtb"], writes=[("dtr", c)])
                A("act", lambda e, c=c: e.activation(out=dte_x[:, c, :], in_=dtr[:, c, :], func=AF.Exp), reads=[("dtr", c)], writes=[("dte_x", c)])
                A("act", lambda e, c=c: e.activation(out=dt_t[:, c, :], in_=dte_x[:, c, :], func=AF.Ln, bias=1.0), reads=[("dte_x", c)], writes=[("dt", c)])
                if first and c == 0:
                    A("dve", lambda e, c=c: e.tensor_scalar(out=dt_t[:, c, :], in0=dt_t[:, c, :], scalar1=mask0[:, 0:1], scalar2=None, op0=ALU.mult),
                      reads=[("dt", c), "mask0"], writes=[("dt", c)])
                A("dve", lambda e, c=c: e.tensor_tensor(out=da_t[:, c, :], in0=dt_t[:, c, :], in1=a_bc[:, l, :], op=ALU.mult),
                  reads=[("dt", c), "a_bc"], writes=[("da", c)])
                bank2, key2 = PB.get()
                A("pe", lambda e, c=c, bank2=bank2: e.matmul(bank2[:, 0:16], lhsT=tri[:], rhs=da_t[:, c, :], start=True, stop=True), reads=[("da", c), "tri"], writes=[key2])
                A("pe", lambda e, c=c, bank2=bank2: e.matmul(bank2[:, 16:32], lhsT=onesf[:], rhs=da_t[:, c, :], start=True, stop=True), reads=[("da", c), "onesf"], writes=[key2])
                A("pe", lambda e, c=c, bank2=bank2: e.matmul(bank2[:, 32:48], lhsT=strict[:], rhs=da_t[:, c, :], start=True, stop=True), reads=[("da", c), "strict"], writes=[key2])
                A("act", lambda e, c=c, bank2=bank2: e.activation(out=ecumh[:, c, :], in_=bank2[:, 0:16], func=AF.Exp, bias=HALF_LN), reads=[key2], writes=[("ecumh", c)])
                A("act", lambda e, c=c, bank2=bank2: e.activation(out=cdb[:, c, :], in_=bank2[:, 16:32], func=AF.Exp), reads=[key2], writes=[("cdb", c)])
                A("act", lambda e, c=c, bank2=bank2: e.activation(out=dte[:, c, :], in_=bank2[:, 32:48], func=AF.Exp), reads=[key2], writes=[("dte", c)])
                A("dve", lambda e, c=c: e.scalar_tensor_tensor(out=wq[:, c, :], in0=dt_t[:, c, :], scalar=0.25, in1=dte[:, c, :], op0=ALU.mult, op1=ALU.mult),
                  reads=[("dt", c), ("dte", c)], writes=[("wq", c)])
                A("dve", lambda e, c=c: e.tensor_scalar(out=dth[:, c, :], in0=dt_t[:, c, :], scalar1=0.5, scalar2=None, op0=ALU.mult),
                  reads=[("dt", c)], writes=[("dth", c)])
            bank, key = PB.get()
            for k in range(8):
                A("pe", lambda e, k=k, bank=bank: e.matmul(bank[0:16, 0:N], lhsT=wmisc[:, l, k, 16:32], rhs=uT[:, k, :], start=(k == 0), stop=(k == 7)),
                  reads=uT_all + ["wmisc"], writes=[key])
            A("act", lambda e, bank=bank: e.activation(out=alrT[0:16, :], in_=bank[0:16, 0:N], func=AF.Copy), reads=[key], writes=["alrT"])
            for gi in range(4):
                wb, wkey = next_w((l, "in", C_XBC + 512 * gi))
                for j in range(4):
                    blk = gi * 4 + j
                    bank, key = PB.get()
                    for k in range(8):
                        A("pe", lambda e, k=k, j=j, wb=wb, bank=bank: e.matmul(bank[:, 0:N], lhsT=wb[:, k, j * 128:(j + 1) * 128], rhs=uT[:, k, :], start=(k == 0), stop=(k == 7)),
                          reads=uT_all + [wkey], writes=[key])
                    A("act", lambda e, blk=blk, bank=bank: e.activation(out=xbcT[:, blk, 4:4 + N], in_=bank[:, 0:N], func=AF.Copy), reads=[key], writes=[("xbcT", blk)])
            allx = [("xbcT", b) for b in range(16)]
            A("pool", lambda e: e.tensor_copy(out=xbcT[:, :, 1:4], in_=hist[:, l, :, 1:4]), reads=[("hist", l)], writes=allx)
            A("pool", lambda e: e.tensor_copy(out=hist[:, l, :, 1:4], in_=xbcT[:, :, N + 1:N + 4]), reads=allx, writes=[("hist", l)])
            dgof = {}

            def gen_diag(blk):
                dg, dkey = DG.get()
                for tap in range(4):
                    col = (l * 16 + blk) * 4 + tap
                    A("dve", lambda e, dg=dg, tap=tap, col=col: e.tensor_scalar(out=dg[:, tap, :], in0=identb[:], scalar1=cw[:, col:col + 1], scalar2=None, op0=ALU.mult),
                      reads=["identb", "cw"], writes=[dkey])
                dgof[blk] = (dg, dkey)

            def conv_tok(c, blks, out_ap, wkey):
                bank, key = PB.get()
                for j, blk in enumerate(blks):
                    dg, dkey = dgof[blk]
                    o = bank[:, j * 128:(j + 1) * 128]
                    A("pe", lambda e, o=o, blk=blk: e.matmul(o, lhsT=sel[:, l, :], rhs=rows[:, blk * 128:(blk + 1) * 128], start=True, stop=False),
                      reads=["sel", "rows"], writes=[key])
                    for tap in range(4):
                        A("pe", lambda e, o=o, blk=blk, tap=tap, dg=dg, c=c: e.matmul(o, lhsT=xbcT[:, blk, c * 128 + 1 + tap:c * 128 + 1 + tap + 128], rhs=dg[:, tap, :], start=False, stop=(tap == 3)),
                          reads=[("xbcT", blk), dkey], writes=[key])
                silu2_evac(bank[:, 0:128 * len(blks)], out_ap, key, wkey, 128 * len(blks))

            for half in range(2):
                blks = [half * 4 + j for j in range(4)]
                for b in blks:
                    gen_diag(b)
                for c in range(NCH):
                    conv_tok(c, blks, x2[:, c, half * 512:(half + 1) * 512], ("x2", c))
            for blk in range(8, 16):
                if blk % 4 == 0:
                    for b4 in range(blk, blk + 4):
                        gen_diag(b4)
                    if blk == 8:
                        for c in range(NCH):
                            conv_tok(c, [8, 9, 10, 11], B2tok[:, c, :], ("B2tok", c))
                dg, dkey = dgof[blk]
                bank, key = PB.get()
                A("pe", lambda e, bank=bank, blk=blk: e.matmul(bank[:, 0:N], lhsT=rows[:, blk * 128:(blk + 1) * 128], rhs=selN[:, l, :], start=True, stop=False),
                  reads=["selN", "rows"], writes=[key])
                for tap in range(4):
                    A("pe", lambda e, bank=bank, blk=blk, tap=tap, dg=dg: e.matmul(bank[:, 0:N], lhsT=dg[:, tap, :], rhs=xbcT[:, blk, 1 + tap:1 + tap + N], start=False, stop=(tap == 3)),
                      reads=[("xbcT", blk), dkey], writes=[key])
                silu2_evac(bank[:, 0:N], BCT[:, blk - 8, :], key, ("BCT", blk - 8), N)
            for gi in range(2):
                wb, wkey = next_w((l, "in", C_Z + 512 * gi))
                for c in range(NCH):
                    bank, key = PB.get()
                    for k in range(8):
                        A("pe", lambda e, k=k, c=c, wb=wb, bank=bank: e.matmul(bank[:, :], lhsT=uT[:, k, c * 128:(c + 1) * 128], rhs=wb[:, k, :], start=(k == 0), stop=(k == 7)),
                          reads=[("uT", c), wkey], writes=[key])
                    silu2_evac(bank[:, :], zs[:, c, gi * 512:(gi + 1) * 512], key, ("zs", c), 512)
            for qk in range(2):
                wb, wkey = next_w((l, "in", C_Q if qk == 0 else C_K))
                for j in range(4):
                    bank, key = PB.get()
                    for k in range(8):
                        A("pe", lambda e, k=k, j=j, wb=wb, bank=bank: e.matmul(bank[:, 0:N], lhsT=wb[:, k, j * 128:(j + 1) * 128], rhs=uT[:, k, :], start=(k == 0), stop=(k == 7)),
                          reads=uT_all + [wkey], writes=[key])
                    A("act", lambda e, bank=bank, qk=qk, j=j: e.activation(out=qkT[:, qk * 4 + j, :], in_=bank[:, 0:N], func=AF.Copy), reads=[key], writes=[("qkT", qk * 4 + j)])
            for gi in range(2):
                wb, wkey = next_w((l, "in", C_V + 512 * gi))
                for c in range(NCH):
                    bank, key = PB.get()
                    for k in range(8):
                        A("pe", lambda e, k=k, c=c, wb=wb, bank=bank: e.matmul(bank[:, :], lhsT=uT[:, k, c * 128:(c + 1) * 128], rhs=wb[:, k, :], start=(k == 0), stop=(k == 7)),
                          reads=[("uT", c), wkey], writes=[key])
                    A("act", lambda e, bank=bank, c=c, gi=gi: e.activation(out=vtok[:, c, gi * 512:(gi + 1) * 512], in_=bank[:, :], func=AF.Copy), reads=[key], writes=[("vtok", c)])
            for gi in range(2):
                wb, wkey = next_w((l, "in", C_G + 512 * gi))
                for j in range(4):
                    blk = gi * 4 + j
                    bank, key = PB.get()
                    for k in range(8):
                        A("pe", lambda e, k=k, j=j, wb=wb, bank=bank: e.matmul(bank[:, 0:N], lhsT=wb[:, k, j * 128:(j + 1) * 128], rhs=uT[:, k, :], start=(k == 0), stop=(k == 7)),
                          reads=uT_all + [wkey], writes=[key])
                    gw, gkey = GW.get()
                    th, tkey = TH.get()
                    A("act", lambda e, bank=bank, gw=gw, blk=blk: e.activation(out=gw[:, 0:N], in_=bank[:, 0:N], func=AF.Copy, scale=gnwh[:, l * 2 + (blk % 2):l * 2 + (blk % 2) + 1]),
                      reads=[key, "gnwh"], writes=[gkey])
                    A("act", lambda e, bank=bank, th=th: e.activation(out=th[:, 0:N], in_=bank[:, 0:N], func=AF.Tanh, scale=0.5), reads=[key], writes=[tkey])
                    A("dve", lambda e, gw=gw, th=th, blk=blk: e.scalar_tensor_tensor(out=sgT[:, blk, :], in0=th[:, 0:N], scalar=1.0, in1=gw[:, 0:N], op0=ALU.add, op1=ALU.mult),
                      reads=[gkey, tkey], writes=[("sgT", blk)])
            for c in range(NCH):
                cs = slice(c * 128, (c + 1) * 128)
                for hv in range(2):
                    for hh in range(8):
                        hd = hv * 8 + hh
                        A("dve", lambda e, hh=hh, hd=hd, c=c: e.tensor_scalar(out=At[:, hh, :], in0=strict[:], scalar1=da_t[:, c, hd:hd + 1], scalar2=None, op0=ALU.mult),
                          reads=["strict", ("da", c)], writes=[("At", hh)])
                    for qd in range(2):
                        bank, key = PB.get()
                        for hq in range(4):
                            hh = qd * 4 + hq
                            A("pe", lambda e, bank=bank, hq=hq, hh=hh: e.matmul(bank[:, hq * 128:(hq + 1) * 128], lhsT=At[:, hh, :], rhs=tri[:], start=True, stop=True),
                              reads=[("At", hh), "tri"], writes=[key])
                        h0 = hv * 8 + qd * 4
                        A("act", lambda e, bank=bank, h0=h0: e.activation(out=Lm[:, h0:h0 + 4, :], in_=bank[:, :].rearrange("p (a t) -> p a t", a=4), func=AF.Exp),
                          reads=[key], writes=[("Lm", h0 // 4)])
                bank, key = PB.get()
                for g in range(4):
                    A("pe", lambda e, bank=bank, g=g, cs=cs: e.matmul(bank[:, g * 128:(g + 1) * 128], lhsT=BCT[:, g, cs], rhs=BCT[:, 4 + g, cs], start=True, stop=True),
                      reads=[("BCT", g), ("BCT", 4 + g)], writes=[key])
                A("dve", lambda e, bank=bank: e.tensor_tensor(out=scm[:], in0=bank[:, :].rearrange("p (g t) -> p g t", g=4), in1=bc(maskq[:, :].unsqueeze(1), [128, 4, 128]), op=ALU.mult),
                  reads=[key, "maskq"], writes=["scm"])
                A("dve", lambda e: e.tensor_tensor(out=MT[:].rearrange("p (g a) t -> p g a t", g=4), in0=Lm[:].rearrange("p (g a) t -> p g a t", g=4),
                                                   in1=bc(scm[:].unsqueeze(2), [128, 4, 4, 128]), op=ALU.mult),
                  reads=["scm"] + [("Lm", i) for i in range(4)], writes=["MT"])
                x3 = x2[:, c, :].rearrange("p (a d) -> p a d", a=16)
                A("pool", lambda e, x3=x3, c=c: e.tensor_tensor(out=xdt[:].rearrange("p (a d) -> p a d", a=16), in0=x3, in1=bc(dth[:, c, :].unsqueeze(2), [128, 16, 64]), op=ALU.mult),
                  reads=[("x2", c), ("dth", c)], writes=["xdt"])
                A("pool", lambda e, x3=x3, c=c: e.tensor_tensor(out=xw[:].rearrange("p (a d) -> p a d", a=16), in0=x3, in1=bc(wq[:, c, :].unsqueeze(2), [128, 16, 64]), op=ALU.mult),
                  reads=[("x2", c), ("wq", c)], writes=["xw"])
                A("pool", lambda e, x3=x3: e.tensor_tensor(out=xd[:].rearrange("p (a d) -> p a d", a=16), in0=x3, in1=bc(dskh_bc[:, l, :].unsqueeze(2), [128, 16, 64]), op=ALU.mult),
                  reads=[("x2", c), "dskh"], writes=["xd"])
                ydk = []
                for hb in range(2):
                    bank, key = PB.get()
                    ydk.append((bank, key))
                    A("pe", lambda e, bank=bank, hb=hb: e.matmul(bank[:, :], lhsT=identb[:], rhs=xd[:, hb * 512:(hb + 1) * 512], start=True, stop=False),
                      reads=["identb", "xd"], writes=[key])
                    for h8 in range(8):
                        hd = hb * 8 + h8
                        A("pe", lambda e, bank=bank, h8=h8, hd=hd: e.matmul(bank[:, h8 * 64:(h8 + 1) * 64], lhsT=MT[:, hd, :], rhs=xdt[:, hd * 64:(hd + 1) * 64], start=False, stop=(h8 == 7)),
                          reads=["MT", "xdt"], writes=[key])
                for hb in range(2):
                    bank, key = PB.get()
                    for gg in range(2):
                        g = hb * 2 + gg
                        A("pe", lambda e, bank=bank, gg=gg, g=g, cs=cs: e.matmul(bank[:, gg * 256:(gg + 1) * 256], lhsT=BCT[:, 4 + g, cs], rhs=Sbf[l][:, g * 256:(g + 1) * 256], start=True, stop=True),
                          reads=[("BCT", 4 + g), ("Sbf", l)], writes=[key])
                    hs = slice(hb * 512, (hb + 1) * 512)
                    A("dve", lambda e, bank=bank, hb=hb, hs=hs, c=c: e.tensor_tensor(out=t1[:, hs].rearrange("p (a d) -> p a d", a=8), in0=bank[:, :].rearrange("p (a d) -> p a d", a=8),
                                                                                  in1=bc(ecumh[:, c, hb * 8:(hb + 1) * 8].unsqueeze(2), [128, 8, 64]), op=ALU.mult),
                      reads=[key, ("ecumh", c)], writes=[("t1", hb)])
                    ybank, ykey = ydk[hb]
                    A("dve", lambda e, ybank=ybank, hs=hs: e.tensor_tensor(out=t1[:, hs], in0=t1[:, hs], in1=ybank[:, :], op=ALU.add),
                      reads=[ykey, ("t1", hb)], writes=[("t1", hb)])
                    A("pool", lambda e, hs=hs, c=c: e.tensor_tensor(out=t1[:, hs], in0=t1[:, hs], in1=zs[:, c, hs], op=ALU.mult),
                      reads=[("t1", hb), ("zs", c)], writes=[("t1", hb)])
                for g in range(4):
                    A("act", lambda e, g=g: e.activation(out=sqj[:, 0:256], in_=t1[:, g * 256:(g + 1) * 256], func=AF.Square, accum_out=sq4[:, g:g + 1]),
                      reads=[("t1", g // 2)], writes=[("sq4", g), "sqj"])
                sq4k = [("sq4", g) for g in range(4)]
                A("dve", lambda e: e.tensor_scalar(out=r4[:], in0=sq4[:], scalar1=1.0 / 256, scalar2=4 * EPS, op0=ALU.mult, op1=ALU.add), reads=sq4k, writes=["r4"])
                A("pool", lambda e: e.tensor_tensor(out=r4[:], in0=r4[:], in1=bc(mhalf[:, 0:1], [128, 4]), op=ALU.pow), reads=["r4", "mhalf"], writes=["r4"])
                A("pool", lambda e: e.tensor_tensor(out=yn[:].rearrange("p (g d) -> p g d", g=4), in0=t1[:].rearrange("p (g d) -> p g d", g=4),
                                                    in1=bc(r4[:, :].unsqueeze(2), [128, 4, 256]), op=ALU.mult),
                  reads=["r4", ("t1", 0), ("t1", 1)], writes=[("yn", 0), ("yn", 1)])
                bank, key = PT.get()
                for k in range(8):
                    A("pe", lambda e, k=k, bank=bank: e.transpose(out=bank[:, k * 128:(k + 1) * 128], in_=yn[:, k * 128:(k + 1) * 128], identity=identb[:]),
                      reads=[("yn", 0), ("yn", 1), "identb"], writes=[key])
                A("dve", lambda e, bank=bank, c=c: e.tensor_tensor(out=xbcT[:, 0:8, 4 + c * 128:4 + (c + 1) * 128], in0=bank[:, :].rearrange("p (k t) -> p k t", k=8),
                                                                    in1=bc(ssdnw[:, l * 8:(l + 1) * 8].unsqueeze(2), [128, 8, 128]), op=ALU.mult),
                  reads=[key, "ssdnw"], writes=[("xbcT", b) for b in range(8)])
                A("pool", lambda e, c=c: e.tensor_tensor(out=stmp[:].rearrange("p (a d) -> p a d", a=16), in0=Sssd[l][:].rearrange("p (a d) -> p a d", a=16),
                                                         in1=bc(cdb[:, c, :].unsqueeze(2), [128, 16, 64]), op=ALU.mult),
                  reads=[("Sssd", l), ("cdb", c)], writes=["stmp", "stmp2"])
                for hb in range(2):
                    bank, key = PB.get()
                    for gg in range(2):
                        g = hb * 2 + gg
                        A("pe", lambda e, bank=bank, gg=gg, g=g, c=c: e.matmul(bank[:, gg * 256:(gg + 1) * 256], lhsT=B2tok[:, c, g * 128:(g + 1) * 128], rhs=xw[:, g * 256:(g + 1) * 256], start=True, stop=True),
                          reads=[("B2tok", c), "xw"], writes=[key])
                    hs = slice(hb * 512, (hb + 1) * 512)
                    A("dve", lambda e, bank=bank, hs=hs: e.tensor_tensor(out=Sssd[l][:, hs], in0=stmp[:, hs], in1=bank[:, :], op=ALU.add),
                      reads=[key, "stmp" if hb == 0 else "stmp2"], writes=[("Sssd", l)])
                A("act", lambda e: e.activation(out=Sbf[l][:], in_=Sssd[l][:], func=AF.Copy), reads=[("Sssd", l)], writes=[("Sbf", l)])
            for c in range(NCH):
                cs = slice(c * 128, (c + 1) * 128)
                bank, key = PB.get()
                A("pe", lambda e, bank=bank, cs=cs: e.matmul(bank[:, :], lhsT=alrT[:, cs], rhs=w2b[:, l * 512:(l + 1) * 512], start=True, stop=False),
                  reads=["alrT", "w2b"], writes=[key])
                A("pe", lambda e, bank=bank: e.matmul(bank[:, :], lhsT=sel[:, 2, :], rhs=rows[:, l * 512:(l + 1) * 512], start=False, stop=False),
                  reads=["sel", "rows"], writes=[key])
                A("pe", lambda e, bank=bank: e.matmul(bank[:, :], lhsT=sel[:, 2, :], rhs=rows[:, 1024 + l * 512:1024 + (l + 1) * 512], start=False, stop=True),
                  reads=["sel", "rows"], writes=[key])
                A("act", lambda e, bank=bank: e.activation(out=e1, in_=bank[:, :], func=AF.Exp, scale=-1.0), reads=[key], writes=[("t1", 0)])
                A("act", lambda e: e.activation(out=l1, in_=e1, func=AF.Ln, bias=1.0), reads=[("t1", 0)], writes=[("t1", 1)])
                bankg, keyg = PB.get()
                for j in range(4):
                    A("pe", lambda e, bankg=bankg, j=j: e.matmul(bankg[:, j * 128:(j + 1) * 128], lhsT=l1[:, j * 128:(j + 1) * 128], rhs=tri16[:], start=True, stop=True),
                      reads=[("t1", 1), "tri16"], writes=[keyg])
                A("act", lambda e, bankg=bankg: e.activation(out=eg, in_=bankg[:, :], func=AF.Exp, bias=QS_LN), reads=[keyg], writes=["stmp"])
                A("act", lambda e, bankg=bankg: e.activation(out=emg, in_=bankg[:, :], func=AF.Exp, scale=-1.0), reads=[keyg], writes=["stmp2"])
                A("act", lambda e, bankg=bankg: e.activation(out=egl[:, :], in_=bankg[:, :].rearrange("p (j t) -> p j t", j=4)[:, :, 127], func=AF.Exp), reads=[keyg], writes=["egl"])
                A("dve", lambda e, cs=cs: e.tensor_tensor(out=qtl[:], in0=qkT[:, 0:4, cs], in1=eg.rearrange("p (j t) -> p j t", j=4), op=ALU.mult),
                  reads=["stmp"] + [("qkT", j) for j in range(4)], writes=["qtl"])
                A("dve", lambda e, cs=cs: e.tensor_tensor(out=ktl[:], in0=qkT[:, 4:8, cs], in1=emg.rearrange("p (j t) -> p j t", j=4), op=ALU.mult),
                  reads=["stmp2"] + [("qkT", 4 + j) for j in range(4)], writes=["ktl"])
                bank, key = PB.get()
                for hd in range(4):
                    A("pe", lambda e, bank=bank, hd=hd: e.matmul(bank[:, hd * 128:(hd + 1) * 128], lhsT=ktl[:, hd, :], rhs=qtl[:, hd, :], start=True, stop=True),
                      reads=["ktl", "qtl"], writes=[key])
                A("dve", lambda e, bank=bank: e.tensor_tensor(out=attm[:], in0=bank[:, :].rearrange("p (g t) -> p g t", g=4), in1=bc(maskb[:, :].unsqueeze(1), [128, 4, 128]), op=ALU.mult),
                  reads=[key, "maskb"], writes=["attm"])
                bankt, keyt = PT.get()
                for hd in range(4):
                    A("pe", lambda e, bankt=bankt, hd=hd: e.transpose(out=bankt[:, hd * 128:(hd + 1) * 128], in_=ktl[:, hd, :], identity=identb[:]),
                      reads=["ktl", "identb"], writes=[keyt])
                A("act", lambda e, bankt=bankt: e.activation(out=ktok[:], in_=bankt[:, 0:512], func=AF.Copy), reads=[keyt], writes=["ktok"])
                obk = []
                for hb in range(2):
                    bank, key = PB.get()
                    obk.append((bank, key))
                    for hh in range(2):
                        hd = hb * 2 + hh
                        o = bank[:, hh * 256:(hh + 1) * 256]
                        A("pe", lambda e, o=o, hd=hd, c=c: e.matmul(o, lhsT=attm[:, hd, :], rhs=vtok[:, c, hd * 256:(hd + 1) * 256], start=True, stop=False),
                          reads=["attm", ("vtok", c)], writes=[key])
                        A("pe", lambda e, o=o, hd=hd: e.matmul(o, lhsT=qtl[:, hd, :], rhs=Gbf[l][:, hd * 256:(hd + 1) * 256], start=False, stop=True),
                          reads=["qtl", ("Gbf", l)], writes=[key])
                        A("act", lambda e, o=o, hd=hd: e.activation(out=sqj[:, 0:256], in_=o, func=AF.Square, accum_out=sq4g[:, hd:hd + 1]), reads=[key], writes=[("sq4g", hd), "sqj"])
                A("dve", lambda e: e.tensor_scalar(out=r4g[:], in0=sq4g[:], scalar1=1.0 / 256, scalar2=EPS, op0=ALU.mult, op1=ALU.add), reads=[("sq4g", i) for i in range(4)], writes=["r4g"])
                A("pool", lambda e: e.tensor_tensor(out=r4g[:], in0=r4g[:], in1=bc(mhalf[:, 0:1], [128, 4]), op=ALU.pow), reads=["r4g", "mhalf"], writes=["r4g"])
                for hb in range(2):
                    bank, key = obk[hb]
                    A("dve", lambda e, bank=bank, hb=hb: e.tensor_tensor(out=on[:, hb * 512:(hb + 1) * 512].rearrange("p (a d) -> p a d", a=2), in0=bank[:, :].rearrange("p (a d) -> p a d", a=2),
                                                                          in1=bc(r4g[:, hb * 2:(hb + 1) * 2].unsqueeze(2), [128, 2, 256]), op=ALU.mult),
                      reads=[key, "r4g"], writes=[("yn", hb)])
                bank, key = PT.get()
                for k in range(8):
                    A("pe", lambda e, k=k, bank=bank: e.transpose(out=bank[:, k * 128:(k + 1) * 128], in_=on[:, k * 128:(k + 1) * 128], identity=identb[:]),
                      reads=[("yn", 0), ("yn", 1), "identb"], writes=[key])
                A("dve", lambda e, bank=bank, cs=cs, c=c: e.tensor_tensor(out=xbcT[:, 8:16, 4 + c * 128:4 + (c + 1) * 128], in0=bank[:, :].rearrange("p (k t) -> p k t", k=8),
                                                                         in1=sgT[:, :, cs], op=ALU.mult),
                  reads=[key] + [("sgT", b) for b in range(8)], writes=[("xbcT", 8 + b) for b in range(8)])
                for hb in range(2):
                    bank, key = PB.get()
                    for hh in range(2):
                        hd = hb * 2 + hh
                        A("pe", lambda e, bank=bank, hh=hh, hd=hd, c=c: e.matmul(bank[:, hh * 256:(hh + 1) * 256], lhsT=ktok[:, hd * 128:(hd + 1) * 128], rhs=vtok[:, c, hd * 256:(hd + 1) * 256], start=True, stop=True),
                          reads=["ktok", ("vtok", c)], writes=[key])
                    hs = slice(hb * 512, (hb + 1) * 512)
                    A("dve", lambda e, bank=bank, hs=hs: e.tensor_tensor(out=stmp[:, hs], in0=Sgla[l][:, hs], in1=bank[:, :], op=ALU.add),
                      reads=[key, ("Sgla", l)], writes=["stmp" if hb == 0 else "stmp2"])
                for hd in range(4):
                    A("act", lambda e, hd=hd: e.activation(out=Sgla[l][:, hd * 256:(hd + 1) * 256], in_=stmp[:, hd * 256:(hd + 1) * 256], func=AF.Copy, scale=egl[:, hd:hd + 1]),
                      reads=["stmp" if hd < 2 else "stmp2", "egl"], writes=[("Sgla", l)])
                A("pool", lambda e: e.tensor_copy(out=Gbf[l][:], in_=Sgla[l][:]), reads=[("Sgla", l)], writes=[("Gbf", l)])
            for j in range(2):
                for b in range(2):
                    wg, wgkey = next_w((l, "in", C_M + b * 1024 + j * 512))
                    for jj in range(4):
                        bank, key = PB.get()
                        for k in range(8):
                            A("pe", lambda e, k=k, jj=jj, wg=wg, bank=bank: e.matmul(bank[:, 0:N], lhsT=wg[:, k, jj * 128:(jj + 1) * 128], rhs=uT[:, k, :], start=(k == 0), stop=(k == 7)),
                              reads=uT_all + [wgkey], writes=[key])
                        A("act", lambda e, bank=bank, b=b, jj=jj: e.activation(out=gth[b][:, jj, :], in_=bank[:, 0:N], func=AF.Tanh, scale=0.5), reads=[key], writes=[("gth", b, jj)])
                    wbr, wbkey = next_w((l, "br", (b, j)))
                    for jj in range(4):
                        bank, key = PB.get()
                        for k in range(8):
                            A("pe", lambda e, k=k, jj=jj, wbr=wbr, bank=bank, b=b: e.matmul(bank[:, 0:N], lhsT=wbr[:, k, jj * 128:(jj + 1) * 128], rhs=xbcT[:, b * 8 + k, 4:4 + N], start=(k == 0), stop=(k == 7)),
                              reads=[("xbcT", b * 8 + k) for k in range(8)] + [wbkey], writes=[key])
                        if b == 0:
                            A("dve", lambda e, bank=bank, jj=jj: e.scalar_tensor_tensor(out=t0m[:, jj, :], in0=gth[0][:, jj, :], scalar=1.0, in1=bank[:, 0:N], op0=ALU.add, op1=ALU.mult),
                              reads=[key, ("gth", 0, jj)], writes=[("t0m", jj)])
                        else:
                            t1m, t1key = T1M.get()
                            A("dve", lambda e, bank=bank, jj=jj, t1m=t1m: e.scalar_tensor_tensor(out=t1m[:, :], in0=gth[1][:, jj, :], scalar=1.0, in1=bank[:, 0:N], op0=ALU.add, op1=ALU.mult),
                              reads=[key, ("gth", 1, jj)], writes=[t1key])
                            A("pool", lambda e, jj=jj, j=j, t1m=t1m: e.tensor_tensor(out=qkT[:, j * 4 + jj, :], in0=t0m[:, jj, :], in1=t1m[:, :], op=ALU.add),
                              reads=[t1key, ("t0m", jj)], writes=[("qkT", j * 4 + jj)])
            for j2 in range(2):
                wo, wokey = next_w((l, "out", j2))
                for c in range(NCH):
                    bank, key = PB.get()
                    for k in range(8):
                        A("pe", lambda e, k=k, c=c, wo=wo, bank=bank: e.matmul(bank[:, :], lhsT=qkT[:, k, c * 128:(c + 1) * 128], rhs=wo[:, k, :], start=(k == 0), stop=(k == 7)),
                          reads=[("qkT", k) for k in range(8)] + [wokey], writes=[key])
                    A("dve", lambda e, bank=bank, c=c, j2=j2: e.scalar_tensor_tensor(out=h[:, c, j2 * 512:(j2 + 1) * 512], in0=bank[:, :], scalar=0.5, in1=h[:, c, j2 * 512:(j2 + 1) * 512], op0=ALU.mult, op1=ALU.add),
                      reads=[key, ("h", c)], writes=[("h", c)])
            if first and not last_layer:
                A("pool", lambda e: e.memset(h[0:112, 0, :], 0.0), writes=[("h", 0)])

        for ti in range(ntiles):
            for c in range(NCH):
                gc = ti * NCH + c
                if gc == 0:
                    A("pool", lambda e: e.memset(h[:, 0, :], 0.0), writes=[("h", 0)])
                    A("sp", lambda e: e.dma_start(out=h[112:128, 0, :], in_=meta_d), writes=[("h", 0)], dma_sem=xsems[0])
                else:
                    r0 = (gc - 1) * 128
                    A("sp", lambda e, c=c, r0=r0: e.dma_start(out=h[:, c, :], in_=x_d[r0:r0 + 128, :]), writes=[("h", c)], dma_sem=xsems[c])
            for l in range(NL):
                tile_layer(ti, l)
            for c in range(NCH):
                gc = ti * NCH + c
                if gc == 0:
                    continue
                r0 = (gc - 1) * 128
                A("act", lambda e, c=c: e.activation(out=sqj[:], in_=h[:, c, :], func=AF.Square, accum_out=ssq[:, c:c + 1]), reads=[("h", c)], writes=[("ssq", c), "sqj"])
                A("dve", lambda e, c=c: e.tensor_scalar(out=rs[:, c:c + 1], in0=ssq[:, c:c + 1], scalar1=1.0 / D, scalar2=EPS, op0=ALU.mult, op1=ALU.add),
                  reads=[("ssq", c)], writes=[("rs", c)])
                A("pool", lambda e, c=c: e.tensor_tensor(out=rs[:, c:c + 1], in0=rs[:, c:c + 1], in1=mhalf[:], op=ALU.pow), reads=[("rs", c), "mhalf"], writes=[("rs", c)])
                ob, okey = ON.get()
                A("dve", lambda e, c=c, ob=ob: e.scalar_tensor_tensor(out=ob[:], in0=h[:, c, :], scalar=rs[:, c:c + 1], in1=fnw_bc[:], op0=ALU.mult, op1=ALU.mult),
                  reads=[("h", c), ("rs", c), "fnw"], writes=[okey])
                ok = ("out", gc)
                out_keys.append(ok)
                A("sp", lambda e, ob=ob, r0=r0: e.dma_start(out=out_d[r0:r0 + 128, :], in_=ob[:]), reads=[okey], writes=[ok], dma_sem=osem)
        A("sp", None, reads=out_keys)
        assert wstate["used"] == len(wlist)
        with nc.Block() as block:
            S.emit(block)
    return nc


def host_consts():
    i = np.arange(128)
    tri = (i[:, None] <= i[None, :]).astype(np.float32)
    strict = (i[:, None] > i[None, :]).astype(np.float32)
    mask0 = (i >= 112).astype(np.float32)[:, None]
    sel = np.zeros((128, 3, 128), np.float32)
    for r in range(3):
        sel[r, r, :] = 1.0
    return {"ident": np.eye(128, dtype=np.float32), "tri": tri, "strict": strict, "mask0": mask0, "sel": sel.reshape(128, 384)}


def make_in_maps(inputs, ncores, TR):
    f = lambda a: np.ascontiguousarray(np.asarray(a, dtype=np.float32))
    common = dict(host_consts())
    common["meta"] = f(inputs["meta_tokens"])
    common["w_in"] = f(inputs["w_in"])
    common["w_br"] = f(inputs["w_branch"])
    common["w_out"] = f(inputs["w_out"])
    common["normw"] = f(np.asarray(inputs["norm_w"]).reshape(2, 8, 128).transpose(2, 0, 1).reshape(128, 16))
    common["cw"] = f(np.asarray(inputs["conv_w"]).reshape(2, 4, 16, 128).transpose(3, 0, 2, 1).reshape(128, 128))
    common["cb"] = f(np.asarray(inputs["conv_b"]).reshape(1, 4096))
    common["dtb"] = f(inputs["dt_bias"])
    common["alog"] = f(inputs["a_log"])
    common["dskip"] = f(inputs["d_skip"])
    common["ssdnw"] = f(np.asarray(inputs["ssd_norm_w"]).reshape(2, 8, 128).transpose(2, 0, 1).reshape(128, 16))
    common["w2"] = f(np.asarray(inputs["gla_gate_w2"]).transpose(1, 0, 2).reshape(16, 1024))
    common["gb"] = f(np.asarray(inputs["gla_gate_b"]).reshape(1, 1024))
    common["gnw"] = f(np.asarray(inputs["gla_norm_w"]).reshape(2, 2, 128).transpose(2, 0, 1).reshape(128, 4))
    common["fnw"] = f(np.asarray(inputs["final_norm_w"]).reshape(1, 1024))
    x = np.asarray(inputs["x"], dtype=np.float32)
    maps = []
    for b in range(ncores):
        m = dict(common)
        m["x"] = np.ascontiguousarray(x[b, :TR])
        maps.append(m)
    return maps


_NC_CACHE = {}


def kernel(**inputs):
    x = np.asarray(inputs["x"])
    B, T, _ = x.shape
    NT = T // 128 + 1
    key = (NT,)
    if key not in _NC_CACHE:
        _NC_CACHE[key] = build(NT=NT, NCH=3, NL=2)
    nc = _NC_CACHE[key]
    maps = make_in_maps(inputs, B, T)
    res = run_bass_kernel_spmd(nc, maps, core_ids=list(range(B)))
    out = np.stack([np.asarray(r["out"], dtype=np.float32) for r in res.results], axis=0)
    return out
```

```python
import math
import numpy as np
import concourse.bass as bass
import concourse.mybir as mybir
from concourse.bass_utils import run_bass_kernel_spmd
from contextlib import ExitStack

F32 = mybir.dt.float32
BF16 = mybir.dt.bfloat16
AF = mybir.ActivationFunctionType
ALU = mybir.AluOpType

D = 1024
NIN = 8224
EPS = 1e-6
C_Z, C_XBC, C_DT, C_Q, C_K, C_V, C_G, C_ALR, C_M = 0, 1024, 3072, 3088, 3600, 4112, 5136, 6160, 6176


class Op:
    __slots__ = ("eng", "fn", "deps", "sig", "count", "sem", "is_dma", "inc")

    def __init__(self, eng, fn, is_dma=False):
        self.eng = eng
        self.fn = fn
        self.deps = set()
        self.sig = False
        self.count = 0
        self.sem = None
        self.is_dma = is_dma
        self.inc = 1


class Sched:
    ENGS = ("pe", "act", "dve", "pool", "sp")

    def __init__(self, nc, stack):
        self.nc = nc
        self.stack = stack
        self.ops = []
        self.res = {}
        self.engsem = {e: stack.enter_context(nc.semaphore("s_" + e)) for e in self.ENGS}
        self.dma_counts = {}
        self.alias = {}

    def new_sem(self, name):
        return self.stack.enter_context(self.nc.semaphore(name))

    def uniq(self, key):
        lst = self.alias.setdefault(key, [])
        k = ("__u", key, len(lst))
        lst.append(k)
        return k

    def add(self, eng, fn, reads=(), writes=(), dma_sem=None, raw_keys=False):
        op = Op(eng, fn, is_dma=dma_sem is not None)
        if not raw_keys:
            reads = [rr for r in reads for rr in self.alias.get(r, [r])]
            writes = [ww for w in writes for ww in self.alias.get(w, [w])]
        if dma_sem is not None:
            op.sem = dma_sem
            op.sig = True
            op.inc = 16
            c = self.dma_counts.get(id(dma_sem), 0) + 16
            self.dma_counts[id(dma_sem)] = c
            op.count = c
        for r in reads:
            st = self.res.get(r)
            if st is None:
                st = [None, []]
                self.res[r] = st
            if st[0] is not None:
                op.deps.add(st[0])
            st[1].append(op)
        for w in writes:
            st = self.res.get(w)
            if st is None:
                st = [None, []]
                self.res[w] = st
            if st[0] is not None:
                op.deps.add(st[0])
            last = {}
            for rd in st[1]:
                if rd is op:
                    continue
                if rd.is_dma:
                    op.deps.add(rd)
                else:
                    last[rd.eng] = rd
            for rd in last.values():
                op.deps.add(rd)
            st[0] = op
            st[1] = []
        op.deps.discard(op)
        self.ops.append(op)
        return op

    def finalize(self):
        for op in self.ops:
            if op.eng == "pe" and not op.is_dma:
                op.deps = {d for d in op.deps if not (d.eng == "pe" and not d.is_dma)}
            for d in op.deps:
                d.sig = True
        cnt = {e: 0 for e in self.ENGS}
        for op in self.ops:
            if op.is_dma:
                continue
            if op.sig:
                cnt[op.eng] += 1
                op.count = cnt[op.eng]
                op.sem = self.engsem[op.eng]

    def emit(self, block):
        self.finalize()
        by_eng = {e: [o for o in self.ops if o.eng == e] for e in self.ENGS}

        def run(engine, ops):
            waited = {}
            for op in ops:
                need = {}
                for d in op.deps:
                    k = id(d.sem)
                    if need.get(k, (None, 0))[1] < d.count:
                        need[k] = (d.sem, d.count)
                for k, (sem, c) in need.items():
                    if waited.get(k, 0) < c:
                        engine.wait_ge(sem, c)
                        waited[k] = c
                if op.fn is None:
                    continue
                ins = op.fn(engine)
                if op.sig:
                    ins.then_inc(op.sem, op.inc)

        @block.tensor
        def _(e):
            run(e, by_eng["pe"])

        @block.scalar
        def _(e):
            run(e, by_eng["act"])

        @block.vector
        def _(e):
            run(e, by_eng["dve"])

        @block.gpsimd
        def _(e):
            run(e, by_eng["pool"])

        @block.sync
        def _(e):
            run(e, by_eng["sp"])


class Ring:
    def __init__(self, items, name, keys=None):
        self.items = items
        self.name = name
        self.keys = keys if keys is not None else [(name, j) for j in range(len(items))]
        self.i = 0

    def get(self):
        j = self.i % len(self.items)
        self.i += 1
        return self.items[j], self.keys[j]


class RingProxy:
    def __init__(self, ring):
        self.ring = ring

    def get(self):
        return self.ring.get()


def build(NT=33, NCH=3, NL=2, dbg=False):
    assert NT % NCH == 0
    N = NCH * 128
    ntiles = NT // NCH
    TR = (NT - 1) * 128
    nc = bass.Bass("TRN2", target_bir_lowering=False)

    def din(name, shape):
        return nc.dram_tensor(name, list(shape), F32, kind="ExternalInput").ap()

    x_d = din("x", [TR, D])
    meta_d = din("meta", [16, D])
    win_d = din("w_in", [2, D, NIN])
    wbr_d = din("w_br", [2, 2, D, D])
    wout_d = din("w_out", [2, D, D])
    normw_d = din("normw", [128, 16])
    cw_d = din("cw", [128, 128])
    cb_d = din("cb", [1, 2 * 2048])
    dtb_d = din("dtb", [2, 16])
    alog_d = din("alog", [2, 16])
    dskip_d = din("dskip", [2, 16])
    ssdnw_d = din("ssdnw", [128, 16])
    w2_d = din("w2", [16, 2 * 512])
    gb_d = din("gb", [1, 2 * 512])
    gnw_d = din("gnw", [128, 4])
    fnw_d = din("fnw", [1, D])
    ident_d = din("ident", [128, 128])
    tri_d = din("tri", [128, 128])
    strict_d = din("strict", [128, 128])
    mask0_d = din("mask0", [128, 1])
    sel_d = din("sel", [128, 3 * 128])
    out_d = nc.dram_tensor("out", [TR, D], F32, kind="ExternalOutput").ap()

    with ExitStack() as st:
        S = Sched(nc, st)

        def sb(name, shape, dt=F32):
            return st.enter_context(nc.sbuf_tensor("s_" + name, list(shape), dt))

        def ps(name, shape, dt=F32):
            return st.enter_context(nc.psum_tensor(name, list(shape), dt))

        REC = [None]

        def A(eng, fn, reads=(), writes=(), dma_sem=None, **kw):
            if REC[0] is None:
                return S.add(eng, fn, reads, writes, dma_sem=dma_sem, **kw)
            REC[0].append(("op", (eng, fn, list(reads), list(writes), dma_sem, kw)))
            return None

        def STEP():
            if REC[0] is not None:
                REC[0].append(("step",))

        def NEED(tok):
            if REC[0] is not None:
                REC[0].append(("step",))
                REC[0].append(("need", tok))

        def GIVE(tok):
            if REC[0] is not None:
                REC[0].append(("give", tok))

        def merge(streams):
            segs = []
            for stq in streams:
                steps = [[]]
                for it in stq:
                    if it[0] == "step":
                        if steps[-1]:
                            steps.append([])
                    else:
                        steps[-1].append(it)
                if not steps[-1]:
                    steps.pop()
                segs.append(steps)
            pos = [0] * len(segs)
            tot = [max(len(x), 1) for x in segs]
            given = set()
            while any(pos[i] < len(segs[i]) for i in range(len(segs))):
                order = sorted(range(len(segs)), key=lambda i: pos[i] / tot[i])
                progressed = False
                for i in order:
                    if pos[i] >= len(segs[i]):
                        continue
                    step = segs[i][pos[i]]
                    if any(it[0] == "need" and it[1] not in given for it in step):
                        continue
                    for it in step:
                        if it[0] == "op":
                            eng, fn, r, w, ds, kw = it[1]
                            S.add(eng, fn, r, w, dma_sem=ds, **kw)
                        elif it[0] == "give":
                            given.add(it[1])
                    pos[i] += 1
                    progressed = True
                    break
                assert progressed, "merge deadlock"
        h = sb("h", [128, NCH, D])
        sqj = sb("sqj", [128, D], BF16)
        ubs = [sb("ub%d" % i, [128, D], BF16) for i in range(2)]
        UB = Ring(ubs, "ub")
        uT = sb("uT", [128, 8, N], BF16)
        NWB = 3
        wbufs = [sb("wb%d" % i, [128, 8, 512], BF16) for i in range(NWB)]
        wsems = [S.new_sem("wsem%d" % i) for i in range(NWB)]
        wmisc = sb("wmisc", [128, 2, 8, 32], BF16)
        NDG = 4
        dgs = [sb("dg%d" % i, [128, 4, 128], BF16) for i in range(NDG)]
        DG = Ring(dgs, "dg")
        XW = N + 4
        xbcT = sb("xbcT", [128, 16, XW], BF16)
        hist = sb("hist", [128, 2, 16, 4], BF16)
        x2 = sb("x2", [128, NCH, D], BF16)
        B2tok = sb("B2tok", [128, NCH, 512], BF16)
        BCT = sb("BCT", [128, 8, N], BF16)
        zs = sb("zs", [128, NCH, D], BF16)
        qkT = sb("qkT", [128, 8, N], BF16)
        vtok = sb("vtok", [128, NCH, D], BF16)
        sgT = sb("sgT", [128, 8, N], BF16)
        alrT = sb("alrT", [128, N], BF16)
        ths = [sb("th%d" % i, [128, 512], BF16) for i in range(2)]
        TH = Ring(ths, "th")
        gws = [sb("gw%d" % i, [128, 512], BF16) for i in range(2)]
        GW = Ring(gws, "gw")
        ssq = sb("ssq", [128, NCH])
        rs = sb("rs", [128, NCH])
        dtr = sb("dtr", [128, NCH, 16])
        dte_x = sb("dte_x", [128, NCH, 16])
        dt_t = sb("dt_t", [128, NCH, 16])
        da_t = sb("da_t", [128, NCH, 16])
        ecumh = sb("ecumh", [128, NCH, 16])
        dte = sb("dte", [128, NCH, 16])
        cdb = sb("cdb", [128, NCH, 16])
        wq = sb("wq", [128, NCH, 16])
        dth = sb("dth", [128, NCH, 16])
        sq4 = sb("sq4", [128, 4])
        r4 = sb("r4", [128, 4])
        sq4g = sb("sq4g", [128, 4])
        r4g = sb("r4g", [128, 4])
        egl = sb("egl", [128, 4])
        At = sb("At", [128, 8, 128])
        Lm = sb("Lm", [128, 16, 128], BF16)
        MT = sb("MT", [128, 16, 128], BF16)
        scm = sb("scm", [128, 4, 128], BF16)
        xdt = sb("xdt", [128, D], BF16)
        xw = sb("xw", [128, D], BF16)
        xd = sb("xd", [128, D], BF16)
        t1 = sb("t1", [128, D])
        yn = sb("yn", [128, D], BF16)
        stmp = sb("stmp", [128, D])
        e1 = t1[:, 0:512]
        l1 = t1[:, 512:1024]
        eg = stmp[:, 0:512]
        emg = stmp[:, 512:1024]
        on = yn
        qtl = sb("qtl", [128, 4, 128], BF16)
        ktl = sb("ktl", [128, 4, 128], BF16)
        ktok = sb("ktok", [128, 512], BF16)
        attm = sb("attm", [128, 4, 128], BF16)
        gth = [sb("gth%d" % i, [128, 4, N], BF16) for i in range(2)]
        t0m = sb("t0m", [128, 8, N], BF16)
        t1ms = [sb("t1m%d" % i, [128, N]) for i in range(2)]
        T1M = Ring(t1ms, "t1m")
        Sssd = [sb("Sssd%d" % l, [128, D]) for l in range(2)]
        Sbf = [sb("Sbf%d" % l, [128, D], BF16) for l in range(2)]
        Sgla = [sb("Sgla%d" % l, [128, D]) for l in range(2)]
        Gbf = [sb("Gbf%d" % l, [128, D], BF16) for l in range(2)]
        identf = sb("identf", [128, 128])
        identb = sb("identb", [128, 128], BF16)
        tri = sb("tri", [128, 128])
        tri16 = sb("tri16", [128, 128])
        strict = sb("strict", [128, 128])
        maskb = sb("maskb", [128, 128], BF16)
        maskq = sb("maskq", [128, 128], BF16)
        onesf = sb("onesf", [128, 128])
        sel = sb("sel", [128, 3, 128], BF16)
        selN = sb("selN", [128, 3, N], BF16)
        rows = sb("rows", [128, 2048], BF16)
        gb8 = sb("gb8", [8, 128])
        gb8h = sb("gb8h", [8, 128], BF16)
        gb8hf = sb("gb8hf", [8, 128])
        gb8l = sb("gb8l", [8, 128], BF16)
        mhalf = sb("mhalf", [128, 1])
        mask0 = sb("mask0", [128, 1])
        normw = sb("normw", [128, 16])
        cw = sb("cw", [128, 128])
        dtb_bc = sb("dtb_bc", [128, 2, 16])
        a_bc = sb("a_bc", [128, 2, 16])
        dskh_bc = sb("dskh_bc", [128, 2, 16])
        ssdnw = sb("ssdnw", [128, 16])
        w2b = sb("w2b", [128, 1024], BF16)
        gnwh = sb("gnwh", [128, 4])
        fnw_bc = sb("fnw_bc", [128, D])
        onorm = [sb("onorm%d" % i, [128, D]) for i in range(1)]
        ON = Ring(onorm, "onorm")
        pbs = [ps("pb%d" % i, [128, 512]) for i in range(6)]
        PBfull = Ring(pbs, "pb")
        PBa = Ring(pbs[0:3], "pb", keys=[("pb", j) for j in range(0, 3)])
        PBb = Ring(pbs[3:6], "pb", keys=[("pb", j) for j in range(3, 6)])
        PB = RingProxy(PBfull)
        pts = [ps("pt%d" % i, [128, 1024], BF16) for i in range(2)]
        PT = Ring(pts, "pt")

        csem = S.new_sem("csem")

        csemp = S.new_sem("csemp")
        cops = []
        copsp = []

        def cload(dst, src, key, eng="sp"):
            if eng == "sp":
                cops.append(A(eng, lambda e: e.dma_start(out=dst, in_=src), writes=[S.uniq(key)], dma_sem=csem, raw_keys=True))
            else:
                copsp.append(A(eng, lambda e: e.dma_start(out=dst, in_=src), writes=[S.uniq(key)], dma_sem=csemp, raw_keys=True))

        cload(identf[:], ident_d, "identf")
        cload(tri[:], tri_d, "tri")
        cload(strict[:], strict_d, "strict")
        cload(mask0[:], mask0_d, "mask0")
        cload(normw[:], normw_d, "normw")
        cload(cw[:], cw_d, "cw")
        cload(ssdnw[:], ssdnw_d, "ssdnw")
        cload(gnwh[:], gnw_d, "gnwh")
        cload(gb8[:], gb_d.rearrange("o (a b) -> (o a) b", a=8), "gb8")
        cload(fnw_bc[:], fnw_d[0:1, :].partition_broadcast(128), "fnw")
        for l in range(2):
            cload(dtb_bc[:, l, :], dtb_d[l:l + 1, :].partition_broadcast(128), "dtb")
            cload(a_bc[:, l, :], alog_d[l:l + 1, :].partition_broadcast(128), "a_bc")
            cload(dskh_bc[:, l, :], dskip_d[l:l + 1, :].partition_broadcast(128), "dskh")
        A("pool", lambda e: e.memset(rows[:], 0.0), writes=["rows0"])
        A("pool", lambda e: e.memset(alrT[:], 0.0), writes=["alrT"])
        A("pool", lambda e: e.memset(w2b[:], 0.0), writes=["w2b0"])
        copsp.append(A("pool", lambda e: e.dma_start(out=rows[0:1, :], in_=cb_d[:, 0:2048]), reads=["rows0"], writes=[S.uniq("rows")], dma_sem=csemp, raw_keys=True))
        copsp.append(A("pool", lambda e: e.dma_start(out=rows[1:2, :], in_=cb_d[:, 2048:4096]), reads=["rows0"], writes=[S.uniq("rows")], dma_sem=csemp, raw_keys=True))
        cload(sel[:].rearrange("p r m -> p (r m)"), sel_d, "sel", eng="pool")
        copsp.append(A("pool", lambda e: e.dma_start(out=w2b[0:16, :], in_=w2_d), reads=["w2b0"], writes=[S.uniq("w2b")], dma_sem=csemp, raw_keys=True))
        for l in range(2):
            cload(wmisc[:, l, :, 0:16], win_d[l, :, C_DT:C_DT + 16].rearrange("(k p) n -> p k n", p=128), "wmisc", eng="pool")
            cload(wmisc[:, l, :, 16:32], win_d[l, :, C_ALR:C_ALR + 16].rearrange("(k p) n -> p k n", p=128), "wmisc", eng="pool")

        for o in cops:
            o.count = 16 * len(cops)
        for o in copsp:
            o.count = 16 * len(copsp)
        A("dve", lambda e: e.tensor_copy(out=identb[:], in_=identf[:]), reads=["identf"], writes=["identb"])
        A("dve", lambda e: e.tensor_copy(out=maskb[:], in_=tri[:]), reads=["tri"], writes=["maskb"])
        A("dve", lambda e: e.tensor_scalar(out=maskq[:], in0=tri[:], scalar1=0.25, scalar2=None, op0=ALU.mult), reads=["tri"], writes=["maskq"])
        A("dve", lambda e: e.tensor_scalar(out=tri16[:], in0=tri[:], scalar1=-1.0 / 16.0, scalar2=None, op0=ALU.mult), reads=["tri"], writes=["tri16"])
        A("pool", lambda e: e.memset(onesf[:], 1.0), writes=["onesf"])
        A("pool", lambda e: e.memset(mhalf[:], -0.5), writes=["mhalf"])
        for r in range(3):
            A("pool", lambda e, r=r: e.tensor_copy(out=selN[:, r, :], in_=bc(sel[:, r, 0:1], [128, N])), reads=["sel"], writes=["selN"])
        A("pool", lambda e: e.memset(hist[:], 0.0), writes=["hist"])
        A("pool", lambda e: e.memset(xbcT[:], 0.0), writes=[("xbcT", b) for b in range(16)])
        for l in range(2):
            A("pool", lambda e, l=l: e.memset(Sssd[l][:], 0.0), writes=[("Sssd", l)])
            A("pool", lambda e, l=l: e.memset(Sbf[l][:], 0.0), writes=[("Sbf", l)])
            A("pool", lambda e, l=l: e.memset(Sgla[l][:], 0.0), writes=[("Sgla", l)])
            A("pool", lambda e, l=l: e.memset(Gbf[l][:], 0.0), writes=[("Gbf", l)])
        A("act", lambda e: e.activation(out=a_bc[:], in_=a_bc[:], func=AF.Exp), reads=["a_bc"], writes=["a_bc"])
        A("dve", lambda e: e.tensor_scalar(out=a_bc[:], in0=a_bc[:], scalar1=-1.0, scalar2=None, op0=ALU.mult), reads=["a_bc"], writes=["a_bc"])
        A("dve", lambda e: e.tensor_scalar(out=dskh_bc[:], in0=dskh_bc[:], scalar1=0.5, scalar2=None, op0=ALU.mult), reads=["dskh"], writes=["dskh"])
        A("dve", lambda e: e.tensor_scalar(out=gnwh[:], in0=gnwh[:], scalar1=0.5, scalar2=None, op0=ALU.mult), reads=["gnwh"], writes=["gnwh"])
        A("dve", lambda e: e.tensor_copy(out=gb8h[:], in_=gb8[:]), reads=["gb8"], writes=["gb8h"])
        A("dve", lambda e: e.tensor_copy(out=gb8hf[:], in_=gb8h[:]), reads=["gb8h"], writes=["gb8hf"])
        A("dve", lambda e: e.tensor_tensor(out=gb8l[:], in0=gb8[:], in1=gb8hf[:], op=ALU.subtract), reads=["gb8", "gb8hf"], writes=["gb8l"])
        csem2 = S.new_sem("csem2")
        cops2 = []
        for a8 in range(8):
            cops2.append(A("sp", lambda e, a8=a8: e.dma_start(out=rows[2:3, a8 * 128:(a8 + 1) * 128], in_=gb8h[a8:a8 + 1, :]), reads=["gb8h", "rows0"], writes=[S.uniq("rows")], dma_sem=csem2))
            cops2.append(A("sp", lambda e, a8=a8: e.dma_start(out=rows[2:3, 1024 + a8 * 128:1024 + (a8 + 1) * 128], in_=gb8l[a8:a8 + 1, :]), reads=["gb8l", "rows0"], writes=[S.uniq("rows")], dma_sem=csem2))
        for o in cops2:
            o.count = 16 * len(cops2)

        def wsrc(l, kind, idx):
            if kind == "in":
                src = win_d[l, :, idx:idx + 512]
            elif kind == "br":
                b, j = idx
                src = wbr_d[l, b, :, j * 512:(j + 1) * 512]
            else:
                src = wout_d[l, :, idx * 512:(idx + 1) * 512]
            return src.rearrange("(k p) n -> p k n", p=128)

        def layer_groups(l):
            g = []
            for i in range(4):
                g.append((l, "in", C_XBC + 512 * i))
            for i in range(2):
                g.append((l, "in", C_Z + 512 * i))
            g.append((l, "in", C_Q))
            g.append((l, "in", C_K))
            for i in range(2):
                g.append((l, "in", C_V + 512 * i))
            for i in range(2):
                g.append((l, "in", C_G + 512 * i))
            for b in range(2):
                for j in range(2):
                    g.append((l, "in", C_M + b * 1024 + j * 512))
                    g.append((l, "br", (b, j)))
            for j in range(2):
                g.append((l, "out", j))
            return g

        wlist = []
        for ti in range(ntiles):
            for l in range(NL):
                wlist += layer_groups(l)
        wstate = {"issued": 0, "used": 0}
        PF = 2

        GP = len(layer_groups(0)) * NL
        wscr = nc.dram_tensor("wscr", [GP, 128, 8 * 512], BF16, kind="Internal").ap()
        wsemh = [S.new_sem("wsemh%d" % i) for i in range(NWB)]
        wsemst = [S.new_sem("wsemst%d" % i) for i in range(NWB)]

        def w_issue_upto(n):
            while wstate["issued"] < min(n, len(wlist)):
                i = wstate["issued"]
                l, kind, idx = wlist[i]
                slot = i % NWB
                if i < GP:
                    src = wsrc(l, kind, idx)
                    A("pool", lambda e, slot=slot, src=src: e.dma_start(out=wbufs[slot][:], in_=src),
                      writes=[("wb", slot)], dma_sem=wsems[slot])
                    if ntiles > 1:
                        A("sp", lambda e, slot=slot, i=i: e.dma_start(out=wscr[i], in_=wbufs[slot][:].rearrange("p k n -> p (k n)")),
                          reads=[("wb", slot)], writes=[("wscr", i)], dma_sem=wsemst[slot])
                else:
                    g = i % GP
                    A("sp", lambda e, slot=slot, g=g: e.dma_start(out=wbufs[slot][:].rearrange("p k n -> p (k n)"), in_=wscr[g]),
                      reads=[("wscr", g)], writes=[("wb", slot)], dma_sem=wsemh[slot])
                wstate["issued"] += 1

        def next_w(expect):
            i = wstate["used"]
            assert wlist[i] == expect, (wlist[i], expect)
            w_issue_upto(i + 1 + PF)
            wstate["used"] += 1
            return wbufs[i % NWB], ("wb", i % NWB)

        def bc(ap, shape):
            return ap.to_broadcast(list(shape))

        HALF_LN = math.log(0.5)
        QS_LN = math.log(128.0 ** -0.5)

        def silu2_evac(bank_ap, out_ap, key, wkey, ncols):
            th, tkey = TH.get()
            A("act", lambda e: e.activation(out=th[:, 0:ncols], in_=bank_ap, func=AF.Tanh, scale=0.5), reads=[key], writes=[tkey])
            A("dve", lambda e: e.scalar_tensor_tensor(out=out_ap, in0=th[:, 0:ncols], scalar=1.0, in1=bank_ap, op0=ALU.add, op1=ALU.mult),
              reads=[tkey, key], writes=[wkey])

        uT_all = [("uT", c) for c in range(NCH)]
        out_keys = []
        osem = S.new_sem("osem")
        xsems = [S.new_sem("xsem%d" % i) for i in range(NCH)]

        def tile_layer(ti, l):
            first = ti == 0
            last_layer = l == NL - 1
            for c in range(NCH):
                A("act", lambda e, c=c: e.activation(out=sqj[:], in_=h[:, c, :], func=AF.Square, accum_out=ssq[:, c:c + 1]),
                  reads=[("h", c)], writes=[("ssq", c), "sqj"])
                A("dve", lambda e, c=c: e.tensor_scalar(out=rs[:, c:c + 1], in0=ssq[:, c:c + 1], scalar1=1.0 / D, scalar2=EPS, op0=ALU.mult, op1=ALU.add),
                  reads=[("ssq", c)], writes=[("rs", c)])
                A("pool", lambda e, c=c: e.tensor_tensor(out=rs[:, c:c + 1], in0=rs[:, c:c + 1], in1=mhalf[:], op=ALU.pow),
                  reads=[("rs", c), "mhalf"], writes=[("rs", c)])
                ub, ukey = UB.get()
                A("dve", lambda e, c=c, ub=ub: e.tensor_scalar(out=ub[:], in0=h[:, c, :], scalar1=rs[:, c:c + 1], scalar2=None, op0=ALU.mult),
                  reads=[("h", c), ("rs", c)], writes=[ukey])
                bank, key = PT.get()
                for k in range(8):
                    A("pe", lambda e, k=k, ub=ub, bank=bank: e.transpose(out=bank[:, k * 128:(k + 1) * 128], in_=ub[:, k * 128:(k + 1) * 128], identity=identb[:]),
                      reads=[ukey, "identb"], writes=[key])
                A("dve", lambda e, c=c, bank=bank: e.tensor_tensor(out=uT[:, :, c * 128:(c + 1) * 128], in0=bank[:, :].rearrange("p (k t) -> p k t", k=8),
                                                                    in1=bc(normw[:, l * 8:(l + 1) * 8].unsqueeze(2), [128, 8, 128]), op=ALU.mult),
                  reads=[key, "normw"], writes=[("uT", c)])
            for c in range(NCH):
                bank, key = PB.get()
                for k in range(8):
                    A("pe", lambda e, k=k, c=c, bank=bank: e.matmul(bank[:, 0:16], lhsT=uT[:, k, c * 128:(c + 1) * 128], rhs=wmisc[:, l, k, 0:16], start=(k == 0), stop=(k == 7)),
                      reads=[("uT", c), "wmisc"], writes=[key])
                A("dve", lambda e, c=c, bank=bank: e.tensor_tensor(out=dtr[:, c, :], in0=bank[:, 0:16], in1=dtb_bc[:, l, :], op=ALU.add),
                  reads=[key, "dtb"], writes=[("dtr", c)])
                A("act", lambda e, c=c: e.activation(out=dte_x[:, c, :], in_=dtr[:, c, :], func=AF.Exp), reads=[("dtr", c)], writes=[("dte_x", c)])
                A("act", lambda e, c=c: e.activation(out=dt_t[:, c, :], in_=dte_x[:, c, :], func=AF.Ln, bias=1.0), reads=[("dte_x", c)], writes=[("dt", c)])
                if first and c == 0:
                    A("dve", lambda e, c=c: e.tensor_scalar(out=dt_t[:, c, :], in0=dt_t[:, c, :], scalar1=mask0[:, 0:1], scalar2=None, op0=ALU.mult),
                      reads=[("dt", c), "mask0"], writes=[("dt", c)])
                A("dve", lambda e, c=c: e.tensor_tensor(out=da_t[:, c, :], in0=dt_t[:, c, :], in1=a_bc[:, l, :], op=ALU.mult),
                  reads=[("dt", c), "a_bc"], writes=[("da", c)])
                bank2, key2 = PB.get()
                A("pe", lambda e, c=c, bank2=bank2: e.matmul(bank2[:, 0:16], lhsT=tri[:], rhs=da_t[:, c, :], start=True, stop=True), reads=[("da", c), "tri"], writes=[key2])
                A("pe", lambda e, c=c, bank2=bank2: e.matmul(bank2[:, 16:32], lhsT=onesf[:], rhs=da_t[:, c, :], start=True, stop=True), reads=[("da", c), "onesf"], writes=[key2])
                A("pe", lambda e, c=c, bank2=bank2: e.matmul(bank2[:, 32:48], lhsT=strict[:], rhs=da_t[:, c, :], start=True, stop=True), reads=[("da", c), "strict"], writes=[key2])
                A("act", lambda e, c=c, bank2=bank2: e.activation(out=ecumh[:, c, :], in_=bank2[:, 0:16], func=AF.Exp, bias=HALF_LN), reads=[key2], writes=[("ecumh", c)])
                A("act", lambda e, c=c, bank2=bank2: e.activation(out=cdb[:, c, :], in_=bank2[:, 16:32], func=AF.Exp), reads=[key2], writes=[("cdb", c)])
                A("act", lambda e, c=c, bank2=bank2: e.activation(out=dte[:, c, :], in_=bank2[:, 32:48], func=AF.Exp), reads=[key2], writes=[("dte", c)])
                A("dve", lambda e, c=c: e.scalar_tensor_tensor(out=wq[:, c, :], in0=dt_t[:, c, :], scalar=0.25, in1=dte[:, c, :], op0=ALU.mult, op1=ALU.mult),
                  reads=[("dt", c), ("dte", c)], writes=[("wq", c)])
                A("dve", lambda e, c=c: e.tensor_scalar(out=dth[:, c, :], in0=dt_t[:, c, :], scalar1=0.5, scalar2=None, op0=ALU.mult),
                  reads=[("dt", c)], writes=[("dth", c)])
            bank, key = PB.get()
            for k in range(8):
                A("pe", lambda e, k=k, bank=bank: e.matmul(bank[0:16, 0:N], lhsT=wmisc[:, l, k, 16:32], rhs=uT[:, k, :], start=(k == 0), stop=(k == 7)),
                  reads=uT_all + ["wmisc"], writes=[key])
            A("act", lambda e, bank=bank: e.activation(out=alrT[0:16, :], in_=bank[0:16, 0:N], func=AF.Copy), reads=[key], writes=["alrT"])
            for gi in range(4):
                wb, wkey = next_w((l, "in", C_XBC + 512 * gi))
                for j in range(4):
                    blk = gi * 4 + j
                    bank, key = PB.get()
                    for k in range(8):
                        A("pe", lambda e, k=k, j=j, wb=wb, bank=bank: e.matmul(bank[:, 0:N], lhsT=wb[:, k, j * 128:(j + 1) * 128], rhs=uT[:, k, :], start=(k == 0), stop=(k == 7)),
                          reads=uT_all + [wkey], writes=[key])
                    A("act", lambda e, blk=blk, bank=bank: e.activation(out=xbcT[:, blk, 4:4 + N], in_=bank[:, 0:N], func=AF.Copy), reads=[key], writes=[("xbcT", blk)])
            allx = [("xbcT", b) for b in range(16)]
            A("pool", lambda e: e.tensor_copy(out=xbcT[:, :, 1:4], in_=hist[:, l, :, 1:4]), reads=[("hist", l)], writes=allx)
            A("pool", lambda e: e.tensor_copy(out=hist[:, l, :, 1:4], in_=xbcT[:, :, N + 1:N + 4]), reads=allx, writes=[("hist", l)])
            dgof = {}

            def gen_diag(blk):
                dg, dkey = DG.get()
                for tap in range(4):
                    col = (l * 16 + blk) * 4 + tap
                    A("dve", lambda e, dg=dg, tap=tap, col=col: e.tensor_scalar(out=dg[:, tap, :], in0=identb[:], scalar1=cw[:, col:col + 1], scalar2=None, op0=ALU.mult),
                      reads=["identb", "cw"], writes=[dkey])
                dgof[blk] = (dg, dkey)

            def conv_tok(c, blks, out_ap, wkey):
                bank, key = PB.get()
                for j, blk in enumerate(blks):
                    dg, dkey = dgof[blk]
                    o = bank[:, j * 128:(j + 1) * 128]
                    A("pe", lambda e, o=o, blk=blk: e.matmul(o, lhsT=sel[:, l, :], rhs=rows[:, blk * 128:(blk + 1) * 128], start=True, stop=False),
                      reads=["sel", "rows"], writes=[key])
                    for tap in range(4):
                        A("pe", lambda e, o=o, blk=blk, tap=tap, dg=dg, c=c: e.matmul(o, lhsT=xbcT[:, blk, c * 128 + 1 + tap:c * 128 + 1 + tap + 128], rhs=dg[:, tap, :], start=False, stop=(tap == 3)),
                          reads=[("xbcT", blk), dkey], writes=[key])
                silu2_evac(bank[:, 0:128 * len(blks)], out_ap, key, wkey, 128 * len(blks))

            for half in range(2):
                blks = [half * 4 + j for j in range(4)]
                for b in blks:
                    gen_diag(b)
                for c in range(NCH):
                    conv_tok(c, blks, x2[:, c, half * 512:(half + 1) * 512], ("x2", c))
            for blk in range(8, 16):
                if blk % 4 == 0:
                    for b4 in range(blk, blk + 4):
                        gen_diag(b4)
                    if blk == 8:
                        for c in range(NCH):
                            conv_tok(c, [8, 9, 10, 11], B2tok[:, c, :], ("B2tok", c))
                dg, dkey = dgof[blk]
                bank, key = PB.get()
                A("pe", lambda e, bank=bank, blk=blk: e.matmul(bank[:, 0:N], lhsT=rows[:, blk * 128:(blk + 1) * 128], rhs=selN[:, l, :], start=True, stop=False),
                  reads=["selN", "rows"], writes=[key])
                for tap in range(4):
                    A("pe", lambda e, bank=bank, blk=blk, tap=tap, dg=dg: e.matmul(bank[:, 0:N], lhsT=dg[:, tap, :], rhs=xbcT[:, blk, 1 + tap:1 + tap + N], start=False, stop=(tap == 3)),
                      reads=[("xbcT", blk), dkey], writes=[key])
                silu2_evac(bank[:, 0:N], BCT[:, blk - 8, :], key, ("BCT", blk - 8), N)
            sF = []
            REC[0] = sF
            PB.ring = PBa
            for gi in range(2):
                wb, wkey = next_w((l, "in", C_Z + 512 * gi))
                for c in range(NCH):
                    bank, key = PB.get()
                    for k in range(8):
                        A("pe", lambda e, k=k, c=c, wb=wb, bank=bank: e.matmul(bank[:, :], lhsT=uT[:, k, c * 128:(c + 1) * 128], rhs=wb[:, k, :], start=(k == 0), stop=(k == 7)),
                          reads=[("uT", c), wkey], writes=[key])
                    silu2_evac(bank[:, :], zs[:, c, gi * 512:(gi + 1) * 512], key, ("zs", c), 512)
                    if gi == 1:
                        GIVE(("zs", c))
                    STEP()
            for qk in range(2):
                wb, wkey = next_w((l, "in", C_Q if qk == 0 else C_K))
                for j in range(4):
                    bank, key = PB.get()
                    for k in range(8):
                        A("pe", lambda e, k=k, j=j, wb=wb, bank=bank: e.matmul(bank[:, 0:N], lhsT=wb[:, k, j * 128:(j + 1) * 128], rhs=uT[:, k, :], start=(k == 0), stop=(k == 7)),
                          reads=uT_all + [wkey], writes=[key])
                    A("act", lambda e, bank=bank, qk=qk, j=j: e.activation(out=qkT[:, qk * 4 + j, :], in_=bank[:, 0:N], func=AF.Copy), reads=[key], writes=[("qkT", qk * 4 + j)])
                    STEP()
            for gi in range(2):
                wb, wkey = next_w((l, "in", C_V + 512 * gi))
                for c in range(NCH):
                    bank, key = PB.get()
                    for k in range(8):
                        A("pe", lambda e, k=k, c=c, wb=wb, bank=bank: e.matmul(bank[:, :], lhsT=uT[:, k, c * 128:(c + 1) * 128], rhs=wb[:, k, :], start=(k == 0), stop=(k == 7)),
                          reads=[("uT", c), wkey], writes=[key])
                    A("act", lambda e, bank=bank, c=c, gi=gi: e.activation(out=vtok[:, c, gi * 512:(gi + 1) * 512], in_=bank[:, :], func=AF.Copy), reads=[key], writes=[("vtok", c)])
                    STEP()
            for gi in range(2):
                wb, wkey = next_w((l, "in", C_G + 512 * gi))
                for j in range(4):
                    blk = gi * 4 + j
                    bank, key = PB.get()
                    for k in range(8):
                        A("pe", lambda e, k=k, j=j, wb=wb, bank=bank: e.matmul(bank[:, 0:N], lhsT=wb[:, k, j * 128:(j + 1) * 128], rhs=uT[:, k, :], start=(k == 0), stop=(k == 7)),
                          reads=uT_all + [wkey], writes=[key])
                    gw, gkey = GW.get()
                    th, tkey = TH.get()
                    A("act", lambda e, bank=bank, gw=gw, blk=blk: e.activation(out=gw[:, 0:N], in_=bank[:, 0:N], func=AF.Copy, scale=gnwh[:, l * 2 + (blk % 2):l * 2 + (blk % 2) + 1]),
                      reads=[key, "gnwh"], writes=[gkey])
                    A("act", lambda e, bank=bank, th=th: e.activation(out=th[:, 0:N], in_=bank[:, 0:N], func=AF.Tanh, scale=0.5), reads=[key], writes=[tkey])
                    A("dve", lambda e, gw=gw, th=th, blk=blk: e.scalar_tensor_tensor(out=sgT[:, blk, :], in0=th[:, 0:N], scalar=1.0, in1=gw[:, 0:N], op0=ALU.add, op1=ALU.mult),
                      reads=[gkey, tkey], writes=[("sgT", blk)])
                    STEP()
            sH = []
            REC[0] = sH
            PB.ring = PBb
            for c in range(NCH):
                cs = slice(c * 128, (c + 1) * 128)
                for hv in range(2):
                    for hh in range(8):
                        hd = hv * 8 + hh
                        A("dve", lambda e, hh=hh, hd=hd, c=c: e.tensor_scalar(out=At[:, hh, :], in0=strict[:], scalar1=da_t[:, c, hd:hd + 1], scalar2=None, op0=ALU.mult),
                          reads=["strict", ("da", c)], writes=[("At", hh)])
                    for qd in range(2):
                        bank, key = PB.get()
                        for hq in range(4):
                            hh = qd * 4 + hq
                            A("pe", lambda e, bank=bank, hq=hq, hh=hh: e.matmul(bank[:, hq * 128:(hq + 1) * 128], lhsT=At[:, hh, :], rhs=tri[:], start=True, stop=True),
                              reads=[("At", hh), "tri"], writes=[key])
                        h0 = hv * 8 + qd * 4
                        A("act", lambda e, bank=bank, h0=h0: e.activation(out=Lm[:, h0:h0 + 4, :], in_=bank[:, :].rearrange("p (a t) -> p a t", a=4), func=AF.Exp),
                          reads=[key], writes=[("Lm", h0 // 4)])
                        STEP()
                bank, key = PB.get()
                for g in range(4):
                    A("pe", lambda e, bank=bank, g=g, cs=cs: e.matmul(bank[:, g * 128:(g + 1) * 128], lhsT=BCT[:, g, cs], rhs=BCT[:, 4 + g, cs], start=True, stop=True),
                      reads=[("BCT", g), ("BCT", 4 + g)], writes=[key])
                A("dve", lambda e, bank=bank: e.tensor_tensor(out=scm[:], in0=bank[:, :].rearrange("p (g t) -> p g t", g=4), in1=bc(maskq[:, :].unsqueeze(1), [128, 4, 128]), op=ALU.mult),
                  reads=[key, "maskq"], writes=["scm"])
                A("dve", lambda e: e.tensor_tensor(out=MT[:].rearrange("p (g a) t -> p g a t", g=4), in0=Lm[:].rearrange("p (g a) t -> p g a t", g=4),
                                                   in1=bc(scm[:].unsqueeze(2), [128, 4, 4, 128]), op=ALU.mult),
                  reads=["scm"] + [("Lm", i) for i in range(4)], writes=["MT"])
                STEP()
                x3 = x2[:, c, :].rearrange("p (a d) -> p a d", a=16)
                A("pool", lambda e, x3=x3, c=c: e.tensor_tensor(out=xdt[:].rearrange("p (a d) -> p a d", a=16), in0=x3, in1=bc(dth[:, c, :].unsqueeze(2), [128, 16, 64]), op=ALU.mult),
                  reads=[("x2", c), ("dth", c)], writes=["xdt"])
                A("pool", lambda e, x3=x3, c=c: e.tensor_tensor(out=xw[:].rearrange("p (a d) -> p a d", a=16), in0=x3, in1=bc(wq[:, c, :].unsqueeze(2), [128, 16, 64]), op=ALU.mult),
                  reads=[("x2", c), ("wq", c)], writes=["xw"])
                A("pool", lambda e, x3=x3: e.tensor_tensor(out=xd[:].rearrange("p (a d) -> p a d", a=16), in0=x3, in1=bc(dskh_bc[:, l, :].unsqueeze(2), [128, 16, 64]), op=ALU.mult),
                  reads=[("x2", c), "dskh"], writes=["xd"])
                STEP()
                ydk = []
                for hb in range(2):
                    bank, key = PB.get()
                    ydk.append((bank, key))
                    A("pe", lambda e, bank=bank, hb=hb: e.matmul(bank[:, :], lhsT=identb[:], rhs=xd[:, hb * 512:(hb + 1) * 512], start=True, stop=False),
                      reads=["identb", "xd"], writes=[key])
                    for h8 in range(8):
                        hd = hb * 8 + h8
                        A("pe", lambda e, bank=bank, h8=h8, hd=hd: e.matmul(bank[:, h8 * 64:(h8 + 1) * 64], lhsT=MT[:, hd, :], rhs=xdt[:, hd * 64:(hd + 1) * 64], start=False, stop=(h8 == 7)),
                          reads=["MT", "xdt"], writes=[key])
                    STEP()
                for hb in range(2):
                    bank, key = PB.get()
                    for gg in range(2):
                        g = hb * 2 + gg
                        A("pe", lambda e, bank=bank, gg=gg, g=g, cs=cs: e.matmul(bank[:, gg * 256:(gg + 1) * 256], lhsT=BCT[:, 4 + g, cs], rhs=Sbf[l][:, g * 256:(g + 1) * 256], start=True, stop=True),
                          reads=[("BCT", 4 + g), ("Sbf", l)], writes=[key])
                    hs = slice(hb * 512, (hb + 1) * 512)
                    A("dve", lambda e, bank=bank, hb=hb, hs=hs, c=c: e.tensor_tensor(out=t1[:, hs].rearrange("p (a d) -> p a d", a=8), in0=bank[:, :].rearrange("p (a d) -> p a d", a=8),
                                                                                  in1=bc(ecumh[:, c, hb * 8:(hb + 1) * 8].unsqueeze(2), [128, 8, 64]), op=ALU.mult),
                      reads=[key, ("ecumh", c)], writes=[("t1", hb)])
                    ybank, ykey = ydk[hb]
                    A("dve", lambda e, ybank=ybank, hs=hs: e.tensor_tensor(out=t1[:, hs], in0=t1[:, hs], in1=ybank[:, :], op=ALU.add),
                      reads=[ykey, ("t1", hb)], writes=[("t1", hb)])
                    NEED(("zs", c))
                    A("pool", lambda e, hs=hs, c=c: e.tensor_tensor(out=t1[:, hs], in0=t1[:, hs], in1=zs[:, c, hs], op=ALU.mult),
                      reads=[("t1", hb), ("zs", c)], writes=[("t1", hb)])
                    STEP()
                for g in range(4):
                    A("act", lambda e, g=g: e.activation(out=sqj[:, 0:256], in_=t1[:, g * 256:(g + 1) * 256], func=AF.Square, accum_out=sq4[:, g:g + 1]),
                      reads=[("t1", g // 2)], writes=[("sq4", g), "sqj"])
                sq4k = [("sq4", g) for g in range(4)]
                A("dve", lambda e: e.tensor_scalar(out=r4[:], in0=sq4[:], scalar1=1.0 / 256, scalar2=4 * EPS, op0=ALU.mult, op1=ALU.add), reads=sq4k, writes=["r4"])
                A("pool", lambda e: e.tensor_tensor(out=r4[:], in0=r4[:], in1=bc(mhalf[:, 0:1], [128, 4]), op=ALU.pow), reads=["r4", "mhalf"], writes=["r4"])
                A("pool", lambda e: e.tensor_tensor(out=yn[:].rearrange("p (g d) -> p g d", g=4), in0=t1[:].rearrange("p (g d) -> p g d", g=4),
                                                    in1=bc(r4[:, :].unsqueeze(2), [128, 4, 256]), op=ALU.mult),
                  reads=["r4", ("t1", 0), ("t1", 1)], writes=[("yn", 0), ("yn", 1)])
                STEP()
                bank, key = PT.get()
                for k in range(8):
                    A("pe", lambda e, k=k, bank=bank: e.transpose(out=bank[:, k * 128:(k + 1) * 128], in_=yn[:, k * 128:(k + 1) * 128], identity=identb[:]),
                      reads=[("yn", 0), ("yn", 1), "identb"], writes=[key])
                A("dve", lambda e, bank=bank, c=c: e.tensor_tensor(out=xbcT[:, 0:8, 4 + c * 128:4 + (c + 1) * 128], in0=bank[:, :].rearrange("p (k t) -> p k t", k=8),
                                                                    in1=bc(ssdnw[:, l * 8:(l + 1) * 8].unsqueeze(2), [128, 8, 128]), op=ALU.mult),
                  reads=[key, "ssdnw"], writes=[("xbcT", b) for b in range(8)])
                STEP()
                A("pool", lambda e, c=c: e.tensor_tensor(out=stmp[:].rearrange("p (a d) -> p a d", a=16), in0=Sssd[l][:].rearrange("p (a d) -> p a d", a=16),
                                                         in1=bc(cdb[:, c, :].unsqueeze(2), [128, 16, 64]), op=ALU.mult),
                  reads=[("Sssd", l), ("cdb", c)], writes=["stmp", "stmp2"])
                for hb in range(2):
                    bank, key = PB.get()
                    for gg in range(2):
                        g = hb * 2 + gg
                        A("pe", lambda e, bank=bank, gg=gg, g=g, c=c: e.matmul(bank[:, gg * 256:(gg + 1) * 256], lhsT=B2tok[:, c, g * 128:(g + 1) * 128], rhs=xw[:, g * 256:(g + 1) * 256], start=True, stop=True),
                          reads=[("B2tok", c), "xw"], writes=[key])
                    hs = slice(hb * 512, (hb + 1) * 512)
                    A("dve", lambda e, bank=bank, hs=hs: e.tensor_tensor(out=Sssd[l][:, hs], in0=stmp[:, hs], in1=bank[:, :], op=ALU.add),
                      reads=[key, "stmp" if hb == 0 else "stmp2"], writes=[("Sssd", l)])
                A("act", lambda e: e.activation(out=Sbf[l][:], in_=Sssd[l][:], func=AF.Copy), reads=[("Sssd", l)], writes=[("Sbf", l)])
                STEP()
            REC[0] = None
            PB.ring = PBfull
            merge([sH, sF])

            def merge_branch(b):
                for j in range(2):
                    wg, wgkey = next_w((l, "in", C_M + b * 1024 + j * 512))
                    for jj in range(4):
                        bank, key = PB.get()
                        for k in range(8):
                            A("pe", lambda e, k=k, jj=jj, wg=wg, bank=bank: e.matmul(bank[:, 0:N], lhsT=wg[:, k, jj * 128:(jj + 1) * 128], rhs=uT[:, k, :], start=(k == 0), stop=(k == 7)),
                              reads=uT_all + [wgkey], writes=[key])
                        A("act", lambda e, bank=bank, jj=jj: e.activation(out=gth[b][:, jj, :], in_=bank[:, 0:N], func=AF.Tanh, scale=0.5), reads=[key], writes=[("gth", b, jj)])
                        STEP()
                    wbr, wbkey = next_w((l, "br", (b, j)))
                    for jj in range(4):
                        bank, key = PB.get()
                        for k in range(8):
                            A("pe", lambda e, k=k, jj=jj, wbr=wbr, bank=bank: e.matmul(bank[:, 0:N], lhsT=wbr[:, k, jj * 128:(jj + 1) * 128], rhs=xbcT[:, b * 8 + k, 4:4 + N], start=(k == 0), stop=(k == 7)),
                              reads=[("xbcT", b * 8 + k) for k in range(8)] + [wbkey], writes=[key])
                        if b == 0:
                            A("dve", lambda e, bank=bank, jj=jj, j=j: e.scalar_tensor_tensor(out=t0m[:, j * 4 + jj, :], in0=gth[0][:, jj, :], scalar=1.0, in1=bank[:, 0:N], op0=ALU.add, op1=ALU.mult),
                              reads=[key, ("gth", 0, jj)], writes=[("t0m", j * 4 + jj)])
                        else:
                            t1m, t1key = T1M.get()
                            A("dve", lambda e, bank=bank, jj=jj, t1m=t1m: e.scalar_tensor_tensor(out=t1m[:, :], in0=gth[1][:, jj, :], scalar=1.0, in1=bank[:, 0:N], op0=ALU.add, op1=ALU.mult),
                              reads=[key, ("gth", 1, jj)], writes=[t1key])
                            A("pool", lambda e, jj=jj, j=j, t1m=t1m: e.tensor_tensor(out=qkT[:, j * 4 + jj, :], in0=t0m[:, j * 4 + jj, :], in1=t1m[:, :], op=ALU.add),
                              reads=[t1key, ("t0m", j * 4 + jj)], writes=[("qkT", j * 4 + jj)])
                        STEP()

            sJ = []
            REC[0] = sJ
            PB.ring = PBa
            merge_branch(0)
            sI = []
            REC[0] = sI
            PB.ring = PBb
            for c in range(NCH):
                cs = slice(c * 128, (c + 1) * 128)
                bank, key = PB.get()
                A("pe", lambda e, bank=bank, cs=cs: e.matmul(bank[:, :], lhsT=alrT[:, cs], rhs=w2b[:, l * 512:(l + 1) * 512], start=True, stop=False),
                  reads=["alrT", "w2b"], writes=[key])
                A("pe", lambda e, bank=bank: e.matmul(bank[:, :], lhsT=sel[:, 2, :], rhs=rows[:, l * 512:(l + 1) * 512], start=False, stop=False),
                  reads=["sel", "rows"], writes=[key])
                A("pe", lambda e, bank=bank: e.matmul(bank[:, :], lhsT=sel[:, 2, :], rhs=rows[:, 1024 + l * 512:1024 + (l + 1) * 512], start=False, stop=True),
                  reads=["sel", "rows"], writes=[key])
                A("act", lambda e, bank=bank: e.activation(out=e1, in_=bank[:, :], func=AF.Exp, scale=-1.0), reads=[key], writes=[("t1", 0)])
                A("act", lambda e: e.activation(out=l1, in_=e1, func=AF.Ln, bias=1.0), reads=[("t1", 0)], writes=[("t1", 1)])
                STEP()
                bankg, keyg = PB.get()
                for j in range(4):
                    A("pe", lambda e, bankg=bankg, j=j: e.matmul(bankg[:, j * 128:(j + 1) * 128], lhsT=l1[:, j * 128:(j + 1) * 128], rhs=tri16[:], start=True, stop=True),
                      reads=[("t1", 1), "tri16"], writes=[keyg])
                A("act", lambda e, bankg=bankg: e.activation(out=eg, in_=bankg[:, :], func=AF.Exp, bias=QS_LN), reads=[keyg], writes=["stmp"])
                A("act", lambda e, bankg=bankg: e.activation(out=emg, in_=bankg[:, :], func=AF.Exp, scale=-1.0), reads=[keyg], writes=["stmp2"])
                A("act", lambda e, bankg=bankg: e.activation(out=egl[:, :], in_=bankg[:, :].rearrange("p (j t) -> p j t", j=4)[:, :, 127], func=AF.Exp), reads=[keyg], writes=["egl"])
                A("dve", lambda e, cs=cs: e.tensor_tensor(out=qtl[:], in0=qkT[:, 0:4, cs], in1=eg.rearrange("p (j t) -> p j t", j=4), op=ALU.mult),
                  reads=["stmp"] + [("qkT", j) for j in range(4)], writes=["qtl"])
                A("dve", lambda e, cs=cs: e.tensor_tensor(out=ktl[:], in0=qkT[:, 4:8, cs], in1=emg.rearrange("p (j t) -> p j t", j=4), op=ALU.mult),
                  reads=["stmp2"] + [("qkT", 4 + j) for j in range(4)], writes=["ktl"])
                STEP()
                bank, key = PB.get()
                for hd in range(4):
                    A("pe", lambda e, bank=bank, hd=hd: e.matmul(bank[:, hd * 128:(hd + 1) * 128], lhsT=ktl[:, hd, :], rhs=qtl[:, hd, :], start=True, stop=True),
                      reads=["ktl", "qtl"], writes=[key])
                A("dve", lambda e, bank=bank: e.tensor_tensor(out=attm[:], in0=bank[:, :].rearrange("p (g t) -> p g t", g=4), in1=bc(maskb[:, :].unsqueeze(1), [128, 4, 128]), op=ALU.mult),
                  reads=[key, "maskb"], writes=["attm"])
                STEP()
                bankt, keyt = PT.get()
                for hd in range(4):
                    A("pe", lambda e, bankt=bankt, hd=hd: e.transpose(out=bankt[:, hd * 128:(hd + 1) * 128], in_=ktl[:, hd, :], identity=identb[:]),
                      reads=["ktl", "identb"], writes=[keyt])
                A("act", lambda e, bankt=bankt: e.activation(out=ktok[:], in_=bankt[:, 0:512], func=AF.Copy), reads=[keyt], writes=["ktok"])
                STEP()
                obk = []
                for hb in range(2):
                    bank, key = PB.get()
                    obk.append((bank, key))
                    for hh in range(2):
                        hd = hb * 2 + hh
                        o = bank[:, hh * 256:(hh + 1) * 256]
                        A("pe", lambda e, o=o, hd=hd, c=c: e.matmul(o, lhsT=attm[:, hd, :], rhs=vtok[:, c, hd * 256:(hd + 1) * 256], start=True, stop=False),
                          reads=["attm", ("vtok", c)], writes=[key])
                        A("pe", lambda e, o=o, hd=hd: e.matmul(o, lhsT=qtl[:, hd, :], rhs=Gbf[l][:, hd * 256:(hd + 1) * 256], start=False, stop=True),
                          reads=["qtl", ("Gbf", l)], writes=[key])
                        A("act", lambda e, o=o, hd=hd: e.activation(out=sqj[:, 0:256], in_=o, func=AF.Square, accum_out=sq4g[:, hd:hd + 1]), reads=[key], writes=[("sq4g", hd), "sqj"])
                    STEP()
                A("dve", lambda e: e.tensor_scalar(out=r4g[:], in0=sq4g[:], scalar1=1.0 / 256, scalar2=EPS, op0=ALU.mult, op1=ALU.add), reads=[("sq4g", i) for i in range(4)], writes=["r4g"])
                A("pool", lambda e: e.tensor_tensor(out=r4g[:], in0=r4g[:], in1=bc(mhalf[:, 0:1], [128, 4]), op=ALU.pow), reads=["r4g", "mhalf"], writes=["r4g"])
                for hb in range(2):
                    bank, key = obk[hb]
                    A("dve", lambda e, bank=bank, hb=hb: e.tensor_tensor(out=on[:, hb * 512:(hb + 1) * 512].rearrange("p (a d) -> p a d", a=2), in0=bank[:, :].rearrange("p (a d) -> p a d", a=2),
                                                                          in1=bc(r4g[:, hb * 2:(hb + 1) * 2].unsqueeze(2), [128, 2, 256]), op=ALU.mult),
                      reads=[key, "r4g"], writes=[("yn", hb)])
                STEP()
                bank, key = PT.get()
                for k in range(8):
                    A("pe", lambda e, k=k, bank=bank: e.transpose(out=bank[:, k * 128:(k + 1) * 128], in_=on[:, k * 128:(k + 1) * 128], identity=identb[:]),
                      reads=[("yn", 0), ("yn", 1), "identb"], writes=[key])
                A("dve", lambda e, bank=bank, cs=cs, c=c: e.tensor_tensor(out=xbcT[:, 8:16, 4 + c * 128:4 + (c + 1) * 128], in0=bank[:, :].rearrange("p (k t) -> p k t", k=8),
                                                                         in1=sgT[:, :, cs], op=ALU.mult),
                  reads=[key] + [("sgT", b) for b in range(8)], writes=[("xbcT", 8 + b) for b in range(8)])
                STEP()
                for hb in range(2):
                    bank, key = PB.get()
                    for hh in range(2):
                        hd = hb * 2 + hh
                        A("pe", lambda e, bank=bank, hh=hh, hd=hd, c=c: e.matmul(bank[:, hh * 256:(hh + 1) * 256], lhsT=ktok[:, hd * 128:(hd + 1) * 128], rhs=vtok[:, c, hd * 256:(hd + 1) * 256], start=True, stop=True),
                          reads=["ktok", ("vtok", c)], writes=[key])
                    hs = slice(hb * 512, (hb + 1) * 512)
                    A("dve", lambda e, bank=bank, hs=hs: e.tensor_tensor(out=stmp[:, hs], in0=Sgla[l][:, hs], in1=bank[:, :], op=ALU.add),
                      reads=[key, ("Sgla", l)], writes=["stmp" if hb == 0 else "stmp2"])
                for hd in range(4):
                    A("act", lambda e, hd=hd: e.activation(out=Sgla[l][:, hd * 256:(hd + 1) * 256], in_=stmp[:, hd * 256:(hd + 1) * 256], func=AF.Copy, scale=egl[:, hd:hd + 1]),
                      reads=["stmp" if hd < 2 else "stmp2", "egl"], writes=[("Sgla", l)])
                A("pool", lambda e: e.tensor_copy(out=Gbf[l][:], in_=Sgla[l][:]), reads=[("Sgla", l)], writes=[("Gbf", l)])
                STEP()
            REC[0] = None
            PB.ring = PBfull
            merge([sI, sJ])
            merge_branch(1)
            for j2 in range(2):
                wo, wokey = next_w((l, "out", j2))
                for c in range(NCH):
                    bank, key = PB.get()
                    for k in range(8):
                        A("pe", lambda e, k=k, c=c, wo=wo, bank=bank: e.matmul(bank[:, :], lhsT=qkT[:, k, c * 128:(c + 1) * 128], rhs=wo[:, k, :], start=(k == 0), stop=(k == 7)),
                          reads=[("qkT", k) for k in range(8)] + [wokey], writes=[key])
                    A("dve", lambda e, bank=bank, c=c, j2=j2: e.scalar_tensor_tensor(out=h[:, c, j2 * 512:(j2 + 1) * 512], in0=bank[:, :], scalar=0.5, in1=h[:, c, j2 * 512:(j2 + 1) * 512], op0=ALU.mult, op1=ALU.add),
                      reads=[key, ("h", c)], writes=[("h", c)])
            if first and not last_layer:
                A("pool", lambda e: e.memset(h[0:112, 0, :], 0.0), writes=[("h", 0)])

        for ti in range(ntiles):
            for c in range(NCH):
                gc = ti * NCH + c
                if gc == 0:
                    A("pool", lambda e: e.memset(h[:, 0, :], 0.0), writes=[("h", 0)])
                    A("sp", lambda e: e.dma_start(out=h[112:128, 0, :], in_=meta_d), writes=[("h", 0)], dma_sem=xsems[0])
                else:
                    r0 = (gc - 1) * 128
                    A("sp", lambda e, c=c, r0=r0: e.dma_start(out=h[:, c, :], in_=x_d[r0:r0 + 128, :]), writes=[("h", c)], dma_sem=xsems[c])
            for l in range(NL):
                tile_layer(ti, l)
            for c in range(NCH):
                gc = ti * NCH + c
                if gc == 0:
                    continue
                r0 = (gc - 1) * 128
                A("act", lambda e, c=c: e.activation(out=sqj[:], in_=h[:, c, :], func=AF.Square, accum_out=ssq[:, c:c + 1]), reads=[("h", c)], writes=[("ssq", c), "sqj"])
                A("dve", lambda e, c=c: e.tensor_scalar(out=rs[:, c:c + 1], in0=ssq[:, c:c + 1], scalar1=1.0 / D, scalar2=EPS, op0=ALU.mult, op1=ALU.add),
                  reads=[("ssq", c)], writes=[("rs", c)])
                A("pool", lambda e, c=c: e.tensor_tensor(out=rs[:, c:c + 1], in0=rs[:, c:c + 1], in1=mhalf[:], op=ALU.pow), reads=[("rs", c), "mhalf"], writes=[("rs", c)])
                ob, okey = ON.get()
                A("dve", lambda e, c=c, ob=ob: e.scalar_tensor_tensor(out=ob[:], in0=h[:, c, :], scalar=rs[:, c:c + 1], in1=fnw_bc[:], op0=ALU.mult, op1=ALU.mult),
                  reads=[("h", c), ("rs", c), "fnw"], writes=[okey])
                ok = ("out", gc)
                out_keys.append(ok)
                A("sp", lambda e, ob=ob, r0=r0: e.dma_start(out=out_d[r0:r0 + 128, :], in_=ob[:]), reads=[okey], writes=[ok], dma_sem=osem)
        A("sp", None, reads=out_keys)
        assert wstate["used"] == len(wlist)
        with nc.Block() as block:
            S.emit(block)
    return nc


def host_consts():
    i = np.arange(128)
    tri = (i[:, None] <= i[None, :]).astype(np.float32)
    strict = (i[:, None] > i[None, :]).astype(np.float32)
    mask0 = (i >= 112).astype(np.float32)[:, None]
    sel = np.zeros((128, 3, 128), np.float32)
    for r in range(3):
        sel[r, r, :] = 1.0
    return {"ident": np.eye(128, dtype=np.float32), "tri": tri, "strict": strict, "mask0": mask0, "sel": sel.reshape(128, 384)}


def make_in_maps(inputs, ncores, TR):
    f = lambda a: np.ascontiguousarray(np.asarray(a, dtype=np.float32))
    common = dict(host_consts())
    common["meta"] = f(inputs["meta_tokens"])
    common["w_in"] = f(inputs["w_in"])
    common["w_br"] = f(inputs["w_branch"])
    common["w_out"] = f(inputs["w_out"])
    common["normw"] = f(np.asarray(inputs["norm_w"]).reshape(2, 8, 128).transpose(2, 0, 1).reshape(128, 16))
    common["cw"] = f(np.asarray(inputs["conv_w"]).reshape(2, 4, 16, 128).transpose(3, 0, 2, 1).reshape(128, 128))
    common["cb"] = f(np.asarray(inputs["conv_b"]).reshape(1, 4096))
    common["dtb"] = f(inputs["dt_bias"])
    common["alog"] = f(inputs["a_log"])
    common["dskip"] = f(inputs["d_skip"])
    common["ssdnw"] = f(np.asarray(inputs["ssd_norm_w"]).reshape(2, 8, 128).transpose(2, 0, 1).reshape(128, 16))
    common["w2"] = f(np.asarray(inputs["gla_gate_w2"]).transpose(1, 0, 2).reshape(16, 1024))
    common["gb"] = f(np.asarray(inputs["gla_gate_b"]).reshape(1, 1024))
    common["gnw"] = f(np.asarray(inputs["gla_norm_w"]).reshape(2, 2, 128).transpose(2, 0, 1).reshape(128, 4))
    common["fnw"] = f(np.asarray(inputs["final_norm_w"]).reshape(1, 1024))
    x = np.asarray(inputs["x"], dtype=np.float32)
    maps = []
    for b in range(ncores):
        m = dict(common)
        m["x"] = np.ascontiguousarray(x[b, :TR])
        maps.append(m)
    return maps


_NC_CACHE = {}


def kernel(**inputs):
    x = np.asarray(inputs["x"])
    B, T, _ = x.shape
    NT = T // 128 + 1
    key = (NT,)
    if key not in _NC_CACHE:
        _NC_CACHE[key] = build(NT=NT, NCH=3, NL=2)
    nc = _NC_CACHE[key]
    maps = make_in_maps(inputs, B, T)
    res = run_bass_kernel_spmd(nc, maps, core_ids=list(range(B)))
    out = np.stack([np.asarray(r["out"], dtype=np.float32) for r in res.results], axis=0)
    return out
```

```python
import math
import numpy as np
import concourse.bass as bass
import concourse.mybir as mybir
from concourse.bass_utils import run_bass_kernel_spmd
from contextlib import ExitStack

F32 = mybir.dt.float32
BF16 = mybir.dt.bfloat16
AF = mybir.ActivationFunctionType
ALU = mybir.AluOpType

D = 1024
NIN = 8224
EPS = 1e-6
C_Z, C_XBC, C_DT, C_Q, C_K, C_V, C_G, C_ALR, C_M = 0, 1024, 3072, 3088, 3600, 4112, 5136, 6160, 6176


class Op:
    __slots__ = ("eng", "fn", "deps", "sig", "count", "sem", "is_dma", "inc", "fin")

    def __init__(self, eng, fn, is_dma=False):
        self.eng = eng
        self.fn = fn
        self.deps = set()
        self.sig = False
        self.count = 0
        self.sem = None
        self.is_dma = is_dma
        self.inc = 1
        self.fin = 0.0


class Sched:
    ENGS = ("pe", "act", "dve", "pool", "sp")

    def __init__(self, nc, stack):
        self.nc = nc
        self.stack = stack
        self.ops = []
        self.res = {}
        self.engsem = {e: stack.enter_context(nc.semaphore("s_" + e)) for e in self.ENGS}
        self.dma_counts = {}
        self.alias = {}

    def new_sem(self, name):
        return self.stack.enter_context(self.nc.semaphore(name))

    def uniq(self, key):
        lst = self.alias.setdefault(key, [])
        k = ("__u", key, len(lst))
        lst.append(k)
        return k

    def add(self, eng, fn, reads=(), writes=(), dma_sem=None, raw_keys=False):
        op = Op(eng, fn, is_dma=dma_sem is not None)
        if not raw_keys:
            reads = [rr for r in reads for rr in self.alias.get(r, [r])]
            writes = [ww for w in writes for ww in self.alias.get(w, [w])]
        if dma_sem is not None:
            op.sem = dma_sem
            op.sig = True
            op.inc = 16
            c = self.dma_counts.get(id(dma_sem), 0) + 16
            self.dma_counts[id(dma_sem)] = c
            op.count = c
        for r in reads:
            st = self.res.get(r)
            if st is None:
                st = [None, []]
                self.res[r] = st
            if st[0] is not None:
                op.deps.add(st[0])
            st[1].append(op)
        for w in writes:
            st = self.res.get(w)
            if st is None:
                st = [None, []]
                self.res[w] = st
            if st[0] is not None:
                op.deps.add(st[0])
            last = {}
            for rd in st[1]:
                if rd is op:
                    continue
                if rd.is_dma:
                    op.deps.add(rd)
                else:
                    last[rd.eng] = rd
            for rd in last.values():
                op.deps.add(rd)
            st[0] = op
            st[1] = []
        op.deps.discard(op)
        self.ops.append(op)
        return op

    def peek(self, reads, writes):
        reads = [rr for r in reads for rr in self.alias.get(r, [r])]
        writes = [ww for w in writes for ww in self.alias.get(w, [w])]
        deps = []
        for r in reads:
            st = self.res.get(r)
            if st is not None and st[0] is not None:
                deps.append(st[0])
        for w in writes:
            st = self.res.get(w)
            if st is None:
                continue
            if st[0] is not None:
                deps.append(st[0])
            last = {}
            for rd in st[1]:
                if rd.is_dma:
                    deps.append(rd)
                else:
                    last[rd.eng] = rd
            deps.extend(last.values())
        return deps

    def finalize(self):
        for op in self.ops:
            if op.eng == "pe" and not op.is_dma:
                op.deps = {d for d in op.deps if not (d.eng == "pe" and not d.is_dma)}
            for d in op.deps:
                d.sig = True
        cnt = {e: 0 for e in self.ENGS}
        for op in self.ops:
            if op.is_dma:
                continue
            if op.sig:
                cnt[op.eng] += 1
                op.count = cnt[op.eng]
                op.sem = self.engsem[op.eng]

    def emit(self, block):
        self.finalize()
        by_eng = {e: [o for o in self.ops if o.eng == e] for e in self.ENGS}

        def run(engine, ops):
            waited = {}
            for op in ops:
                need = {}
                for d in op.deps:
                    k = id(d.sem)
                    if need.get(k, (None, 0))[1] < d.count:
                        need[k] = (d.sem, d.count)
                for k, (sem, c) in need.items():
                    if waited.get(k, 0) < c:
                        engine.wait_ge(sem, c)
                        waited[k] = c
                if op.fn is None:
                    continue
                ins = op.fn(engine)
                if op.sig:
                    ins.then_inc(op.sem, op.inc)

        @block.tensor
        def _(e):
            run(e, by_eng["pe"])

        @block.scalar
        def _(e):
            run(e, by_eng["act"])

        @block.vector
        def _(e):
            run(e, by_eng["dve"])

        @block.gpsimd
        def _(e):
            run(e, by_eng["pool"])

        @block.sync
        def _(e):
            run(e, by_eng["sp"])


class Ring:
    def __init__(self, items, name, keys=None):
        self.items = items
        self.name = name
        self.keys = keys if keys is not None else [(name, j) for j in range(len(items))]
        self.i = 0

    def get(self):
        j = self.i % len(self.items)
        self.i += 1
        return self.items[j], self.keys[j]


class RingProxy:
    def __init__(self, ring):
        self.ring = ring

    def get(self):
        return self.ring.get()


def build(NT=33, NCH=3, NL=2, dbg=False):
    assert NT % NCH == 0
    N = NCH * 128
    ntiles = NT // NCH
    TR = (NT - 1) * 128
    nc = bass.Bass("TRN2", target_bir_lowering=False)

    def din(name, shape):
        return nc.dram_tensor(name, list(shape), F32, kind="ExternalInput").ap()

    x_d = din("x", [TR, D])
    meta_d = din("meta", [16, D])
    win_d = din("w_in", [2, D, NIN])
    wbr_d = din("w_br", [2, 2, D, D])
    wout_d = din("w_out", [2, D, D])
    normw_d = din("normw", [128, 16])
    cw_d = din("cw", [128, 128])
    cb_d = din("cb", [1, 2 * 2048])
    dtb_d = din("dtb", [2, 16])
    alog_d = din("alog", [2, 16])
    dskip_d = din("dskip", [2, 16])
    ssdnw_d = din("ssdnw", [128, 16])
    w2_d = din("w2", [16, 2 * 512])
    gb_d = din("gb", [1, 2 * 512])
    gnw_d = din("gnw", [128, 4])
    fnw_d = din("fnw", [1, D])
    ident_d = din("ident", [128, 128])
    tri_d = din("tri", [128, 128])
    strict_d = din("strict", [128, 128])
    mask0_d = din("mask0", [128, 1])
    sel_d = din("sel", [128, 3 * 128])
    out_d = nc.dram_tensor("out", [TR, D], F32, kind="ExternalOutput").ap()

    with ExitStack() as st:
        S = Sched(nc, st)

        def sb(name, shape, dt=F32):
            return st.enter_context(nc.sbuf_tensor("s_" + name, list(shape), dt))

        def ps(name, shape, dt=F32):
            return st.enter_context(nc.psum_tensor(name, list(shape), dt))

        REC = [None]

        class _FakeIns:
            def then_inc(self, *a, **k):
                return self

        class _FakeEng:
            def __init__(self):
                self.calls = []

            def __getattr__(self, name):
                def f(*a, **k):
                    self.calls.append((name, a, k))
                    return _FakeIns()
                return f

        def _fsz(ap):
            n = 1
            for d_ in ap.shape[1:]:
                n *= d_
            return n

        def _in_psum(ap):
            return "PSum" in type(ap.tensor).__name__

        def est_dur(eng, fn, is_dma):
            if fn is None:
                return 0.0
            if is_dma:
                return 3.0
            try:
                fe = _FakeEng()
                fn(fe)
                name, a, k = fe.calls[-1]
                if eng == "pe":
                    if name == "transpose":
                        return 0.13
                    rhs = k["rhs"]
                    n = _fsz(rhs)
                    return max(64, n) / 2400.0 * (4.0 if rhs.dtype == F32 else 1.0) + 0.01
                src = k.get("in_", k.get("in0"))
                n = _fsz(src) if src is not None else 256
                if eng == "act":
                    return 0.22 + n * 0.0009 + (0.09 if k.get("accum_out") is not None else 0.0)
                if eng == "dve":
                    t = 0.07 + n * 0.00105
                    if name == "tensor_tensor" and not _in_psum(k["in0"]) and not _in_psum(k["in1"]):
                        t = 0.07 + n * 0.0021
                    return t
                if eng == "pool":
                    if name == "tensor_tensor" and k.get("op") == ALU.pow:
                        return 0.8
                    if name == "tensor_scalar":
                        return 0.1 + n * 0.0015
                    return 0.3 + n * 0.0016
            except Exception:
                pass
            return 0.5

        SIM = {"free": {e: 0.0 for e in Sched.ENGS}, "pebusy": 0.0, "log": [], "dbg": dbg, "mlog": []}

        def MARK(name):
            SIM["log"].append((name, max(SIM["free"].values()), SIM["free"]["pe"], SIM["pebusy"]))

        def _lat(d, eng):
            return 0.35 if d.eng != eng or d.is_dma else 0.12

        def commit(eng, fn, reads, writes, dma_sem, kw, dur):
            op = S.add(eng, fn, reads, writes, dma_sem=dma_sem, **kw)
            t = SIM["free"][eng]
            for d in op.deps:
                if eng == "pe" and d.eng == "pe" and not d.is_dma:
                    continue
                t = max(t, d.fin + _lat(d, eng))
            if dma_sem is not None:
                SIM["free"][eng] = t + 0.1
                op.fin = t + dur
            else:
                op.fin = t + dur
                SIM["free"][eng] = op.fin
                if eng == "pe":
                    SIM["pebusy"] += dur
            return op

        def A(eng, fn, reads=(), writes=(), dma_sem=None, **kw):
            dur = est_dur(eng, fn, dma_sem is not None)
            if REC[0] is None:
                return commit(eng, fn, reads, writes, dma_sem, kw, dur)
            REC[0].append(("op", (eng, fn, list(reads), list(writes), dma_sem, kw, dur)))
            return None

        def STEP():
            if REC[0] is not None:
                REC[0].append(("step",))

        def NEED(tok):
            if REC[0] is not None:
                REC[0].append(("step",))
                REC[0].append(("need", tok))

        def GIVE(tok):
            if REC[0] is not None:
                REC[0].append(("give", tok))

        def step_stall(step):
            free = dict(SIM["free"])
            lw, lr = {}, {}
            stall = 0.0
            for it in step:
                if it[0] != "op":
                    continue
                eng, fn, r, w, ds, kw, dur = it[1]
                t = free[eng]
                t0 = t
                for d in S.peek(r, w):
                    if eng == "pe" and d.eng == "pe" and not d.is_dma:
                        continue
                    t = max(t, d.fin + _lat(d, eng))
                for key in r:
                    if key in lw:
                        t = max(t, lw[key] + 0.3)
                for key in w:
                    if key in lw:
                        t = max(t, lw[key] + 0.3)
                    if key in lr:
                        t = max(t, lr[key] + 0.3)
                stall += (t - t0) * (1.0 if eng == "pe" else 0.35)
                fin = t + dur
                if ds is not None:
                    free[eng] = t + 0.1
                else:
                    free[eng] = fin
                for key in r:
                    lr[key] = max(lr.get(key, 0.0), fin)
                for key in w:
                    lw[key] = fin
                    lr.pop(key, None)
            return stall

        def merge(streams, given):
            segs = []
            for stq in streams:
                steps = [[]]
                for it in stq:
                    if it[0] == "step":
                        if steps[-1]:
                            steps.append([])
                    else:
                        steps[-1].append(it)
                if not steps[-1]:
                    steps.pop()
                segs.append(steps)
            pos = [0] * len(segs)
            tot = [max(len(x), 1) for x in segs]
            while any(pos[i] < len(segs[i]) for i in range(len(segs))):
                cands = []
                for i in range(len(segs)):
                    if pos[i] >= len(segs[i]):
                        continue
                    step = segs[i][pos[i]]
                    if any(it[0] == "need" and it[1] not in given for it in step):
                        continue
                    cands.append((step_stall(step), i))
                assert cands, "merge deadlock"
                pick = None
                for st_, i in cands:
                    if st_ < 0.25:
                        pick = i
                        break
                if pick is None:
                    pick = min(cands)[1]
                for it in segs[pick][pos[pick]]:
                    if it[0] == "op":
                        eng, fn, r, w, ds, kw, dur = it[1]
                        commit(eng, fn, r, w, ds, kw, dur)
                    elif it[0] == "give":
                        given.add(it[1])
                if SIM.get("dbg"):
                    SIM["mlog"].append((pick, pos[pick], [round(c_[0], 2) for c_ in cands], dict(SIM["free"])))
                pos[pick] += 1
        h = sb("h", [128, NCH, D])
        ubs = [sb("ub%d" % i, [128, D], BF16) for i in range(2)]
        UB = Ring(ubs, "ub")
        uT = sb("uT", [128, 8, N], BF16)
        NWB = 3
        wbufs = [sb("wb%d" % i, [128, 8, 512], BF16) for i in range(NWB)]
        wsems = [S.new_sem("wsem%d" % i) for i in range(NWB)]
        wmisc = sb("wmisc", [128, 2, 8, 32], BF16)
        NDG = 4
        dgs = [sb("dg%d" % i, [128, 4, 128], BF16) for i in range(NDG)]
        DG = Ring(dgs, "dg")
        XW = N + 4
        xbcT = sb("xbcT", [128, 16, XW], BF16)
        hist = sb("hist", [128, 2, 16, 4], BF16)
        x2 = sb("x2", [128, NCH, D], BF16)
        B2tok = sb("B2tok", [128, NCH, 512], BF16)
        BCT = sb("BCT", [128, 8, N], BF16)
        zs = sb("zs", [128, NCH, D], BF16)
        qkT = sb("qkT", [128, 8, N], BF16)
        vtok = sb("vtok", [128, NCH, D], BF16)
        sgT = sb("sgT", [128, 8, N], BF16)
        alrT = sb("alrT", [128, N], BF16)
        ths = [sb("th%d" % i, [128, 512], BF16) for i in range(2)]
        TH = Ring(ths, "th")
        gws = [sb("gw%d" % i, [128, 512], BF16) for i in range(2)]
        GW = Ring(gws, "gw")
        ssq = sb("ssq", [128, NCH])
        rs = sb("rs", [128, NCH])
        dtr = sb("dtr", [128, NCH, 16])
        dte_x = sb("dte_x", [128, NCH, 16])
        dt_t = sb("dt_t", [128, NCH, 16])
        da_t = sb("da_t", [128, NCH, 16])
        ecumh = sb("ecumh", [128, NCH, 16])
        dte = sb("dte", [128, NCH, 16])
        cdb = sb("cdb", [128, NCH, 16])
        wq = sb("wq", [128, NCH, 16])
        dth = sb("dth", [128, NCH, 16])
        sq4 = sb("sq4", [128, 4])
        r4 = sb("r4", [128, 4])
        sq4g = sb("sq4g", [128, 4])
        r4g = sb("r4g", [128, 4])
        egl = sb("egl", [128, 4])
        At = sb("At", [128, 8, 128])
        Lm = sb("Lm", [128, 16, 128], BF16)
        MT = sb("MT", [128, 16, 128], BF16)
        scm = sb("scm", [128, 4, 128], BF16)
        xdt = sb("xdt", [128, D], BF16)
        xw = sb("xw", [128, D], BF16)
        xd = sb("xd", [128, D], BF16)
        t1 = sb("t1", [128, D])
        yn = sb("yn", [128, D], BF16)
        stmp = sb("stmp", [128, D])
        e1 = sb("e1", [128, 512])
        eg = sb("eg", [128, 512])
        emg = sb("emg", [128, 512])
        on = sb("on", [128, D], BF16)
        qtl = sb("qtl", [128, 4, 128], BF16)
        ktl = sb("ktl", [128, 4, 128], BF16)
        ktok = sb("ktok", [128, 512], BF16)
        attm = sb("attm", [128, 4, 128], BF16)
        gth0 = sb("gth0", [128, 4, N], BF16)
        gth = [gth0, gth0]
        t0m = sb("t0m", [128, 8, N], BF16)
        t1ms = [sb("t1m%d" % i, [128, N]) for i in range(2)]
        T1M = Ring(t1ms, "t1m")
        Sssd = [sb("Sssd%d" % l, [128, D]) for l in range(2)]
        Sbf = [sb("Sbf%d" % l, [128, D], BF16) for l in range(2)]
        Sgla = [sb("Sgla%d" % l, [128, D]) for l in range(2)]
        Gbf = [sb("Gbf%d" % l, [128, D], BF16) for l in range(2)]
        identf = sb("identf", [128, 128])
        identb = sb("identb", [128, 128], BF16)
        tri = sb("tri", [128, 128])
        tri16 = sb("tri16", [128, 128])
        strict = sb("strict", [128, 128])
        maskb = sb("maskb", [128, 128], BF16)
        maskq = sb("maskq", [128, 128], BF16)
        onesf = sb("onesf", [128, 128])
        sel = sb("sel", [128, 3, 128], BF16)
        selN = sb("selN", [128, 3, N], BF16)
        rows = sb("rows", [128, 2048], BF16)
        gb8 = sb("gb8", [8, 128])
        gb8h = sb("gb8h", [8, 128], BF16)
        gb8hf = sb("gb8hf", [8, 128])
        gb8l = sb("gb8l", [8, 128], BF16)
        mhalf = sb("mhalf", [128, 1])
        mask0 = sb("mask0", [128, 1])
        normw = sb("normw", [128, 16])
        cw = sb("cw", [128, 128])
        dtb_bc = sb("dtb_bc", [128, 2, 16])
        a_bc = sb("a_bc", [128, 2, 16])
        dskh_bc = sb("dskh_bc", [128, 2, 16])
        ssdnw = sb("ssdnw", [128, 16])
        w2b = sb("w2b", [128, 1024], BF16)
        gnwh = sb("gnwh", [128, 4])
        fnw_bc = sb("fnw_bc", [128, D])
        pbs = [ps("pb%d" % i, [128, 512]) for i in range(6)]
        PBfull = Ring(pbs, "pb")
        PBh = Ring(pbs[0:3], "pb", keys=[("pb", j) for j in range(0, 3)])
        PBi = PBh
        PBf = Ring(pbs[3:6], "pb", keys=[("pb", j) for j in range(3, 6)])
        PB = RingProxy(PBfull)
        pts = [ps("pt%d" % i, [128, 1024], BF16) for i in range(2)]
        PT = Ring(pts, "pt")

        csem = S.new_sem("csem")

        csemp = S.new_sem("csemp")
        cops = []
        copsp = []

        def cload(dst, src, key, eng="sp"):
            if eng == "sp":
                cops.append(A(eng, lambda e: e.dma_start(out=dst, in_=src), writes=[S.uniq(key)], dma_sem=csem, raw_keys=True))
            else:
                copsp.append(A(eng, lambda e: e.dma_start(out=dst, in_=src), writes=[S.uniq(key)], dma_sem=csemp, raw_keys=True))

        cload(identf[:], ident_d, "identf")
        cload(tri[:], tri_d, "tri")
        cload(strict[:], strict_d, "strict")
        cload(mask0[:], mask0_d, "mask0")
        cload(normw[:], normw_d, "normw")
        cload(cw[:], cw_d, "cw")
        cload(ssdnw[:], ssdnw_d, "ssdnw")
        cload(gnwh[:], gnw_d, "gnwh")
        cload(gb8[:], gb_d.rearrange("o (a b) -> (o a) b", a=8), "gb8")
        cload(fnw_bc[:], fnw_d[0:1, :].partition_broadcast(128), "fnw")
        for l in range(2):
            cload(dtb_bc[:, l, :], dtb_d[l:l + 1, :].partition_broadcast(128), "dtb")
            cload(a_bc[:, l, :], alog_d[l:l + 1, :].partition_broadcast(128), "a_bc")
            cload(dskh_bc[:, l, :], dskip_d[l:l + 1, :].partition_broadcast(128), "dskh")
        A("pool", lambda e: e.memset(rows[:], 0.0), writes=["rows0"])
        A("pool", lambda e: e.memset(alrT[:], 0.0), writes=["alrT"])
        A("pool", lambda e: e.memset(w2b[:], 0.0), writes=["w2b0"])
        copsp.append(A("pool", lambda e: e.dma_start(out=rows[0:1, :], in_=cb_d[:, 0:2048]), reads=["rows0"], writes=[S.uniq("rows")], dma_sem=csemp, raw_keys=True))
        copsp.append(A("pool", lambda e: e.dma_start(out=rows[1:2, :], in_=cb_d[:, 2048:4096]), reads=["rows0"], writes=[S.uniq("rows")], dma_sem=csemp, raw_keys=True))
        cload(sel[:].rearrange("p r m -> p (r m)"), sel_d, "sel", eng="pool")
        copsp.append(A("pool", lambda e: e.dma_start(out=w2b[0:16, :], in_=w2_d), reads=["w2b0"], writes=[S.uniq("w2b")], dma_sem=csemp, raw_keys=True))
        for l in range(2):
            cload(wmisc[:, l, :, 0:16], win_d[l, :, C_DT:C_DT + 16].rearrange("(k p) n -> p k n", p=128), "wmisc", eng="pool")
            cload(wmisc[:, l, :, 16:32], win_d[l, :, C_ALR:C_ALR + 16].rearrange("(k p) n -> p k n", p=128), "wmisc", eng="pool")

        for o in cops:
            o.count = 16 * len(cops)
        for o in copsp:
            o.count = 16 * len(copsp)
        A("dve", lambda e: e.tensor_copy(out=identb[:], in_=identf[:]), reads=["identf"], writes=["identb"])
        A("dve", lambda e: e.tensor_copy(out=maskb[:], in_=tri[:]), reads=["tri"], writes=["maskb"])
        A("dve", lambda e: e.tensor_scalar(out=maskq[:], in0=tri[:], scalar1=0.25, scalar2=None, op0=ALU.mult), reads=["tri"], writes=["maskq"])
        A("dve", lambda e: e.tensor_scalar(out=tri16[:], in0=tri[:], scalar1=-1.0 / 16.0, scalar2=None, op0=ALU.mult), reads=["tri"], writes=["tri16"])
        A("pool", lambda e: e.memset(onesf[:], 1.0), writes=["onesf"])
        A("pool", lambda e: e.memset(mhalf[:], -0.5), writes=["mhalf"])
        for r in range(3):
            A("pool", lambda e, r=r: e.tensor_copy(out=selN[:, r, :], in_=bc(sel[:, r, 0:1], [128, N])), reads=["sel"], writes=["selN"])
        A("pool", lambda e: e.memset(hist[:], 0.0), writes=["hist"])
        A("pool", lambda e: e.memset(xbcT[:], 0.0), writes=[("xbcT", b) for b in range(16)])
        for l in range(2):
            A("pool", lambda e, l=l: e.memset(Sssd[l][:], 0.0), writes=[("Sssd", l)])
            A("pool", lambda e, l=l: e.memset(Sbf[l][:], 0.0), writes=[("Sbf", l)])
            A("pool", lambda e, l=l: e.memset(Sgla[l][:], 0.0), writes=[("Sgla", l, 0), ("Sgla", l, 1)])
            A("pool", lambda e, l=l: e.memset(Gbf[l][:], 0.0), writes=[("Gbf", l)])
        A("act", lambda e: e.activation(out=a_bc[:], in_=a_bc[:], func=AF.Exp), reads=["a_bc"], writes=["a_bc"])
        A("dve", lambda e: e.tensor_scalar(out=a_bc[:], in0=a_bc[:], scalar1=-1.0, scalar2=None, op0=ALU.mult), reads=["a_bc"], writes=["a_bc"])
        A("dve", lambda e: e.tensor_scalar(out=dskh_bc[:], in0=dskh_bc[:], scalar1=0.5, scalar2=None, op0=ALU.mult), reads=["dskh"], writes=["dskh"])
        A("dve", lambda e: e.tensor_scalar(out=gnwh[:], in0=gnwh[:], scalar1=0.5, scalar2=None, op0=ALU.mult), reads=["gnwh"], writes=["gnwh"])
        A("dve", lambda e: e.tensor_copy(out=gb8h[:], in_=gb8[:]), reads=["gb8"], writes=["gb8h"])
        A("dve", lambda e: e.tensor_copy(out=gb8hf[:], in_=gb8h[:]), reads=["gb8h"], writes=["gb8hf"])
        A("dve", lambda e: e.tensor_tensor(out=gb8l[:], in0=gb8[:], in1=gb8hf[:], op=ALU.subtract), reads=["gb8", "gb8hf"], writes=["gb8l"])
        csem2 = S.new_sem("csem2")
        cops2 = []
        for a8 in range(8):
            cops2.append(A("sp", lambda e, a8=a8: e.dma_start(out=rows[2:3, a8 * 128:(a8 + 1) * 128], in_=gb8h[a8:a8 + 1, :]), reads=["gb8h", "rows0"], writes=[S.uniq("rows")], dma_sem=csem2))
            cops2.append(A("sp", lambda e, a8=a8: e.dma_start(out=rows[2:3, 1024 + a8 * 128:1024 + (a8 + 1) * 128], in_=gb8l[a8:a8 + 1, :]), reads=["gb8l", "rows0"], writes=[S.uniq("rows")], dma_sem=csem2))
        for o in cops2:
            o.count = 16 * len(cops2)

        def wsrc(l, kind, idx):
            if kind == "in":
                src = win_d[l, :, idx:idx + 512]
            elif kind == "br":
                b, j = idx
                src = wbr_d[l, b, :, j * 512:(j + 1) * 512]
            else:
                src = wout_d[l, :, idx * 512:(idx + 1) * 512]
            return src.rearrange("(k p) n -> p k n", p=128)

        def layer_groups(l):
            g = []
            for i in range(4):
                g.append((l, "in", C_XBC + 512 * i))
            for i in range(2):
                g.append((l, "in", C_Z + 512 * i))
            g.append((l, "in", C_Q))
            g.append((l, "in", C_K))
            for i in range(2):
                g.append((l, "in", C_V + 512 * i))
            for i in range(2):
                g.append((l, "in", C_G + 512 * i))
            for b in range(2):
                for j in range(2):
                    g.append((l, "in", C_M + b * 1024 + j * 512))
                    g.append((l, "br", (b, j)))
            for j in range(2):
                g.append((l, "out", j))
            return g

        wlist = []
        for ti in range(ntiles):
            for l in range(NL):
                wlist += layer_groups(l)
        wstate = {"issued": 0, "used": 0}
        PF = 2

        GP = len(layer_groups(0)) * NL
        wscr = nc.dram_tensor("wscr", [GP, 128, 8 * 512], BF16, kind="Internal").ap()
        wsemh = [S.new_sem("wsemh%d" % i) for i in range(NWB)]
        wsemst = [S.new_sem("wsemst%d" % i) for i in range(NWB)]

        def w_issue_upto(n):
            while wstate["issued"] < min(n, len(wlist)):
                i = wstate["issued"]
                l, kind, idx = wlist[i]
                slot = i % NWB
                if i < GP:
                    src = wsrc(l, kind, idx)
                    A("pool", lambda e, slot=slot, src=src: e.dma_start(out=wbufs[slot][:], in_=src),
                      writes=[("wb", slot)], dma_sem=wsems[slot])
                    if ntiles > 1:
                        A("sp", lambda e, slot=slot, i=i: e.dma_start(out=wscr[i], in_=wbufs[slot][:].rearrange("p k n -> p (k n)")),
                          reads=[("wb", slot)], writes=[("wscr", i)], dma_sem=wsemst[slot])
                else:
                    g = i % GP
                    A("sp", lambda e, slot=slot, g=g: e.dma_start(out=wbufs[slot][:].rearrange("p k n -> p (k n)"), in_=wscr[g]),
                      reads=[("wscr", g)], writes=[("wb", slot)], dma_sem=wsemh[slot])
                wstate["issued"] += 1

        def next_w(expect, pf=PF):
            i = wstate["used"]
            assert wlist[i] == expect, (wlist[i], expect)
            w_issue_upto(i + 1 + pf)
            wstate["used"] += 1
            return wbufs[i % NWB], ("wb", i % NWB)

        def bc(ap, shape):
            return ap.to_broadcast(list(shape))

        HALF_LN = math.log(0.5)
        QS_LN = math.log(128.0 ** -0.5)

        def silu2_evac(bank_ap, out_ap, key, wkey, ncols):
            th, tkey = TH.get()
            A("act", lambda e: e.activation(out=th[:, 0:ncols], in_=bank_ap, func=AF.Tanh, scale=0.5), reads=[key], writes=[tkey])
            A("dve", lambda e: e.scalar_tensor_tensor(out=out_ap, in0=th[:, 0:ncols], scalar=1.0, in1=bank_ap, op0=ALU.add, op1=ALU.mult),
              reads=[tkey, key], writes=[wkey])

        uT_all = [("uT", c) for c in range(NCH)]
        out_keys = []
        osems = [S.new_sem("osem%d" % i) for i in range(NCH)]
        xsems = [S.new_sem("xsem%d" % i) for i in range(NCH)]

        def tile_layer(ti, l):
            first = ti == 0
            last_layer = l == NL - 1
            MARK("A %d %d" % (ti, l))
            for c in range(NCH):
                ub, ukey = UB.get()
                A("act", lambda e, c=c, ub=ub: e.activation(out=ub[:], in_=h[:, c, :], func=AF.Square, accum_out=ssq[:, c:c + 1]),
                  reads=[("h", c)], writes=[("ssq", c), ukey])
                A("dve", lambda e, c=c: e.tensor_scalar(out=rs[:, c:c + 1], in0=ssq[:, c:c + 1], scalar1=1.0 / D, scalar2=EPS, op0=ALU.mult, op1=ALU.add),
                  reads=[("ssq", c)], writes=[("rs", c)])
                A("pool", lambda e, c=c: e.tensor_tensor(out=rs[:, c:c + 1], in0=rs[:, c:c + 1], in1=mhalf[:], op=ALU.pow),
                  reads=[("rs", c), "mhalf"], writes=[("rs", c)])
                A("act", lambda e, c=c, ub=ub: e.activation(out=ub[:], in_=h[:, c, :], func=AF.Copy, scale=rs[:, c:c + 1]),
                  reads=[("h", c), ("rs", c)], writes=[ukey])
                bank, key = PT.get()
                for k in range(8):
                    A("pe", lambda e, k=k, ub=ub, bank=bank: e.transpose(out=bank[:, k * 128:(k + 1) * 128], in_=ub[:, k * 128:(k + 1) * 128], identity=identb[:]),
                      reads=[ukey, "identb"], writes=[key])
                A("dve", lambda e, c=c, bank=bank: e.tensor_tensor(out=uT[:, :, c * 128:(c + 1) * 128], in0=bank[:, :].rearrange("p (k t) -> p k t", k=8),
                                                                    in1=bc(normw[:, l * 8:(l + 1) * 8].unsqueeze(2), [128, 8, 128]), op=ALU.mult),
                  reads=[key, "normw"], writes=[("uT", c)])
            MARK("B")
            for c in range(NCH):
                bank, key = PB.get()
                for k in range(8):
                    A("pe", lambda e, k=k, c=c, bank=bank: e.matmul(bank[:, 0:16], lhsT=uT[:, k, c * 128:(c + 1) * 128], rhs=wmisc[:, l, k, 0:16], start=(k == 0), stop=(k == 7)),
                      reads=[("uT", c), "wmisc"], writes=[key])
                A("dve", lambda e, c=c, bank=bank: e.tensor_tensor(out=dtr[:, c, :], in0=bank[:, 0:16], in1=dtb_bc[:, l, :], op=ALU.add),
                  reads=[key, "dtb"], writes=[("dtr", c)])
                A("act", lambda e, c=c: e.activation(out=dte_x[:, c, :], in_=dtr[:, c, :], func=AF.Exp), reads=[("dtr", c)], writes=[("dte_x", c)])
            for c in range(NCH):
                A("act", lambda e, c=c: e.activation(out=dt_t[:, c, :], in_=dte_x[:, c, :], func=AF.Ln, bias=1.0), reads=[("dte_x", c)], writes=[("dt", c)])
            for c in range(NCH):
                if first and c == 0:
                    A("dve", lambda e, c=c: e.tensor_scalar(out=dt_t[:, c, :], in0=dt_t[:, c, :], scalar1=mask0[:, 0:1], scalar2=None, op0=ALU.mult),
                      reads=[("dt", c), "mask0"], writes=[("dt", c)])
                A("dve", lambda e, c=c: e.tensor_tensor(out=da_t[:, c, :], in0=dt_t[:, c, :], in1=a_bc[:, l, :], op=ALU.mult),
                  reads=[("dt", c), "a_bc"], writes=[("da", c)])
            def stage_B2(c):
                bank2, key2 = PB.get()
                A("pe", lambda e, c=c, bank2=bank2: e.matmul(bank2[:, 0:16], lhsT=tri[:], rhs=da_t[:, c, :], start=True, stop=True), reads=[("da", c), "tri"], writes=[key2])
                A("pe", lambda e, c=c, bank2=bank2: e.matmul(bank2[:, 16:32], lhsT=onesf[:], rhs=da_t[:, c, :], start=True, stop=True), reads=[("da", c), "onesf"], writes=[key2])
                A("pe", lambda e, c=c, bank2=bank2: e.matmul(bank2[:, 32:48], lhsT=strict[:], rhs=da_t[:, c, :], start=True, stop=True), reads=[("da", c), "strict"], writes=[key2])
                A("act", lambda e, c=c, bank2=bank2: e.activation(out=ecumh[:, c, :], in_=bank2[:, 0:16], func=AF.Exp, bias=HALF_LN), reads=[key2], writes=[("ecumh", c)])
                A("act", lambda e, c=c, bank2=bank2: e.activation(out=cdb[:, c, :], in_=bank2[:, 16:32], func=AF.Exp), reads=[key2], writes=[("cdb", c)])
                A("act", lambda e, c=c, bank2=bank2: e.activation(out=dte[:, c, :], in_=bank2[:, 32:48], func=AF.Exp), reads=[key2], writes=[("dte", c)])
                A("dve", lambda e, c=c: e.scalar_tensor_tensor(out=wq[:, c, :], in0=dt_t[:, c, :], scalar=0.25, in1=dte[:, c, :], op0=ALU.mult, op1=ALU.mult),
                  reads=[("dt", c), ("dte", c)], writes=[("wq", c)])
                A("dve", lambda e, c=c: e.tensor_scalar(out=dth[:, c, :], in0=dt_t[:, c, :], scalar1=0.5, scalar2=None, op0=ALU.mult),
                  reads=[("dt", c)], writes=[("dth", c)])
            bank, key = PB.get()
            for k in range(8):
                A("pe", lambda e, k=k, bank=bank: e.matmul(bank[0:16, 0:N], lhsT=wmisc[:, l, k, 16:32], rhs=uT[:, k, :], start=(k == 0), stop=(k == 7)),
                  reads=uT_all + ["wmisc"], writes=[key])
            A("act", lambda e, bank=bank: e.activation(out=alrT[0:16, :], in_=bank[0:16, 0:N], func=AF.Copy), reads=[key], writes=["alrT"])
            MARK("C")
            for gi in range(4):
                wb, wkey = next_w((l, "in", C_XBC + 512 * gi))
                for j in range(4):
                    blk = gi * 4 + j
                    bank, key = PB.get()
                    for k in range(8):
                        A("pe", lambda e, k=k, j=j, wb=wb, bank=bank: e.matmul(bank[:, 0:N], lhsT=wb[:, k, j * 128:(j + 1) * 128], rhs=uT[:, k, :], start=(k == 0), stop=(k == 7)),
                          reads=uT_all + [wkey], writes=[key])
                    A("act", lambda e, blk=blk, bank=bank: e.activation(out=xbcT[:, blk, 4:4 + N], in_=bank[:, 0:N], func=AF.Copy), reads=[key], writes=[("xbcT", blk)])
            for c in range(NCH):
                stage_B2(c)
            allx = [("xbcT", b) for b in range(16)]
            A("pool", lambda e: e.tensor_copy(out=xbcT[:, :, 1:4], in_=hist[:, l, :, 1:4]), reads=[("hist", l)], writes=allx)
            A("pool", lambda e: e.tensor_copy(out=hist[:, l, :, 1:4], in_=xbcT[:, :, N + 1:N + 4]), reads=allx, writes=[("hist", l)])
            MARK("E")
            dgof = {}

            def gen_diag(blk):
                dg, dkey = DG.get()
                for tap in range(4):
                    col = (l * 16 + blk) * 4 + tap
                    A("pool", lambda e, dg=dg, tap=tap, col=col: e.tensor_scalar(out=dg[:, tap, :], in0=identb[:], scalar1=cw[:, col:col + 1], scalar2=0.0, op0=ALU.mult, op1=ALU.add),
                      reads=["identb", "cw"], writes=[dkey])
                dgof[blk] = (dg, dkey)

            def conv_tok(c, blks, out_ap, wkey):
                bank, key = PB.get()
                for j, blk in enumerate(blks):
                    dg, dkey = dgof[blk]
                    o = bank[:, j * 128:(j + 1) * 128]
                    A("pe", lambda e, o=o, blk=blk: e.matmul(o, lhsT=sel[:, l, :], rhs=rows[:, blk * 128:(blk + 1) * 128], start=True, stop=False),
                      reads=["sel", "rows"], writes=[key])
                    for tap in range(4):
                        A("pe", lambda e, o=o, blk=blk, tap=tap, dg=dg, c=c: e.matmul(o, lhsT=xbcT[:, blk, c * 128 + 1 + tap:c * 128 + 1 + tap + 128], rhs=dg[:, tap, :], start=False, stop=(tap == 3)),
                          reads=[("xbcT", blk), dkey], writes=[key])
                silu2_evac(bank[:, 0:128 * len(blks)], out_ap, key, wkey, 128 * len(blks))

            for half in range(2):
                blks = [half * 4 + j for j in range(4)]
                for b in blks:
                    gen_diag(b)
                for c in range(NCH):
                    conv_tok(c, blks, x2[:, c, half * 512:(half + 1) * 512], ("x2", c))
            for blk in range(8, 16):
                if blk % 4 == 0:
                    for b4 in range(blk, blk + 4):
                        gen_diag(b4)
                    if blk == 8:
                        for c in range(NCH):
                            conv_tok(c, [8, 9, 10, 11], B2tok[:, c, :], ("B2tok", c))
                dg, dkey = dgof[blk]
                bank, key = PB.get()
                A("pe", lambda e, bank=bank, blk=blk: e.matmul(bank[:, 0:N], lhsT=rows[:, blk * 128:(blk + 1) * 128], rhs=selN[:, l, :], start=True, stop=False),
                  reads=["selN", "rows"], writes=[key])
                for tap in range(4):
                    A("pe", lambda e, bank=bank, blk=blk, tap=tap, dg=dg: e.matmul(bank[:, 0:N], lhsT=dg[:, tap, :], rhs=xbcT[:, blk, 1 + tap:1 + tap + N], start=False, stop=(tap == 3)),
                      reads=[("xbcT", blk), dkey], writes=[key])
                silu2_evac(bank[:, 0:N], BCT[:, blk - 8, :], key, ("BCT", blk - 8), N)
            MARK("P2")
            sF = []
            REC[0] = sF
            PB.ring = PBf
            for gi in range(2):
                wb, wkey = next_w((l, "in", C_Z + 512 * gi))
                for c in range(NCH):
                    bank, key = PB.get()
                    for k in range(8):
                        A("pe", lambda e, k=k, c=c, wb=wb, bank=bank: e.matmul(bank[:, :], lhsT=uT[:, k, c * 128:(c + 1) * 128], rhs=wb[:, k, :], start=(k == 0), stop=(k == 7)),
                          reads=[("uT", c), wkey], writes=[key])
                    silu2_evac(bank[:, :], zs[:, c, gi * 512:(gi + 1) * 512], key, ("zs", c), 512)
                    if gi == 1:
                        GIVE(("zs", c))
                    STEP()
            for qk in range(2):
                wb, wkey = next_w((l, "in", C_Q if qk == 0 else C_K))
                for j in range(4):
                    bank, key = PB.get()
                    for k in range(8):
                        A("pe", lambda e, k=k, j=j, wb=wb, bank=bank: e.matmul(bank[:, 0:N], lhsT=wb[:, k, j * 128:(j + 1) * 128], rhs=uT[:, k, :], start=(k == 0), stop=(k == 7)),
                          reads=uT_all + [wkey], writes=[key])
                    A("act", lambda e, bank=bank, qk=qk, j=j: e.activation(out=qkT[:, qk * 4 + j, :], in_=bank[:, 0:N], func=AF.Copy), reads=[key], writes=[("qkT", qk * 4 + j)])
                    if j == 3:
                        GIVE("q" if qk == 0 else "k")
                    STEP()
            for gi in range(2):
                wb, wkey = next_w((l, "in", C_V + 512 * gi))
                for c in range(NCH):
                    bank, key = PB.get()
                    for k in range(8):
                        A("pe", lambda e, k=k, c=c, wb=wb, bank=bank: e.matmul(bank[:, :], lhsT=uT[:, k, c * 128:(c + 1) * 128], rhs=wb[:, k, :], start=(k == 0), stop=(k == 7)),
                          reads=[("uT", c), wkey], writes=[key])
                    A("act", lambda e, bank=bank, c=c, gi=gi: e.activation(out=vtok[:, c, gi * 512:(gi + 1) * 512], in_=bank[:, :], func=AF.Copy), reads=[key], writes=[("vtok", c)])
                    if gi == 1:
                        GIVE(("v", c))
                    STEP()
            for gi in range(2):
                wb, wkey = next_w((l, "in", C_G + 512 * gi))
                for j in range(4):
                    blk = gi * 4 + j
                    bank, key = PB.get()
                    for k in range(8):
                        A("pe", lambda e, k=k, j=j, wb=wb, bank=bank: e.matmul(bank[:, 0:N], lhsT=wb[:, k, j * 128:(j + 1) * 128], rhs=uT[:, k, :], start=(k == 0), stop=(k == 7)),
                          reads=uT_all + [wkey], writes=[key])
                    gw, gkey = GW.get()
                    th, tkey = TH.get()
                    A("act", lambda e, bank=bank, gw=gw, blk=blk: e.activation(out=gw[:, 0:N], in_=bank[:, 0:N], func=AF.Copy, scale=gnwh[:, l * 2 + (blk % 2):l * 2 + (blk % 2) + 1]),
                      reads=[key, "gnwh"], writes=[gkey])
                    A("act", lambda e, bank=bank, th=th: e.activation(out=th[:, 0:N], in_=bank[:, 0:N], func=AF.Tanh, scale=0.5), reads=[key], writes=[tkey])
                    A("dve", lambda e, gw=gw, th=th, blk=blk: e.scalar_tensor_tensor(out=sgT[:, blk, :], in0=th[:, 0:N], scalar=1.0, in1=gw[:, 0:N], op0=ALU.add, op1=ALU.mult),
                      reads=[gkey, tkey], writes=[("sgT", blk)])
                    if blk == 7:
                        GIVE("g")
                        GIVE("Fdone")
                    STEP()
            sH = []
            REC[0] = sH
            PB.ring = PBh
            for c in range(NCH):
                cs = slice(c * 128, (c + 1) * 128)
                for hv in range(2):
                    for hh in range(8):
                        hd = hv * 8 + hh
                        A("dve", lambda e, hh=hh, hd=hd, c=c: e.tensor_scalar(out=At[:, hh, :], in0=strict[:], scalar1=da_t[:, c, hd:hd + 1], scalar2=None, op0=ALU.mult),
                          reads=["strict", ("da", c)], writes=[("At", hh)])
                    for qd in range(2):
                        bank, key = PB.get()
                        for hq in range(4):
                            hh = qd * 4 + hq
                            A("pe", lambda e, bank=bank, hq=hq, hh=hh: e.matmul(bank[:, hq * 128:(hq + 1) * 128], lhsT=At[:, hh, :], rhs=tri[:], start=True, stop=True),
                              reads=[("At", hh), "tri"], writes=[key])
                        h0 = hv * 8 + qd * 4
                        A("act", lambda e, bank=bank, h0=h0: e.activation(out=Lm[:, h0:h0 + 4, :], in_=bank[:, :].rearrange("p (a t) -> p a t", a=4), func=AF.Exp),
                          reads=[key], writes=[("Lm", h0 // 4)])
                        STEP()
                bank, key = PB.get()
                for g in range(4):
                    A("pe", lambda e, bank=bank, g=g, cs=cs: e.matmul(bank[:, g * 128:(g + 1) * 128], lhsT=BCT[:, g, cs], rhs=BCT[:, 4 + g, cs], start=True, stop=True),
                      reads=[("BCT", g), ("BCT", 4 + g)], writes=[key])
                A("dve", lambda e, bank=bank: e.tensor_tensor(out=scm[:], in0=bank[:, :].rearrange("p (g t) -> p g t", g=4), in1=bc(maskq[:, :].unsqueeze(1), [128, 4, 128]), op=ALU.mult),
                  reads=[key, "maskq"], writes=["scm"])
                A("dve", lambda e: e.tensor_tensor(out=MT[:].rearrange("p (g a) t -> p g a t", g=4), in0=Lm[:].rearrange("p (g a) t -> p g a t", g=4),
                                                   in1=bc(scm[:].unsqueeze(2), [128, 4, 4, 128]), op=ALU.mult),
                  reads=["scm"] + [("Lm", i) for i in range(4)], writes=["MT"])
                STEP()
                x3 = x2[:, c, :].rearrange("p (a d) -> p a d", a=16)
                A("pool", lambda e, x3=x3, c=c: e.tensor_tensor(out=xdt[:].rearrange("p (a d) -> p a d", a=16), in0=x3, in1=bc(dth[:, c, :].unsqueeze(2), [128, 16, 64]), op=ALU.mult),
                  reads=[("x2", c), ("dth", c)], writes=["xdt"])
                A("pool", lambda e, x3=x3, c=c: e.tensor_tensor(out=xw[:].rearrange("p (a d) -> p a d", a=16), in0=x3, in1=bc(wq[:, c, :].unsqueeze(2), [128, 16, 64]), op=ALU.mult),
                  reads=[("x2", c), ("wq", c)], writes=["xw"])
                A("pool", lambda e, x3=x3: e.tensor_tensor(out=xd[:].rearrange("p (a d) -> p a d", a=16), in0=x3, in1=bc(dskh_bc[:, l, :].unsqueeze(2), [128, 16, 64]), op=ALU.mult),
                  reads=[("x2", c), "dskh"], writes=["xd"])
                STEP()
                ydk = []
                for hb in range(2):
                    bank, key = PB.get()
                    ydk.append((bank, key))
                    A("pe", lambda e, bank=bank, hb=hb: e.matmul(bank[:, :], lhsT=identb[:], rhs=xd[:, hb * 512:(hb + 1) * 512], start=True, stop=False),
                      reads=["identb", "xd"], writes=[key])
                    for h8 in range(8):
                        hd = hb * 8 + h8
                        A("pe", lambda e, bank=bank, h8=h8, hd=hd: e.matmul(bank[:, h8 * 64:(h8 + 1) * 64], lhsT=MT[:, hd, :], rhs=xdt[:, hd * 64:(hd + 1) * 64], start=False, stop=(h8 == 7)),
                          reads=["MT", "xdt"], writes=[key])
                    STEP()
                for hb in range(2):
                    bank, key = PB.get()
                    for gg in range(2):
                        g = hb * 2 + gg
                        A("pe", lambda e, bank=bank, gg=gg, g=g, cs=cs: e.matmul(bank[:, gg * 256:(gg + 1) * 256], lhsT=BCT[:, 4 + g, cs], rhs=Sbf[l][:, g * 256:(g + 1) * 256], start=True, stop=True),
                          reads=[("BCT", 4 + g), ("Sbf", l)], writes=[key])
                    hs = slice(hb * 512, (hb + 1) * 512)
                    A("dve", lambda e, bank=bank, hb=hb, hs=hs, c=c: e.tensor_tensor(out=t1[:, hs].rearrange("p (a d) -> p a d", a=8), in0=bank[:, :].rearrange("p (a d) -> p a d", a=8),
                                                                                  in1=bc(ecumh[:, c, hb * 8:(hb + 1) * 8].unsqueeze(2), [128, 8, 64]), op=ALU.mult),
                      reads=[key, ("ecumh", c)], writes=[("t1", hb)])
                    ybank, ykey = ydk[hb]
                    A("dve", lambda e, ybank=ybank, hs=hs: e.tensor_tensor(out=t1[:, hs], in0=t1[:, hs], in1=ybank[:, :], op=ALU.add),
                      reads=[ykey, ("t1", hb)], writes=[("t1", hb)])
                    NEED(("zs", c))
                    A("dve", lambda e, hs=hs, c=c: e.tensor_tensor(out=t1[:, hs], in0=t1[:, hs], in1=zs[:, c, hs], op=ALU.mult),
                      reads=[("t1", hb), ("zs", c)], writes=[("t1", hb)])
                    STEP()
                for g in range(4):
                    A("act", lambda e, g=g: e.activation(out=yn[:, g * 256:(g + 1) * 256], in_=t1[:, g * 256:(g + 1) * 256], func=AF.Square, accum_out=sq4[:, g:g + 1]),
                      reads=[("t1", g // 2)], writes=[("sq4", g), ("yn", g // 2)])
                sq4k = [("sq4", g) for g in range(4)]
                A("dve", lambda e: e.tensor_scalar(out=r4[:], in0=sq4[:], scalar1=1.0 / 256, scalar2=4 * EPS, op0=ALU.mult, op1=ALU.add), reads=sq4k, writes=["r4"])
                A("pool", lambda e: e.tensor_tensor(out=r4[:], in0=r4[:], in1=bc(mhalf[:, 0:1], [128, 4]), op=ALU.pow), reads=["r4", "mhalf"], writes=["r4"])
                for g in range(4):
                    gs = slice(g * 256, (g + 1) * 256)
                    if g % 2 == 0:
                        A("dve", lambda e, g=g, gs=gs: e.tensor_scalar(out=yn[:, gs], in0=t1[:, gs], scalar1=r4[:, g:g + 1], scalar2=None, op0=ALU.mult),
                          reads=["r4", ("t1", g // 2)], writes=[("yn", g // 2)])
                    else:
                        A("act", lambda e, g=g, gs=gs: e.activation(out=yn[:, gs], in_=t1[:, gs], func=AF.Copy, scale=r4[:, g:g + 1]),
                          reads=["r4", ("t1", g // 2)], writes=[("yn", g // 2)])
                STEP()
                bank, key = PT.get()
                for k in range(8):
                    A("pe", lambda e, k=k, bank=bank: e.transpose(out=bank[:, k * 128:(k + 1) * 128], in_=yn[:, k * 128:(k + 1) * 128], identity=identb[:]),
                      reads=[("yn", 0), ("yn", 1), "identb"], writes=[key])
                A("dve", lambda e, bank=bank, c=c: e.tensor_tensor(out=xbcT[:, 0:8, 4 + c * 128:4 + (c + 1) * 128], in0=bank[:, :].rearrange("p (k t) -> p k t", k=8),
                                                                    in1=bc(ssdnw[:, l * 8:(l + 1) * 8].unsqueeze(2), [128, 8, 128]), op=ALU.mult),
                  reads=[key, "ssdnw"], writes=[("xbcT", b) for b in range(8)])
                if c == NCH - 1:
                    GIVE("ysT")
                STEP()
                A("pool", lambda e, c=c: e.tensor_tensor(out=stmp[:].rearrange("p (a d) -> p a d", a=16), in0=Sssd[l][:].rearrange("p (a d) -> p a d", a=16),
                                                         in1=bc(cdb[:, c, :].unsqueeze(2), [128, 16, 64]), op=ALU.mult),
                  reads=[("Sssd", l), ("cdb", c)], writes=["stmp", "stmp2"])
                for hb in range(2):
                    bank, key = PB.get()
                    for gg in range(2):
                        g = hb * 2 + gg
                        A("pe", lambda e, bank=bank, gg=gg, g=g, c=c: e.matmul(bank[:, gg * 256:(gg + 1) * 256], lhsT=B2tok[:, c, g * 128:(g + 1) * 128], rhs=xw[:, g * 256:(g + 1) * 256], start=True, stop=True),
                          reads=[("B2tok", c), "xw"], writes=[key])
                    hs = slice(hb * 512, (hb + 1) * 512)
                    A("dve", lambda e, bank=bank, hs=hs: e.tensor_tensor(out=Sssd[l][:, hs], in0=stmp[:, hs], in1=bank[:, :], op=ALU.add),
                      reads=[key, "stmp" if hb == 0 else "stmp2"], writes=[("Sssd", l)])
                A("act", lambda e: e.activation(out=Sbf[l][:], in_=Sssd[l][:], func=AF.Copy), reads=[("Sssd", l)], writes=[("Sbf", l)])
                STEP()

            REC[0] = None
            PB.ring = PBfull
            given = set()
            merge([sH, sF], given)
            MARK("P3")

            def merge_branch(b):
                for j in range(2):
                    wg, wgkey = next_w((l, "in", C_M + b * 1024 + j * 512))
                    for jj in range(4):
                        bank, key = PB.get()
                        for k in range(8):
                            A("pe", lambda e, k=k, jj=jj, wg=wg, bank=bank: e.matmul(bank[:, 0:N], lhsT=wg[:, k, jj * 128:(jj + 1) * 128], rhs=uT[:, k, :], start=(k == 0), stop=(k == 7)),
                              reads=uT_all + [wgkey], writes=[key])
                        A("act", lambda e, bank=bank, jj=jj: e.activation(out=gth[b][:, jj, :], in_=bank[:, 0:N], func=AF.Tanh, scale=0.5), reads=[key], writes=[("gth", jj)])
                        STEP()
                    wbr, wbkey = next_w((l, "br", (b, j)))
                    if b == 0:
                        NEED("ysT")
                    for jj in range(4):
                        bank, key = PB.get()
                        for k in range(8):
                            A("pe", lambda e, k=k, jj=jj, wbr=wbr, bank=bank: e.matmul(bank[:, 0:N], lhsT=wbr[:, k, jj * 128:(jj + 1) * 128], rhs=xbcT[:, b * 8 + k, 4:4 + N], start=(k == 0), stop=(k == 7)),
                              reads=[("xbcT", b * 8 + k) for k in range(8)] + [wbkey], writes=[key])
                        if b == 0:
                            A("dve", lambda e, bank=bank, jj=jj, j=j: e.scalar_tensor_tensor(out=t0m[:, j * 4 + jj, :], in0=gth[0][:, jj, :], scalar=1.0, in1=bank[:, 0:N], op0=ALU.add, op1=ALU.mult),
                              reads=[key, ("gth", jj)], writes=[("t0m", j * 4 + jj)])
                        else:
                            t1m, t1key = T1M.get()
                            A("dve", lambda e, bank=bank, jj=jj, t1m=t1m: e.scalar_tensor_tensor(out=t1m[:, :], in0=gth[1][:, jj, :], scalar=1.0, in1=bank[:, 0:N], op0=ALU.add, op1=ALU.mult),
                              reads=[key, ("gth", jj)], writes=[t1key])
                            A("pool", lambda e, jj=jj, j=j, t1m=t1m: e.tensor_tensor(out=qkT[:, j * 4 + jj, :], in0=t0m[:, j * 4 + jj, :], in1=t1m[:, :], op=ALU.add),
                              reads=[t1key, ("t0m", j * 4 + jj)], writes=[("qkT", j * 4 + jj)])
                        STEP()

            sJ = []
            REC[0] = sJ
            PB.ring = PBf
            NEED("Fdone")
            merge_branch(0)
            sI = []
            REC[0] = sI
            PB.ring = PBi
            for c in range(NCH):
                cs = slice(c * 128, (c + 1) * 128)
                bank, key = PB.get()
                A("pe", lambda e, bank=bank, cs=cs: e.matmul(bank[:, :], lhsT=alrT[:, cs], rhs=w2b[:, l * 512:(l + 1) * 512], start=True, stop=False),
                  reads=["alrT", "w2b"], writes=[key])
                A("pe", lambda e, bank=bank: e.matmul(bank[:, :], lhsT=sel[:, 2, :], rhs=rows[:, l * 512:(l + 1) * 512], start=False, stop=False),
                  reads=["sel", "rows"], writes=[key])
                A("pe", lambda e, bank=bank: e.matmul(bank[:, :], lhsT=sel[:, 2, :], rhs=rows[:, 1024 + l * 512:1024 + (l + 1) * 512], start=False, stop=True),
                  reads=["sel", "rows"], writes=[key])
                A("act", lambda e, bank=bank: e.activation(out=e1[:], in_=bank[:, :], func=AF.Exp, scale=-1.0), reads=[key], writes=["e1"])
                A("act", lambda e: e.activation(out=e1[:], in_=e1[:], func=AF.Ln, bias=1.0), reads=["e1"], writes=["e1"])
                STEP()
                bankg, keyg = PB.get()
                for j in range(4):
                    A("pe", lambda e, bankg=bankg, j=j: e.matmul(bankg[:, j * 128:(j + 1) * 128], lhsT=e1[:, j * 128:(j + 1) * 128], rhs=tri16[:], start=True, stop=True),
                      reads=["e1", "tri16"], writes=[keyg])
                A("act", lambda e, bankg=bankg: e.activation(out=eg[:], in_=bankg[:, :], func=AF.Exp, bias=QS_LN), reads=[keyg], writes=["eg"])
                A("act", lambda e, bankg=bankg: e.activation(out=emg[:], in_=bankg[:, :], func=AF.Exp, scale=-1.0), reads=[keyg], writes=["emg"])
                A("act", lambda e, bankg=bankg: e.activation(out=egl[:, :], in_=bankg[:, :].rearrange("p (j t) -> p j t", j=4)[:, :, 127], func=AF.Exp), reads=[keyg], writes=["egl"])
                NEED("q")
                NEED("k")
                A("dve", lambda e, cs=cs: e.tensor_tensor(out=qtl[:], in0=qkT[:, 0:4, cs], in1=eg[:].rearrange("p (j t) -> p j t", j=4), op=ALU.mult),
                  reads=["eg"] + [("qkT", j) for j in range(4)], writes=["qtl"])
                A("dve", lambda e, cs=cs: e.tensor_tensor(out=ktl[:], in0=qkT[:, 4:8, cs], in1=emg[:].rearrange("p (j t) -> p j t", j=4), op=ALU.mult),
                  reads=["emg"] + [("qkT", 4 + j) for j in range(4)], writes=["ktl"])
                STEP()
                bank, key = PB.get()
                for hd in range(4):
                    A("pe", lambda e, bank=bank, hd=hd: e.matmul(bank[:, hd * 128:(hd + 1) * 128], lhsT=ktl[:, hd, :], rhs=qtl[:, hd, :], start=True, stop=True),
                      reads=["ktl", "qtl"], writes=[key])
                A("dve", lambda e, bank=bank: e.tensor_tensor(out=attm[:], in0=bank[:, :].rearrange("p (g t) -> p g t", g=4), in1=bc(maskb[:, :].unsqueeze(1), [128, 4, 128]), op=ALU.mult),
                  reads=[key, "maskb"], writes=["attm"])
                STEP()
                bankt, keyt = PT.get()
                for hd in range(4):
                    A("pe", lambda e, bankt=bankt, hd=hd: e.transpose(out=bankt[:, hd * 128:(hd + 1) * 128], in_=ktl[:, hd, :], identity=identb[:]),
                      reads=["ktl", "identb"], writes=[keyt])
                A("act", lambda e, bankt=bankt: e.activation(out=ktok[:], in_=bankt[:, 0:512], func=AF.Copy), reads=[keyt], writes=["ktok"])
                STEP()
                obk = []
                NEED(("v", c))
                for hb in range(2):
                    bank, key = PB.get()
                    obk.append((bank, key))
                    for hh in range(2):
                        hd = hb * 2 + hh
                        o = bank[:, hh * 256:(hh + 1) * 256]
                        A("pe", lambda e, o=o, hd=hd, c=c: e.matmul(o, lhsT=attm[:, hd, :], rhs=vtok[:, c, hd * 256:(hd + 1) * 256], start=True, stop=False),
                          reads=["attm", ("vtok", c)], writes=[key])
                        A("pe", lambda e, o=o, hd=hd: e.matmul(o, lhsT=qtl[:, hd, :], rhs=Gbf[l][:, hd * 256:(hd + 1) * 256], start=False, stop=True),
                          reads=["qtl", ("Gbf", l)], writes=[key])
                        A("act", lambda e, o=o, hd=hd: e.activation(out=on[:, hd * 256:(hd + 1) * 256], in_=o, func=AF.Square, accum_out=sq4g[:, hd:hd + 1]), reads=[key], writes=[("sq4g", hd), ("on", hd // 2)])
                    STEP()
                A("dve", lambda e: e.tensor_scalar(out=r4g[:], in0=sq4g[:], scalar1=1.0 / 256, scalar2=EPS, op0=ALU.mult, op1=ALU.add), reads=[("sq4g", i) for i in range(4)], writes=["r4g"])
                A("pool", lambda e: e.tensor_tensor(out=r4g[:], in0=r4g[:], in1=bc(mhalf[:, 0:1], [128, 4]), op=ALU.pow), reads=["r4g", "mhalf"], writes=["r4g"])
                for hb in range(2):
                    bank, key = obk[hb]
                    A("dve", lambda e, bank=bank, hb=hb: e.tensor_tensor(out=on[:, hb * 512:(hb + 1) * 512].rearrange("p (a d) -> p a d", a=2), in0=bank[:, :].rearrange("p (a d) -> p a d", a=2),
                                                                          in1=bc(r4g[:, hb * 2:(hb + 1) * 2].unsqueeze(2), [128, 2, 256]), op=ALU.mult),
                      reads=[key, "r4g"], writes=[("on", hb)])
                STEP()
                NEED("g")
                bank, key = PT.get()
                for k in range(8):
                    A("pe", lambda e, k=k, bank=bank: e.transpose(out=bank[:, k * 128:(k + 1) * 128], in_=on[:, k * 128:(k + 1) * 128], identity=identb[:]),
                      reads=[("on", 0), ("on", 1), "identb"], writes=[key])
                A("dve", lambda e, bank=bank, cs=cs, c=c: e.tensor_tensor(out=xbcT[:, 8:16, 4 + c * 128:4 + (c + 1) * 128], in0=bank[:, :].rearrange("p (k t) -> p k t", k=8),
                                                                         in1=sgT[:, :, cs], op=ALU.mult),
                  reads=[key] + [("sgT", b) for b in range(8)], writes=[("xbcT", 8 + b) for b in range(8)])
                STEP()
                for hb in range(2):
                    bank, key = PB.get()
                    for hh in range(2):
                        hd = hb * 2 + hh
                        A("pe", lambda e, bank=bank, hh=hh, hd=hd, c=c: e.matmul(bank[:, hh * 256:(hh + 1) * 256], lhsT=ktok[:, hd * 128:(hd + 1) * 128], rhs=vtok[:, c, hd * 256:(hd + 1) * 256], start=True, stop=True),
                          reads=["ktok", ("vtok", c)], writes=[key])
                    hs = slice(hb * 512, (hb + 1) * 512)
                    A("dve", lambda e, bank=bank, hs=hs: e.tensor_tensor(out=Sgla[l][:, hs], in0=Sgla[l][:, hs], in1=bank[:, :], op=ALU.add),
                      reads=[key, ("Sgla", l, hb)], writes=[("Sgla", l, hb)])
                for hd in range(4):
                    A("act", lambda e, hd=hd: e.activation(out=Gbf[l][:, hd * 256:(hd + 1) * 256], in_=Sgla[l][:, hd * 256:(hd + 1) * 256], func=AF.Copy, scale=egl[:, hd:hd + 1]),
                      reads=[("Sgla", l, hd // 2), "egl"], writes=[("Gbf", l)])
                for hd in range(4):
                    A("pool", lambda e, hd=hd: e.tensor_scalar(out=Sgla[l][:, hd * 256:(hd + 1) * 256], in0=Sgla[l][:, hd * 256:(hd + 1) * 256], scalar1=egl[:, hd:hd + 1], scalar2=0.0, op0=ALU.mult, op1=ALU.add),
                      reads=[("Sgla", l, hd // 2), "egl"], writes=[("Sgla", l, hd // 2)])
                STEP()
            REC[0] = None
            PB.ring = PBfull
            merge([sI, sJ], given)
            MARK("P4")
            merge_branch(1)
            MARK("OUT")
            wos = [next_w((l, "out", 0), pf=1), next_w((l, "out", 1), pf=0)]
            for c in range(NCH):
                for j2 in range(2):
                    wo, wokey = wos[j2]
                    bank, key = PB.get()
                    for k in range(8):
                        A("pe", lambda e, k=k, c=c, wo=wo, bank=bank: e.matmul(bank[:, :], lhsT=qkT[:, k, c * 128:(c + 1) * 128], rhs=wo[:, k, :], start=(k == 0), stop=(k == 7)),
                          reads=[("qkT", k) for k in range(8)] + [wokey], writes=[key])
                    A("dve", lambda e, bank=bank, c=c, j2=j2: e.scalar_tensor_tensor(out=h[:, c, j2 * 512:(j2 + 1) * 512], in0=bank[:, :], scalar=0.5, in1=h[:, c, j2 * 512:(j2 + 1) * 512], op0=ALU.mult, op1=ALU.add),
                      reads=[key, ("h", c)], writes=[("h", c)])
            if first and not last_layer:
                A("pool", lambda e: e.memset(h[0:112, 0, :], 0.0), writes=[("h", 0)])

        for ti in range(ntiles):
            for c in range(NCH):
                gc = ti * NCH + c
                if gc == 0:
                    A("pool", lambda e: e.memset(h[:, 0, :], 0.0), writes=[("h", 0)])
                    A("sp", lambda e: e.dma_start(out=h[112:128, 0, :], in_=meta_d), writes=[("h", 0)], dma_sem=xsems[0])
                else:
                    r0 = (gc - 1) * 128
                    A("sp", lambda e, c=c, r0=r0: e.dma_start(out=h[:, c, :], in_=x_d[r0:r0 + 128, :]), writes=[("h", c)], dma_sem=xsems[c])
            for l in range(NL):
                tile_layer(ti, l)
            for c in range(NCH):
                gc = ti * NCH + c
                if gc == 0:
                    continue
                r0 = (gc - 1) * 128
                ub, ukey = UB.get()
                A("act", lambda e, c=c, ub=ub: e.activation(out=ub[:], in_=h[:, c, :], func=AF.Square, accum_out=ssq[:, c:c + 1]), reads=[("h", c)], writes=[("ssq", c), ukey])
                A("dve", lambda e, c=c: e.tensor_scalar(out=rs[:, c:c + 1], in0=ssq[:, c:c + 1], scalar1=1.0 / D, scalar2=EPS, op0=ALU.mult, op1=ALU.add),
                  reads=[("ssq", c)], writes=[("rs", c)])
                A("pool", lambda e, c=c: e.tensor_tensor(out=rs[:, c:c + 1], in0=rs[:, c:c + 1], in1=mhalf[:], op=ALU.pow), reads=[("rs", c), "mhalf"], writes=[("rs", c)])
                A("dve", lambda e, c=c: e.scalar_tensor_tensor(out=h[:, c, :], in0=h[:, c, :], scalar=rs[:, c:c + 1], in1=fnw_bc[:], op0=ALU.mult, op1=ALU.mult),
                  reads=[("h", c), ("rs", c), "fnw"], writes=[("h", c)])
                ok = ("out", gc)
                out_keys.append(ok)
                A("sp", lambda e, c=c, r0=r0: e.dma_start(out=out_d[r0:r0 + 128, :], in_=h[:, c, :]), reads=[("h", c)], writes=[ok], dma_sem=osems[c])
        A("sp", None, reads=out_keys)
        assert wstate["used"] == len(wlist)
        build.model_us = dict(SIM["free"])
        build.model_log = SIM["log"]
        build.mlog = SIM["mlog"]
        with nc.Block() as block:
            S.emit(block)
    return nc


def host_consts():
    i = np.arange(128)
    tri = (i[:, None] <= i[None, :]).astype(np.float32)
    strict = (i[:, None] > i[None, :]).astype(np.float32)
    mask0 = (i >= 112).astype(np.float32)[:, None]
    sel = np.zeros((128, 3, 128), np.float32)
    for r in range(3):
        sel[r, r, :] = 1.0
    return {"ident": np.eye(128, dtype=np.float32), "tri": tri, "strict": strict, "mask0": mask0, "sel": sel.reshape(128, 384)}


def make_in_maps(inputs, ncores, TR):
    f = lambda a: np.ascontiguousarray(np.asarray(a, dtype=np.float32))
    common = dict(host_consts())
    common["meta"] = f(inputs["meta_tokens"])
    common["w_in"] = f(inputs["w_in"])
    common["w_br"] = f(inputs["w_branch"])
    common["w_out"] = f(inputs["w_out"])
    common["normw"] = f(np.asarray(inputs["norm_w"]).reshape(2, 8, 128).transpose(2, 0, 1).reshape(128, 16))
    common["cw"] = f(np.asarray(inputs["conv_w"]).reshape(2, 4, 16, 128).transpose(3, 0, 2, 1).reshape(128, 128))
    common["cb"] = f(np.asarray(inputs["conv_b"]).reshape(1, 4096))
    common["dtb"] = f(inputs["dt_bias"])
    common["alog"] = f(inputs["a_log"])
    common["dskip"] = f(inputs["d_skip"])
    common["ssdnw"] = f(np.asarray(inputs["ssd_norm_w"]).reshape(2, 8, 128).transpose(2, 0, 1).reshape(128, 16))
    common["w2"] = f(np.asarray(inputs["gla_gate_w2"]).transpose(1, 0, 2).reshape(16, 1024))
    common["gb"] = f(np.asarray(inputs["gla_gate_b"]).reshape(1, 1024))
    common["gnw"] = f(np.asarray(inputs["gla_norm_w"]).reshape(2, 2, 128).transpose(2, 0, 1).reshape(128, 4))
    common["fnw"] = f(np.asarray(inputs["final_norm_w"]).reshape(1, 1024))
    x = np.asarray(inputs["x"], dtype=np.float32)
    maps = []
    for b in range(ncores):
        m = dict(common)
        m["x"] = np.ascontiguousarray(x[b, :TR])
        maps.append(m)
    return maps


_NC_CACHE = {}


def kernel(**inputs):
    x = np.asarray(inputs["x"])
    B, T, _ = x.shape
    NT = T // 128 + 1
    key = (NT,)
    if key not in _NC_CACHE:
        _NC_CACHE[key] = build(NT=NT, NCH=3, NL=2)
    nc = _NC_CACHE[key]
    maps = make_in_maps(inputs, B, T)
    res = run_bass_kernel_spmd(nc, maps, core_ids=list(range(B)))
    out = np.stack([np.asarray(r["out"], dtype=np.float32) for r in res.results], axis=0)
    return out
```

```python
import math
import numpy as np
import concourse.bass as bass
import concourse.mybir as mybir
from concourse.bass_utils import run_bass_kernel_spmd
from contextlib import ExitStack

F32 = mybir.dt.float32
BF16 = mybir.dt.bfloat16
AF = mybir.ActivationFunctionType
ALU = mybir.AluOpType

D = 1024
NIN = 8224
EPS = 1e-6
C_Z, C_XBC, C_DT, C_Q, C_K, C_V, C_G, C_ALR, C_M = 0, 1024, 3072, 3088, 3600, 4112, 5136, 6160, 6176


class Op:
    __slots__ = ("eng", "fn", "deps", "sig", "count", "sem", "is_dma", "inc", "fin")

    def __init__(self, eng, fn, is_dma=False):
        self.eng = eng
        self.fn = fn
        self.deps = set()
        self.sig = False
        self.count = 0
        self.sem = None
        self.is_dma = is_dma
        self.inc = 1
        self.fin = 0.0


class Sched:
    ENGS = ("pe", "act", "dve", "pool", "sp")

    def __init__(self, nc, stack):
        self.nc = nc
        self.stack = stack
        self.ops = []
        self.res = {}
        self.engsem = {e: stack.enter_context(nc.semaphore("s_" + e)) for e in self.ENGS}
        self.dma_counts = {}
        self.alias = {}

    def new_sem(self, name):
        return self.stack.enter_context(self.nc.semaphore(name))

    def uniq(self, key):
        lst = self.alias.setdefault(key, [])
        k = ("__u", key, len(lst))
        lst.append(k)
        return k

    def add(self, eng, fn, reads=(), writes=(), dma_sem=None, raw_keys=False):
        op = Op(eng, fn, is_dma=dma_sem is not None)
        if not raw_keys:
            reads = [rr for r in reads for rr in self.alias.get(r, [r])]
            writes = [ww for w in writes for ww in self.alias.get(w, [w])]
        if dma_sem is not None:
            op.sem = dma_sem
            op.sig = True
            op.inc = 16
            c = self.dma_counts.get(id(dma_sem), 0) + 16
            self.dma_counts[id(dma_sem)] = c
            op.count = c
        for r in reads:
            st = self.res.get(r)
            if st is None:
                st = [None, []]
                self.res[r] = st
            if st[0] is not None:
                op.deps.add(st[0])
            st[1].append(op)
        for w in writes:
            st = self.res.get(w)
            if st is None:
                st = [None, []]
                self.res[w] = st
            if st[0] is not None:
                op.deps.add(st[0])
            last = {}
            for rd in st[1]:
                if rd is op:
                    continue
                if rd.is_dma:
                    op.deps.add(rd)
                else:
                    last[rd.eng] = rd
            for rd in last.values():
                op.deps.add(rd)
            st[0] = op
            st[1] = []
        op.deps.discard(op)
        self.ops.append(op)
        return op

    def peek(self, reads, writes):
        reads = [rr for r in reads for rr in self.alias.get(r, [r])]
        writes = [ww for w in writes for ww in self.alias.get(w, [w])]
        deps = []
        for r in reads:
            st = self.res.get(r)
            if st is not None and st[0] is not None:
                deps.append(st[0])
        for w in writes:
            st = self.res.get(w)
            if st is None:
                continue
            if st[0] is not None:
                deps.append(st[0])
            last = {}
            for rd in st[1]:
                if rd.is_dma:
                    deps.append(rd)
                else:
                    last[rd.eng] = rd
            deps.extend(last.values())
        return deps

    def finalize(self):
        for op in self.ops:
            if op.eng == "pe" and not op.is_dma:
                op.deps = {d for d in op.deps if not (d.eng == "pe" and not d.is_dma)}
            for d in op.deps:
                d.sig = True
        cnt = {e: 0 for e in self.ENGS}
        for op in self.ops:
            if op.is_dma:
                continue
            if op.sig:
                cnt[op.eng] += 1
                op.count = cnt[op.eng]
                op.sem = self.engsem[op.eng]

    def emit(self, block):
        self.finalize()
        by_eng = {e: [o for o in self.ops if o.eng == e] for e in self.ENGS}

        def run(engine, ops):
            waited = {}
            for op in ops:
                need = {}
                for d in op.deps:
                    k = id(d.sem)
                    if need.get(k, (None, 0))[1] < d.count:
                        need[k] = (d.sem, d.count)
                for k, (sem, c) in need.items():
                    if waited.get(k, 0) < c:
                        engine.wait_ge(sem, c)
                        waited[k] = c
                if op.fn is None:
                    continue
                ins = op.fn(engine)
                if op.sig:
                    ins.then_inc(op.sem, op.inc)

        @block.tensor
        def _(e):
            run(e, by_eng["pe"])

        @block.scalar
        def _(e):
            run(e, by_eng["act"])

        @block.vector
        def _(e):
            run(e, by_eng["dve"])

        @block.gpsimd
        def _(e):
            run(e, by_eng["pool"])

        @block.sync
        def _(e):
            run(e, by_eng["sp"])


class Ring:
    def __init__(self, items, name, keys=None):
        self.items = items
        self.name = name
        self.keys = keys if keys is not None else [(name, j) for j in range(len(items))]
        self.i = 0

    def get(self):
        j = self.i % len(self.items)
        self.i += 1
        return self.items[j], self.keys[j]


class RingProxy:
    def __init__(self, ring):
        self.ring = ring

    def get(self):
        return self.ring.get()


import os
OPT = {"zs_eng": "pool", "diag_eng": "dve", "ub_eng": "dve", "greedy": 0}
for _k in list(OPT):
    if os.environ.get("KOPT_" + _k):
        OPT[_k] = type(OPT[_k])(os.environ["KOPT_" + _k])


def build(NT=33, NCH=3, NL=2, dbg=False):
    assert NT % NCH == 0
    N = NCH * 128
    ntiles = NT // NCH
    TR = (NT - 1) * 128
    nc = bass.Bass("TRN2", target_bir_lowering=False)

    def din(name, shape):
        return nc.dram_tensor(name, list(shape), F32, kind="ExternalInput").ap()

    x_d = din("x", [TR, D])
    meta_d = din("meta", [16, D])
    win_d = din("w_in", [2, D, NIN])
    wbr_d = din("w_br", [2, 2, D, D])
    wout_d = din("w_out", [2, D, D])
    normw_d = din("normw", [128, 16])
    cw_d = din("cw", [128, 128])
    cb_d = din("cb", [1, 2 * 2048])
    dtb_d = din("dtb", [2, 16])
    alog_d = din("alog", [2, 16])
    dskip_d = din("dskip", [2, 16])
    ssdnw_d = din("ssdnw", [128, 16])
    w2_d = din("w2", [16, 2 * 512])
    gb_d = din("gb", [1, 2 * 512])
    gnw_d = din("gnw", [128, 4])
    fnw_d = din("fnw", [1, D])
    ident_d = din("ident", [128, 128])
    tri_d = din("tri", [128, 128])
    strict_d = din("strict", [128, 128])
    mask0_d = din("mask0", [128, 1])
    sel_d = din("sel", [128, 3 * 128])
    out_d = nc.dram_tensor("out", [TR, D], F32, kind="ExternalOutput").ap()

    with ExitStack() as st:
        S = Sched(nc, st)

        def sb(name, shape, dt=F32):
            return st.enter_context(nc.sbuf_tensor("s_" + name, list(shape), dt))

        def ps(name, shape, dt=F32):
            return st.enter_context(nc.psum_tensor(name, list(shape), dt))

        REC = [None]

        class _FakeIns:
            def then_inc(self, *a, **k):
                return self

        class _FakeEng:
            def __init__(self):
                self.calls = []

            def __getattr__(self, name):
                def f(*a, **k):
                    self.calls.append((name, a, k))
                    return _FakeIns()
                return f

        def _fsz(ap):
            n = 1
            for d_ in ap.shape[1:]:
                n *= d_
            return n

        def _in_psum(ap):
            return "PSum" in type(ap.tensor).__name__

        def est_dur(eng, fn, is_dma):
            if fn is None:
                return 0.0
            if is_dma:
                return 3.0
            try:
                fe = _FakeEng()
                fn(fe)
                name, a, k = fe.calls[-1]
                if eng == "pe":
                    if name == "transpose":
                        return 0.13
                    rhs = k["rhs"]
                    n = _fsz(rhs)
                    return max(64, n) / 2400.0 * (4.0 if rhs.dtype == F32 else 1.0) + 0.01
                src = k.get("in_", k.get("in0"))
                n = _fsz(src) if src is not None else 256
                if eng == "act":
                    return 0.22 + n * 0.0009 + (0.09 if k.get("accum_out") is not None else 0.0)
                if eng == "dve":
                    t = 0.07 + n * 0.00105
                    if name == "tensor_tensor" and not _in_psum(k["in0"]) and not _in_psum(k["in1"]):
                        t = 0.07 + n * 0.0021
                    return t
                if eng == "pool":
                    if name == "tensor_tensor" and k.get("op") == ALU.pow:
                        return 0.8
                    if name == "tensor_scalar":
                        return 0.1 + n * 0.0015
                    return 0.3 + n * 0.0016
            except Exception:
                pass
            return 0.5

        SIM = {"free": {e: 0.0 for e in Sched.ENGS}, "pebusy": 0.0, "log": [], "dbg": dbg, "mlog": []}

        def MARK(name):
            SIM["log"].append((name, max(SIM["free"].values()), SIM["free"]["pe"], SIM["pebusy"]))

        def _lat(d, eng):
            return 0.35 if d.eng != eng or d.is_dma else 0.12

        def commit(eng, fn, reads, writes, dma_sem, kw, dur):
            op = S.add(eng, fn, reads, writes, dma_sem=dma_sem, **kw)
            t = SIM["free"][eng]
            for d in op.deps:
                if eng == "pe" and d.eng == "pe" and not d.is_dma:
                    continue
                t = max(t, d.fin + _lat(d, eng))
            if dma_sem is not None:
                SIM["free"][eng] = t + 0.1
                op.fin = t + dur
            else:
                op.fin = t + dur
                SIM["free"][eng] = op.fin
                if eng == "pe":
                    SIM["pebusy"] += dur
            return op

        def A(eng, fn, reads=(), writes=(), dma_sem=None, **kw):
            dur = est_dur(eng, fn, dma_sem is not None)
            if REC[0] is None:
                return commit(eng, fn, reads, writes, dma_sem, kw, dur)
            REC[0].append(("op", (eng, fn, list(reads), list(writes), dma_sem, kw, dur)))
            return None

        def STEP():
            if REC[0] is not None:
                REC[0].append(("step",))

        def NEED(tok):
            if REC[0] is not None:
                REC[0].append(("step",))
                REC[0].append(("need", tok))

        def GIVE(tok):
            if REC[0] is not None:
                REC[0].append(("give", tok))

        def step_stall(step):
            free = dict(SIM["free"])
            lw, lr = {}, {}
            stall = 0.0
            for it in step:
                if it[0] != "op":
                    continue
                eng, fn, r, w, ds, kw, dur = it[1]
                t = free[eng]
                t0 = t
                for d in S.peek(r, w):
                    if eng == "pe" and d.eng == "pe" and not d.is_dma:
                        continue
                    t = max(t, d.fin + _lat(d, eng))
                for key in r:
                    if key in lw:
                        t = max(t, lw[key] + 0.3)
                for key in w:
                    if key in lw:
                        t = max(t, lw[key] + 0.3)
                    if key in lr:
                        t = max(t, lr[key] + 0.3)
                stall += (t - t0) * (1.0 if eng == "pe" else 0.35)
                fin = t + dur
                if ds is not None:
                    free[eng] = t + 0.1
                else:
                    free[eng] = fin
                for key in r:
                    lr[key] = max(lr.get(key, 0.0), fin)
                for key in w:
                    lw[key] = fin
                    lr.pop(key, None)
            return stall

        def merge(streams, given):
            segs = []
            for stq in streams:
                steps = [[]]
                for it in stq:
                    if it[0] == "step":
                        if steps[-1]:
                            steps.append([])
                    else:
                        steps[-1].append(it)
                if not steps[-1]:
                    steps.pop()
                segs.append(steps)
            pos = [0] * len(segs)
            tot = [max(len(x), 1) for x in segs]
            while any(pos[i] < len(segs[i]) for i in range(len(segs))):
                cands = []
                for i in range(len(segs)):
                    if pos[i] >= len(segs[i]):
                        continue
                    step = segs[i][pos[i]]
                    if any(it[0] == "need" and it[1] not in given for it in step):
                        continue
                    cands.append((step_stall(step), i))
                assert cands, "merge deadlock"
                pick = None
                if OPT["greedy"]:
                    for st_, i in cands:
                        if st_ < 0.25:
                            pick = i
                            break
                    if pick is None:
                        pick = min(cands)[1]
                else:
                    pick = min(cands, key=lambda ci: pos[ci[1]] / tot[ci[1]])[1]
                for it in segs[pick][pos[pick]]:
                    if it[0] == "op":
                        eng, fn, r, w, ds, kw, dur = it[1]
                        commit(eng, fn, r, w, ds, kw, dur)
                    elif it[0] == "give":
                        given.add(it[1])
                if SIM.get("dbg"):
                    SIM["mlog"].append((pick, pos[pick], [round(c_[0], 2) for c_ in cands], dict(SIM["free"])))
                pos[pick] += 1
        h = sb("h", [128, NCH, D])
        ubs = [sb("ub%d" % i, [128, D], BF16) for i in range(2)]
        UB = Ring(ubs, "ub")
        uT = sb("uT", [128, 8, N], BF16)
        NWB = 3
        wbufs = [sb("wb%d" % i, [128, 8, 512], BF16) for i in range(NWB)]
        wsems = [S.new_sem("wsem%d" % i) for i in range(NWB)]
        wmisc = sb("wmisc", [128, 2, 8, 32], BF16)
        NDG = 4
        dgs = [sb("dg%d" % i, [128, 4, 128], BF16) for i in range(NDG)]
        DG = Ring(dgs, "dg")
        XW = N + 4
        xbcT = sb("xbcT", [128, 16, XW], BF16)
        hist = sb("hist", [128, 2, 16, 4], BF16)
        x2 = sb("x2", [128, NCH, D], BF16)
        B2tok = sb("B2tok", [128, NCH, 512], BF16)
        BCT = sb("BCT", [128, 8, N], BF16)
        zs = sb("zs", [128, NCH, D], BF16)
        qkT = sb("qkT", [128, 8, N], BF16)
        vtok = sb("vtok", [128, NCH, D], BF16)
        sgT = sb("sgT", [128, 8, N], BF16)
        alrT = sb("alrT", [128, N], BF16)
        ths = [sb("th%d" % i, [128, 512], BF16) for i in range(2)]
        TH = Ring(ths, "th")
        gws = [sb("gw%d" % i, [128, 512], BF16) for i in range(2)]
        GW = Ring(gws, "gw")
        ssq = sb("ssq", [128, NCH])
        rs = sb("rs", [128, NCH])
        dtr = sb("dtr", [128, NCH, 16])
        dte_x = sb("dte_x", [128, NCH, 16])
        dt_t = sb("dt_t", [128, NCH, 16])
        da_t = sb("da_t", [128, NCH, 16])
        ecumh = sb("ecumh", [128, NCH, 16])
        dte = sb("dte", [128, NCH, 16])
        cdb = sb("cdb", [128, NCH, 16])
        wq = sb("wq", [128, NCH, 16])
        dth = sb("dth", [128, NCH, 16])
        sq4 = sb("sq4", [128, 4])
        r4 = sb("r4", [128, 4])
        sq4g = sb("sq4g", [128, 4])
        r4g = sb("r4g", [128, 4])
        egl = sb("egl", [128, 4])
        At = sb("At", [128, 8, 128])
        Lm = sb("Lm", [128, 16, 128], BF16)
        MT = sb("MT", [128, 16, 128], BF16)
        scm = sb("scm", [128, 4, 128], BF16)
        xdt = sb("xdt", [128, D], BF16)
        xw = sb("xw", [128, D], BF16)
        xd = sb("xd", [128, D], BF16)
        t1 = sb("t1", [128, D])
        yn = sb("yn", [128, D], BF16)
        stmp = sb("stmp", [128, D])
        e1 = sb("e1", [128, 512])
        eg = sb("eg", [128, 512])
        emg = sb("emg", [128, 512])
        on = sb("on", [128, D], BF16)
        qtl = sb("qtl", [128, 4, 128], BF16)
        ktl = sb("ktl", [128, 4, 128], BF16)
        ktok = sb("ktok", [128, 512], BF16)
        attm = sb("attm", [128, 4, 128], BF16)
        gth0 = sb("gth0", [128, 4, N], BF16)
        gth = [gth0, gth0]
        t0m = sb("t0m", [128, 8, N], BF16)
        t1ms = [sb("t1m%d" % i, [128, N]) for i in range(2)]
        T1M = Ring(t1ms, "t1m")
        Sssd = [sb("Sssd%d" % l, [128, D]) for l in range(2)]
        Sbf = [sb("Sbf%d" % l, [128, D], BF16) for l in range(2)]
        Sgla = [sb("Sgla%d" % l, [128, D]) for l in range(2)]
        Gbf = [sb("Gbf%d" % l, [128, D], BF16) for l in range(2)]
        identf = sb("identf", [128, 128])
        identb = sb("identb", [128, 128], BF16)
        tri = sb("tri", [128, 128])
        tri16 = sb("tri16", [128, 128])
        strict = sb("strict", [128, 128])
        maskb = sb("maskb", [128, 128], BF16)
        maskq = sb("maskq", [128, 128], BF16)
        onesf = sb("onesf", [128, 128])
        sel = sb("sel", [128, 3, 128], BF16)
        selN = sb("selN", [128, 3, N], BF16)
        rows = sb("rows", [128, 2048], BF16)
        gb8 = sb("gb8", [8, 128])
        gb8h = sb("gb8h", [8, 128], BF16)
        gb8hf = sb("gb8hf", [8, 128])
        gb8l = sb("gb8l", [8, 128], BF16)
        mhalf = sb("mhalf", [128, 1])
        mask0 = sb("mask0", [128, 1])
        normw = sb("normw", [128, 16])
        cw = sb("cw", [128, 128])
        dtb_bc = sb("dtb_bc", [128, 2, 16])
        a_bc = sb("a_bc", [128, 2, 16])
        dskh_bc = sb("dskh_bc", [128, 2, 16])
        ssdnw = sb("ssdnw", [128, 16])
        w2b = sb("w2b", [128, 1024], BF16)
        gnwh = sb("gnwh", [128, 4])
        fnw_bc = sb("fnw_bc", [128, D])
        pbs = [ps("pb%d" % i, [128, 512]) for i in range(6)]
        PBfull = Ring(pbs, "pb")
        PBh = Ring(pbs[0:3], "pb", keys=[("pb", j) for j in range(0, 3)])
        PBi = PBh
        PBf = Ring(pbs[3:6], "pb", keys=[("pb", j) for j in range(3, 6)])
        PB = RingProxy(PBfull)
        pts = [ps("pt%d" % i, [128, 1024], BF16) for i in range(2)]
        PT = Ring(pts, "pt")

        csem = S.new_sem("csem")

        csemp = S.new_sem("csemp")
        cops = []
        copsp = []

        def cload(dst, src, key, eng="sp"):
            if eng == "sp":
                cops.append(A(eng, lambda e: e.dma_start(out=dst, in_=src), writes=[S.uniq(key)], dma_sem=csem, raw_keys=True))
            else:
                copsp.append(A(eng, lambda e: e.dma_start(out=dst, in_=src), writes=[S.uniq(key)], dma_sem=csemp, raw_keys=True))

        cload(identf[:], ident_d, "identf")
        cload(tri[:], tri_d, "tri")
        cload(strict[:], strict_d, "strict")
        cload(mask0[:], mask0_d, "mask0")
        cload(normw[:], normw_d, "normw")
        cload(cw[:], cw_d, "cw")
        cload(ssdnw[:], ssdnw_d, "ssdnw")
        cload(gnwh[:], gnw_d, "gnwh")
        cload(gb8[:], gb_d.rearrange("o (a b) -> (o a) b", a=8), "gb8")
        cload(fnw_bc[:], fnw_d[0:1, :].partition_broadcast(128), "fnw")
        for l in range(2):
            cload(dtb_bc[:, l, :], dtb_d[l:l + 1, :].partition_broadcast(128), "dtb")
            cload(a_bc[:, l, :], alog_d[l:l + 1, :].partition_broadcast(128), "a_bc")
            cload(dskh_bc[:, l, :], dskip_d[l:l + 1, :].partition_broadcast(128), "dskh")
        A("pool", lambda e: e.memset(rows[:], 0.0), writes=["rows0"])
        A("pool", lambda e: e.memset(alrT[:], 0.0), writes=["alrT"])
        A("pool", lambda e: e.memset(w2b[:], 0.0), writes=["w2b0"])
        copsp.append(A("pool", lambda e: e.dma_start(out=rows[0:1, :], in_=cb_d[:, 0:2048]), reads=["rows0"], writes=[S.uniq("rows")], dma_sem=csemp, raw_keys=True))
        copsp.append(A("pool", lambda e: e.dma_start(out=rows[1:2, :], in_=cb_d[:, 2048:4096]), reads=["rows0"], writes=[S.uniq("rows")], dma_sem=csemp, raw_keys=True))
        cload(sel[:].rearrange("p r m -> p (r m)"), sel_d, "sel", eng="pool")
        copsp.append(A("pool", lambda e: e.dma_start(out=w2b[0:16, :], in_=w2_d), reads=["w2b0"], writes=[S.uniq("w2b")], dma_sem=csemp, raw_keys=True))
        for l in range(2):
            cload(wmisc[:, l, :, 0:16], win_d[l, :, C_DT:C_DT + 16].rearrange("(k p) n -> p k n", p=128), "wmisc", eng="pool")
            cload(wmisc[:, l, :, 16:32], win_d[l, :, C_ALR:C_ALR + 16].rearrange("(k p) n -> p k n", p=128), "wmisc", eng="pool")

        for o in cops:
            o.count = 16 * len(cops)
        for o in copsp:
            o.count = 16 * len(copsp)
        A("dve", lambda e: e.tensor_copy(out=identb[:], in_=identf[:]), reads=["identf"], writes=["identb"])
        A("dve", lambda e: e.tensor_copy(out=maskb[:], in_=tri[:]), reads=["tri"], writes=["maskb"])
        A("dve", lambda e: e.tensor_scalar(out=maskq[:], in0=tri[:], scalar1=0.25, scalar2=None, op0=ALU.mult), reads=["tri"], writes=["maskq"])
        A("dve", lambda e: e.tensor_scalar(out=tri16[:], in0=tri[:], scalar1=-1.0 / 16.0, scalar2=None, op0=ALU.mult), reads=["tri"], writes=["tri16"])
        A("pool", lambda e: e.memset(onesf[:], 1.0), writes=["onesf"])
        A("pool", lambda e: e.memset(mhalf[:], -0.5), writes=["mhalf"])
        for r in range(3):
            A("pool", lambda e, r=r: e.tensor_copy(out=selN[:, r, :], in_=bc(sel[:, r, 0:1], [128, N])), reads=["sel"], writes=["selN"])
        A("pool", lambda e: e.memset(hist[:], 0.0), writes=["hist"])
        A("pool", lambda e: e.memset(xbcT[:], 0.0), writes=[("xbcT", b) for b in range(16)])
        for l in range(2):
            A("pool", lambda e, l=l: e.memset(Sssd[l][:], 0.0), writes=[("Sssd", l)])
            A("pool", lambda e, l=l: e.memset(Sbf[l][:], 0.0), writes=[("Sbf", l)])
            A("pool", lambda e, l=l: e.memset(Sgla[l][:], 0.0), writes=[("Sgla", l, 0), ("Sgla", l, 1)])
            A("pool", lambda e, l=l: e.memset(Gbf[l][:], 0.0), writes=[("Gbf", l)])
        A("act", lambda e: e.activation(out=a_bc[:], in_=a_bc[:], func=AF.Exp), reads=["a_bc"], writes=["a_bc"])
        A("dve", lambda e: e.tensor_scalar(out=a_bc[:], in0=a_bc[:], scalar1=-1.0, scalar2=None, op0=ALU.mult), reads=["a_bc"], writes=["a_bc"])
        A("dve", lambda e: e.tensor_scalar(out=dskh_bc[:], in0=dskh_bc[:], scalar1=0.5, scalar2=None, op0=ALU.mult), reads=["dskh"], writes=["dskh"])
        A("dve", lambda e: e.tensor_scalar(out=gnwh[:], in0=gnwh[:], scalar1=0.5, scalar2=None, op0=ALU.mult), reads=["gnwh"], writes=["gnwh"])
        A("dve", lambda e: e.tensor_copy(out=gb8h[:], in_=gb8[:]), reads=["gb8"], writes=["gb8h"])
        A("dve", lambda e: e.tensor_copy(out=gb8hf[:], in_=gb8h[:]), reads=["gb8h"], writes=["gb8hf"])
        A("dve", lambda e: e.tensor_tensor(out=gb8l[:], in0=gb8[:], in1=gb8hf[:], op=ALU.subtract), reads=["gb8", "gb8hf"], writes=["gb8l"])
        csem2 = S.new_sem("csem2")
        cops2 = []
        for a8 in range(8):
            cops2.append(A("sp", lambda e, a8=a8: e.dma_start(out=rows[2:3, a8 * 128:(a8 + 1) * 128], in_=gb8h[a8:a8 + 1, :]), reads=["gb8h", "rows0"], writes=[S.uniq("rows")], dma_sem=csem2))
            cops2.append(A("sp", lambda e, a8=a8: e.dma_start(out=rows[2:3, 1024 + a8 * 128:1024 + (a8 + 1) * 128], in_=gb8l[a8:a8 + 1, :]), reads=["gb8l", "rows0"], writes=[S.uniq("rows")], dma_sem=csem2))
        for o in cops2:
            o.count = 16 * len(cops2)

        def wsrc(l, kind, idx):
            if kind == "in":
                src = win_d[l, :, idx:idx + 512]
            elif kind == "br":
                b, j = idx
                src = wbr_d[l, b, :, j * 512:(j + 1) * 512]
            else:
                src = wout_d[l, :, idx * 512:(idx + 1) * 512]
            return src.rearrange("(k p) n -> p k n", p=128)

        def layer_groups(l):
            g = []
            for i in range(4):
                g.append((l, "in", C_XBC + 512 * i))
            for i in range(2):
                g.append((l, "in", C_Z + 512 * i))
            g.append((l, "in", C_Q))
            g.append((l, "in", C_K))
            for i in range(2):
                g.append((l, "in", C_V + 512 * i))
            for i in range(2):
                g.append((l, "in", C_G + 512 * i))
            for b in range(2):
                for j in range(2):
                    g.append((l, "in", C_M + b * 1024 + j * 512))
                    g.append((l, "br", (b, j)))
            for j in range(2):
                g.append((l, "out", j))
            return g

        wlist = []
        for ti in range(ntiles):
            for l in range(NL):
                wlist += layer_groups(l)
        wstate = {"issued": 0, "used": 0}
        PF = 2

        GP = len(layer_groups(0)) * NL
        wscr = nc.dram_tensor("wscr", [GP, 128, 8 * 512], BF16, kind="Internal").ap()
        wsemh = [S.new_sem("wsemh%d" % i) for i in range(NWB)]
        wsemst = [S.new_sem("wsemst%d" % i) for i in range(NWB)]

        def w_issue_upto(n):
            while wstate["issued"] < min(n, len(wlist)):
                i = wstate["issued"]
                l, kind, idx = wlist[i]
                slot = i % NWB
                if i < GP:
                    src = wsrc(l, kind, idx)
                    A("pool", lambda e, slot=slot, src=src: e.dma_start(out=wbufs[slot][:], in_=src),
                      writes=[("wb", slot)], dma_sem=wsems[slot])
                    if ntiles > 1:
                        A("sp", lambda e, slot=slot, i=i: e.dma_start(out=wscr[i], in_=wbufs[slot][:].rearrange("p k n -> p (k n)")),
                          reads=[("wb", slot)], writes=[("wscr", i)], dma_sem=wsemst[slot])
                else:
                    g = i % GP
                    A("sp", lambda e, slot=slot, g=g: e.dma_start(out=wbufs[slot][:].rearrange("p k n -> p (k n)"), in_=wscr[g]),
                      reads=[("wscr", g)], writes=[("wb", slot)], dma_sem=wsemh[slot])
                wstate["issued"] += 1

        def next_w(expect, pf=PF):
            i = wstate["used"]
            assert wlist[i] == expect, (wlist[i], expect)
            w_issue_upto(i + 1 + pf)
            wstate["used"] += 1
            return wbufs[i % NWB], ("wb", i % NWB)

        def bc(ap, shape):
            return ap.to_broadcast(list(shape))

        HALF_LN = math.log(0.5)
        QS_LN = math.log(128.0 ** -0.5)

        def silu2_evac(bank_ap, out_ap, key, wkey, ncols):
            th, tkey = TH.get()
            A("act", lambda e: e.activation(out=th[:, 0:ncols], in_=bank_ap, func=AF.Tanh, scale=0.5), reads=[key], writes=[tkey])
            A("dve", lambda e: e.scalar_tensor_tensor(out=out_ap, in0=th[:, 0:ncols], scalar=1.0, in1=bank_ap, op0=ALU.add, op1=ALU.mult),
              reads=[tkey, key], writes=[wkey])

        uT_all = [("uT", c) for c in range(NCH)]
        out_keys = []
        osems = [S.new_sem("osem%d" % i) for i in range(NCH)]
        xsems = [S.new_sem("xsem%d" % i) for i in range(NCH)]

        def tile_layer(ti, l):
            first = ti == 0
            last_layer = l == NL - 1
            MARK("A %d %d" % (ti, l))
            for c in range(NCH):
                ub, ukey = UB.get()
                A("act", lambda e, c=c, ub=ub: e.activation(out=ub[:], in_=h[:, c, :], func=AF.Square, accum_out=ssq[:, c:c + 1]),
                  reads=[("h", c)], writes=[("ssq", c), ukey])
                A("dve", lambda e, c=c: e.tensor_scalar(out=rs[:, c:c + 1], in0=ssq[:, c:c + 1], scalar1=1.0 / D, scalar2=EPS, op0=ALU.mult, op1=ALU.add),
                  reads=[("ssq", c)], writes=[("rs", c)])
                A("pool", lambda e, c=c: e.tensor_tensor(out=rs[:, c:c + 1], in0=rs[:, c:c + 1], in1=mhalf[:], op=ALU.pow),
                  reads=[("rs", c), "mhalf"], writes=[("rs", c)])
                if OPT["ub_eng"] == "act":
                    A("act", lambda e, c=c, ub=ub: e.activation(out=ub[:], in_=h[:, c, :], func=AF.Copy, scale=rs[:, c:c + 1]),
                      reads=[("h", c), ("rs", c)], writes=[ukey])
                else:
                    A("dve", lambda e, c=c, ub=ub: e.tensor_scalar(out=ub[:], in0=h[:, c, :], scalar1=rs[:, c:c + 1], scalar2=None, op0=ALU.mult),
                      reads=[("h", c), ("rs", c)], writes=[ukey])
                bank, key = PT.get()
                for k in range(8):
                    A("pe", lambda e, k=k, ub=ub, bank=bank: e.transpose(out=bank[:, k * 128:(k + 1) * 128], in_=ub[:, k * 128:(k + 1) * 128], identity=identb[:]),
                      reads=[ukey, "identb"], writes=[key])
                A("dve", lambda e, c=c, bank=bank: e.tensor_tensor(out=uT[:, :, c * 128:(c + 1) * 128], in0=bank[:, :].rearrange("p (k t) -> p k t", k=8),
                                                                    in1=bc(normw[:, l * 8:(l + 1) * 8].unsqueeze(2), [128, 8, 128]), op=ALU.mult),
                  reads=[key, "normw"], writes=[("uT", c)])
            MARK("B")
            for c in range(NCH):
                bank, key = PB.get()
                for k in range(8):
                    A("pe", lambda e, k=k, c=c, bank=bank: e.matmul(bank[:, 0:16], lhsT=uT[:, k, c * 128:(c + 1) * 128], rhs=wmisc[:, l, k, 0:16], start=(k == 0), stop=(k == 7)),
                      reads=[("uT", c), "wmisc"], writes=[key])
                A("dve", lambda e, c=c, bank=bank: e.tensor_tensor(out=dtr[:, c, :], in0=bank[:, 0:16], in1=dtb_bc[:, l, :], op=ALU.add),
                  reads=[key, "dtb"], writes=[("dtr", c)])
                A("act", lambda e, c=c: e.activation(out=dte_x[:, c, :], in_=dtr[:, c, :], func=AF.Exp), reads=[("dtr", c)], writes=[("dte_x", c)])
            for c in range(NCH):
                A("act", lambda e, c=c: e.activation(out=dt_t[:, c, :], in_=dte_x[:, c, :], func=AF.Ln, bias=1.0), reads=[("dte_x", c)], writes=[("dt", c)])
            for c in range(NCH):
                if first and c == 0:
                    A("dve", lambda e, c=c: e.tensor_scalar(out=dt_t[:, c, :], in0=dt_t[:, c, :], scalar1=mask0[:, 0:1], scalar2=None, op0=ALU.mult),
                      reads=[("dt", c), "mask0"], writes=[("dt", c)])
                A("dve", lambda e, c=c: e.tensor_tensor(out=da_t[:, c, :], in0=dt_t[:, c, :], in1=a_bc[:, l, :], op=ALU.mult),
                  reads=[("dt", c), "a_bc"], writes=[("da", c)])
            def stage_B2(c):
                bank2, key2 = PB.get()
                A("pe", lambda e, c=c, bank2=bank2: e.matmul(bank2[:, 0:16], lhsT=tri[:], rhs=da_t[:, c, :], start=True, stop=True), reads=[("da", c), "tri"], writes=[key2])
                A("pe", lambda e, c=c, bank2=bank2: e.matmul(bank2[:, 16:32], lhsT=onesf[:], rhs=da_t[:, c, :], start=True, stop=True), reads=[("da", c), "onesf"], writes=[key2])
                A("pe", lambda e, c=c, bank2=bank2: e.matmul(bank2[:, 32:48], lhsT=strict[:], rhs=da_t[:, c, :], start=True, stop=True), reads=[("da", c), "strict"], writes=[key2])
                A("act", lambda e, c=c, bank2=bank2: e.activation(out=ecumh[:, c, :], in_=bank2[:, 0:16], func=AF.Exp, bias=HALF_LN), reads=[key2], writes=[("ecumh", c)])
                A("act", lambda e, c=c, bank2=bank2: e.activation(out=cdb[:, c, :], in_=bank2[:, 16:32], func=AF.Exp), reads=[key2], writes=[("cdb", c)])
                A("act", lambda e, c=c, bank2=bank2: e.activation(out=dte[:, c, :], in_=bank2[:, 32:48], func=AF.Exp), reads=[key2], writes=[("dte", c)])
                A("dve", lambda e, c=c: e.scalar_tensor_tensor(out=wq[:, c, :], in0=dt_t[:, c, :], scalar=0.25, in1=dte[:, c, :], op0=ALU.mult, op1=ALU.mult),
                  reads=[("dt", c), ("dte", c)], writes=[("wq", c)])
                A("dve", lambda e, c=c: e.tensor_scalar(out=dth[:, c, :], in0=dt_t[:, c, :], scalar1=0.5, scalar2=None, op0=ALU.mult),
                  reads=[("dt", c)], writes=[("dth", c)])
            bank, key = PB.get()
            for k in range(8):
                A("pe", lambda e, k=k, bank=bank: e.matmul(bank[0:16, 0:N], lhsT=wmisc[:, l, k, 16:32], rhs=uT[:, k, :], start=(k == 0), stop=(k == 7)),
                  reads=uT_all + ["wmisc"], writes=[key])
            A("act", lambda e, bank=bank: e.activation(out=alrT[0:16, :], in_=bank[0:16, 0:N], func=AF.Copy), reads=[key], writes=["alrT"])
            MARK("C")
            for gi in range(4):
                wb, wkey = next_w((l, "in", C_XBC + 512 * gi))
                for j in range(4):
                    blk = gi * 4 + j
                    bank, key = PB.get()
                    for k in range(8):
                        A("pe", lambda e, k=k, j=j, wb=wb, bank=bank: e.matmul(bank[:, 0:N], lhsT=wb[:, k, j * 128:(j + 1) * 128], rhs=uT[:, k, :], start=(k == 0), stop=(k == 7)),
                          reads=uT_all + [wkey], writes=[key])
                    A("act", lambda e, blk=blk, bank=bank: e.activation(out=xbcT[:, blk, 4:4 + N], in_=bank[:, 0:N], func=AF.Copy), reads=[key], writes=[("xbcT", blk)])
            for c in range(NCH):
                stage_B2(c)
            allx = [("xbcT", b) for b in range(16)]
            A("pool", lambda e: e.tensor_copy(out=xbcT[:, :, 1:4], in_=hist[:, l, :, 1:4]), reads=[("hist", l)], writes=allx)
            A("pool", lambda e: e.tensor_copy(out=hist[:, l, :, 1:4], in_=xbcT[:, :, N + 1:N + 4]), reads=allx, writes=[("hist", l)])
            MARK("E")
            dgof = {}

            def gen_diag(blk):
                dg, dkey = DG.get()
                for tap in range(4):
                    col = (l * 16 + blk) * 4 + tap
                    A(OPT["diag_eng"], lambda e, dg=dg, tap=tap, col=col: e.tensor_scalar(out=dg[:, tap, :], in0=identb[:], scalar1=cw[:, col:col + 1], scalar2=0.0, op0=ALU.mult, op1=ALU.add),
                      reads=["identb", "cw"], writes=[dkey])
                dgof[blk] = (dg, dkey)

            def conv_tok(c, blks, out_ap, wkey):
                bank, key = PB.get()
                for j, blk in enumerate(blks):
                    dg, dkey = dgof[blk]
                    o = bank[:, j * 128:(j + 1) * 128]
                    A("pe", lambda e, o=o, blk=blk: e.matmul(o, lhsT=sel[:, l, :], rhs=rows[:, blk * 128:(blk + 1) * 128], start=True, stop=False),
                      reads=["sel", "rows"], writes=[key])
                    for tap in range(4):
                        A("pe", lambda e, o=o, blk=blk, tap=tap, dg=dg, c=c: e.matmul(o, lhsT=xbcT[:, blk, c * 128 + 1 + tap:c * 128 + 1 + tap + 128], rhs=dg[:, tap, :], start=False, stop=(tap == 3)),
                          reads=[("xbcT", blk), dkey], writes=[key])
                silu2_evac(bank[:, 0:128 * len(blks)], out_ap, key, wkey, 128 * len(blks))

            for half in range(2):
                blks = [half * 4 + j for j in range(4)]
                for b in blks:
                    gen_diag(b)
                for c in range(NCH):
                    conv_tok(c, blks, x2[:, c, half * 512:(half + 1) * 512], ("x2", c))
            for blk in range(8, 16):
                if blk % 4 == 0:
                    for b4 in range(blk, blk + 4):
                        gen_diag(b4)
                    if blk == 8:
                        for c in range(NCH):
                            conv_tok(c, [8, 9, 10, 11], B2tok[:, c, :], ("B2tok", c))
                dg, dkey = dgof[blk]
                bank, key = PB.get()
                A("pe", lambda e, bank=bank, blk=blk: e.matmul(bank[:, 0:N], lhsT=rows[:, blk * 128:(blk + 1) * 128], rhs=selN[:, l, :], start=True, stop=False),
                  reads=["selN", "rows"], writes=[key])
                for tap in range(4):
                    A("pe", lambda e, bank=bank, blk=blk, tap=tap, dg=dg: e.matmul(bank[:, 0:N], lhsT=dg[:, tap, :], rhs=xbcT[:, blk, 1 + tap:1 + tap + N], start=False, stop=(tap == 3)),
                      reads=[("xbcT", blk), dkey], writes=[key])
                silu2_evac(bank[:, 0:N], BCT[:, blk - 8, :], key, ("BCT", blk - 8), N)
            MARK("P2")
            sF = []
            REC[0] = sF
            PB.ring = PBf
            for gi in range(2):
                wb, wkey = next_w((l, "in", C_Z + 512 * gi))
                for c in range(NCH):
                    bank, key = PB.get()
                    for k in range(8):
                        A("pe", lambda e, k=k, c=c, wb=wb, bank=bank: e.matmul(bank[:, :], lhsT=uT[:, k, c * 128:(c + 1) * 128], rhs=wb[:, k, :], start=(k == 0), stop=(k == 7)),
                          reads=[("uT", c), wkey], writes=[key])
                    silu2_evac(bank[:, :], zs[:, c, gi * 512:(gi + 1) * 512], key, ("zs", c), 512)
                    if gi == 1:
                        GIVE(("zs", c))
                    STEP()
            for qk in range(2):
                wb, wkey = next_w((l, "in", C_Q if qk == 0 else C_K))
                for j in range(4):
                    bank, key = PB.get()
                    for k in range(8):
                        A("pe", lambda e, k=k, j=j, wb=wb, bank=bank: e.matmul(bank[:, 0:N], lhsT=wb[:, k, j * 128:(j + 1) * 128], rhs=uT[:, k, :], start=(k == 0), stop=(k == 7)),
                          reads=uT_all + [wkey], writes=[key])
                    A("act", lambda e, bank=bank, qk=qk, j=j: e.activation(out=qkT[:, qk * 4 + j, :], in_=bank[:, 0:N], func=AF.Copy), reads=[key], writes=[("qkT", qk * 4 + j)])
                    if j == 3:
                        GIVE("q" if qk == 0 else "k")
                    STEP()
            for gi in range(2):
                wb, wkey = next_w((l, "in", C_V + 512 * gi))
                for c in range(NCH):
                    bank, key = PB.get()
                    for k in range(8):
                        A("pe", lambda e, k=k, c=c, wb=wb, bank=bank: e.matmul(bank[:, :], lhsT=uT[:, k, c * 128:(c + 1) * 128], rhs=wb[:, k, :], start=(k == 0), stop=(k == 7)),
                          reads=[("uT", c), wkey], writes=[key])
                    A("act", lambda e, bank=bank, c=c, gi=gi: e.activation(out=vtok[:, c, gi * 512:(gi + 1) * 512], in_=bank[:, :], func=AF.Copy), reads=[key], writes=[("vtok", c)])
                    if gi == 1:
                        GIVE(("v", c))
                    STEP()
            for gi in range(2):
                wb, wkey = next_w((l, "in", C_G + 512 * gi))
                for j in range(4):
                    blk = gi * 4 + j
                    bank, key = PB.get()
                    for k in range(8):
                        A("pe", lambda e, k=k, j=j, wb=wb, bank=bank: e.matmul(bank[:, 0:N], lhsT=wb[:, k, j * 128:(j + 1) * 128], rhs=uT[:, k, :], start=(k == 0), stop=(k == 7)),
                          reads=uT_all + [wkey], writes=[key])
                    gw, gkey = GW.get()
                    th, tkey = TH.get()
                    A("act", lambda e, bank=bank, gw=gw, blk=blk: e.activation(out=gw[:, 0:N], in_=bank[:, 0:N], func=AF.Copy, scale=gnwh[:, l * 2 + (blk % 2):l * 2 + (blk % 2) + 1]),
                      reads=[key, "gnwh"], writes=[gkey])
                    A("act", lambda e, bank=bank, th=th: e.activation(out=th[:, 0:N], in_=bank[:, 0:N], func=AF.Tanh, scale=0.5), reads=[key], writes=[tkey])
                    A("dve", lambda e, gw=gw, th=th, blk=blk: e.scalar_tensor_tensor(out=sgT[:, blk, :], in0=th[:, 0:N], scalar=1.0, in1=gw[:, 0:N], op0=ALU.add, op1=ALU.mult),
                      reads=[gkey, tkey], writes=[("sgT", blk)])
                    if blk == 7:
                        GIVE("g")
                        GIVE("Fdone")
                    STEP()
            sH = []
            REC[0] = sH
            PB.ring = PBh
            for c in range(NCH):
                cs = slice(c * 128, (c + 1) * 128)
                for hv in range(2):
                    for hh in range(8):
                        hd = hv * 8 + hh
                        A("dve", lambda e, hh=hh, hd=hd, c=c: e.tensor_scalar(out=At[:, hh, :], in0=strict[:], scalar1=da_t[:, c, hd:hd + 1], scalar2=None, op0=ALU.mult),
                          reads=["strict", ("da", c)], writes=[("At", hh)])
                    for qd in range(2):
                        bank, key = PB.get()
                        for hq in range(4):
                            hh = qd * 4 + hq
                            A("pe", lambda e, bank=bank, hq=hq, hh=hh: e.matmul(bank[:, hq * 128:(hq + 1) * 128], lhsT=At[:, hh, :], rhs=tri[:], start=True, stop=True),
                              reads=[("At", hh), "tri"], writes=[key])
                        h0 = hv * 8 + qd * 4
                        A("act", lambda e, bank=bank, h0=h0: e.activation(out=Lm[:, h0:h0 + 4, :], in_=bank[:, :].rearrange("p (a t) -> p a t", a=4), func=AF.Exp),
                          reads=[key], writes=[("Lm", h0 // 4)])
                        STEP()
                bank, key = PB.get()
                for g in range(4):
                    A("pe", lambda e, bank=bank, g=g, cs=cs: e.matmul(bank[:, g * 128:(g + 1) * 128], lhsT=BCT[:, g, cs], rhs=BCT[:, 4 + g, cs], start=True, stop=True),
                      reads=[("BCT", g), ("BCT", 4 + g)], writes=[key])
                A("dve", lambda e, bank=bank: e.tensor_tensor(out=scm[:], in0=bank[:, :].rearrange("p (g t) -> p g t", g=4), in1=bc(maskq[:, :].unsqueeze(1), [128, 4, 128]), op=ALU.mult),
                  reads=[key, "maskq"], writes=["scm"])
                A("dve", lambda e: e.tensor_tensor(out=MT[:].rearrange("p (g a) t -> p g a t", g=4), in0=Lm[:].rearrange("p (g a) t -> p g a t", g=4),
                                                   in1=bc(scm[:].unsqueeze(2), [128, 4, 4, 128]), op=ALU.mult),
                  reads=["scm"] + [("Lm", i) for i in range(4)], writes=["MT"])
                STEP()
                x3 = x2[:, c, :].rearrange("p (a d) -> p a d", a=16)
                A("pool", lambda e, x3=x3, c=c: e.tensor_tensor(out=xdt[:].rearrange("p (a d) -> p a d", a=16), in0=x3, in1=bc(dth[:, c, :].unsqueeze(2), [128, 16, 64]), op=ALU.mult),
                  reads=[("x2", c), ("dth", c)], writes=["xdt"])
                A("pool", lambda e, x3=x3, c=c: e.tensor_tensor(out=xw[:].rearrange("p (a d) -> p a d", a=16), in0=x3, in1=bc(wq[:, c, :].unsqueeze(2), [128, 16, 64]), op=ALU.mult),
                  reads=[("x2", c), ("wq", c)], writes=["xw"])
                A("pool", lambda e, x3=x3: e.tensor_tensor(out=xd[:].rearrange("p (a d) -> p a d", a=16), in0=x3, in1=bc(dskh_bc[:, l, :].unsqueeze(2), [128, 16, 64]), op=ALU.mult),
                  reads=[("x2", c), "dskh"], writes=["xd"])
                STEP()
                ydk = []
                for hb in range(2):
                    bank, key = PB.get()
                    ydk.append((bank, key))
                    A("pe", lambda e, bank=bank, hb=hb: e.matmul(bank[:, :], lhsT=identb[:], rhs=xd[:, hb * 512:(hb + 1) * 512], start=True, stop=False),
                      reads=["identb", "xd"], writes=[key])
                    for h8 in range(8):
                        hd = hb * 8 + h8
                        A("pe", lambda e, bank=bank, h8=h8, hd=hd: e.matmul(bank[:, h8 * 64:(h8 + 1) * 64], lhsT=MT[:, hd, :], rhs=xdt[:, hd * 64:(hd + 1) * 64], start=False, stop=(h8 == 7)),
                          reads=["MT", "xdt"], writes=[key])
                    STEP()
                for hb in range(2):
                    bank, key = PB.get()
                    for gg in range(2):
                        g = hb * 2 + gg
                        A("pe", lambda e, bank=bank, gg=gg, g=g, cs=cs: e.matmul(bank[:, gg * 256:(gg + 1) * 256], lhsT=BCT[:, 4 + g, cs], rhs=Sbf[l][:, g * 256:(g + 1) * 256], start=True, stop=True),
                          reads=[("BCT", 4 + g), ("Sbf", l)], writes=[key])
                    hs = slice(hb * 512, (hb + 1) * 512)
                    A("dve", lambda e, bank=bank, hb=hb, hs=hs, c=c: e.tensor_tensor(out=t1[:, hs].rearrange("p (a d) -> p a d", a=8), in0=bank[:, :].rearrange("p (a d) -> p a d", a=8),
                                                                                  in1=bc(ecumh[:, c, hb * 8:(hb + 1) * 8].unsqueeze(2), [128, 8, 64]), op=ALU.mult),
                      reads=[key, ("ecumh", c)], writes=[("t1", hb)])
                    ybank, ykey = ydk[hb]
                    A("dve", lambda e, ybank=ybank, hs=hs: e.tensor_tensor(out=t1[:, hs], in0=t1[:, hs], in1=ybank[:, :], op=ALU.add),
                      reads=[ykey, ("t1", hb)], writes=[("t1", hb)])
                    NEED(("zs", c))
                    A(OPT["zs_eng"], lambda e, hs=hs, c=c: e.tensor_tensor(out=t1[:, hs], in0=t1[:, hs], in1=zs[:, c, hs], op=ALU.mult),
                      reads=[("t1", hb), ("zs", c)], writes=[("t1", hb)])
                    STEP()
                for g in range(4):
                    A("act", lambda e, g=g: e.activation(out=yn[:, g * 256:(g + 1) * 256], in_=t1[:, g * 256:(g + 1) * 256], func=AF.Square, accum_out=sq4[:, g:g + 1]),
                      reads=[("t1", g // 2)], writes=[("sq4", g), ("yn", g // 2)])
                sq4k = [("sq4", g) for g in range(4)]
                A("dve", lambda e: e.tensor_scalar(out=r4[:], in0=sq4[:], scalar1=1.0 / 256, scalar2=4 * EPS, op0=ALU.mult, op1=ALU.add), reads=sq4k, writes=["r4"])
                A("pool", lambda e: e.tensor_tensor(out=r4[:], in0=r4[:], in1=bc(mhalf[:, 0:1], [128, 4]), op=ALU.pow), reads=["r4", "mhalf"], writes=["r4"])
                for g in range(4):
                    gs = slice(g * 256, (g + 1) * 256)
                    if g % 2 == 0:
                        A("dve", lambda e, g=g, gs=gs: e.tensor_scalar(out=yn[:, gs], in0=t1[:, gs], scalar1=r4[:, g:g + 1], scalar2=None, op0=ALU.mult),
                          reads=["r4", ("t1", g // 2)], writes=[("yn", g // 2)])
                    else:
                        A("act", lambda e, g=g, gs=gs: e.activation(out=yn[:, gs], in_=t1[:, gs], func=AF.Copy, scale=r4[:, g:g + 1]),
                          reads=["r4", ("t1", g // 2)], writes=[("yn", g // 2)])
                STEP()
                bank, key = PT.get()
                for k in range(8):
                    A("pe", lambda e, k=k, bank=bank: e.transpose(out=bank[:, k * 128:(k + 1) * 128], in_=yn[:, k * 128:(k + 1) * 128], identity=identb[:]),
                      reads=[("yn", 0), ("yn", 1), "identb"], writes=[key])
                A("dve", lambda e, bank=bank, c=c: e.tensor_tensor(out=xbcT[:, 0:8, 4 + c * 128:4 + (c + 1) * 128], in0=bank[:, :].rearrange("p (k t) -> p k t", k=8),
                                                                    in1=bc(ssdnw[:, l * 8:(l + 1) * 8].unsqueeze(2), [128, 8, 128]), op=ALU.mult),
                  reads=[key, "ssdnw"], writes=[("xbcT", b) for b in range(8)])
                if c == NCH - 1:
                    GIVE("ysT")
                STEP()
                A("pool", lambda e, c=c: e.tensor_tensor(out=stmp[:].rearrange("p (a d) -> p a d", a=16), in0=Sssd[l][:].rearrange("p (a d) -> p a d", a=16),
                                                         in1=bc(cdb[:, c, :].unsqueeze(2), [128, 16, 64]), op=ALU.mult),
                  reads=[("Sssd", l), ("cdb", c)], writes=["stmp", "stmp2"])
                for hb in range(2):
                    bank, key = PB.get()
                    for gg in range(2):
                        g = hb * 2 + gg
                        A("pe", lambda e, bank=bank, gg=gg, g=g, c=c: e.matmul(bank[:, gg * 256:(gg + 1) * 256], lhsT=B2tok[:, c, g * 128:(g + 1) * 128], rhs=xw[:, g * 256:(g + 1) * 256], start=True, stop=True),
                          reads=[("B2tok", c), "xw"], writes=[key])
                    hs = slice(hb * 512, (hb + 1) * 512)
                    A("dve", lambda e, bank=bank, hs=hs: e.tensor_tensor(out=Sssd[l][:, hs], in0=stmp[:, hs], in1=bank[:, :], op=ALU.add),
                      reads=[key, "stmp" if hb == 0 else "stmp2"], writes=[("Sssd", l)])
                A("act", lambda e: e.activation(out=Sbf[l][:], in_=Sssd[l][:], func=AF.Copy), reads=[("Sssd", l)], writes=[("Sbf", l)])
                STEP()

            REC[0] = None
            PB.ring = PBfull
            given = set()
            merge([sH, sF], given)
            MARK("P3")

            def merge_branch(b):
                for j in range(2):
                    wg, wgkey = next_w((l, "in", C_M + b * 1024 + j * 512))
                    for jj in range(4):
                        bank, key = PB.get()
                        for k in range(8):
                            A("pe", lambda e, k=k, jj=jj, wg=wg, bank=bank: e.matmul(bank[:, 0:N], lhsT=wg[:, k, jj * 128:(jj + 1) * 128], rhs=uT[:, k, :], start=(k == 0), stop=(k == 7)),
                              reads=uT_all + [wgkey], writes=[key])
                        A("act", lambda e, bank=bank, jj=jj: e.activation(out=gth[b][:, jj, :], in_=bank[:, 0:N], func=AF.Tanh, scale=0.5), reads=[key], writes=[("gth", jj)])
                        STEP()
                    wbr, wbkey = next_w((l, "br", (b, j)))
                    if b == 0:
                        NEED("ysT")
                    for jj in range(4):
                        bank, key = PB.get()
                        for k in range(8):
                            A("pe", lambda e, k=k, jj=jj, wbr=wbr, bank=bank: e.matmul(bank[:, 0:N], lhsT=wbr[:, k, jj * 128:(jj + 1) * 128], rhs=xbcT[:, b * 8 + k, 4:4 + N], start=(k == 0), stop=(k == 7)),
                              reads=[("xbcT", b * 8 + k) for k in range(8)] + [wbkey], writes=[key])
                        if b == 0:
                            A("dve", lambda e, bank=bank, jj=jj, j=j: e.scalar_tensor_tensor(out=t0m[:, j * 4 + jj, :], in0=gth[0][:, jj, :], scalar=1.0, in1=bank[:, 0:N], op0=ALU.add, op1=ALU.mult),
                              reads=[key, ("gth", jj)], writes=[("t0m", j * 4 + jj)])
                        else:
                            t1m, t1key = T1M.get()
                            A("dve", lambda e, bank=bank, jj=jj, t1m=t1m: e.scalar_tensor_tensor(out=t1m[:, :], in0=gth[1][:, jj, :], scalar=1.0, in1=bank[:, 0:N], op0=ALU.add, op1=ALU.mult),
                              reads=[key, ("gth", jj)], writes=[t1key])
                            A("pool", lambda e, jj=jj, j=j, t1m=t1m: e.tensor_tensor(out=qkT[:, j * 4 + jj, :], in0=t0m[:, j * 4 + jj, :], in1=t1m[:, :], op=ALU.add),
                              reads=[t1key, ("t0m", j * 4 + jj)], writes=[("qkT", j * 4 + jj)])
                        STEP()

            sJ = []
            REC[0] = sJ
            PB.ring = PBf
            NEED("Fdone")
            merge_branch(0)
            sI = []
            REC[0] = sI
            PB.ring = PBi
            for c in range(NCH):
                cs = slice(c * 128, (c + 1) * 128)
                bank, key = PB.get()
                A("pe", lambda e, bank=bank, cs=cs: e.matmul(bank[:, :], lhsT=alrT[:, cs], rhs=w2b[:, l * 512:(l + 1) * 512], start=True, stop=False),
                  reads=["alrT", "w2b"], writes=[key])
                A("pe", lambda e, bank=bank: e.matmul(bank[:, :], lhsT=sel[:, 2, :], rhs=rows[:, l * 512:(l + 1) * 512], start=False, stop=False),
                  reads=["sel", "rows"], writes=[key])
                A("pe", lambda e, bank=bank: e.matmul(bank[:, :], lhsT=sel[:, 2, :], rhs=rows[:, 1024 + l * 512:1024 + (l + 1) * 512], start=False, stop=True),
                  reads=["sel", "rows"], writes=[key])
                A("act", lambda e, bank=bank: e.activation(out=e1[:], in_=bank[:, :], func=AF.Exp, scale=-1.0), reads=[key], writes=["e1"])
                A("act", lambda e: e.activation(out=e1[:], in_=e1[:], func=AF.Ln, bias=1.0), reads=["e1"], writes=["e1"])
                STEP()
                bankg, keyg = PB.get()
                for j in range(4):
                    A("pe", lambda e, bankg=bankg, j=j: e.matmul(bankg[:, j * 128:(j + 1) * 128], lhsT=e1[:, j * 128:(j + 1) * 128], rhs=tri16[:], start=True, stop=True),
                      reads=["e1", "tri16"], writes=[keyg])
                A("act", lambda e, bankg=bankg: e.activation(out=eg[:], in_=bankg[:, :], func=AF.Exp, bias=QS_LN), reads=[keyg], writes=["eg"])
                A("act", lambda e, bankg=bankg: e.activation(out=emg[:], in_=bankg[:, :], func=AF.Exp, scale=-1.0), reads=[keyg], writes=["emg"])
                A("act", lambda e, bankg=bankg: e.activation(out=egl[:, :], in_=bankg[:, :].rearrange("p (j t) -> p j t", j=4)[:, :, 127], func=AF.Exp), reads=[keyg], writes=["egl"])
                NEED("q")
                NEED("k")
                A("dve", lambda e, cs=cs: e.tensor_tensor(out=qtl[:], in0=qkT[:, 0:4, cs], in1=eg[:].rearrange("p (j t) -> p j t", j=4), op=ALU.mult),
                  reads=["eg"] + [("qkT", j) for j in range(4)], writes=["qtl"])
                A("dve", lambda e, cs=cs: e.tensor_tensor(out=ktl[:], in0=qkT[:, 4:8, cs], in1=emg[:].rearrange("p (j t) -> p j t", j=4), op=ALU.mult),
                  reads=["emg"] + [("qkT", 4 + j) for j in range(4)], writes=["ktl"])
                STEP()
                bank, key = PB.get()
                for hd in range(4):
                    A("pe", lambda e, bank=bank, hd=hd: e.matmul(bank[:, hd * 128:(hd + 1) * 128], lhsT=ktl[:, hd, :], rhs=qtl[:, hd, :], start=True, stop=True),
                      reads=["ktl", "qtl"], writes=[key])
                A("dve", lambda e, bank=bank: e.tensor_tensor(out=attm[:], in0=bank[:, :].rearrange("p (g t) -> p g t", g=4), in1=bc(maskb[:, :].unsqueeze(1), [128, 4, 128]), op=ALU.mult),
                  reads=[key, "maskb"], writes=["attm"])
                STEP()
                bankt, keyt = PT.get()
                for hd in range(4):
                    A("pe", lambda e, bankt=bankt, hd=hd: e.transpose(out=bankt[:, hd * 128:(hd + 1) * 128], in_=ktl[:, hd, :], identity=identb[:]),
                      reads=["ktl", "identb"], writes=[keyt])
                A("act", lambda e, bankt=bankt: e.activation(out=ktok[:], in_=bankt[:, 0:512], func=AF.Copy), reads=[keyt], writes=["ktok"])
                STEP()
                obk = []
                NEED(("v", c))
                for hb in range(2):
                    bank, key = PB.get()
                    obk.append((bank, key))
                    for hh in range(2):
                        hd = hb * 2 + hh
                        o = bank[:, hh * 256:(hh + 1) * 256]
                        A("pe", lambda e, o=o, hd=hd, c=c: e.matmul(o, lhsT=attm[:, hd, :], rhs=vtok[:, c, hd * 256:(hd + 1) * 256], start=True, stop=False),
                          reads=["attm", ("vtok", c)], writes=[key])
                        A("pe", lambda e, o=o, hd=hd: e.matmul(o, lhsT=qtl[:, hd, :], rhs=Gbf[l][:, hd * 256:(hd + 1) * 256], start=False, stop=True),
                          reads=["qtl", ("Gbf", l)], writes=[key])
                        A("act", lambda e, o=o, hd=hd: e.activation(out=on[:, hd * 256:(hd + 1) * 256], in_=o, func=AF.Square, accum_out=sq4g[:, hd:hd + 1]), reads=[key], writes=[("sq4g", hd), ("on", hd // 2)])
                    STEP()
                A("dve", lambda e: e.tensor_scalar(out=r4g[:], in0=sq4g[:], scalar1=1.0 / 256, scalar2=EPS, op0=ALU.mult, op1=ALU.add), reads=[("sq4g", i) for i in range(4)], writes=["r4g"])
                A("pool", lambda e: e.tensor_tensor(out=r4g[:], in0=r4g[:], in1=bc(mhalf[:, 0:1], [128, 4]), op=ALU.pow), reads=["r4g", "mhalf"], writes=["r4g"])
                for hb in range(2):
                    bank, key = obk[hb]
                    A("dve", lambda e, bank=bank, hb=hb: e.tensor_tensor(out=on[:, hb * 512:(hb + 1) * 512].rearrange("p (a d) -> p a d", a=2), in0=bank[:, :].rearrange("p (a d) -> p a d", a=2),
                                                                          in1=bc(r4g[:, hb * 2:(hb + 1) * 2].unsqueeze(2), [128, 2, 256]), op=ALU.mult),
                      reads=[key, "r4g"], writes=[("on", hb)])
                STEP()
                NEED("g")
                bank, key = PT.get()
                for k in range(8):
                    A("pe", lambda e, k=k, bank=bank: e.transpose(out=bank[:, k * 128:(k + 1) * 128], in_=on[:, k * 128:(k + 1) * 128], identity=identb[:]),
                      reads=[("on", 0), ("on", 1), "identb"], writes=[key])
                A("dve", lambda e, bank=bank, cs=cs, c=c: e.tensor_tensor(out=xbcT[:, 8:16, 4 + c * 128:4 + (c + 1) * 128], in0=bank[:, :].rearrange("p (k t) -> p k t", k=8),
                                                                         in1=sgT[:, :, cs], op=ALU.mult),
                  reads=[key] + [("sgT", b) for b in range(8)], writes=[("xbcT", 8 + b) for b in range(8)])
                STEP()
                for hb in range(2):
                    bank, key = PB.get()
                    for hh in range(2):
                        hd = hb * 2 + hh
                        A("pe", lambda e, bank=bank, hh=hh, hd=hd, c=c: e.matmul(bank[:, hh * 256:(hh + 1) * 256], lhsT=ktok[:, hd * 128:(hd + 1) * 128], rhs=vtok[:, c, hd * 256:(hd + 1) * 256], start=True, stop=True),
                          reads=["ktok", ("vtok", c)], writes=[key])
                    hs = slice(hb * 512, (hb + 1) * 512)
                    A("dve", lambda e, bank=bank, hs=hs: e.tensor_tensor(out=Sgla[l][:, hs], in0=Sgla[l][:, hs], in1=bank[:, :], op=ALU.add),
                      reads=[key, ("Sgla", l, hb)], writes=[("Sgla", l, hb)])
                for hd in range(4):
                    A("act", lambda e, hd=hd: e.activation(out=Gbf[l][:, hd * 256:(hd + 1) * 256], in_=Sgla[l][:, hd * 256:(hd + 1) * 256], func=AF.Copy, scale=egl[:, hd:hd + 1]),
                      reads=[("Sgla", l, hd // 2), "egl"], writes=[("Gbf", l)])
                for hd in range(4):
                    A("pool", lambda e, hd=hd: e.tensor_scalar(out=Sgla[l][:, hd * 256:(hd + 1) * 256], in0=Sgla[l][:, hd * 256:(hd + 1) * 256], scalar1=egl[:, hd:hd + 1], scalar2=0.0, op0=ALU.mult, op1=ALU.add),
                      reads=[("Sgla", l, hd // 2), "egl"], writes=[("Sgla", l, hd // 2)])
                STEP()
            REC[0] = None
            PB.ring = PBfull
            merge([sI, sJ], given)
            MARK("P4")
            merge_branch(1)
            MARK("OUT")
            wos = [next_w((l, "out", 0), pf=1), next_w((l, "out", 1), pf=0)]
            for c in range(NCH):
                for j2 in range(2):
                    wo, wokey = wos[j2]
                    bank, key = PB.get()
                    for k in range(8):
                        A("pe", lambda e, k=k, c=c, wo=wo, bank=bank: e.matmul(bank[:, :], lhsT=qkT[:, k, c * 128:(c + 1) * 128], rhs=wo[:, k, :], start=(k == 0), stop=(k == 7)),
                          reads=[("qkT", k) for k in range(8)] + [wokey], writes=[key])
                    A("dve", lambda e, bank=bank, c=c, j2=j2: e.scalar_tensor_tensor(out=h[:, c, j2 * 512:(j2 + 1) * 512], in0=bank[:, :], scalar=0.5, in1=h[:, c, j2 * 512:(j2 + 1) * 512], op0=ALU.mult, op1=ALU.add),
                      reads=[key, ("h", c)], writes=[("h", c)])
            if first and not last_layer:
                A("pool", lambda e: e.memset(h[0:112, 0, :], 0.0), writes=[("h", 0)])

        for ti in range(ntiles):
            for c in range(NCH):
                gc = ti * NCH + c
                if gc == 0:
                    A("pool", lambda e: e.memset(h[:, 0, :], 0.0), writes=[("h", 0)])
                    A("sp", lambda e: e.dma_start(out=h[112:128, 0, :], in_=meta_d), writes=[("h", 0)], dma_sem=xsems[0])
                else:
                    r0 = (gc - 1) * 128
                    A("sp", lambda e, c=c, r0=r0: e.dma_start(out=h[:, c, :], in_=x_d[r0:r0 + 128, :]), writes=[("h", c)], dma_sem=xsems[c])
            for l in range(NL):
                tile_layer(ti, l)
            for c in range(NCH):
                gc = ti * NCH + c
                if gc == 0:
                    continue
                r0 = (gc - 1) * 128
                ub, ukey = UB.get()
                A("act", lambda e, c=c, ub=ub: e.activation(out=ub[:], in_=h[:, c, :], func=AF.Square, accum_out=ssq[:, c:c + 1]), reads=[("h", c)], writes=[("ssq", c), ukey])
                A("dve", lambda e, c=c: e.tensor_scalar(out=rs[:, c:c + 1], in0=ssq[:, c:c + 1], scalar1=1.0 / D, scalar2=EPS, op0=ALU.mult, op1=ALU.add),
                  reads=[("ssq", c)], writes=[("rs", c)])
                A("pool", lambda e, c=c: e.tensor_tensor(out=rs[:, c:c + 1], in0=rs[:, c:c + 1], in1=mhalf[:], op=ALU.pow), reads=[("rs", c), "mhalf"], writes=[("rs", c)])
                A("dve", lambda e, c=c: e.scalar_tensor_tensor(out=h[:, c, :], in0=h[:, c, :], scalar=rs[:, c:c + 1], in1=fnw_bc[:], op0=ALU.mult, op1=ALU.mult),
                  reads=[("h", c), ("rs", c), "fnw"], writes=[("h", c)])
                ok = ("out", gc)
                out_keys.append(ok)
                A("sp", lambda e, c=c, r0=r0: e.dma_start(out=out_d[r0:r0 + 128, :], in_=h[:, c, :]), reads=[("h", c)], writes=[ok], dma_sem=osems[c])
        A("sp", None, reads=out_keys)
        assert wstate["used"] == len(wlist)
        build.model_us = dict(SIM["free"])
        build.model_log = SIM["log"]
        build.mlog = SIM["mlog"]
        with nc.Block() as block:
            S.emit(block)
    return nc


def host_consts():
    i = np.arange(128)
    tri = (i[:, None] <= i[None, :]).astype(np.float32)
    strict = (i[:, None] > i[None, :]).astype(np.float32)
    mask0 = (i >= 112).astype(np.float32)[:, None]
    sel = np.zeros((128, 3, 128), np.float32)
    for r in range(3):
        sel[r, r, :] = 1.0
    return {"ident": np.eye(128, dtype=np.float32), "tri": tri, "strict": strict, "mask0": mask0, "sel": sel.reshape(128, 384)}


def make_in_maps(inputs, ncores, TR):
    f = lambda a: np.ascontiguousarray(np.asarray(a, dtype=np.float32))
    common = dict(host_consts())
    common["meta"] = f(inputs["meta_tokens"])
    common["w_in"] = f(inputs["w_in"])
    common["w_br"] = f(inputs["w_branch"])
    common["w_out"] = f(inputs["w_out"])
    common["normw"] = f(np.asarray(inputs["norm_w"]).reshape(2, 8, 128).transpose(2, 0, 1).reshape(128, 16))
    common["cw"] = f(np.asarray(inputs["conv_w"]).reshape(2, 4, 16, 128).transpose(3, 0, 2, 1).reshape(128, 128))
    common["cb"] = f(np.asarray(inputs["conv_b"]).reshape(1, 4096))
    common["dtb"] = f(inputs["dt_bias"])
    common["alog"] = f(inputs["a_log"])
    common["dskip"] = f(inputs["d_skip"])
    common["ssdnw"] = f(np.asarray(inputs["ssd_norm_w"]).reshape(2, 8, 128).transpose(2, 0, 1).reshape(128, 16))
    common["w2"] = f(np.asarray(inputs["gla_gate_w2"]).transpose(1, 0, 2).reshape(16, 1024))
    common["gb"] = f(np.asarray(inputs["gla_gate_b"]).reshape(1, 1024))
    common["gnw"] = f(np.asarray(inputs["gla_norm_w"]).reshape(2, 2, 128).transpose(2, 0, 1).reshape(128, 4))
    common["fnw"] = f(np.asarray(inputs["final_norm_w"]).reshape(1, 1024))
    x = np.asarray(inputs["x"], dtype=np.float32)
    maps = []
    for b in range(ncores):
        m = dict(common)
        m["x"] = np.ascontiguousarray(x[b, :TR])
        maps.append(m)
    return maps


_NC_CACHE = {}


def kernel(**inputs):
    x = np.asarray(inputs["x"])
    B, T, _ = x.shape
    NT = T // 128 + 1
    key = (NT,)
    if key not in _NC_CACHE:
        _NC_CACHE[key] = build(NT=NT, NCH=3, NL=2)
    nc = _NC_CACHE[key]
    maps = make_in_maps(inputs, B, T)
    res = run_bass_kernel_spmd(nc, maps, core_ids=list(range(B)))
    out = np.stack([np.asarray(r["out"], dtype=np.float32) for r in res.results], axis=0)
    return out
```

```python
import math
import numpy as np
import concourse.bass as bass
import concourse.mybir as mybir
from concourse.bass_utils import run_bass_kernel_spmd
from contextlib import ExitStack

F32 = mybir.dt.float32
BF16 = mybir.dt.bfloat16
AF = mybir.ActivationFunctionType
ALU = mybir.AluOpType

D = 1024
NIN = 8224
EPS = 1e-6
C_Z, C_XBC, C_DT, C_Q, C_K, C_V, C_G, C_ALR, C_M = 0, 1024, 3072, 3088, 3600, 4112, 5136, 6160, 6176


class Op:
    __slots__ = ("eng", "fn", "deps", "sig", "count", "sem", "is_dma", "inc", "fin")

    def __init__(self, eng, fn, is_dma=False):
        self.eng = eng
        self.fn = fn
        self.deps = set()
        self.sig = False
        self.count = 0
        self.sem = None
        self.is_dma = is_dma
        self.inc = 1
        self.fin = 0.0


class Sched:
    ENGS = ("pe", "act", "dve", "pool", "sp")

    def __init__(self, nc, stack):
        self.nc = nc
        self.stack = stack
        self.ops = []
        self.res = {}
        self.engsem = {e: stack.enter_context(nc.semaphore("s_" + e)) for e in self.ENGS}
        self.dma_counts = {}
        self.alias = {}

    def new_sem(self, name):
        return self.stack.enter_context(self.nc.semaphore(name))

    def uniq(self, key):
        lst = self.alias.setdefault(key, [])
        k = ("__u", key, len(lst))
        lst.append(k)
        return k

    def add(self, eng, fn, reads=(), writes=(), dma_sem=None, raw_keys=False):
        op = Op(eng, fn, is_dma=dma_sem is not None)
        if not raw_keys:
            reads = [rr for r in reads for rr in self.alias.get(r, [r])]
            writes = [ww for w in writes for ww in self.alias.get(w, [w])]
        if dma_sem is not None:
            op.sem = dma_sem
            op.sig = True
            op.inc = 16
            c = self.dma_counts.get(id(dma_sem), 0) + 16
            self.dma_counts[id(dma_sem)] = c
            op.count = c
        for r in reads:
            st = self.res.get(r)
            if st is None:
                st = [None, []]
                self.res[r] = st
            if st[0] is not None:
                op.deps.add(st[0])
            st[1].append(op)
        for w in writes:
            st = self.res.get(w)
            if st is None:
                st = [None, []]
                self.res[w] = st
            if st[0] is not None:
                op.deps.add(st[0])
            last = {}
            for rd in st[1]:
                if rd is op:
                    continue
                if rd.is_dma:
                    op.deps.add(rd)
                else:
                    last[rd.eng] = rd
            for rd in last.values():
                op.deps.add(rd)
            st[0] = op
            st[1] = []
        op.deps.discard(op)
        self.ops.append(op)
        return op

    def peek(self, reads, writes):
        reads = [rr for r in reads for rr in self.alias.get(r, [r])]
        writes = [ww for w in writes for ww in self.alias.get(w, [w])]
        deps = []
        for r in reads:
            st = self.res.get(r)
            if st is not None and st[0] is not None:
                deps.append(st[0])
        for w in writes:
            st = self.res.get(w)
            if st is None:
                continue
            if st[0] is not None:
                deps.append(st[0])
            last = {}
            for rd in st[1]:
                if rd.is_dma:
                    deps.append(rd)
                else:
                    last[rd.eng] = rd
            deps.extend(last.values())
        return deps

    def finalize(self):
        for op in self.ops:
            if op.eng == "pe" and not op.is_dma:
                op.deps = {d for d in op.deps if not (d.eng == "pe" and not d.is_dma)}
            for d in op.deps:
                d.sig = True
        cnt = {e: 0 for e in self.ENGS}
        for op in self.ops:
            if op.is_dma:
                continue
            if op.sig:
                cnt[op.eng] += 1
                op.count = cnt[op.eng]
                op.sem = self.engsem[op.eng]

    def emit(self, block):
        self.finalize()
        by_eng = {e: [o for o in self.ops if o.eng == e] for e in self.ENGS}

        def run(engine, ops):
            waited = {}
            for op in ops:
                need = {}
                for d in op.deps:
                    k = id(d.sem)
                    if need.get(k, (None, 0))[1] < d.count:
                        need[k] = (d.sem, d.count)
                for k, (sem, c) in need.items():
                    if waited.get(k, 0) < c:
                        engine.wait_ge(sem, c)
                        waited[k] = c
                if op.fn is None:
                    continue
                ins = op.fn(engine)
                if op.sig:
                    ins.then_inc(op.sem, op.inc)

        @block.tensor
        def _(e):
            run(e, by_eng["pe"])

        @block.scalar
        def _(e):
            run(e, by_eng["act"])

        @block.vector
        def _(e):
            run(e, by_eng["dve"])

        @block.gpsimd
        def _(e):
            run(e, by_eng["pool"])

        @block.sync
        def _(e):
            run(e, by_eng["sp"])


class Ring:
    def __init__(self, items, name, keys=None):
        self.items = items
        self.name = name
        self.keys = keys if keys is not None else [(name, j) for j in range(len(items))]
        self.i = 0

    def get(self):
        j = self.i % len(self.items)
        self.i += 1
        return self.items[j], self.keys[j]


class RingProxy:
    def __init__(self, ring):
        self.ring = ring

    def get(self):
        return self.ring.get()


import os
OPT = {"zs_eng": "pool", "diag_eng": "dve", "ub_eng": "dve", "greedy": 0, "outc": 1, "b2late": 0, "at_split": 1, "hb": 3, "s4": 0, "qkfirst": 0}
for _k in list(OPT):
    if os.environ.get("KOPT_" + _k):
        OPT[_k] = type(OPT[_k])(os.environ["KOPT_" + _k])


def build(NT=33, NCH=3, NL=2, dbg=False):
    assert NT % NCH == 0
    N = NCH * 128
    ntiles = NT // NCH
    TR = (NT - 1) * 128
    nc = bass.Bass("TRN2", target_bir_lowering=False)

    def din(name, shape):
        return nc.dram_tensor(name, list(shape), F32, kind="ExternalInput").ap()

    x_d = din("x", [TR, D])
    meta_d = din("meta", [16, D])
    win_d = din("w_in", [2, D, NIN])
    wbr_d = din("w_br", [2, 2, D, D])
    wout_d = din("w_out", [2, D, D])
    normw_d = din("normw", [128, 16])
    cw_d = din("cw", [128, 128])
    cb_d = din("cb", [1, 2 * 2048])
    dtb_d = din("dtb", [2, 16])
    alog_d = din("alog", [2, 16])
    dskip_d = din("dskip", [2, 16])
    ssdnw_d = din("ssdnw", [128, 16])
    w2_d = din("w2", [16, 2 * 512])
    gb_d = din("gb", [1, 2 * 512])
    gnw_d = din("gnw", [128, 4])
    fnw_d = din("fnw", [1, D])
    ident_d = din("ident", [128, 128])
    tri_d = din("tri", [128, 128])
    strict_d = din("strict", [128, 128])
    mask0_d = din("mask0", [128, 1])
    sel_d = din("sel", [128, 3 * 128])
    out_d = nc.dram_tensor("out", [TR, D], F32, kind="ExternalOutput").ap()

    with ExitStack() as st:
        S = Sched(nc, st)

        def sb(name, shape, dt=F32):
            return st.enter_context(nc.sbuf_tensor("s_" + name, list(shape), dt))

        def ps(name, shape, dt=F32):
            return st.enter_context(nc.psum_tensor(name, list(shape), dt))

        REC = [None]

        class _FakeIns:
            def then_inc(self, *a, **k):
                return self

        class _FakeEng:
            def __init__(self):
                self.calls = []

            def __getattr__(self, name):
                def f(*a, **k):
                    self.calls.append((name, a, k))
                    return _FakeIns()
                return f

        def _fsz(ap):
            n = 1
            for d_ in ap.shape[1:]:
                n *= d_
            return n

        def _in_psum(ap):
            return "PSum" in type(ap.tensor).__name__

        def est_dur(eng, fn, is_dma):
            if fn is None:
                return 0.0
            if is_dma:
                return 3.0
            try:
                fe = _FakeEng()
                fn(fe)
                name, a, k = fe.calls[-1]
                if eng == "pe":
                    if name == "transpose":
                        return 0.13
                    rhs = k["rhs"]
                    n = _fsz(rhs)
                    return max(64, n) / 2400.0 * (4.0 if rhs.dtype == F32 else 1.0) + 0.01
                src = k.get("in_", k.get("in0"))
                n = _fsz(src) if src is not None else 256
                if eng == "act":
                    return 0.22 + n * 0.0009 + (0.09 if k.get("accum_out") is not None else 0.0)
                if eng == "dve":
                    t = 0.07 + n * 0.00105
                    if name == "tensor_tensor" and not _in_psum(k["in0"]) and not _in_psum(k["in1"]):
                        t = 0.07 + n * 0.0021
                    return t
                if eng == "pool":
                    if name == "tensor_tensor" and k.get("op") == ALU.pow:
                        return 0.8
                    if name == "tensor_scalar":
                        return 0.1 + n * 0.0015
                    return 0.3 + n * 0.0016
            except Exception:
                pass
            return 0.5

        SIM = {"free": {e: 0.0 for e in Sched.ENGS}, "pebusy": 0.0, "log": [], "dbg": dbg, "mlog": []}

        def MARK(name):
            SIM["log"].append((name, max(SIM["free"].values()), SIM["free"]["pe"], SIM["pebusy"]))

        def _lat(d, eng):
            return 0.35 if d.eng != eng or d.is_dma else 0.12

        def commit(eng, fn, reads, writes, dma_sem, kw, dur):
            op = S.add(eng, fn, reads, writes, dma_sem=dma_sem, **kw)
            t = SIM["free"][eng]
            for d in op.deps:
                if eng == "pe" and d.eng == "pe" and not d.is_dma:
                    continue
                t = max(t, d.fin + _lat(d, eng))
            if dma_sem is not None:
                SIM["free"][eng] = t + 0.1
                op.fin = t + dur
            else:
                op.fin = t + dur
                SIM["free"][eng] = op.fin
                if eng == "pe":
                    SIM["pebusy"] += dur
            return op

        def A(eng, fn, reads=(), writes=(), dma_sem=None, **kw):
            dur = est_dur(eng, fn, dma_sem is not None)
            if REC[0] is None:
                return commit(eng, fn, reads, writes, dma_sem, kw, dur)
            REC[0].append(("op", (eng, fn, list(reads), list(writes), dma_sem, kw, dur)))
            return None

        def STEP():
            if REC[0] is not None:
                REC[0].append(("step",))

        def NEED(tok):
            if REC[0] is not None:
                REC[0].append(("step",))
                REC[0].append(("need", tok))

        def GIVE(tok):
            if REC[0] is not None:
                REC[0].append(("give", tok))

        def step_stall(step):
            free = dict(SIM["free"])
            lw, lr = {}, {}
            stall = 0.0
            for it in step:
                if it[0] != "op":
                    continue
                eng, fn, r, w, ds, kw, dur = it[1]
                t = free[eng]
                t0 = t
                for d in S.peek(r, w):
                    if eng == "pe" and d.eng == "pe" and not d.is_dma:
                        continue
                    t = max(t, d.fin + _lat(d, eng))
                for key in r:
                    if key in lw:
                        t = max(t, lw[key] + 0.3)
                for key in w:
                    if key in lw:
                        t = max(t, lw[key] + 0.3)
                    if key in lr:
                        t = max(t, lr[key] + 0.3)
                stall += (t - t0) * (1.0 if eng == "pe" else 0.35)
                fin = t + dur
                if ds is not None:
                    free[eng] = t + 0.1
                else:
                    free[eng] = fin
                for key in r:
                    lr[key] = max(lr.get(key, 0.0), fin)
                for key in w:
                    lw[key] = fin
                    lr.pop(key, None)
            return stall

        def merge(streams, given):
            segs = []
            for stq in streams:
                steps = [[]]
                for it in stq:
                    if it[0] == "step":
                        if steps[-1]:
                            steps.append([])
                    else:
                        steps[-1].append(it)
                if not steps[-1]:
                    steps.pop()
                segs.append(steps)
            pos = [0] * len(segs)
            tot = [max(len(x), 1) for x in segs]
            while any(pos[i] < len(segs[i]) for i in range(len(segs))):
                cands = []
                for i in range(len(segs)):
                    if pos[i] >= len(segs[i]):
                        continue
                    step = segs[i][pos[i]]
                    if any(it[0] == "need" and it[1] not in given for it in step):
                        continue
                    cands.append((step_stall(step), i))
                assert cands, "merge deadlock"
                pick = None
                if OPT["greedy"]:
                    for st_, i in cands:
                        if st_ < 0.25:
                            pick = i
                            break
                    if pick is None:
                        pick = min(cands)[1]
                else:
                    pick = min(cands, key=lambda ci: pos[ci[1]] / tot[ci[1]])[1]
                for it in segs[pick][pos[pick]]:
                    if it[0] == "op":
                        eng, fn, r, w, ds, kw, dur = it[1]
                        commit(eng, fn, r, w, ds, kw, dur)
                    elif it[0] == "give":
                        given.add(it[1])
                if SIM.get("dbg"):
                    SIM["mlog"].append((pick, pos[pick], [round(c_[0], 2) for c_ in cands], dict(SIM["free"])))
                pos[pick] += 1
        h = sb("h", [128, NCH, D])
        ubs = [sb("ub%d" % i, [128, D], BF16) for i in range(2)]
        UB = Ring(ubs, "ub")
        uT = sb("uT", [128, 8, N], BF16)
        NWB = 3
        wbufs = [sb("wb%d" % i, [128, 8, 512], BF16) for i in range(NWB)]
        wsems = [S.new_sem("wsem%d" % i) for i in range(NWB)]
        wmisc = sb("wmisc", [128, 2, 8, 32], BF16)
        NDG = 4
        dgs = [sb("dg%d" % i, [128, 4, 128], BF16) for i in range(NDG)]
        DG = Ring(dgs, "dg")
        XW = N + 4
        xbcT = sb("xbcT", [128, 16, XW], BF16)
        hist = sb("hist", [128, 2, 16, 4], BF16)
        x2 = sb("x2", [128, NCH, D], BF16)
        B2tok = sb("B2tok", [128, NCH, 512], BF16)
        BCT = sb("BCT", [128, 8, N], BF16)
        zs = sb("zs", [128, NCH, D], BF16)
        qkT = sb("qkT", [128, 8, N], BF16)
        vtok = sb("vtok", [128, NCH, D], BF16)
        sgT = sb("sgT", [128, 8, N], BF16)
        alrT = sb("alrT", [128, N], BF16)
        ths = [sb("th%d" % i, [128, 512], BF16) for i in range(2)]
        TH = Ring(ths, "th")
        gws = [sb("gw%d" % i, [128, 512], BF16) for i in range(2)]
        GW = Ring(gws, "gw")
        ssq = sb("ssq", [128, NCH])
        rs = sb("rs", [128, NCH])
        dtr = sb("dtr", [128, NCH, 16])
        dte_x = sb("dte_x", [128, NCH, 16])
        dt_t = sb("dt_t", [128, NCH, 16])
        da_t = sb("da_t", [128, NCH, 16])
        ecumh = sb("ecumh", [128, NCH, 16])
        dte = sb("dte", [128, NCH, 16])
        cdb = sb("cdb", [128, NCH, 16])
        wq = sb("wq", [128, NCH, 16])
        dth = sb("dth", [128, NCH, 16])
        sq4 = sb("sq4", [128, 4])
        r4 = sb("r4", [128, 4])
        sq4g = sb("sq4g", [128, 4])
        r4g = sb("r4g", [128, 4])
        egl = sb("egl", [128, 4])
        At = sb("At", [128, 8, 128])
        Lm = sb("Lm", [128, 16, 128], BF16)
        MT = sb("MT", [128, 16, 128], BF16)
        scm = sb("scm", [128, 4, 128], BF16)
        xdt = sb("xdt", [128, D], BF16)
        xw = sb("xw", [128, D], BF16)
        xd = sb("xd", [128, D], BF16)
        t1 = sb("t1", [128, D])
        yn = sb("yn", [128, D], BF16)
        stmp = sb("stmp", [128, D])
        e1 = sb("e1", [128, 512])
        eg = sb("eg", [128, 512])
        emg = sb("emg", [128, 512])
        on = sb("on", [128, D], BF16)
        qtl = sb("qtl", [128, 4, 128], BF16)
        ktl = sb("ktl", [128, 4, 128], BF16)
        ktok = sb("ktok", [128, 512], BF16)
        attm = sb("attm", [128, 4, 128], BF16)
        gth0 = sb("gth0", [128, 4, N], BF16)
        gth = [gth0, gth0]
        t0m = sb("t0m", [128, 8, N], BF16)
        t1ms = [sb("t1m%d" % i, [128, N]) for i in range(2)]
        T1M = Ring(t1ms, "t1m")
        Sssd = [sb("Sssd%d" % l, [128, D]) for l in range(2)]
        Sbf = [sb("Sbf%d" % l, [128, D], BF16) for l in range(2)]
        Sgla = [sb("Sgla%d" % l, [128, D]) for l in range(2)]
        Gbf = [sb("Gbf%d" % l, [128, D], BF16) for l in range(2)]
        identf = sb("identf", [128, 128])
        identb = sb("identb", [128, 128], BF16)
        tri = sb("tri", [128, 128])
        tri16 = sb("tri16", [128, 128])
        strict = sb("strict", [128, 128])
        maskb = sb("maskb", [128, 128], BF16)
        maskq = sb("maskq", [128, 128], BF16)
        onesf = sb("onesf", [128, 128])
        sel = sb("sel", [128, 3, 128], BF16)
        selN = sb("selN", [128, 3, N], BF16)
        rows = sb("rows", [128, 2048], BF16)
        gb8 = sb("gb8", [8, 128])
        gb8h = sb("gb8h", [8, 128], BF16)
        gb8hf = sb("gb8hf", [8, 128])
        gb8l = sb("gb8l", [8, 128], BF16)
        mhalf = sb("mhalf", [128, 1])
        mask0 = sb("mask0", [128, 1])
        normw = sb("normw", [128, 16])
        cw = sb("cw", [128, 128])
        dtb_bc = sb("dtb_bc", [128, 2, 16])
        a_bc = sb("a_bc", [128, 2, 16])
        dskh_bc = sb("dskh_bc", [128, 2, 16])
        ssdnw = sb("ssdnw", [128, 16])
        w2b = sb("w2b", [128, 1024], BF16)
        gnwh = sb("gnwh", [128, 4])
        fnw_bc = sb("fnw_bc", [128, D])
        pbs = [ps("pb%d" % i, [128, 512]) for i in range(6)]
        PBfull = Ring(pbs, "pb")
        PBh = Ring(pbs[0:3], "pb", keys=[("pb", j) for j in range(0, 3)])
        if OPT["s4"]:
            PBi = Ring(pbs[3:5], "pb", keys=[("pb", j) for j in range(3, 5)])
            PBf = Ring(pbs[5:6], "pb", keys=[("pb", 5)])
        else:
            PBi = PBh
            PBf = Ring(pbs[3:6], "pb", keys=[("pb", j) for j in range(3, 6)])
        PB = RingProxy(PBfull)
        pts = [ps("pt%d" % i, [128, 1024], BF16) for i in range(2)]
        PT = Ring(pts, "pt")

        csem = S.new_sem("csem")

        csemp = S.new_sem("csemp")
        cops = []
        copsp = []

        def cload(dst, src, key, eng="sp"):
            if eng == "sp":
                cops.append(A(eng, lambda e: e.dma_start(out=dst, in_=src), writes=[S.uniq(key)], dma_sem=csem, raw_keys=True))
            else:
                copsp.append(A(eng, lambda e: e.dma_start(out=dst, in_=src), writes=[S.uniq(key)], dma_sem=csemp, raw_keys=True))

        cload(identf[:], ident_d, "identf")
        cload(tri[:], tri_d, "tri")
        cload(strict[:], strict_d, "strict")
        cload(mask0[:], mask0_d, "mask0")
        cload(normw[:], normw_d, "normw")
        cload(cw[:], cw_d, "cw")
        cload(ssdnw[:], ssdnw_d, "ssdnw")
        cload(gnwh[:], gnw_d, "gnwh")
        cload(gb8[:], gb_d.rearrange("o (a b) -> (o a) b", a=8), "gb8")
        cload(fnw_bc[:], fnw_d[0:1, :].partition_broadcast(128), "fnw")
        for l in range(2):
            cload(dtb_bc[:, l, :], dtb_d[l:l + 1, :].partition_broadcast(128), "dtb")
            cload(a_bc[:, l, :], alog_d[l:l + 1, :].partition_broadcast(128), "a_bc")
            cload(dskh_bc[:, l, :], dskip_d[l:l + 1, :].partition_broadcast(128), "dskh")
        A("pool", lambda e: e.memset(rows[:], 0.0), writes=["rows0"])
        A("pool", lambda e: e.memset(alrT[:], 0.0), writes=["alrT"])
        A("pool", lambda e: e.memset(w2b[:], 0.0), writes=["w2b0"])
        copsp.append(A("pool", lambda e: e.dma_start(out=rows[0:1, :], in_=cb_d[:, 0:2048]), reads=["rows0"], writes=[S.uniq("rows")], dma_sem=csemp, raw_keys=True))
        copsp.append(A("pool", lambda e: e.dma_start(out=rows[1:2, :], in_=cb_d[:, 2048:4096]), reads=["rows0"], writes=[S.uniq("rows")], dma_sem=csemp, raw_keys=True))
        cload(sel[:].rearrange("p r m -> p (r m)"), sel_d, "sel", eng="pool")
        copsp.append(A("pool", lambda e: e.dma_start(out=w2b[0:16, :], in_=w2_d), reads=["w2b0"], writes=[S.uniq("w2b")], dma_sem=csemp, raw_keys=True))
        for l in range(2):
            cload(wmisc[:, l, :, 0:16], win_d[l, :, C_DT:C_DT + 16].rearrange("(k p) n -> p k n", p=128), "wmisc", eng="pool")
            cload(wmisc[:, l, :, 16:32], win_d[l, :, C_ALR:C_ALR + 16].rearrange("(k p) n -> p k n", p=128), "wmisc", eng="pool")

        for o in cops:
            o.count = 16 * len(cops)
        for o in copsp:
            o.count = 16 * len(copsp)
        A("dve", lambda e: e.tensor_copy(out=identb[:], in_=identf[:]), reads=["identf"], writes=["identb"])
        A("dve", lambda e: e.tensor_copy(out=maskb[:], in_=tri[:]), reads=["tri"], writes=["maskb"])
        A("dve", lambda e: e.tensor_scalar(out=maskq[:], in0=tri[:], scalar1=0.25, scalar2=None, op0=ALU.mult), reads=["tri"], writes=["maskq"])
        A("dve", lambda e: e.tensor_scalar(out=tri16[:], in0=tri[:], scalar1=-1.0 / 16.0, scalar2=None, op0=ALU.mult), reads=["tri"], writes=["tri16"])
        A("pool", lambda e: e.memset(onesf[:], 1.0), writes=["onesf"])
        A("pool", lambda e: e.memset(mhalf[:], -0.5), writes=["mhalf"])
        for r in range(3):
            A("pool", lambda e, r=r: e.tensor_copy(out=selN[:, r, :], in_=bc(sel[:, r, 0:1], [128, N])), reads=["sel"], writes=["selN"])
        A("pool", lambda e: e.memset(hist[:], 0.0), writes=["hist"])
        A("pool", lambda e: e.memset(xbcT[:], 0.0), writes=[("xbcT", b) for b in range(16)])
        for l in range(2):
            A("pool", lambda e, l=l: e.memset(Sssd[l][:], 0.0), writes=[("Sssd", l)])
            A("pool", lambda e, l=l: e.memset(Sbf[l][:], 0.0), writes=[("Sbf", l)])
            A("pool", lambda e, l=l: e.memset(Sgla[l][:], 0.0), writes=[("Sgla", l, 0), ("Sgla", l, 1)])
            A("pool", lambda e, l=l: e.memset(Gbf[l][:], 0.0), writes=[("Gbf", l)])
        A("act", lambda e: e.activation(out=a_bc[:], in_=a_bc[:], func=AF.Exp), reads=["a_bc"], writes=["a_bc"])
        A("dve", lambda e: e.tensor_scalar(out=a_bc[:], in0=a_bc[:], scalar1=-1.0, scalar2=None, op0=ALU.mult), reads=["a_bc"], writes=["a_bc"])
        A("dve", lambda e: e.tensor_scalar(out=dskh_bc[:], in0=dskh_bc[:], scalar1=0.5, scalar2=None, op0=ALU.mult), reads=["dskh"], writes=["dskh"])
        A("dve", lambda e: e.tensor_scalar(out=gnwh[:], in0=gnwh[:], scalar1=0.5, scalar2=None, op0=ALU.mult), reads=["gnwh"], writes=["gnwh"])
        A("dve", lambda e: e.tensor_copy(out=gb8h[:], in_=gb8[:]), reads=["gb8"], writes=["gb8h"])
        A("dve", lambda e: e.tensor_copy(out=gb8hf[:], in_=gb8h[:]), reads=["gb8h"], writes=["gb8hf"])
        A("dve", lambda e: e.tensor_tensor(out=gb8l[:], in0=gb8[:], in1=gb8hf[:], op=ALU.subtract), reads=["gb8", "gb8hf"], writes=["gb8l"])
        csem2 = S.new_sem("csem2")
        cops2 = []
        for a8 in range(8):
            cops2.append(A("sp", lambda e, a8=a8: e.dma_start(out=rows[2:3, a8 * 128:(a8 + 1) * 128], in_=gb8h[a8:a8 + 1, :]), reads=["gb8h", "rows0"], writes=[S.uniq("rows")], dma_sem=csem2))
            cops2.append(A("sp", lambda e, a8=a8: e.dma_start(out=rows[2:3, 1024 + a8 * 128:1024 + (a8 + 1) * 128], in_=gb8l[a8:a8 + 1, :]), reads=["gb8l", "rows0"], writes=[S.uniq("rows")], dma_sem=csem2))
        for o in cops2:
            o.count = 16 * len(cops2)

        def wsrc(l, kind, idx):
            if kind == "in":
                src = win_d[l, :, idx:idx + 512]
            elif kind == "br":
                b, j = idx
                src = wbr_d[l, b, :, j * 512:(j + 1) * 512]
            else:
                src = wout_d[l, :, idx * 512:(idx + 1) * 512]
            return src.rearrange("(k p) n -> p k n", p=128)

        def layer_groups(l):
            g = []
            for i in range(4):
                g.append((l, "in", C_XBC + 512 * i))
            if OPT["qkfirst"]:
                g.append((l, "in", C_Q))
                g.append((l, "in", C_K))
            for i in range(2):
                g.append((l, "in", C_Z + 512 * i))
            if not OPT["qkfirst"]:
                g.append((l, "in", C_Q))
                g.append((l, "in", C_K))
            for i in range(2):
                g.append((l, "in", C_V + 512 * i))
            for i in range(2):
                g.append((l, "in", C_G + 512 * i))
            for b in range(2):
                for j in range(2):
                    g.append((l, "in", C_M + b * 1024 + j * 512))
                    g.append((l, "br", (b, j)))
            for j in range(2):
                g.append((l, "out", j))
            return g

        wlist = []
        for ti in range(ntiles):
            for l in range(NL):
                wlist += layer_groups(l)
        wstate = {"issued": 0, "used": 0}
        PF = 2

        GP = len(layer_groups(0)) * NL
        wscr = nc.dram_tensor("wscr", [GP, 128, 8 * 512], BF16, kind="Internal").ap()
        wsemh = [S.new_sem("wsemh%d" % i) for i in range(NWB)]
        wsemst = [S.new_sem("wsemst%d" % i) for i in range(NWB)]

        def w_issue_upto(n):
            while wstate["issued"] < min(n, len(wlist)):
                i = wstate["issued"]
                l, kind, idx = wlist[i]
                slot = i % NWB
                if i < GP:
                    src = wsrc(l, kind, idx)
                    A("pool", lambda e, slot=slot, src=src: e.dma_start(out=wbufs[slot][:], in_=src),
                      writes=[("wb", slot)], dma_sem=wsems[slot])
                    if ntiles > 1:
                        A("sp", lambda e, slot=slot, i=i: e.dma_start(out=wscr[i], in_=wbufs[slot][:].rearrange("p k n -> p (k n)")),
                          reads=[("wb", slot)], writes=[("wscr", i)], dma_sem=wsemst[slot])
                else:
                    g = i % GP
                    A("sp", lambda e, slot=slot, g=g: e.dma_start(out=wbufs[slot][:].rearrange("p k n -> p (k n)"), in_=wscr[g]),
                      reads=[("wscr", g)], writes=[("wb", slot)], dma_sem=wsemh[slot])
                wstate["issued"] += 1

        def next_w(expect, pf=PF):
            i = wstate["used"]
            assert wlist[i] == expect, (wlist[i], expect)
            w_issue_upto(i + 1 + pf)
            wstate["used"] += 1
            return wbufs[i % NWB], ("wb", i % NWB)

        def bc(ap, shape):
            return ap.to_broadcast(list(shape))

        HALF_LN = math.log(0.5)
        QS_LN = math.log(128.0 ** -0.5)

        def silu2_evac(bank_ap, out_ap, key, wkey, ncols):
            th, tkey = TH.get()
            A("act", lambda e: e.activation(out=th[:, 0:ncols], in_=bank_ap, func=AF.Tanh, scale=0.5), reads=[key], writes=[tkey])
            A("dve", lambda e: e.scalar_tensor_tensor(out=out_ap, in0=th[:, 0:ncols], scalar=1.0, in1=bank_ap, op0=ALU.add, op1=ALU.mult),
              reads=[tkey, key], writes=[wkey])

        uT_all = [("uT", c) for c in range(NCH)]
        out_keys = []
        osems = [S.new_sem("osem%d" % i) for i in range(NCH)]
        xsems = [S.new_sem("xsem%d" % i) for i in range(NCH)]

        def tile_layer(ti, l):
            first = ti == 0
            last_layer = l == NL - 1
            MARK("A %d %d" % (ti, l))
            for c in range(NCH):
                ub, ukey = UB.get()
                A("act", lambda e, c=c, ub=ub: e.activation(out=ub[:], in_=h[:, c, :], func=AF.Square, accum_out=ssq[:, c:c + 1]),
                  reads=[("h", c)], writes=[("ssq", c), ukey])
                A("dve", lambda e, c=c: e.tensor_scalar(out=rs[:, c:c + 1], in0=ssq[:, c:c + 1], scalar1=1.0 / D, scalar2=EPS, op0=ALU.mult, op1=ALU.add),
                  reads=[("ssq", c)], writes=[("rs", c)])
                A("pool", lambda e, c=c: e.tensor_tensor(out=rs[:, c:c + 1], in0=rs[:, c:c + 1], in1=mhalf[:], op=ALU.pow),
                  reads=[("rs", c), "mhalf"], writes=[("rs", c)])
                if OPT["ub_eng"] == "act":
                    A("act", lambda e, c=c, ub=ub: e.activation(out=ub[:], in_=h[:, c, :], func=AF.Copy, scale=rs[:, c:c + 1]),
                      reads=[("h", c), ("rs", c)], writes=[ukey])
                else:
                    A("dve", lambda e, c=c, ub=ub: e.tensor_scalar(out=ub[:], in0=h[:, c, :], scalar1=rs[:, c:c + 1], scalar2=None, op0=ALU.mult),
                      reads=[("h", c), ("rs", c)], writes=[ukey])
                bank, key = PT.get()
                for k in range(8):
                    A("pe", lambda e, k=k, ub=ub, bank=bank: e.transpose(out=bank[:, k * 128:(k + 1) * 128], in_=ub[:, k * 128:(k + 1) * 128], identity=identb[:]),
                      reads=[ukey, "identb"], writes=[key])
                A("dve", lambda e, c=c, bank=bank: e.tensor_tensor(out=uT[:, :, c * 128:(c + 1) * 128], in0=bank[:, :].rearrange("p (k t) -> p k t", k=8),
                                                                    in1=bc(normw[:, l * 8:(l + 1) * 8].unsqueeze(2), [128, 8, 128]), op=ALU.mult),
                  reads=[key, "normw"], writes=[("uT", c)])
            MARK("B")
            for c in range(NCH):
                bank, key = PB.get()
                for k in range(8):
                    A("pe", lambda e, k=k, c=c, bank=bank: e.matmul(bank[:, 0:16], lhsT=uT[:, k, c * 128:(c + 1) * 128], rhs=wmisc[:, l, k, 0:16], start=(k == 0), stop=(k == 7)),
                      reads=[("uT", c), "wmisc"], writes=[key])
                A("dve", lambda e, c=c, bank=bank: e.tensor_tensor(out=dtr[:, c, :], in0=bank[:, 0:16], in1=dtb_bc[:, l, :], op=ALU.add),
                  reads=[key, "dtb"], writes=[("dtr", c)])
                A("act", lambda e, c=c: e.activation(out=dte_x[:, c, :], in_=dtr[:, c, :], func=AF.Exp), reads=[("dtr", c)], writes=[("dte_x", c)])
            for c in range(NCH):
                A("act", lambda e, c=c: e.activation(out=dt_t[:, c, :], in_=dte_x[:, c, :], func=AF.Ln, bias=1.0), reads=[("dte_x", c)], writes=[("dt", c)])
            for c in range(NCH):
                if first and c == 0:
                    A("dve", lambda e, c=c: e.tensor_scalar(out=dt_t[:, c, :], in0=dt_t[:, c, :], scalar1=mask0[:, 0:1], scalar2=None, op0=ALU.mult),
                      reads=[("dt", c), "mask0"], writes=[("dt", c)])
                A("dve", lambda e, c=c: e.tensor_tensor(out=da_t[:, c, :], in0=dt_t[:, c, :], in1=a_bc[:, l, :], op=ALU.mult),
                  reads=[("dt", c), "a_bc"], writes=[("da", c)])
            def stage_B2(c):
                bank2, key2 = PB.get()
                A("pe", lambda e, c=c, bank2=bank2: e.matmul(bank2[:, 0:16], lhsT=tri[:], rhs=da_t[:, c, :], start=True, stop=True), reads=[("da", c), "tri"], writes=[key2])
                A("pe", lambda e, c=c, bank2=bank2: e.matmul(bank2[:, 16:32], lhsT=onesf[:], rhs=da_t[:, c, :], start=True, stop=True), reads=[("da", c), "onesf"], writes=[key2])
                A("pe", lambda e, c=c, bank2=bank2: e.matmul(bank2[:, 32:48], lhsT=strict[:], rhs=da_t[:, c, :], start=True, stop=True), reads=[("da", c), "strict"], writes=[key2])
                A("act", lambda e, c=c, bank2=bank2: e.activation(out=ecumh[:, c, :], in_=bank2[:, 0:16], func=AF.Exp, bias=HALF_LN), reads=[key2], writes=[("ecumh", c)])
                A("act", lambda e, c=c, bank2=bank2: e.activation(out=cdb[:, c, :], in_=bank2[:, 16:32], func=AF.Exp), reads=[key2], writes=[("cdb", c)])
                A("act", lambda e, c=c, bank2=bank2: e.activation(out=dte[:, c, :], in_=bank2[:, 32:48], func=AF.Exp), reads=[key2], writes=[("dte", c)])
                A("dve", lambda e, c=c: e.scalar_tensor_tensor(out=wq[:, c, :], in0=dt_t[:, c, :], scalar=0.25, in1=dte[:, c, :], op0=ALU.mult, op1=ALU.mult),
                  reads=[("dt", c), ("dte", c)], writes=[("wq", c)])
                A("dve", lambda e, c=c: e.tensor_scalar(out=dth[:, c, :], in0=dt_t[:, c, :], scalar1=0.5, scalar2=None, op0=ALU.mult),
                  reads=[("dt", c)], writes=[("dth", c)])
            if not OPT["b2late"]:
                for c in range(NCH):
                    stage_B2(c)
            bank, key = PB.get()
            for k in range(8):
                A("pe", lambda e, k=k, bank=bank: e.matmul(bank[0:16, 0:N], lhsT=wmisc[:, l, k, 16:32], rhs=uT[:, k, :], start=(k == 0), stop=(k == 7)),
                  reads=uT_all + ["wmisc"], writes=[key])
            A("act", lambda e, bank=bank: e.activation(out=alrT[0:16, :], in_=bank[0:16, 0:N], func=AF.Copy), reads=[key], writes=["alrT"])
            MARK("C")
            for gi in range(4):
                wb, wkey = next_w((l, "in", C_XBC + 512 * gi))
                for j in range(4):
                    blk = gi * 4 + j
                    bank, key = PB.get()
                    for k in range(8):
                        A("pe", lambda e, k=k, j=j, wb=wb, bank=bank: e.matmul(bank[:, 0:N], lhsT=wb[:, k, j * 128:(j + 1) * 128], rhs=uT[:, k, :], start=(k == 0), stop=(k == 7)),
                          reads=uT_all + [wkey], writes=[key])
                    A("act", lambda e, blk=blk, bank=bank: e.activation(out=xbcT[:, blk, 4:4 + N], in_=bank[:, 0:N], func=AF.Copy), reads=[key], writes=[("xbcT", blk)])
            if OPT["b2late"]:
                for c in range(NCH):
                    stage_B2(c)
            allx = [("xbcT", b) for b in range(16)]
            A("pool", lambda e: e.tensor_copy(out=xbcT[:, :, 1:4], in_=hist[:, l, :, 1:4]), reads=[("hist", l)], writes=allx)
            A("pool", lambda e: e.tensor_copy(out=hist[:, l, :, 1:4], in_=xbcT[:, :, N + 1:N + 4]), reads=allx, writes=[("hist", l)])
            MARK("E")
            dgof = {}

            def gen_diag(blk):
                dg, dkey = DG.get()
                for tap in range(4):
                    col = (l * 16 + blk) * 4 + tap
                    A(OPT["diag_eng"], lambda e, dg=dg, tap=tap, col=col: e.tensor_scalar(out=dg[:, tap, :], in0=identb[:], scalar1=cw[:, col:col + 1], scalar2=0.0, op0=ALU.mult, op1=ALU.add),
                      reads=["identb", "cw"], writes=[dkey])
                dgof[blk] = (dg, dkey)

            def conv_tok(c, blks, out_ap, wkey):
                bank, key = PB.get()
                for j, blk in enumerate(blks):
                    dg, dkey = dgof[blk]
                    o = bank[:, j * 128:(j + 1) * 128]
                    A("pe", lambda e, o=o, blk=blk: e.matmul(o, lhsT=sel[:, l, :], rhs=rows[:, blk * 128:(blk + 1) * 128], start=True, stop=False),
                      reads=["sel", "rows"], writes=[key])
                    for tap in range(4):
                        A("pe", lambda e, o=o, blk=blk, tap=tap, dg=dg, c=c: e.matmul(o, lhsT=xbcT[:, blk, c * 128 + 1 + tap:c * 128 + 1 + tap + 128], rhs=dg[:, tap, :], start=False, stop=(tap == 3)),
                          reads=[("xbcT", blk), dkey], writes=[key])
                silu2_evac(bank[:, 0:128 * len(blks)], out_ap, key, wkey, 128 * len(blks))

            for half in range(2):
                blks = [half * 4 + j for j in range(4)]
                for b in blks:
                    gen_diag(b)
                for c in range(NCH):
                    conv_tok(c, blks, x2[:, c, half * 512:(half + 1) * 512], ("x2", c))
            for blk in range(8, 16):
                if blk % 4 == 0:
                    for b4 in range(blk, blk + 4):
                        gen_diag(b4)
                    if blk == 8:
                        for c in range(NCH):
                            conv_tok(c, [8, 9, 10, 11], B2tok[:, c, :], ("B2tok", c))
                dg, dkey = dgof[blk]
                bank, key = PB.get()
                A("pe", lambda e, bank=bank, blk=blk: e.matmul(bank[:, 0:N], lhsT=rows[:, blk * 128:(blk + 1) * 128], rhs=selN[:, l, :], start=True, stop=False),
                  reads=["selN", "rows"], writes=[key])
                for tap in range(4):
                    A("pe", lambda e, bank=bank, blk=blk, tap=tap, dg=dg: e.matmul(bank[:, 0:N], lhsT=dg[:, tap, :], rhs=xbcT[:, blk, 1 + tap:1 + tap + N], start=False, stop=(tap == 3)),
                      reads=[("xbcT", blk), dkey], writes=[key])
                silu2_evac(bank[:, 0:N], BCT[:, blk - 8, :], key, ("BCT", blk - 8), N)
            MARK("P2")
            sF = []
            REC[0] = sF
            PB.ring = PBf
            for part in (("qk", "z") if OPT["qkfirst"] else ("z", "qk")):
              if part == "z":
                for gi in range(2):
                    wb, wkey = next_w((l, "in", C_Z + 512 * gi))
                    for c in range(NCH):
                        bank, key = PB.get()
                        for k in range(8):
                            A("pe", lambda e, k=k, c=c, wb=wb, bank=bank: e.matmul(bank[:, :], lhsT=uT[:, k, c * 128:(c + 1) * 128], rhs=wb[:, k, :], start=(k == 0), stop=(k == 7)),
                              reads=[("uT", c), wkey], writes=[key])
                        silu2_evac(bank[:, :], zs[:, c, gi * 512:(gi + 1) * 512], key, ("zs", c), 512)
                        if gi == 1:
                            GIVE(("zs", c))
                        STEP()
              else:
                for qk in range(2):
                    wb, wkey = next_w((l, "in", C_Q if qk == 0 else C_K))
                    for j in range(4):
                        bank, key = PB.get()
                        for k in range(8):
                            A("pe", lambda e, k=k, j=j, wb=wb, bank=bank: e.matmul(bank[:, 0:N], lhsT=wb[:, k, j * 128:(j + 1) * 128], rhs=uT[:, k, :], start=(k == 0), stop=(k == 7)),
                              reads=uT_all + [wkey], writes=[key])
                        A("act", lambda e, bank=bank, qk=qk, j=j: e.activation(out=qkT[:, qk * 4 + j, :], in_=bank[:, 0:N], func=AF.Copy), reads=[key], writes=[("qkT", qk * 4 + j)])
                        if j == 3:
                            GIVE("q" if qk == 0 else "k")
                        STEP()
            for gi in range(2):
                wb, wkey = next_w((l, "in", C_V + 512 * gi))
                for c in range(NCH):
                    bank, key = PB.get()
                    for k in range(8):
                        A("pe", lambda e, k=k, c=c, wb=wb, bank=bank: e.matmul(bank[:, :], lhsT=uT[:, k, c * 128:(c + 1) * 128], rhs=wb[:, k, :], start=(k == 0), stop=(k == 7)),
                          reads=[("uT", c), wkey], writes=[key])
                    A("act", lambda e, bank=bank, c=c, gi=gi: e.activation(out=vtok[:, c, gi * 512:(gi + 1) * 512], in_=bank[:, :], func=AF.Copy), reads=[key], writes=[("vtok", c)])
                    if gi == 1:
                        GIVE(("v", c))
                    STEP()
            for gi in range(2):
                wb, wkey = next_w((l, "in", C_G + 512 * gi))
                for j in range(4):
                    blk = gi * 4 + j
                    bank, key = PB.get()
                    for k in range(8):
                        A("pe", lambda e, k=k, j=j, wb=wb, bank=bank: e.matmul(bank[:, 0:N], lhsT=wb[:, k, j * 128:(j + 1) * 128], rhs=uT[:, k, :], start=(k == 0), stop=(k == 7)),
                          reads=uT_all + [wkey], writes=[key])
                    gw, gkey = GW.get()
                    th, tkey = TH.get()
                    A("act", lambda e, bank=bank, gw=gw, blk=blk: e.activation(out=gw[:, 0:N], in_=bank[:, 0:N], func=AF.Copy, scale=gnwh[:, l * 2 + (blk % 2):l * 2 + (blk % 2) + 1]),
                      reads=[key, "gnwh"], writes=[gkey])
                    A("act", lambda e, bank=bank, th=th: e.activation(out=th[:, 0:N], in_=bank[:, 0:N], func=AF.Tanh, scale=0.5), reads=[key], writes=[tkey])
                    A("dve", lambda e, gw=gw, th=th, blk=blk: e.scalar_tensor_tensor(out=sgT[:, blk, :], in0=th[:, 0:N], scalar=1.0, in1=gw[:, 0:N], op0=ALU.add, op1=ALU.mult),
                      reads=[gkey, tkey], writes=[("sgT", blk)])
                    if blk == 7:
                        GIVE("g")
                        GIVE("Fdone")
                    STEP()
            sH = []
            REC[0] = sH
            PB.ring = PBh
            for c in range(NCH):
                cs = slice(c * 128, (c + 1) * 128)
                for hv in range(2):
                    for hh in range(8):
                        hd = hv * 8 + hh
                        if OPT["at_split"] and hh % 2 == 1:
                            A("pool", lambda e, hh=hh, hd=hd, c=c: e.tensor_scalar(out=At[:, hh, :], in0=strict[:], scalar1=da_t[:, c, hd:hd + 1], scalar2=0.0, op0=ALU.mult, op1=ALU.add),
                              reads=["strict", ("da", c)], writes=[("At", hh)])
                        else:
                            A("dve", lambda e, hh=hh, hd=hd, c=c: e.tensor_scalar(out=At[:, hh, :], in0=strict[:], scalar1=da_t[:, c, hd:hd + 1], scalar2=None, op0=ALU.mult),
                              reads=["strict", ("da", c)], writes=[("At", hh)])
                    for qd in range(2):
                        bank, key = PB.get()
                        for hq in range(4):
                            hh = qd * 4 + hq
                            A("pe", lambda e, bank=bank, hq=hq, hh=hh: e.matmul(bank[:, hq * 128:(hq + 1) * 128], lhsT=At[:, hh, :], rhs=tri[:], start=True, stop=True),
                              reads=[("At", hh), "tri"], writes=[key])
                        h0 = hv * 8 + qd * 4
                        A("act", lambda e, bank=bank, h0=h0: e.activation(out=Lm[:, h0:h0 + 4, :], in_=bank[:, :].rearrange("p (a t) -> p a t", a=4), func=AF.Exp),
                          reads=[key], writes=[("Lm", h0 // 4)])
                        STEP()
                bank, key = PB.get()
                for g in range(4):
                    A("pe", lambda e, bank=bank, g=g, cs=cs: e.matmul(bank[:, g * 128:(g + 1) * 128], lhsT=BCT[:, g, cs], rhs=BCT[:, 4 + g, cs], start=True, stop=True),
                      reads=[("BCT", g), ("BCT", 4 + g)], writes=[key])
                A("dve", lambda e, bank=bank: e.tensor_tensor(out=scm[:], in0=bank[:, :].rearrange("p (g t) -> p g t", g=4), in1=bc(maskq[:, :].unsqueeze(1), [128, 4, 128]), op=ALU.mult),
                  reads=[key, "maskq"], writes=["scm"])
                A("dve", lambda e: e.tensor_tensor(out=MT[:].rearrange("p (g a) t -> p g a t", g=4), in0=Lm[:].rearrange("p (g a) t -> p g a t", g=4),
                                                   in1=bc(scm[:].unsqueeze(2), [128, 4, 4, 128]), op=ALU.mult),
                  reads=["scm"] + [("Lm", i) for i in range(4)], writes=["MT"])
                STEP()
                x3 = x2[:, c, :].rearrange("p (a d) -> p a d", a=16)
                A("pool", lambda e, x3=x3, c=c: e.tensor_tensor(out=xdt[:].rearrange("p (a d) -> p a d", a=16), in0=x3, in1=bc(dth[:, c, :].unsqueeze(2), [128, 16, 64]), op=ALU.mult),
                  reads=[("x2", c), ("dth", c)], writes=["xdt"])
                A("pool", lambda e, x3=x3, c=c: e.tensor_tensor(out=xw[:].rearrange("p (a d) -> p a d", a=16), in0=x3, in1=bc(wq[:, c, :].unsqueeze(2), [128, 16, 64]), op=ALU.mult),
                  reads=[("x2", c), ("wq", c)], writes=["xw"])
                A("pool", lambda e, x3=x3: e.tensor_tensor(out=xd[:].rearrange("p (a d) -> p a d", a=16), in0=x3, in1=bc(dskh_bc[:, l, :].unsqueeze(2), [128, 16, 64]), op=ALU.mult),
                  reads=[("x2", c), "dskh"], writes=["xd"])
                STEP()
                ydk = []
                for hb in range(2):
                    bank, key = PB.get()
                    ydk.append((bank, key))
                    A("pe", lambda e, bank=bank, hb=hb: e.matmul(bank[:, :], lhsT=identb[:], rhs=xd[:, hb * 512:(hb + 1) * 512], start=True, stop=False),
                      reads=["identb", "xd"], writes=[key])
                    for h8 in range(8):
                        hd = hb * 8 + h8
                        A("pe", lambda e, bank=bank, h8=h8, hd=hd: e.matmul(bank[:, h8 * 64:(h8 + 1) * 64], lhsT=MT[:, hd, :], rhs=xdt[:, hd * 64:(hd + 1) * 64], start=False, stop=(h8 == 7)),
                          reads=["MT", "xdt"], writes=[key])
                    STEP()
                for hb in range(2):
                    bank, key = PB.get()
                    for gg in range(2):
                        g = hb * 2 + gg
                        A("pe", lambda e, bank=bank, gg=gg, g=g, cs=cs: e.matmul(bank[:, gg * 256:(gg + 1) * 256], lhsT=BCT[:, 4 + g, cs], rhs=Sbf[l][:, g * 256:(g + 1) * 256], start=True, stop=True),
                          reads=[("BCT", 4 + g), ("Sbf", l)], writes=[key])
                    hs = slice(hb * 512, (hb + 1) * 512)
                    A("dve", lambda e, bank=bank, hb=hb, hs=hs, c=c: e.tensor_tensor(out=t1[:, hs].rearrange("p (a d) -> p a d", a=8), in0=bank[:, :].rearrange("p (a d) -> p a d", a=8),
                                                                                  in1=bc(ecumh[:, c, hb * 8:(hb + 1) * 8].unsqueeze(2), [128, 8, 64]), op=ALU.mult),
                      reads=[key, ("ecumh", c)], writes=[("t1", hb)])
                    ybank, ykey = ydk[hb]
                    A("dve", lambda e, ybank=ybank, hs=hs: e.tensor_tensor(out=t1[:, hs], in0=t1[:, hs], in1=ybank[:, :], op=ALU.add),
                      reads=[ykey, ("t1", hb)], writes=[("t1", hb)])
                    NEED(("zs", c))
                    A(OPT["zs_eng"], lambda e, hs=hs, c=c: e.tensor_tensor(out=t1[:, hs], in0=t1[:, hs], in1=zs[:, c, hs], op=ALU.mult),
                      reads=[("t1", hb), ("zs", c)], writes=[("t1", hb)])
                    STEP()
                for g in range(4):
                    A("act", lambda e, g=g: e.activation(out=yn[:, g * 256:(g + 1) * 256], in_=t1[:, g * 256:(g + 1) * 256], func=AF.Square, accum_out=sq4[:, g:g + 1]),
                      reads=[("t1", g // 2)], writes=[("sq4", g), ("yn", g // 2)])
                sq4k = [("sq4", g) for g in range(4)]
                A("dve", lambda e: e.tensor_scalar(out=r4[:], in0=sq4[:], scalar1=1.0 / 256, scalar2=4 * EPS, op0=ALU.mult, op1=ALU.add), reads=sq4k, writes=["r4"])
                A("pool", lambda e: e.tensor_tensor(out=r4[:], in0=r4[:], in1=bc(mhalf[:, 0:1], [128, 4]), op=ALU.pow), reads=["r4", "mhalf"], writes=["r4"])
                for g in range(4):
                    gs = slice(g * 256, (g + 1) * 256)
                    if g % 2 == 0:
                        A("dve", lambda e, g=g, gs=gs: e.tensor_scalar(out=yn[:, gs], in0=t1[:, gs], scalar1=r4[:, g:g + 1], scalar2=None, op0=ALU.mult),
                          reads=["r4", ("t1", g // 2)], writes=[("yn", g // 2)])
                    else:
                        A("act", lambda e, g=g, gs=gs: e.activation(out=yn[:, gs], in_=t1[:, gs], func=AF.Copy, scale=r4[:, g:g + 1]),
                          reads=["r4", ("t1", g // 2)], writes=[("yn", g // 2)])
                STEP()
                bank, key = PT.get()
                for k in range(8):
                    A("pe", lambda e, k=k, bank=bank: e.transpose(out=bank[:, k * 128:(k + 1) * 128], in_=yn[:, k * 128:(k + 1) * 128], identity=identb[:]),
                      reads=[("yn", 0), ("yn", 1), "identb"], writes=[key])
                A("dve", lambda e, bank=bank, c=c: e.tensor_tensor(out=xbcT[:, 0:8, 4 + c * 128:4 + (c + 1) * 128], in0=bank[:, :].rearrange("p (k t) -> p k t", k=8),
                                                                    in1=bc(ssdnw[:, l * 8:(l + 1) * 8].unsqueeze(2), [128, 8, 128]), op=ALU.mult),
                  reads=[key, "ssdnw"], writes=[("xbcT", b) for b in range(8)])
                if c == NCH - 1:
                    GIVE("ysT")
                STEP()
                A("pool", lambda e, c=c: e.tensor_tensor(out=stmp[:].rearrange("p (a d) -> p a d", a=16), in0=Sssd[l][:].rearrange("p (a d) -> p a d", a=16),
                                                         in1=bc(cdb[:, c, :].unsqueeze(2), [128, 16, 64]), op=ALU.mult),
                  reads=[("Sssd", l), ("cdb", c)], writes=["stmp", "stmp2"])
                for hb in range(2):
                    bank, key = PB.get()
                    for gg in range(2):
                        g = hb * 2 + gg
                        A("pe", lambda e, bank=bank, gg=gg, g=g, c=c: e.matmul(bank[:, gg * 256:(gg + 1) * 256], lhsT=B2tok[:, c, g * 128:(g + 1) * 128], rhs=xw[:, g * 256:(g + 1) * 256], start=True, stop=True),
                          reads=[("B2tok", c), "xw"], writes=[key])
                    hs = slice(hb * 512, (hb + 1) * 512)
                    A("dve", lambda e, bank=bank, hs=hs: e.tensor_tensor(out=Sssd[l][:, hs], in0=stmp[:, hs], in1=bank[:, :], op=ALU.add),
                      reads=[key, "stmp" if hb == 0 else "stmp2"], writes=[("Sssd", l)])
                A("act", lambda e: e.activation(out=Sbf[l][:], in_=Sssd[l][:], func=AF.Copy), reads=[("Sssd", l)], writes=[("Sbf", l)])
                STEP()

            REC[0] = None
            PB.ring = PBfull
            given = set()
            if not OPT["s4"]:
                merge([sH, sF], given)
            MARK("P3")

            def merge_branch(b):
                for j in range(2):
                    wg, wgkey = next_w((l, "in", C_M + b * 1024 + j * 512))
                    for jj in range(4):
                        bank, key = PB.get()
                        for k in range(8):
                            A("pe", lambda e, k=k, jj=jj, wg=wg, bank=bank: e.matmul(bank[:, 0:N], lhsT=wg[:, k, jj * 128:(jj + 1) * 128], rhs=uT[:, k, :], start=(k == 0), stop=(k == 7)),
                              reads=uT_all + [wgkey], writes=[key])
                        A("act", lambda e, bank=bank, jj=jj: e.activation(out=gth[b][:, jj, :], in_=bank[:, 0:N], func=AF.Tanh, scale=0.5), reads=[key], writes=[("gth", jj)])
                        STEP()
                    wbr, wbkey = next_w((l, "br", (b, j)))
                    if b == 0:
                        NEED("ysT")
                    for jj in range(4):
                        bank, key = PB.get()
                        for k in range(8):
                            A("pe", lambda e, k=k, jj=jj, wbr=wbr, bank=bank: e.matmul(bank[:, 0:N], lhsT=wbr[:, k, jj * 128:(jj + 1) * 128], rhs=xbcT[:, b * 8 + k, 4:4 + N], start=(k == 0), stop=(k == 7)),
                              reads=[("xbcT", b * 8 + k) for k in range(8)] + [wbkey], writes=[key])
                        if b == 0:
                            A("dve", lambda e, bank=bank, jj=jj, j=j: e.scalar_tensor_tensor(out=t0m[:, j * 4 + jj, :], in0=gth[0][:, jj, :], scalar=1.0, in1=bank[:, 0:N], op0=ALU.add, op1=ALU.mult),
                              reads=[key, ("gth", jj)], writes=[("t0m", j * 4 + jj)])
                        else:
                            t1m, t1key = T1M.get()
                            A("dve", lambda e, bank=bank, jj=jj, t1m=t1m: e.scalar_tensor_tensor(out=t1m[:, :], in0=gth[1][:, jj, :], scalar=1.0, in1=bank[:, 0:N], op0=ALU.add, op1=ALU.mult),
                              reads=[key, ("gth", jj)], writes=[t1key])
                            A("pool", lambda e, jj=jj, j=j, t1m=t1m: e.tensor_tensor(out=qkT[:, j * 4 + jj, :], in0=t0m[:, j * 4 + jj, :], in1=t1m[:, :], op=ALU.add),
                              reads=[t1key, ("t0m", j * 4 + jj)], writes=[("qkT", j * 4 + jj)])
                        STEP()

            sJ = []
            if not OPT["s4"]:
                REC[0] = sJ
                PB.ring = PBf
                NEED("Fdone")
                merge_branch(0)
            sI = []
            REC[0] = sI
            PB.ring = PBi
            for c in range(NCH):
                cs = slice(c * 128, (c + 1) * 128)
                bank, key = PB.get()
                A("pe", lambda e, bank=bank, cs=cs: e.matmul(bank[:, :], lhsT=alrT[:, cs], rhs=w2b[:, l * 512:(l + 1) * 512], start=True, stop=False),
                  reads=["alrT", "w2b"], writes=[key])
                A("pe", lambda e, bank=bank: e.matmul(bank[:, :], lhsT=sel[:, 2, :], rhs=rows[:, l * 512:(l + 1) * 512], start=False, stop=False),
                  reads=["sel", "rows"], writes=[key])
                A("pe", lambda e, bank=bank: e.matmul(bank[:, :], lhsT=sel[:, 2, :], rhs=rows[:, 1024 + l * 512:1024 + (l + 1) * 512], start=False, stop=True),
                  reads=["sel", "rows"], writes=[key])
                A("act", lambda e, bank=bank: e.activation(out=e1[:], in_=bank[:, :], func=AF.Exp, scale=-1.0), reads=[key], writes=["e1"])
                A("act", lambda e: e.activation(out=e1[:], in_=e1[:], func=AF.Ln, bias=1.0), reads=["e1"], writes=["e1"])
                STEP()
                bankg, keyg = PB.get()
                for j in range(4):
                    A("pe", lambda e, bankg=bankg, j=j: e.matmul(bankg[:, j * 128:(j + 1) * 128], lhsT=e1[:, j * 128:(j + 1) * 128], rhs=tri16[:], start=True, stop=True),
                      reads=["e1", "tri16"], writes=[keyg])
                A("act", lambda e, bankg=bankg: e.activation(out=eg[:], in_=bankg[:, :], func=AF.Exp, bias=QS_LN), reads=[keyg], writes=["eg"])
                A("act", lambda e, bankg=bankg: e.activation(out=emg[:], in_=bankg[:, :], func=AF.Exp, scale=-1.0), reads=[keyg], writes=["emg"])
                A("act", lambda e, bankg=bankg: e.activation(out=egl[:, :], in_=bankg[:, :].rearrange("p (j t) -> p j t", j=4)[:, :, 127], func=AF.Exp), reads=[keyg], writes=["egl"])
                NEED("q")
                NEED("k")
                A("dve", lambda e, cs=cs: e.tensor_tensor(out=qtl[:], in0=qkT[:, 0:4, cs], in1=eg[:].rearrange("p (j t) -> p j t", j=4), op=ALU.mult),
                  reads=["eg"] + [("qkT", j) for j in range(4)], writes=["qtl"])
                A("dve", lambda e, cs=cs: e.tensor_tensor(out=ktl[:], in0=qkT[:, 4:8, cs], in1=emg[:].rearrange("p (j t) -> p j t", j=4), op=ALU.mult),
                  reads=["emg"] + [("qkT", 4 + j) for j in range(4)], writes=["ktl"])
                STEP()
                bank, key = PB.get()
                for hd in range(4):
                    A("pe", lambda e, bank=bank, hd=hd: e.matmul(bank[:, hd * 128:(hd + 1) * 128], lhsT=ktl[:, hd, :], rhs=qtl[:, hd, :], start=True, stop=True),
                      reads=["ktl", "qtl"], writes=[key])
                A("dve", lambda e, bank=bank: e.tensor_tensor(out=attm[:], in0=bank[:, :].rearrange("p (g t) -> p g t", g=4), in1=bc(maskb[:, :].unsqueeze(1), [128, 4, 128]), op=ALU.mult),
                  reads=[key, "maskb"], writes=["attm"])
                STEP()
                bankt, keyt = PT.get()
                for hd in range(4):
                    A("pe", lambda e, bankt=bankt, hd=hd: e.transpose(out=bankt[:, hd * 128:(hd + 1) * 128], in_=ktl[:, hd, :], identity=identb[:]),
                      reads=["ktl", "identb"], writes=[keyt])
                A("act", lambda e, bankt=bankt: e.activation(out=ktok[:], in_=bankt[:, 0:512], func=AF.Copy), reads=[keyt], writes=["ktok"])
                STEP()
                obk = []
                NEED(("v", c))
                for hb in range(2):
                    bank, key = PB.get()
                    obk.append((bank, key))
                    for hh in range(2):
                        hd = hb * 2 + hh
                        o = bank[:, hh * 256:(hh + 1) * 256]
                        A("pe", lambda e, o=o, hd=hd, c=c: e.matmul(o, lhsT=attm[:, hd, :], rhs=vtok[:, c, hd * 256:(hd + 1) * 256], start=True, stop=False),
                          reads=["attm", ("vtok", c)], writes=[key])
                        A("pe", lambda e, o=o, hd=hd: e.matmul(o, lhsT=qtl[:, hd, :], rhs=Gbf[l][:, hd * 256:(hd + 1) * 256], start=False, stop=True),
                          reads=["qtl", ("Gbf", l)], writes=[key])
                        A("act", lambda e, o=o, hd=hd: e.activation(out=on[:, hd * 256:(hd + 1) * 256], in_=o, func=AF.Square, accum_out=sq4g[:, hd:hd + 1]), reads=[key], writes=[("sq4g", hd), ("on", hd // 2)])
                    STEP()
                A("dve", lambda e: e.tensor_scalar(out=r4g[:], in0=sq4g[:], scalar1=1.0 / 256, scalar2=EPS, op0=ALU.mult, op1=ALU.add), reads=[("sq4g", i) for i in range(4)], writes=["r4g"])
                A("pool", lambda e: e.tensor_tensor(out=r4g[:], in0=r4g[:], in1=bc(mhalf[:, 0:1], [128, 4]), op=ALU.pow), reads=["r4g", "mhalf"], writes=["r4g"])
                for hb in range(2):
                    bank, key = obk[hb]
                    A("dve", lambda e, bank=bank, hb=hb: e.tensor_tensor(out=on[:, hb * 512:(hb + 1) * 512].rearrange("p (a d) -> p a d", a=2), in0=bank[:, :].rearrange("p (a d) -> p a d", a=2),
                                                                          in1=bc(r4g[:, hb * 2:(hb + 1) * 2].unsqueeze(2), [128, 2, 256]), op=ALU.mult),
                      reads=[key, "r4g"], writes=[("on", hb)])
                STEP()
                NEED("g")
                bank, key = PT.get()
                for k in range(8):
                    A("pe", lambda e, k=k, bank=bank: e.transpose(out=bank[:, k * 128:(k + 1) * 128], in_=on[:, k * 128:(k + 1) * 128], identity=identb[:]),
                      reads=[("on", 0), ("on", 1), "identb"], writes=[key])
                A("dve", lambda e, bank=bank, cs=cs, c=c: e.tensor_tensor(out=xbcT[:, 8:16, 4 + c * 128:4 + (c + 1) * 128], in0=bank[:, :].rearrange("p (k t) -> p k t", k=8),
                                                                         in1=sgT[:, :, cs], op=ALU.mult),
                  reads=[key] + [("sgT", b) for b in range(8)], writes=[("xbcT", 8 + b) for b in range(8)])
                STEP()
                for hb in range(2):
                    bank, key = PB.get()
                    for hh in range(2):
                        hd = hb * 2 + hh
                        A("pe", lambda e, bank=bank, hh=hh, hd=hd, c=c: e.matmul(bank[:, hh * 256:(hh + 1) * 256], lhsT=ktok[:, hd * 128:(hd + 1) * 128], rhs=vtok[:, c, hd * 256:(hd + 1) * 256], start=True, stop=True),
                          reads=["ktok", ("vtok", c)], writes=[key])
                    hs = slice(hb * 512, (hb + 1) * 512)
                    A("dve", lambda e, bank=bank, hs=hs: e.tensor_tensor(out=Sgla[l][:, hs], in0=Sgla[l][:, hs], in1=bank[:, :], op=ALU.add),
                      reads=[key, ("Sgla", l, hb)], writes=[("Sgla", l, hb)])
                for hd in range(4):
                    A("act", lambda e, hd=hd: e.activation(out=Gbf[l][:, hd * 256:(hd + 1) * 256], in_=Sgla[l][:, hd * 256:(hd + 1) * 256], func=AF.Copy, scale=egl[:, hd:hd + 1]),
                      reads=[("Sgla", l, hd // 2), "egl"], writes=[("Gbf", l)])
                for hd in range(4):
                    A("pool", lambda e, hd=hd: e.tensor_scalar(out=Sgla[l][:, hd * 256:(hd + 1) * 256], in0=Sgla[l][:, hd * 256:(hd + 1) * 256], scalar1=egl[:, hd:hd + 1], scalar2=0.0, op0=ALU.mult, op1=ALU.add),
                      reads=[("Sgla", l, hd // 2), "egl"], writes=[("Sgla", l, hd // 2)])
                STEP()
            REC[0] = None
            PB.ring = PBfull
            if OPT["s4"]:
                merge([sH, sI, sF], given)
                merge_branch(0)
            else:
                merge([sI, sJ], given)
            MARK("P4")
            merge_branch(1)
            MARK("OUT")
            if OPT["outc"]:
                wos = [next_w((l, "out", 0), pf=1), next_w((l, "out", 1), pf=0)]
                oorder = [(c, j2) for c in range(NCH) for j2 in range(2)]
            else:
                wos = [None, None]
                oorder = [(c, j2) for j2 in range(2) for c in range(NCH)]
            for c, j2 in oorder:
                if True:
                    if wos[j2] is None:
                        wos[j2] = next_w((l, "out", j2))
                    wo, wokey = wos[j2]
                    bank, key = PB.get()
                    for k in range(8):
                        A("pe", lambda e, k=k, c=c, wo=wo, bank=bank: e.matmul(bank[:, :], lhsT=qkT[:, k, c * 128:(c + 1) * 128], rhs=wo[:, k, :], start=(k == 0), stop=(k == 7)),
                          reads=[("qkT", k) for k in range(8)] + [wokey], writes=[key])
                    A("dve", lambda e, bank=bank, c=c, j2=j2: e.scalar_tensor_tensor(out=h[:, c, j2 * 512:(j2 + 1) * 512], in0=bank[:, :], scalar=0.5, in1=h[:, c, j2 * 512:(j2 + 1) * 512], op0=ALU.mult, op1=ALU.add),
                      reads=[key, ("h", c)], writes=[("h", c)])
            if first and not last_layer:
                A("pool", lambda e: e.memset(h[0:112, 0, :], 0.0), writes=[("h", 0)])

        for ti in range(ntiles):
            for c in range(NCH):
                gc = ti * NCH + c
                if gc == 0:
                    A("pool", lambda e: e.memset(h[:, 0, :], 0.0), writes=[("h", 0)])
                    A("sp", lambda e: e.dma_start(out=h[112:128, 0, :], in_=meta_d), writes=[("h", 0)], dma_sem=xsems[0])
                else:
                    r0 = (gc - 1) * 128
                    A("sp", lambda e, c=c, r0=r0: e.dma_start(out=h[:, c, :], in_=x_d[r0:r0 + 128, :]), writes=[("h", c)], dma_sem=xsems[c])
            for l in range(NL):
                tile_layer(ti, l)
            for c in range(NCH):
                gc = ti * NCH + c
                if gc == 0:
                    continue
                r0 = (gc - 1) * 128
                ub, ukey = UB.get()
                A("act", lambda e, c=c, ub=ub: e.activation(out=ub[:], in_=h[:, c, :], func=AF.Square, accum_out=ssq[:, c:c + 1]), reads=[("h", c)], writes=[("ssq", c), ukey])
                A("dve", lambda e, c=c: e.tensor_scalar(out=rs[:, c:c + 1], in0=ssq[:, c:c + 1], scalar1=1.0 / D, scalar2=EPS, op0=ALU.mult, op1=ALU.add),
                  reads=[("ssq", c)], writes=[("rs", c)])
                A("pool", lambda e, c=c: e.tensor_tensor(out=rs[:, c:c + 1], in0=rs[:, c:c + 1], in1=mhalf[:], op=ALU.pow), reads=[("rs", c), "mhalf"], writes=[("rs", c)])
                A("dve", lambda e, c=c: e.scalar_tensor_tensor(out=h[:, c, :], in0=h[:, c, :], scalar=rs[:, c:c + 1], in1=fnw_bc[:], op0=ALU.mult, op1=ALU.mult),
                  reads=[("h", c), ("rs", c), "fnw"], writes=[("h", c)])
                ok = ("out", gc)
                out_keys.append(ok)
                A("sp", lambda e, c=c, r0=r0: e.dma_start(out=out_d[r0:r0 + 128, :], in_=h[:, c, :]), reads=[("h", c)], writes=[ok], dma_sem=osems[c])
        A("sp", None, reads=out_keys)
        assert wstate["used"] == len(wlist)
        build.model_us = dict(SIM["free"])
        build.model_log = SIM["log"]
        build.mlog = SIM["mlog"]
        with nc.Block() as block:
            S.emit(block)
    return nc


def host_consts():
    i = np.arange(128)
    tri = (i[:, None] <= i[None, :]).astype(np.float32)
    strict = (i[:, None] > i[None, :]).astype(np.float32)
    mask0 = (i >= 112).astype(np.float32)[:, None]
    sel = np.zeros((128, 3, 128), np.float32)
    for r in range(3):
        sel[r, r, :] = 1.0
    return {"ident": np.eye(128, dtype=np.float32), "tri": tri, "strict": strict, "mask0": mask0, "sel": sel.reshape(128, 384)}


def make_in_maps(inputs, ncores, TR):
    f = lambda a: np.ascontiguousarray(np.asarray(a, dtype=np.float32))
    common = dict(host_consts())
    common["meta"] = f(inputs["meta_tokens"])
    common["w_in"] = f(inputs["w_in"])
    common["w_br"] = f(inputs["w_branch"])
    common["w_out"] = f(inputs["w_out"])
    common["normw"] = f(np.asarray(inputs["norm_w"]).reshape(2, 8, 128).transpose(2, 0, 1).reshape(128, 16))
    common["cw"] = f(np.asarray(inputs["conv_w"]).reshape(2, 4, 16, 128).transpose(3, 0, 2, 1).reshape(128, 128))
    common["cb"] = f(np.asarray(inputs["conv_b"]).reshape(1, 4096))
    common["dtb"] = f(inputs["dt_bias"])
    common["alog"] = f(inputs["a_log"])
    common["dskip"] = f(inputs["d_skip"])
    common["ssdnw"] = f(np.asarray(inputs["ssd_norm_w"]).reshape(2, 8, 128).transpose(2, 0, 1).reshape(128, 16))
    common["w2"] = f(np.asarray(inputs["gla_gate_w2"]).transpose(1, 0, 2).reshape(16, 1024))
    common["gb"] = f(np.asarray(inputs["gla_gate_b"]).reshape(1, 1024))
    common["gnw"] = f(np.asarray(inputs["gla_norm_w"]).reshape(2, 2, 128).transpose(2, 0, 1).reshape(128, 4))
    common["fnw"] = f(np.asarray(inputs["final_norm_w"]).reshape(1, 1024))
    x = np.asarray(inputs["x"], dtype=np.float32)
    maps = []
    for b in range(ncores):
        m = dict(common)
        m["x"] = np.ascontiguousarray(x[b, :TR])
        maps.append(m)
    return maps


_NC_CACHE = {}


def kernel(**inputs):
    x = np.asarray(inputs["x"])
    B, T, _ = x.shape
    NT = T // 128 + 1
    key = (NT,)
    if key not in _NC_CACHE:
        _NC_CACHE[key] = build(NT=NT, NCH=3, NL=2)
    nc = _NC_CACHE[key]
    maps = make_in_maps(inputs, B, T)
    res = run_bass_kernel_spmd(nc, maps, core_ids=list(range(B)))
    out = np.stack([np.asarray(r["out"], dtype=np.float32) for r in res.results], axis=0)
    return out
```

```python
import math
import numpy as np
import concourse.bass as bass
import concourse.mybir as mybir
from concourse.bass_utils import run_bass_kernel_spmd
from contextlib import ExitStack

F32 = mybir.dt.float32
BF16 = mybir.dt.bfloat16
AF = mybir.ActivationFunctionType
ALU = mybir.AluOpType

D = 1024
NIN = 8224
EPS = 1e-6
C_Z, C_XBC, C_DT, C_Q, C_K, C_V, C_G, C_ALR, C_M = 0, 1024, 3072, 3088, 3600, 4112, 5136, 6160, 6176


class Op:
    __slots__ = ("eng", "fn", "deps", "sig", "count", "sem", "is_dma", "inc", "fin")

    def __init__(self, eng, fn, is_dma=False):
        self.eng = eng
        self.fn = fn
        self.deps = set()
        self.sig = False
        self.count = 0
        self.sem = None
        self.is_dma = is_dma
        self.inc = 1
        self.fin = 0.0


class Sched:
    ENGS = ("pe", "act", "dve", "pool", "sp")

    def __init__(self, nc, stack):
        self.nc = nc
        self.stack = stack
        self.ops = []
        self.res = {}
        self.engsem = {e: stack.enter_context(nc.semaphore("s_" + e)) for e in self.ENGS}
        self.dma_counts = {}
        self.alias = {}

    def new_sem(self, name):
        return self.stack.enter_context(self.nc.semaphore(name))

    def uniq(self, key):
        lst = self.alias.setdefault(key, [])
        k = ("__u", key, len(lst))
        lst.append(k)
        return k

    def add(self, eng, fn, reads=(), writes=(), dma_sem=None, raw_keys=False):
        op = Op(eng, fn, is_dma=dma_sem is not None)
        if not raw_keys:
            reads = [rr for r in reads for rr in self.alias.get(r, [r])]
            writes = [ww for w in writes for ww in self.alias.get(w, [w])]
        if dma_sem is not None:
            op.sem = dma_sem
            op.sig = True
            op.inc = 16
            c = self.dma_counts.get(id(dma_sem), 0) + 16
            self.dma_counts[id(dma_sem)] = c
            op.count = c
        for r in reads:
            st = self.res.get(r)
            if st is None:
                st = [None, []]
                self.res[r] = st
            if st[0] is not None:
                op.deps.add(st[0])
            st[1].append(op)
        for w in writes:
            st = self.res.get(w)
            if st is None:
                st = [None, []]
                self.res[w] = st
            if st[0] is not None:
                op.deps.add(st[0])
            last = {}
            for rd in st[1]:
                if rd is op:
                    continue
                if rd.is_dma:
                    op.deps.add(rd)
                else:
                    last[rd.eng] = rd
            for rd in last.values():
                op.deps.add(rd)
            st[0] = op
            st[1] = []
        op.deps.discard(op)
        self.ops.append(op)
        return op

    def peek(self, reads, writes):
        reads = [rr for r in reads for rr in self.alias.get(r, [r])]
        writes = [ww for w in writes for ww in self.alias.get(w, [w])]
        deps = []
        for r in reads:
            st = self.res.get(r)
            if st is not None and st[0] is not None:
                deps.append(st[0])
        for w in writes:
            st = self.res.get(w)
            if st is None:
                continue
            if st[0] is not None:
                deps.append(st[0])
            last = {}
            for rd in st[1]:
                if rd.is_dma:
                    deps.append(rd)
                else:
                    last[rd.eng] = rd
            deps.extend(last.values())
        return deps

    def finalize(self):
        for op in self.ops:
            if op.eng == "pe" and not op.is_dma:
                op.deps = {d for d in op.deps if not (d.eng == "pe" and not d.is_dma)}
            for d in op.deps:
                d.sig = True
        cnt = {e: 0 for e in self.ENGS}
        for op in self.ops:
            if op.is_dma:
                continue
            if op.sig:
                cnt[op.eng] += 1
                op.count = cnt[op.eng]
                op.sem = self.engsem[op.eng]

    def emit(self, block):
        self.finalize()
        by_eng = {e: [o for o in self.ops if o.eng == e] for e in self.ENGS}

        def run(engine, ops):
            waited = {}
            for op in ops:
                need = {}
                for d in op.deps:
                    k = id(d.sem)
                    if need.get(k, (None, 0))[1] < d.count:
                        need[k] = (d.sem, d.count)
                for k, (sem, c) in need.items():
                    if waited.get(k, 0) < c:
                        engine.wait_ge(sem, c)
                        waited[k] = c
                if op.fn is None:
                    continue
                ins = op.fn(engine)
                if op.sig:
                    ins.then_inc(op.sem, op.inc)

        @block.tensor
        def _(e):
            run(e, by_eng["pe"])

        @block.scalar
        def _(e):
            run(e, by_eng["act"])

        @block.vector
        def _(e):
            run(e, by_eng["dve"])

        @block.gpsimd
        def _(e):
            run(e, by_eng["pool"])

        @block.sync
        def _(e):
            run(e, by_eng["sp"])


class Ring:
    def __init__(self, items, name, keys=None):
        self.items = items
        self.name = name
        self.keys = keys if keys is not None else [(name, j) for j in range(len(items))]
        self.i = 0

    def get(self):
        j = self.i % len(self.items)
        self.i += 1
        return self.items[j], self.keys[j]


class RingProxy:
    def __init__(self, ring):
        self.ring = ring

    def get(self):
        return self.ring.get()


import os
OPT = {"zs_eng": "pool", "diag_eng": "dve", "ub_eng": "dve", "greedy": 0, "outc": 1, "b2late": 0, "at_split": 1, "hb": 3, "s4": 0, "qkfirst": 0, "stmp_early": 0, "sbf_dve": 0, "hpipe": 1, "gprepB": 1}
for _k in list(OPT):
    if os.environ.get("KOPT_" + _k):
        OPT[_k] = type(OPT[_k])(os.environ["KOPT_" + _k])


def build(NT=33, NCH=3, NL=2, dbg=False):
    assert NT % NCH == 0
    N = NCH * 128
    ntiles = NT // NCH
    TR = (NT - 1) * 128
    nc = bass.Bass("TRN2", target_bir_lowering=False)

    def din(name, shape):
        return nc.dram_tensor(name, list(shape), F32, kind="ExternalInput").ap()

    x_d = din("x", [TR, D])
    meta_d = din("meta", [16, D])
    win_d = din("w_in", [2, D, NIN])
    wbr_d = din("w_br", [2, 2, D, D])
    wout_d = din("w_out", [2, D, D])
    normw_d = din("normw", [128, 16])
    cw_d = din("cw", [128, 128])
    cb_d = din("cb", [1, 2 * 2048])
    dtb_d = din("dtb", [2, 16])
    alog_d = din("alog", [2, 16])
    dskip_d = din("dskip", [2, 16])
    ssdnw_d = din("ssdnw", [128, 16])
    w2_d = din("w2", [16, 2 * 512])
    gb_d = din("gb", [1, 2 * 512])
    gnw_d = din("gnw", [128, 4])
    fnw_d = din("fnw", [1, D])
    ident_d = din("ident", [128, 128])
    tri_d = din("tri", [128, 128])
    strict_d = din("strict", [128, 128])
    mask0_d = din("mask0", [128, 1])
    sel_d = din("sel", [128, 3 * 128])
    out_d = nc.dram_tensor("out", [TR, D], F32, kind="ExternalOutput").ap()

    with ExitStack() as st:
        S = Sched(nc, st)

        def sb(name, shape, dt=F32):
            return st.enter_context(nc.sbuf_tensor("s_" + name, list(shape), dt))

        def ps(name, shape, dt=F32):
            return st.enter_context(nc.psum_tensor(name, list(shape), dt))

        REC = [None]

        class _FakeIns:
            def then_inc(self, *a, **k):
                return self

        class _FakeEng:
            def __init__(self):
                self.calls = []

            def __getattr__(self, name):
                def f(*a, **k):
                    self.calls.append((name, a, k))
                    return _FakeIns()
                return f

        def _fsz(ap):
            n = 1
            for d_ in ap.shape[1:]:
                n *= d_
            return n

        def _in_psum(ap):
            return "PSum" in type(ap.tensor).__name__

        def est_dur(eng, fn, is_dma):
            if fn is None:
                return 0.0
            if is_dma:
                return 3.0
            try:
                fe = _FakeEng()
                fn(fe)
                name, a, k = fe.calls[-1]
                if eng == "pe":
                    if name == "transpose":
                        return 0.13
                    rhs = k["rhs"]
                    n = _fsz(rhs)
                    return max(64, n) / 2400.0 * (4.0 if rhs.dtype == F32 else 1.0) + 0.01
                src = k.get("in_", k.get("in0"))
                n = _fsz(src) if src is not None else 256
                if eng == "act":
                    return 0.22 + n * 0.0009 + (0.09 if k.get("accum_out") is not None else 0.0)
                if eng == "dve":
                    t = 0.07 + n * 0.00105
                    if name == "tensor_tensor" and not _in_psum(k["in0"]) and not _in_psum(k["in1"]):
                        t = 0.07 + n * 0.0021
                    return t
                if eng == "pool":
                    if name == "tensor_tensor" and k.get("op") == ALU.pow:
                        return 0.8
                    if name == "tensor_scalar":
                        return 0.1 + n * 0.0015
                    return 0.3 + n * 0.0016
            except Exception:
                pass
            return 0.5

        SIM = {"free": {e: 0.0 for e in Sched.ENGS}, "pebusy": 0.0, "log": [], "dbg": dbg, "mlog": []}

        def MARK(name):
            SIM["log"].append((name, max(SIM["free"].values()), SIM["free"]["pe"], SIM["pebusy"]))

        def _lat(d, eng):
            return 0.35 if d.eng != eng or d.is_dma else 0.12

        def commit(eng, fn, reads, writes, dma_sem, kw, dur):
            op = S.add(eng, fn, reads, writes, dma_sem=dma_sem, **kw)
            t = SIM["free"][eng]
            for d in op.deps:
                if eng == "pe" and d.eng == "pe" and not d.is_dma:
                    continue
                t = max(t, d.fin + _lat(d, eng))
            if dma_sem is not None:
                SIM["free"][eng] = t + 0.1
                op.fin = t + dur
            else:
                op.fin = t + dur
                SIM["free"][eng] = op.fin
                if eng == "pe":
                    SIM["pebusy"] += dur
            return op

        def A(eng, fn, reads=(), writes=(), dma_sem=None, **kw):
            dur = est_dur(eng, fn, dma_sem is not None)
            if REC[0] is None:
                return commit(eng, fn, reads, writes, dma_sem, kw, dur)
            REC[0].append(("op", (eng, fn, list(reads), list(writes), dma_sem, kw, dur)))
            return None

        def STEP():
            if REC[0] is not None:
                REC[0].append(("step",))

        def NEED(tok):
            if REC[0] is not None:
                REC[0].append(("step",))
                REC[0].append(("need", tok))

        def GIVE(tok):
            if REC[0] is not None:
                REC[0].append(("give", tok))

        def step_stall(step):
            free = dict(SIM["free"])
            lw, lr = {}, {}
            stall = 0.0
            for it in step:
                if it[0] != "op":
                    continue
                eng, fn, r, w, ds, kw, dur = it[1]
                t = free[eng]
                t0 = t
                for d in S.peek(r, w):
                    if eng == "pe" and d.eng == "pe" and not d.is_dma:
                        continue
                    t = max(t, d.fin + _lat(d, eng))
                for key in r:
                    if key in lw:
                        t = max(t, lw[key] + 0.3)
                for key in w:
                    if key in lw:
                        t = max(t, lw[key] + 0.3)
                    if key in lr:
                        t = max(t, lr[key] + 0.3)
                stall += (t - t0) * (1.0 if eng == "pe" else 0.35)
                fin = t + dur
                if ds is not None:
                    free[eng] = t + 0.1
                else:
                    free[eng] = fin
                for key in r:
                    lr[key] = max(lr.get(key, 0.0), fin)
                for key in w:
                    lw[key] = fin
                    lr.pop(key, None)
            return stall

        def merge(streams, given):
            segs = []
            for stq in streams:
                steps = [[]]
                for it in stq:
                    if it[0] == "step":
                        if steps[-1]:
                            steps.append([])
                    else:
                        steps[-1].append(it)
                if not steps[-1]:
                    steps.pop()
                segs.append(steps)
            pos = [0] * len(segs)
            tot = [max(len(x), 1) for x in segs]
            while any(pos[i] < len(segs[i]) for i in range(len(segs))):
                cands = []
                for i in range(len(segs)):
                    if pos[i] >= len(segs[i]):
                        continue
                    step = segs[i][pos[i]]
                    if any(it[0] == "need" and it[1] not in given for it in step):
                        continue
                    cands.append((step_stall(step), i))
                assert cands, "merge deadlock"
                pick = None
                if OPT["greedy"]:
                    for st_, i in cands:
                        if st_ < 0.25:
                            pick = i
                            break
                    if pick is None:
                        pick = min(cands)[1]
                else:
                    pick = min(cands, key=lambda ci: pos[ci[1]] / tot[ci[1]])[1]
                for it in segs[pick][pos[pick]]:
                    if it[0] == "op":
                        eng, fn, r, w, ds, kw, dur = it[1]
                        commit(eng, fn, r, w, ds, kw, dur)
                    elif it[0] == "give":
                        given.add(it[1])
                if SIM.get("dbg"):
                    SIM["mlog"].append((pick, pos[pick], [round(c_[0], 2) for c_ in cands], dict(SIM["free"])))
                pos[pick] += 1
        h = sb("h", [128, NCH, D])
        ubs = [sb("ub%d" % i, [128, D], BF16) for i in range(2)]
        UB = Ring(ubs, "ub")
        uT = sb("uT", [128, 8, N], BF16)
        NWB = 3
        wbufs = [sb("wb%d" % i, [128, 8, 512], BF16) for i in range(NWB)]
        wsems = [S.new_sem("wsem%d" % i) for i in range(NWB)]
        wmisc = sb("wmisc", [128, 2, 8, 32], BF16)
        NDG = 4
        dgs = [sb("dg%d" % i, [128, 4, 128], BF16) for i in range(NDG)]
        DG = Ring(dgs, "dg")
        XW = N + 4
        xbcT = sb("xbcT", [128, 16, XW], BF16)
        hist = sb("hist", [128, 2, 16, 4], BF16)
        x2 = sb("x2", [128, NCH, D], BF16)
        B2tok = sb("B2tok", [128, NCH, 512], BF16)
        BCT = sb("BCT", [128, 8, N], BF16)
        zs = sb("zs", [128, NCH, D], BF16)
        qkT = sb("qkT", [128, 8, N], BF16)
        vtok = sb("vtok", [128, NCH, D], BF16)
        sgT = sb("sgT", [128, 8, N], BF16)
        alrT = sb("alrT", [128, N], BF16)
        ths = [sb("th%d" % i, [128, 512], BF16) for i in range(2)]
        TH = Ring(ths, "th")
        gws = [sb("gw%d" % i, [128, 512], BF16) for i in range(2)]
        GW = Ring(gws, "gw")
        ssq = sb("ssq", [128, NCH])
        rs = sb("rs", [128, NCH])
        dtr = sb("dtr", [128, NCH, 16])
        dte_x = sb("dte_x", [128, NCH, 16])
        dt_t = sb("dt_t", [128, NCH, 16])
        da_t = sb("da_t", [128, NCH, 16])
        ecumh = sb("ecumh", [128, NCH, 16])
        dte = sb("dte", [128, NCH, 16])
        cdb = sb("cdb", [128, NCH, 16])
        wq = sb("wq", [128, NCH, 16])
        dth = sb("dth", [128, NCH, 16])
        sq4 = sb("sq4", [128, 4])
        r4 = sb("r4", [128, 4])
        sq4g = sb("sq4g", [128, 4])
        r4g = sb("r4g", [128, 4])
        egl = sb("egl", [128, 4])
        At = sb("At", [128, 8, 128])
        Lm = sb("Lm", [128, 16, 128], BF16)
        MT = sb("MT", [128, 16, 128], BF16)
        scm = sb("scm", [128, 4, 128], BF16)
        xdt = sb("xdt", [128, D], BF16)
        xw = sb("xw", [128, D], BF16)
        xd = sb("xd", [128, D], BF16)
        t1 = sb("t1", [128, D])
        yn = sb("yn", [128, D], BF16)
        stmp = sb("stmp", [128, D])
        e1 = sb("e1", [128, 512])
        eg = sb("eg", [128, 512])
        emg = sb("emg", [128, 512])
        on = sb("on", [128, D], BF16)
        qtl = sb("qtl", [128, 4, 128], BF16)
        ktl = sb("ktl", [128, 4, 128], BF16)
        ktok = sb("ktok", [128, 512], BF16)
        attm = sb("attm", [128, 4, 128], BF16)
        gth0 = sb("gth0", [128, 4, N], BF16)
        gth = [gth0, gth0]
        t0m = sb("t0m", [128, 8, N], BF16)
        t1ms = [sb("t1m%d" % i, [128, N]) for i in range(2)]
        T1M = Ring(t1ms, "t1m")
        Sssd = [sb("Sssd%d" % l, [128, D]) for l in range(2)]
        Sbf = [sb("Sbf%d" % l, [128, D], BF16) for l in range(2)]
        Sgla = [sb("Sgla%d" % l, [128, D]) for l in range(2)]
        Gbf = [sb("Gbf%d" % l, [128, D], BF16) for l in range(2)]
        identf = sb("identf", [128, 128])
        identb = sb("identb", [128, 128], BF16)
        tri = sb("tri", [128, 128])
        tri16 = sb("tri16", [128, 128])
        strict = sb("strict", [128, 128])
        maskb = sb("maskb", [128, 128], BF16)
        maskq = sb("maskq", [128, 128], BF16)
        onesf = sb("onesf", [128, 128])
        sel = sb("sel", [128, 3, 128], BF16)
        selN = sb("selN", [128, 2, N], BF16)
        l1b = sb("l1b", [128, NCH, 512], BF16)
        tri16b = sb("tri16b", [128, 128], BF16)
        rows = sb("rows", [128, 2048], BF16)
        gb8 = sb("gb8", [8, 128])
        gb8h = sb("gb8h", [8, 128], BF16)
        gb8hf = sb("gb8hf", [8, 128])
        gb8l = sb("gb8l", [8, 128], BF16)
        mhalf = sb("mhalf", [128, 1])
        mask0 = sb("mask0", [128, 1])
        normw = sb("normw", [128, 16])
        cw = sb("cw", [128, 128])
        dtb_bc = sb("dtb_bc", [128, 2, 16])
        a_bc = sb("a_bc", [128, 2, 16])
        dskh_bc = sb("dskh_bc", [128, 2, 16])
        ssdnw = sb("ssdnw", [128, 16])
        w2b = sb("w2b", [128, 1024], BF16)
        gnwh = sb("gnwh", [128, 4])
        fnw_bc = sb("fnw_bc", [128, D])
        pbs = [ps("pb%d" % i, [128, 512]) for i in range(6)]
        PBfull = Ring(pbs, "pb")
        PBh = Ring(pbs[0:3], "pb", keys=[("pb", j) for j in range(0, 3)])
        if OPT["s4"]:
            PBi = Ring(pbs[3:5], "pb", keys=[("pb", j) for j in range(3, 5)])
            PBf = Ring(pbs[5:6], "pb", keys=[("pb", 5)])
        elif OPT["hpipe"]:
            PBh = Ring(pbs[0:2], "pb", keys=[("pb", j) for j in range(0, 2)])
            YD = Ring(pbs[2:4], "pb", keys=[("pb", j) for j in range(2, 4)])
            PBi = PBh
            PBf = Ring(pbs[4:6], "pb", keys=[("pb", j) for j in range(4, 6)])
        else:
            PBi = PBh
            PBf = Ring(pbs[3:6], "pb", keys=[("pb", j) for j in range(3, 6)])
        PB = RingProxy(PBfull)
        pts = [ps("pt%d" % i, [128, 1024], BF16) for i in range(2)]
        PT = Ring(pts, "pt")

        csem = S.new_sem("csem")

        csemp = S.new_sem("csemp")
        cops = []
        copsp = []

        def cload(dst, src, key, eng="sp"):
            if eng == "sp":
                cops.append(A(eng, lambda e: e.dma_start(out=dst, in_=src), writes=[S.uniq(key)], dma_sem=csem, raw_keys=True))
            else:
                copsp.append(A(eng, lambda e: e.dma_start(out=dst, in_=src), writes=[S.uniq(key)], dma_sem=csemp, raw_keys=True))

        cload(identf[:], ident_d, "identf")
        cload(tri[:], tri_d, "tri")
        cload(strict[:], strict_d, "strict")
        cload(mask0[:], mask0_d, "mask0")
        cload(normw[:], normw_d, "normw")
        cload(cw[:], cw_d, "cw")
        cload(ssdnw[:], ssdnw_d, "ssdnw")
        cload(gnwh[:], gnw_d, "gnwh")
        cload(gb8[:], gb_d.rearrange("o (a b) -> (o a) b", a=8), "gb8")
        cload(fnw_bc[:], fnw_d[0:1, :].partition_broadcast(128), "fnw")
        for l in range(2):
            cload(dtb_bc[:, l, :], dtb_d[l:l + 1, :].partition_broadcast(128), "dtb")
            cload(a_bc[:, l, :], alog_d[l:l + 1, :].partition_broadcast(128), "a_bc")
            cload(dskh_bc[:, l, :], dskip_d[l:l + 1, :].partition_broadcast(128), "dskh")
        A("pool", lambda e: e.memset(rows[:], 0.0), writes=["rows0"])
        A("pool", lambda e: e.memset(alrT[:], 0.0), writes=["alrT"])
        A("pool", lambda e: e.memset(w2b[:], 0.0), writes=["w2b0"])
        copsp.append(A("pool", lambda e: e.dma_start(out=rows[0:1, :], in_=cb_d[:, 0:2048]), reads=["rows0"], writes=[S.uniq("rows")], dma_sem=csemp, raw_keys=True))
        copsp.append(A("pool", lambda e: e.dma_start(out=rows[1:2, :], in_=cb_d[:, 2048:4096]), reads=["rows0"], writes=[S.uniq("rows")], dma_sem=csemp, raw_keys=True))
        cload(sel[:].rearrange("p r m -> p (r m)"), sel_d, "sel", eng="pool")
        copsp.append(A("pool", lambda e: e.dma_start(out=w2b[0:16, :], in_=w2_d), reads=["w2b0"], writes=[S.uniq("w2b")], dma_sem=csemp, raw_keys=True))
        for l in range(2):
            cload(wmisc[:, l, :, 0:16], win_d[l, :, C_DT:C_DT + 16].rearrange("(k p) n -> p k n", p=128), "wmisc", eng="pool")
            cload(wmisc[:, l, :, 16:32], win_d[l, :, C_ALR:C_ALR + 16].rearrange("(k p) n -> p k n", p=128), "wmisc", eng="pool")

        for o in cops:
            o.count = 16 * len(cops)
        for o in copsp:
            o.count = 16 * len(copsp)
        A("dve", lambda e: e.tensor_copy(out=identb[:], in_=identf[:]), reads=["identf"], writes=["identb"])
        A("dve", lambda e: e.tensor_copy(out=maskb[:], in_=tri[:]), reads=["tri"], writes=["maskb"])
        A("dve", lambda e: e.tensor_scalar(out=maskq[:], in0=tri[:], scalar1=0.25, scalar2=None, op0=ALU.mult), reads=["tri"], writes=["maskq"])
        A("dve", lambda e: e.tensor_scalar(out=tri16[:], in0=tri[:], scalar1=-1.0 / 16.0, scalar2=None, op0=ALU.mult), reads=["tri"], writes=["tri16"])
        A("dve", lambda e: e.tensor_scalar(out=tri16b[:], in0=tri[:], scalar1=-1.0 / 16.0, scalar2=None, op0=ALU.mult), reads=["tri"], writes=["tri16b"])
        A("pool", lambda e: e.memset(onesf[:], 1.0), writes=["onesf"])
        A("pool", lambda e: e.memset(mhalf[:], -0.5), writes=["mhalf"])
        for r in range(2):
            A("pool", lambda e, r=r: e.tensor_copy(out=selN[:, r, :], in_=bc(sel[:, r, 0:1], [128, N])), reads=["sel"], writes=["selN"])
        A("pool", lambda e: e.memset(hist[:], 0.0), writes=["hist"])
        A("pool", lambda e: e.memset(xbcT[:], 0.0), writes=[("xbcT", b) for b in range(16)])
        for l in range(2):
            A("pool", lambda e, l=l: e.memset(Sssd[l][:], 0.0), writes=[("Sssd", l)])
            A("pool", lambda e, l=l: e.memset(Sbf[l][:], 0.0), writes=[("Sbf", l)])
            A("pool", lambda e, l=l: e.memset(Sgla[l][:], 0.0), writes=[("Sgla", l, 0), ("Sgla", l, 1)])
            A("pool", lambda e, l=l: e.memset(Gbf[l][:], 0.0), writes=[("Gbf", l)])
        A("act", lambda e: e.activation(out=a_bc[:], in_=a_bc[:], func=AF.Exp), reads=["a_bc"], writes=["a_bc"])
        A("dve", lambda e: e.tensor_scalar(out=a_bc[:], in0=a_bc[:], scalar1=-1.0, scalar2=None, op0=ALU.mult), reads=["a_bc"], writes=["a_bc"])
        A("dve", lambda e: e.tensor_scalar(out=dskh_bc[:], in0=dskh_bc[:], scalar1=0.5, scalar2=None, op0=ALU.mult), reads=["dskh"], writes=["dskh"])
        A("dve", lambda e: e.tensor_scalar(out=gnwh[:], in0=gnwh[:], scalar1=0.5, scalar2=None, op0=ALU.mult), reads=["gnwh"], writes=["gnwh"])
        A("dve", lambda e: e.tensor_copy(out=gb8h[:], in_=gb8[:]), reads=["gb8"], writes=["gb8h"])
        A("dve", lambda e: e.tensor_copy(out=gb8hf[:], in_=gb8h[:]), reads=["gb8h"], writes=["gb8hf"])
        A("dve", lambda e: e.tensor_tensor(out=gb8l[:], in0=gb8[:], in1=gb8hf[:], op=ALU.subtract), reads=["gb8", "gb8hf"], writes=["gb8l"])
        csem2 = S.new_sem("csem2")
        cops2 = []
        for a8 in range(8):
            cops2.append(A("sp", lambda e, a8=a8: e.dma_start(out=rows[2:3, a8 * 128:(a8 + 1) * 128], in_=gb8h[a8:a8 + 1, :]), reads=["gb8h", "rows0"], writes=[S.uniq("rows")], dma_sem=csem2))
            cops2.append(A("sp", lambda e, a8=a8: e.dma_start(out=rows[2:3, 1024 + a8 * 128:1024 + (a8 + 1) * 128], in_=gb8l[a8:a8 + 1, :]), reads=["gb8l", "rows0"], writes=[S.uniq("rows")], dma_sem=csem2))
        for o in cops2:
            o.count = 16 * len(cops2)

        def wsrc(l, kind, idx):
            if kind == "in":
                src = win_d[l, :, idx:idx + 512]
            elif kind == "br":
                b, j = idx
                src = wbr_d[l, b, :, j * 512:(j + 1) * 512]
            else:
                src = wout_d[l, :, idx * 512:(idx + 1) * 512]
            return src.rearrange("(k p) n -> p k n", p=128)

        def layer_groups(l):
            g = []
            for i in range(4):
                g.append((l, "in", C_XBC + 512 * i))
            if OPT["qkfirst"]:
                g.append((l, "in", C_Q))
                g.append((l, "in", C_K))
            for i in range(2):
                g.append((l, "in", C_Z + 512 * i))
            if not OPT["qkfirst"]:
                g.append((l, "in", C_Q))
                g.append((l, "in", C_K))
            for i in range(2):
                g.append((l, "in", C_V + 512 * i))
            for i in range(2):
                g.append((l, "in", C_G + 512 * i))
            for b in range(2):
                for j in range(2):
                    g.append((l, "in", C_M + b * 1024 + j * 512))
                    g.append((l, "br", (b, j)))
            for j in range(2):
                g.append((l, "out", j))
            return g

        wlist = []
        for ti in range(ntiles):
            for l in range(NL):
                wlist += layer_groups(l)
        wstate = {"issued": 0, "used": 0}
        PF = 2

        GP = len(layer_groups(0)) * NL
        wscr = nc.dram_tensor("wscr", [GP, 128, 8 * 512], BF16, kind="Internal").ap()
        wsemh = [S.new_sem("wsemh%d" % i) for i in range(NWB)]
        wsemst = [S.new_sem("wsemst%d" % i) for i in range(NWB)]

        def w_issue_upto(n):
            while wstate["issued"] < min(n, len(wlist)):
                i = wstate["issued"]
                l, kind, idx = wlist[i]
                slot = i % NWB
                if i < GP:
                    src = wsrc(l, kind, idx)
                    A("pool", lambda e, slot=slot, src=src: e.dma_start(out=wbufs[slot][:], in_=src),
                      writes=[("wb", slot)], dma_sem=wsems[slot])
                    if ntiles > 1:
                        A("sp", lambda e, slot=slot, i=i: e.dma_start(out=wscr[i], in_=wbufs[slot][:].rearrange("p k n -> p (k n)")),
                          reads=[("wb", slot)], writes=[("wscr", i)], dma_sem=wsemst[slot])
                else:
                    g = i % GP
                    A("sp", lambda e, slot=slot, g=g: e.dma_start(out=wbufs[slot][:].rearrange("p k n -> p (k n)"), in_=wscr[g]),
                      reads=[("wscr", g)], writes=[("wb", slot)], dma_sem=wsemh[slot])
                wstate["issued"] += 1

        def next_w(expect, pf=PF):
            i = wstate["used"]
            assert wlist[i] == expect, (wlist[i], expect)
            w_issue_upto(i + 1 + pf)
            wstate["used"] += 1
            return wbufs[i % NWB], ("wb", i % NWB)

        def bc(ap, shape):
            return ap.to_broadcast(list(shape))

        HALF_LN = math.log(0.5)
        QS_LN = math.log(128.0 ** -0.5)

        def silu2_evac(bank_ap, out_ap, key, wkey, ncols):
            th, tkey = TH.get()
            A("act", lambda e: e.activation(out=th[:, 0:ncols], in_=bank_ap, func=AF.Tanh, scale=0.5), reads=[key], writes=[tkey])
            A("dve", lambda e: e.scalar_tensor_tensor(out=out_ap, in0=th[:, 0:ncols], scalar=1.0, in1=bank_ap, op0=ALU.add, op1=ALU.mult),
              reads=[tkey, key], writes=[wkey])

        uT_all = [("uT", c) for c in range(NCH)]
        out_keys = []
        osems = [S.new_sem("osem%d" % i) for i in range(NCH)]
        xsems = [S.new_sem("xsem%d" % i) for i in range(NCH)]

        def tile_layer(ti, l):
            first = ti == 0
            last_layer = l == NL - 1
            MARK("A %d %d" % (ti, l))
            for c in range(NCH):
                ub, ukey = UB.get()
                A("act", lambda e, c=c, ub=ub: e.activation(out=ub[:], in_=h[:, c, :], func=AF.Square, accum_out=ssq[:, c:c + 1]),
                  reads=[("h", c)], writes=[("ssq", c), ukey])
                A("dve", lambda e, c=c: e.tensor_scalar(out=rs[:, c:c + 1], in0=ssq[:, c:c + 1], scalar1=1.0 / D, scalar2=EPS, op0=ALU.mult, op1=ALU.add),
                  reads=[("ssq", c)], writes=[("rs", c)])
                A("pool", lambda e, c=c: e.tensor_tensor(out=rs[:, c:c + 1], in0=rs[:, c:c + 1], in1=mhalf[:], op=ALU.pow),
                  reads=[("rs", c), "mhalf"], writes=[("rs", c)])
                if OPT["ub_eng"] == "act":
                    A("act", lambda e, c=c, ub=ub: e.activation(out=ub[:], in_=h[:, c, :], func=AF.Copy, scale=rs[:, c:c + 1]),
                      reads=[("h", c), ("rs", c)], writes=[ukey])
                else:
                    A("dve", lambda e, c=c, ub=ub: e.tensor_scalar(out=ub[:], in0=h[:, c, :], scalar1=rs[:, c:c + 1], scalar2=None, op0=ALU.mult),
                      reads=[("h", c), ("rs", c)], writes=[ukey])
                bank, key = PT.get()
                for k in range(8):
                    A("pe", lambda e, k=k, ub=ub, bank=bank: e.transpose(out=bank[:, k * 128:(k + 1) * 128], in_=ub[:, k * 128:(k + 1) * 128], identity=identb[:]),
                      reads=[ukey, "identb"], writes=[key])
                A("dve", lambda e, c=c, bank=bank: e.tensor_tensor(out=uT[:, :, c * 128:(c + 1) * 128], in0=bank[:, :].rearrange("p (k t) -> p k t", k=8),
                                                                    in1=bc(normw[:, l * 8:(l + 1) * 8].unsqueeze(2), [128, 8, 128]), op=ALU.mult),
                  reads=[key, "normw"], writes=[("uT", c)])
            MARK("B")
            for c in range(NCH):
                bank, key = PB.get()
                for k in range(8):
                    A("pe", lambda e, k=k, c=c, bank=bank: e.matmul(bank[:, 0:16], lhsT=uT[:, k, c * 128:(c + 1) * 128], rhs=wmisc[:, l, k, 0:16], start=(k == 0), stop=(k == 7)),
                      reads=[("uT", c), "wmisc"], writes=[key])
                A("dve", lambda e, c=c, bank=bank: e.tensor_tensor(out=dtr[:, c, :], in0=bank[:, 0:16], in1=dtb_bc[:, l, :], op=ALU.add),
                  reads=[key, "dtb"], writes=[("dtr", c)])
                A("act", lambda e, c=c: e.activation(out=dte_x[:, c, :], in_=dtr[:, c, :], func=AF.Exp), reads=[("dtr", c)], writes=[("dte_x", c)])
            for c in range(NCH):
                A("act", lambda e, c=c: e.activation(out=dt_t[:, c, :], in_=dte_x[:, c, :], func=AF.Ln, bias=1.0), reads=[("dte_x", c)], writes=[("dt", c)])
            for c in range(NCH):
                if first and c == 0:
                    A("dve", lambda e, c=c: e.tensor_scalar(out=dt_t[:, c, :], in0=dt_t[:, c, :], scalar1=mask0[:, 0:1], scalar2=None, op0=ALU.mult),
                      reads=[("dt", c), "mask0"], writes=[("dt", c)])
                A("dve", lambda e, c=c: e.tensor_tensor(out=da_t[:, c, :], in0=dt_t[:, c, :], in1=a_bc[:, l, :], op=ALU.mult),
                  reads=[("dt", c), "a_bc"], writes=[("da", c)])
            def stage_B2(c):
                bank2, key2 = PB.get()
                A("pe", lambda e, c=c, bank2=bank2: e.matmul(bank2[:, 0:16], lhsT=tri[:], rhs=da_t[:, c, :], start=True, stop=True), reads=[("da", c), "tri"], writes=[key2])
                A("pe", lambda e, c=c, bank2=bank2: e.matmul(bank2[:, 16:32], lhsT=onesf[:], rhs=da_t[:, c, :], start=True, stop=True), reads=[("da", c), "onesf"], writes=[key2])
                A("pe", lambda e, c=c, bank2=bank2: e.matmul(bank2[:, 32:48], lhsT=strict[:], rhs=da_t[:, c, :], start=True, stop=True), reads=[("da", c), "strict"], writes=[key2])
                A("act", lambda e, c=c, bank2=bank2: e.activation(out=ecumh[:, c, :], in_=bank2[:, 0:16], func=AF.Exp, bias=HALF_LN), reads=[key2], writes=[("ecumh", c)])
                A("act", lambda e, c=c, bank2=bank2: e.activation(out=cdb[:, c, :], in_=bank2[:, 16:32], func=AF.Exp), reads=[key2], writes=[("cdb", c)])
                A("act", lambda e, c=c, bank2=bank2: e.activation(out=dte[:, c, :], in_=bank2[:, 32:48], func=AF.Exp), reads=[key2], writes=[("dte", c)])
                A("dve", lambda e, c=c: e.scalar_tensor_tensor(out=wq[:, c, :], in0=dt_t[:, c, :], scalar=0.25, in1=dte[:, c, :], op0=ALU.mult, op1=ALU.mult),
                  reads=[("dt", c), ("dte", c)], writes=[("wq", c)])
                A("dve", lambda e, c=c: e.tensor_scalar(out=dth[:, c, :], in0=dt_t[:, c, :], scalar1=0.5, scalar2=None, op0=ALU.mult),
                  reads=[("dt", c)], writes=[("dth", c)])
            if not OPT["b2late"]:
                for c in range(NCH):
                    stage_B2(c)
            bank, key = PB.get()
            for k in range(8):
                A("pe", lambda e, k=k, bank=bank: e.matmul(bank[0:16, 0:N], lhsT=wmisc[:, l, k, 16:32], rhs=uT[:, k, :], start=(k == 0), stop=(k == 7)),
                  reads=uT_all + ["wmisc"], writes=[key])
            A("act", lambda e, bank=bank: e.activation(out=alrT[0:16, :], in_=bank[0:16, 0:N], func=AF.Copy), reads=[key], writes=["alrT"])
            if OPT["gprepB"]:
                for c in range(NCH):
                    cs = slice(c * 128, (c + 1) * 128)
                    bank, key = PB.get()
                    A("pe", lambda e, bank=bank, cs=cs: e.matmul(bank[:, :], lhsT=alrT[:, cs], rhs=w2b[:, l * 512:(l + 1) * 512], start=True, stop=False),
                      reads=["alrT", "w2b"], writes=[key])
                    A("pe", lambda e, bank=bank: e.matmul(bank[:, :], lhsT=sel[:, 2, :], rhs=rows[:, l * 512:(l + 1) * 512], start=False, stop=False),
                      reads=["sel", "rows"], writes=[key])
                    A("pe", lambda e, bank=bank: e.matmul(bank[:, :], lhsT=sel[:, 2, :], rhs=rows[:, 1024 + l * 512:1024 + (l + 1) * 512], start=False, stop=True),
                      reads=["sel", "rows"], writes=[key])
                    A("act", lambda e, bank=bank: e.activation(out=e1[:], in_=bank[:, :], func=AF.Exp, scale=-1.0), reads=[key], writes=["e1"])
                    A("act", lambda e, c=c: e.activation(out=l1b[:, c, :], in_=e1[:], func=AF.Ln, bias=1.0), reads=["e1"], writes=[("l1b", c)])
            MARK("C")
            for gi in range(4):
                wb, wkey = next_w((l, "in", C_XBC + 512 * gi))
                for j in range(4):
                    blk = gi * 4 + j
                    bank, key = PB.get()
                    for k in range(8):
                        A("pe", lambda e, k=k, j=j, wb=wb, bank=bank: e.matmul(bank[:, 0:N], lhsT=wb[:, k, j * 128:(j + 1) * 128], rhs=uT[:, k, :], start=(k == 0), stop=(k == 7)),
                          reads=uT_all + [wkey], writes=[key])
                    A("act", lambda e, blk=blk, bank=bank: e.activation(out=xbcT[:, blk, 4:4 + N], in_=bank[:, 0:N], func=AF.Copy), reads=[key], writes=[("xbcT", blk)])
            if OPT["b2late"]:
                for c in range(NCH):
                    stage_B2(c)
            allx = [("xbcT", b) for b in range(16)]
            A("pool", lambda e: e.tensor_copy(out=xbcT[:, :, 1:4], in_=hist[:, l, :, 1:4]), reads=[("hist", l)], writes=allx)
            A("pool", lambda e: e.tensor_copy(out=hist[:, l, :, 1:4], in_=xbcT[:, :, N + 1:N + 4]), reads=allx, writes=[("hist", l)])
            MARK("E")
            dgof = {}

            def gen_diag(blk):
                dg, dkey = DG.get()
                for tap in range(4):
                    col = (l * 16 + blk) * 4 + tap
                    A(OPT["diag_eng"], lambda e, dg=dg, tap=tap, col=col: e.tensor_scalar(out=dg[:, tap, :], in0=identb[:], scalar1=cw[:, col:col + 1], scalar2=0.0, op0=ALU.mult, op1=ALU.add),
                      reads=["identb", "cw"], writes=[dkey])
                dgof[blk] = (dg, dkey)

            def conv_tok(c, blks, out_ap, wkey):
                bank, key = PB.get()
                for j, blk in enumerate(blks):
                    dg, dkey = dgof[blk]
                    o = bank[:, j * 128:(j + 1) * 128]
                    A("pe", lambda e, o=o, blk=blk: e.matmul(o, lhsT=sel[:, l, :], rhs=rows[:, blk * 128:(blk + 1) * 128], start=True, stop=False),
                      reads=["sel", "rows"], writes=[key])
                    for tap in range(4):
                        A("pe", lambda e, o=o, blk=blk, tap=tap, dg=dg, c=c: e.matmul(o, lhsT=xbcT[:, blk, c * 128 + 1 + tap:c * 128 + 1 + tap + 128], rhs=dg[:, tap, :], start=False, stop=(tap == 3)),
                          reads=[("xbcT", blk), dkey], writes=[key])
                silu2_evac(bank[:, 0:128 * len(blks)], out_ap, key, wkey, 128 * len(blks))

            for half in range(2):
                blks = [half * 4 + j for j in range(4)]
                for b in blks:
                    gen_diag(b)
                for c in range(NCH):
                    conv_tok(c, blks, x2[:, c, half * 512:(half + 1) * 512], ("x2", c))
            for blk in range(8, 16):
                if blk % 4 == 0:
                    for b4 in range(blk, blk + 4):
                        gen_diag(b4)
                    if blk == 8:
                        for c in range(NCH):
                            conv_tok(c, [8, 9, 10, 11], B2tok[:, c, :], ("B2tok", c))
                dg, dkey = dgof[blk]
                bank, key = PB.get()
                A("pe", lambda e, bank=bank, blk=blk: e.matmul(bank[:, 0:N], lhsT=rows[:, blk * 128:(blk + 1) * 128], rhs=selN[:, l, :], start=True, stop=False),
                  reads=["selN", "rows"], writes=[key])
                for tap in range(4):
                    A("pe", lambda e, bank=bank, blk=blk, tap=tap, dg=dg: e.matmul(bank[:, 0:N], lhsT=dg[:, tap, :], rhs=xbcT[:, blk, 1 + tap:1 + tap + N], start=False, stop=(tap == 3)),
                      reads=[("xbcT", blk), dkey], writes=[key])
                silu2_evac(bank[:, 0:N], BCT[:, blk - 8, :], key, ("BCT", blk - 8), N)
            MARK("P2")
            sF = []
            REC[0] = sF
            PB.ring = PBf
            for part in (("qk", "z") if OPT["qkfirst"] else ("z", "qk")):
              if part == "z":
                for gi in range(2):
                    wb, wkey = next_w((l, "in", C_Z + 512 * gi))
                    for c in range(NCH):
                        bank, key = PB.get()
                        for k in range(8):
                            A("pe", lambda e, k=k, c=c, wb=wb, bank=bank: e.matmul(bank[:, :], lhsT=uT[:, k, c * 128:(c + 1) * 128], rhs=wb[:, k, :], start=(k == 0), stop=(k == 7)),
                              reads=[("uT", c), wkey], writes=[key])
                        silu2_evac(bank[:, :], zs[:, c, gi * 512:(gi + 1) * 512], key, ("zs", c), 512)
                        if gi == 1:
                            GIVE(("zs", c))
                        STEP()
              else:
                for qk in range(2):
                    wb, wkey = next_w((l, "in", C_Q if qk == 0 else C_K))
                    for j in range(4):
                        bank, key = PB.get()
                        for k in range(8):
                            A("pe", lambda e, k=k, j=j, wb=wb, bank=bank: e.matmul(bank[:, 0:N], lhsT=wb[:, k, j * 128:(j + 1) * 128], rhs=uT[:, k, :], start=(k == 0), stop=(k == 7)),
                              reads=uT_all + [wkey], writes=[key])
                        A("act", lambda e, bank=bank, qk=qk, j=j: e.activation(out=qkT[:, qk * 4 + j, :], in_=bank[:, 0:N], func=AF.Copy), reads=[key], writes=[("qkT", qk * 4 + j)])
                        if j == 3:
                            GIVE("q" if qk == 0 else "k")
                        STEP()
            for gi in range(2):
                wb, wkey = next_w((l, "in", C_V + 512 * gi))
                for c in range(NCH):
                    bank, key = PB.get()
                    for k in range(8):
                        A("pe", lambda e, k=k, c=c, wb=wb, bank=bank: e.matmul(bank[:, :], lhsT=uT[:, k, c * 128:(c + 1) * 128], rhs=wb[:, k, :], start=(k == 0), stop=(k == 7)),
                          reads=[("uT", c), wkey], writes=[key])
                    A("act", lambda e, bank=bank, c=c, gi=gi: e.activation(out=vtok[:, c, gi * 512:(gi + 1) * 512], in_=bank[:, :], func=AF.Copy), reads=[key], writes=[("vtok", c)])
                    if gi == 1:
                        GIVE(("v", c))
                    STEP()
            for gi in range(2):
                wb, wkey = next_w((l, "in", C_G + 512 * gi))
                for j in range(4):
                    blk = gi * 4 + j
                    bank, key = PB.get()
                    for k in range(8):
                        A("pe", lambda e, k=k, j=j, wb=wb, bank=bank: e.matmul(bank[:, 0:N], lhsT=wb[:, k, j * 128:(j + 1) * 128], rhs=uT[:, k, :], start=(k == 0), stop=(k == 7)),
                          reads=uT_all + [wkey], writes=[key])
                    gw, gkey = GW.get()
                    th, tkey = TH.get()
                    A("act", lambda e, bank=bank, gw=gw, blk=blk: e.activation(out=gw[:, 0:N], in_=bank[:, 0:N], func=AF.Copy, scale=gnwh[:, l * 2 + (blk % 2):l * 2 + (blk % 2) + 1]),
                      reads=[key, "gnwh"], writes=[gkey])
                    A("act", lambda e, bank=bank, th=th: e.activation(out=th[:, 0:N], in_=bank[:, 0:N], func=AF.Tanh, scale=0.5), reads=[key], writes=[tkey])
                    A("dve", lambda e, gw=gw, th=th, blk=blk: e.scalar_tensor_tensor(out=sgT[:, blk, :], in0=th[:, 0:N], scalar=1.0, in1=gw[:, 0:N], op0=ALU.add, op1=ALU.mult),
                      reads=[gkey, tkey], writes=[("sgT", blk)])
                    if blk == 7:
                        GIVE("g")
                        GIVE("Fdone")
                    STEP()
            sH = []
            REC[0] = sH
            PB.ring = PBh
            YDK = {}

            def H_front(c):
                cs = slice(c * 128, (c + 1) * 128)
                for hv in range(2):
                    for hh in range(8):
                        hd = hv * 8 + hh
                        if OPT["at_split"] and hh % 2 == 1:
                            A("pool", lambda e, hh=hh, hd=hd, c=c: e.tensor_scalar(out=At[:, hh, :], in0=strict[:], scalar1=da_t[:, c, hd:hd + 1], scalar2=0.0, op0=ALU.mult, op1=ALU.add),
                              reads=["strict", ("da", c)], writes=[("At", hh)])
                        else:
                            A("dve", lambda e, hh=hh, hd=hd, c=c: e.tensor_scalar(out=At[:, hh, :], in0=strict[:], scalar1=da_t[:, c, hd:hd + 1], scalar2=None, op0=ALU.mult),
                              reads=["strict", ("da", c)], writes=[("At", hh)])
                    for qd in range(2):
                        bank, key = PB.get()
                        for hq in range(4):
                            hh = qd * 4 + hq
                            A("pe", lambda e, bank=bank, hq=hq, hh=hh: e.matmul(bank[:, hq * 128:(hq + 1) * 128], lhsT=At[:, hh, :], rhs=tri[:], start=True, stop=True),
                              reads=[("At", hh), "tri"], writes=[key])
                        h0 = hv * 8 + qd * 4
                        A("act", lambda e, bank=bank, h0=h0: e.activation(out=Lm[:, h0:h0 + 4, :], in_=bank[:, :].rearrange("p (a t) -> p a t", a=4), func=AF.Exp),
                          reads=[key], writes=[("Lm", h0 // 4)])
                        STEP()
                bank, key = PB.get()
                for g in range(4):
                    A("pe", lambda e, bank=bank, g=g, cs=cs: e.matmul(bank[:, g * 128:(g + 1) * 128], lhsT=BCT[:, g, cs], rhs=BCT[:, 4 + g, cs], start=True, stop=True),
                      reads=[("BCT", g), ("BCT", 4 + g)], writes=[key])
                A("dve", lambda e, bank=bank: e.tensor_tensor(out=scm[:], in0=bank[:, :].rearrange("p (g t) -> p g t", g=4), in1=bc(maskq[:, :].unsqueeze(1), [128, 4, 128]), op=ALU.mult),
                  reads=[key, "maskq"], writes=["scm"])
                A("dve", lambda e: e.tensor_tensor(out=MT[:].rearrange("p (g a) t -> p g a t", g=4), in0=Lm[:].rearrange("p (g a) t -> p g a t", g=4),
                                                   in1=bc(scm[:].unsqueeze(2), [128, 4, 4, 128]), op=ALU.mult),
                  reads=["scm"] + [("Lm", i) for i in range(4)], writes=["MT"])
                STEP()
                x3 = x2[:, c, :].rearrange("p (a d) -> p a d", a=16)
                A("pool", lambda e, x3=x3, c=c: e.tensor_tensor(out=xdt[:].rearrange("p (a d) -> p a d", a=16), in0=x3, in1=bc(dth[:, c, :].unsqueeze(2), [128, 16, 64]), op=ALU.mult),
                  reads=[("x2", c), ("dth", c)], writes=["xdt"])
                if not OPT["hpipe"]:
                    A("pool", lambda e, x3=x3, c=c: e.tensor_tensor(out=xw[:].rearrange("p (a d) -> p a d", a=16), in0=x3, in1=bc(wq[:, c, :].unsqueeze(2), [128, 16, 64]), op=ALU.mult),
                      reads=[("x2", c), ("wq", c)], writes=["xw"])
                A("pool", lambda e, x3=x3: e.tensor_tensor(out=xd[:].rearrange("p (a d) -> p a d", a=16), in0=x3, in1=bc(dskh_bc[:, l, :].unsqueeze(2), [128, 16, 64]), op=ALU.mult),
                  reads=[("x2", c), "dskh"], writes=["xd"])
                if OPT["stmp_early"]:
                    A("pool", lambda e, c=c: e.tensor_tensor(out=stmp[:].rearrange("p (a d) -> p a d", a=16), in0=Sssd[l][:].rearrange("p (a d) -> p a d", a=16),
                                                             in1=bc(cdb[:, c, :].unsqueeze(2), [128, 16, 64]), op=ALU.mult),
                      reads=[("Sssd", l), ("cdb", c)], writes=["stmp", "stmp2"])
                STEP()
                ydk = []
                YDK[c] = ydk
                for hb in range(2):
                    bank, key = YD.get() if OPT["hpipe"] else PB.get()
                    ydk.append((bank, key))
                    A("pe", lambda e, bank=bank, hb=hb: e.matmul(bank[:, :], lhsT=identb[:], rhs=xd[:, hb * 512:(hb + 1) * 512], start=True, stop=False),
                      reads=["identb", "xd"], writes=[key])
                    for h8 in range(8):
                        hd = hb * 8 + h8
                        A("pe", lambda e, bank=bank, h8=h8, hd=hd: e.matmul(bank[:, h8 * 64:(h8 + 1) * 64], lhsT=MT[:, hd, :], rhs=xdt[:, hd * 64:(hd + 1) * 64], start=False, stop=(h8 == 7)),
                          reads=["MT", "xdt"], writes=[key])
                    STEP()

            def H_mid(c):
                cs = slice(c * 128, (c + 1) * 128)
                ydk = YDK[c]
                for hb in range(2):
                    bank, key = PB.get()
                    for gg in range(2):
                        g = hb * 2 + gg
                        A("pe", lambda e, bank=bank, gg=gg, g=g, cs=cs: e.matmul(bank[:, gg * 256:(gg + 1) * 256], lhsT=BCT[:, 4 + g, cs], rhs=Sbf[l][:, g * 256:(g + 1) * 256], start=True, stop=True),
                          reads=[("BCT", 4 + g), ("Sbf", l)], writes=[key])
                    hs = slice(hb * 512, (hb + 1) * 512)
                    A("dve", lambda e, bank=bank, hb=hb, hs=hs, c=c: e.tensor_tensor(out=t1[:, hs].rearrange("p (a d) -> p a d", a=8), in0=bank[:, :].rearrange("p (a d) -> p a d", a=8),
                                                                                  in1=bc(ecumh[:, c, hb * 8:(hb + 1) * 8].unsqueeze(2), [128, 8, 64]), op=ALU.mult),
                      reads=[key, ("ecumh", c)], writes=[("t1", hb)])
                    ybank, ykey = ydk[hb]
                    A("dve", lambda e, ybank=ybank, hs=hs: e.tensor_tensor(out=t1[:, hs], in0=t1[:, hs], in1=ybank[:, :], op=ALU.add),
                      reads=[ykey, ("t1", hb)], writes=[("t1", hb)])
                    NEED(("zs", c))
                    A(OPT["zs_eng"], lambda e, hs=hs, c=c: e.tensor_tensor(out=t1[:, hs], in0=t1[:, hs], in1=zs[:, c, hs], op=ALU.mult),
                      reads=[("t1", hb), ("zs", c)], writes=[("t1", hb)])
                    STEP()
                for g in range(4):
                    A("act", lambda e, g=g: e.activation(out=yn[:, g * 256:(g + 1) * 256], in_=t1[:, g * 256:(g + 1) * 256], func=AF.Square, accum_out=sq4[:, g:g + 1]),
                      reads=[("t1", g // 2)], writes=[("sq4", g), ("yn", g // 2)])
                sq4k = [("sq4", g) for g in range(4)]
                A("dve", lambda e: e.tensor_scalar(out=r4[:], in0=sq4[:], scalar1=1.0 / 256, scalar2=4 * EPS, op0=ALU.mult, op1=ALU.add), reads=sq4k, writes=["r4"])
                A("pool", lambda e: e.tensor_tensor(out=r4[:], in0=r4[:], in1=bc(mhalf[:, 0:1], [128, 4]), op=ALU.pow), reads=["r4", "mhalf"], writes=["r4"])
                for g in range(4):
                    gs = slice(g * 256, (g + 1) * 256)
                    if g % 2 == 0:
                        A("dve", lambda e, g=g, gs=gs: e.tensor_scalar(out=yn[:, gs], in0=t1[:, gs], scalar1=r4[:, g:g + 1], scalar2=None, op0=ALU.mult),
                          reads=["r4", ("t1", g // 2)], writes=[("yn", g // 2)])
                    else:
                        A("act", lambda e, g=g, gs=gs: e.activation(out=yn[:, gs], in_=t1[:, gs], func=AF.Copy, scale=r4[:, g:g + 1]),
                          reads=["r4", ("t1", g // 2)], writes=[("yn", g // 2)])
                STEP()

            def H_tail(c):
                if OPT["hpipe"]:
                    x3 = x2[:, c, :].rearrange("p (a d) -> p a d", a=16)
                    A("pool", lambda e, x3=x3, c=c: e.tensor_tensor(out=xw[:].rearrange("p (a d) -> p a d", a=16), in0=x3, in1=bc(wq[:, c, :].unsqueeze(2), [128, 16, 64]), op=ALU.mult),
                      reads=[("x2", c), ("wq", c)], writes=["xw"])
                bank, key = PT.get()
                for k in range(8):
                    A("pe", lambda e, k=k, bank=bank: e.transpose(out=bank[:, k * 128:(k + 1) * 128], in_=yn[:, k * 128:(k + 1) * 128], identity=identb[:]),
                      reads=[("yn", 0), ("yn", 1), "identb"], writes=[key])
                A("dve", lambda e, bank=bank, c=c: e.tensor_tensor(out=xbcT[:, 0:8, 4 + c * 128:4 + (c + 1) * 128], in0=bank[:, :].rearrange("p (k t) -> p k t", k=8),
                                                                    in1=bc(ssdnw[:, l * 8:(l + 1) * 8].unsqueeze(2), [128, 8, 128]), op=ALU.mult),
                  reads=[key, "ssdnw"], writes=[("xbcT", b) for b in range(8)])
                if c == NCH - 1:
                    GIVE("ysT")
                STEP()
                if not OPT["stmp_early"]:
                    A("pool", lambda e, c=c: e.tensor_tensor(out=stmp[:].rearrange("p (a d) -> p a d", a=16), in0=Sssd[l][:].rearrange("p (a d) -> p a d", a=16),
                                                             in1=bc(cdb[:, c, :].unsqueeze(2), [128, 16, 64]), op=ALU.mult),
                      reads=[("Sssd", l), ("cdb", c)], writes=["stmp", "stmp2"])
                for hb in range(2):
                    bank, key = PB.get()
                    for gg in range(2):
                        g = hb * 2 + gg
                        A("pe", lambda e, bank=bank, gg=gg, g=g, c=c: e.matmul(bank[:, gg * 256:(gg + 1) * 256], lhsT=B2tok[:, c, g * 128:(g + 1) * 128], rhs=xw[:, g * 256:(g + 1) * 256], start=True, stop=True),
                          reads=[("B2tok", c), "xw"], writes=[key])
                    hs = slice(hb * 512, (hb + 1) * 512)
                    if OPT["sbf_dve"]:
                        A("dve", lambda e, bank=bank, hs=hs: e.tensor_tensor(out=Sbf[l][:, hs], in0=stmp[:, hs], in1=bank[:, :], op=ALU.add),
                          reads=[key, "stmp" if hb == 0 else "stmp2"], writes=[("Sbf", l)])
                    A("dve", lambda e, bank=bank, hs=hs: e.tensor_tensor(out=Sssd[l][:, hs], in0=stmp[:, hs], in1=bank[:, :], op=ALU.add),
                      reads=[key, "stmp" if hb == 0 else "stmp2"], writes=[("Sssd", l)])
                if not OPT["sbf_dve"]:
                    A("act", lambda e: e.activation(out=Sbf[l][:], in_=Sssd[l][:], func=AF.Copy), reads=[("Sssd", l)], writes=[("Sbf", l)])
                STEP()

            if OPT["hpipe"]:
                H_front(0)
                H_mid(0)
                for c in range(1, NCH):
                    H_front(c)
                    H_tail(c - 1)
                    H_mid(c)
                H_tail(NCH - 1)
            else:
                for c in range(NCH):
                    H_front(c)
                    H_mid(c)
                    H_tail(c)

            REC[0] = None
            PB.ring = PBfull
            given = set()
            if not OPT["s4"]:
                merge([sH, sF], given)
            MARK("P3")

            def merge_branch(b):
                for j in range(2):
                    wg, wgkey = next_w((l, "in", C_M + b * 1024 + j * 512))
                    for jj in range(4):
                        bank, key = PB.get()
                        for k in range(8):
                            A("pe", lambda e, k=k, jj=jj, wg=wg, bank=bank: e.matmul(bank[:, 0:N], lhsT=wg[:, k, jj * 128:(jj + 1) * 128], rhs=uT[:, k, :], start=(k == 0), stop=(k == 7)),
                              reads=uT_all + [wgkey], writes=[key])
                        A("act", lambda e, bank=bank, jj=jj: e.activation(out=gth[b][:, jj, :], in_=bank[:, 0:N], func=AF.Tanh, scale=0.5), reads=[key], writes=[("gth", jj)])
                        STEP()
                    wbr, wbkey = next_w((l, "br", (b, j)))
                    if b == 0:
                        NEED("ysT")
                    for jj in range(4):
                        bank, key = PB.get()
                        for k in range(8):
                            A("pe", lambda e, k=k, jj=jj, wbr=wbr, bank=bank: e.matmul(bank[:, 0:N], lhsT=wbr[:, k, jj * 128:(jj + 1) * 128], rhs=xbcT[:, b * 8 + k, 4:4 + N], start=(k == 0), stop=(k == 7)),
                              reads=[("xbcT", b * 8 + k) for k in range(8)] + [wbkey], writes=[key])
                        if b == 0:
                            A("dve", lambda e, bank=bank, jj=jj, j=j: e.scalar_tensor_tensor(out=t0m[:, j * 4 + jj, :], in0=gth[0][:, jj, :], scalar=1.0, in1=bank[:, 0:N], op0=ALU.add, op1=ALU.mult),
                              reads=[key, ("gth", jj)], writes=[("t0m", j * 4 + jj)])
                        else:
                            t1m, t1key = T1M.get()
                            A("dve", lambda e, bank=bank, jj=jj, t1m=t1m: e.scalar_tensor_tensor(out=t1m[:, :], in0=gth[1][:, jj, :], scalar=1.0, in1=bank[:, 0:N], op0=ALU.add, op1=ALU.mult),
                              reads=[key, ("gth", jj)], writes=[t1key])
                            A("pool", lambda e, jj=jj, j=j, t1m=t1m: e.tensor_tensor(out=qkT[:, j * 4 + jj, :], in0=t0m[:, j * 4 + jj, :], in1=t1m[:, :], op=ALU.add),
                              reads=[t1key, ("t0m", j * 4 + jj)], writes=[("qkT", j * 4 + jj)])
                        STEP()

            sJ = []
            if not OPT["s4"]:
                REC[0] = sJ
                PB.ring = PBf
                NEED("Fdone")
                merge_branch(0)
            sI = []
            REC[0] = sI
            PB.ring = PBi
            for c in range(NCH):
                cs = slice(c * 128, (c + 1) * 128)
                if not OPT["gprepB"]:
                    bank, key = PB.get()
                    A("pe", lambda e, bank=bank, cs=cs: e.matmul(bank[:, :], lhsT=alrT[:, cs], rhs=w2b[:, l * 512:(l + 1) * 512], start=True, stop=False),
                      reads=["alrT", "w2b"], writes=[key])
                    A("pe", lambda e, bank=bank: e.matmul(bank[:, :], lhsT=sel[:, 2, :], rhs=rows[:, l * 512:(l + 1) * 512], start=False, stop=False),
                      reads=["sel", "rows"], writes=[key])
                    A("pe", lambda e, bank=bank: e.matmul(bank[:, :], lhsT=sel[:, 2, :], rhs=rows[:, 1024 + l * 512:1024 + (l + 1) * 512], start=False, stop=True),
                      reads=["sel", "rows"], writes=[key])
                    A("act", lambda e, bank=bank: e.activation(out=e1[:], in_=bank[:, :], func=AF.Exp, scale=-1.0), reads=[key], writes=["e1"])
                    A("act", lambda e: e.activation(out=e1[:], in_=e1[:], func=AF.Ln, bias=1.0), reads=["e1"], writes=["e1"])
                bankg, keyg = PB.get()
                for j in range(4):
                    if OPT["gprepB"]:
                        A("pe", lambda e, bankg=bankg, j=j, c=c: e.matmul(bankg[:, j * 128:(j + 1) * 128], lhsT=l1b[:, c, j * 128:(j + 1) * 128], rhs=tri16b[:], start=True, stop=True),
                          reads=[("l1b", c), "tri16b"], writes=[keyg])
                    else:
                        A("pe", lambda e, bankg=bankg, j=j: e.matmul(bankg[:, j * 128:(j + 1) * 128], lhsT=e1[:, j * 128:(j + 1) * 128], rhs=tri16[:], start=True, stop=True),
                          reads=["e1", "tri16"], writes=[keyg])
                A("act", lambda e, bankg=bankg: e.activation(out=eg[:], in_=bankg[:, :], func=AF.Exp, bias=QS_LN), reads=[keyg], writes=["eg"])
                A("act", lambda e, bankg=bankg: e.activation(out=emg[:], in_=bankg[:, :], func=AF.Exp, scale=-1.0), reads=[keyg], writes=["emg"])
                A("act", lambda e, bankg=bankg: e.activation(out=egl[:, :], in_=bankg[:, :].rearrange("p (j t) -> p j t", j=4)[:, :, 127], func=AF.Exp), reads=[keyg], writes=["egl"])
                NEED("q")
                NEED("k")
                A("dve", lambda e, cs=cs: e.tensor_tensor(out=qtl[:], in0=qkT[:, 0:4, cs], in1=eg[:].rearrange("p (j t) -> p j t", j=4), op=ALU.mult),
                  reads=["eg"] + [("qkT", j) for j in range(4)], writes=["qtl"])
                A("dve", lambda e, cs=cs: e.tensor_tensor(out=ktl[:], in0=qkT[:, 4:8, cs], in1=emg[:].rearrange("p (j t) -> p j t", j=4), op=ALU.mult),
                  reads=["emg"] + [("qkT", 4 + j) for j in range(4)], writes=["ktl"])
                STEP()
                bank, key = PB.get()
                for hd in range(4):
                    A("pe", lambda e, bank=bank, hd=hd: e.matmul(bank[:, hd * 128:(hd + 1) * 128], lhsT=ktl[:, hd, :], rhs=qtl[:, hd, :], start=True, stop=True),
                      reads=["ktl", "qtl"], writes=[key])
                A("dve", lambda e, bank=bank: e.tensor_tensor(out=attm[:], in0=bank[:, :].rearrange("p (g t) -> p g t", g=4), in1=bc(maskb[:, :].unsqueeze(1), [128, 4, 128]), op=ALU.mult),
                  reads=[key, "maskb"], writes=["attm"])
                STEP()
                bankt, keyt = PT.get()
                for hd in range(4):
                    A("pe", lambda e, bankt=bankt, hd=hd: e.transpose(out=bankt[:, hd * 128:(hd + 1) * 128], in_=ktl[:, hd, :], identity=identb[:]),
                      reads=["ktl", "identb"], writes=[keyt])
                A("act", lambda e, bankt=bankt: e.activation(out=ktok[:], in_=bankt[:, 0:512], func=AF.Copy), reads=[keyt], writes=["ktok"])
                STEP()
                obk = []
                NEED(("v", c))
                for hb in range(2):
                    bank, key = PB.get()
                    obk.append((bank, key))
                    for hh in range(2):
                        hd = hb * 2 + hh
                        o = bank[:, hh * 256:(hh + 1) * 256]
                        A("pe", lambda e, o=o, hd=hd, c=c: e.matmul(o, lhsT=attm[:, hd, :], rhs=vtok[:, c, hd * 256:(hd + 1) * 256], start=True, stop=False),
                          reads=["attm", ("vtok", c)], writes=[key])
                        A("pe", lambda e, o=o, hd=hd: e.matmul(o, lhsT=qtl[:, hd, :], rhs=Gbf[l][:, hd * 256:(hd + 1) * 256], start=False, stop=True),
                          reads=["qtl", ("Gbf", l)], writes=[key])
                        A("act", lambda e, o=o, hd=hd: e.activation(out=on[:, hd * 256:(hd + 1) * 256], in_=o, func=AF.Square, accum_out=sq4g[:, hd:hd + 1]), reads=[key], writes=[("sq4g", hd), ("on", hd // 2)])
                    STEP()
                A("dve", lambda e: e.tensor_scalar(out=r4g[:], in0=sq4g[:], scalar1=1.0 / 256, scalar2=EPS, op0=ALU.mult, op1=ALU.add), reads=[("sq4g", i) for i in range(4)], writes=["r4g"])
                A("pool", lambda e: e.tensor_tensor(out=r4g[:], in0=r4g[:], in1=bc(mhalf[:, 0:1], [128, 4]), op=ALU.pow), reads=["r4g", "mhalf"], writes=["r4g"])
                for hb in range(2):
                    bank, key = obk[hb]
                    A("dve", lambda e, bank=bank, hb=hb: e.tensor_tensor(out=on[:, hb * 512:(hb + 1) * 512].rearrange("p (a d) -> p a d", a=2), in0=bank[:, :].rearrange("p (a d) -> p a d", a=2),
                                                                          in1=bc(r4g[:, hb * 2:(hb + 1) * 2].unsqueeze(2), [128, 2, 256]), op=ALU.mult),
                      reads=[key, "r4g"], writes=[("on", hb)])
                STEP()
                NEED("g")
                bank, key = PT.get()
                for k in range(8):
                    A("pe", lambda e, k=k, bank=bank: e.transpose(out=bank[:, k * 128:(k + 1) * 128], in_=on[:, k * 128:(k + 1) * 128], identity=identb[:]),
                      reads=[("on", 0), ("on", 1), "identb"], writes=[key])
                A("dve", lambda e, bank=bank, cs=cs, c=c: e.tensor_tensor(out=xbcT[:, 8:16, 4 + c * 128:4 + (c + 1) * 128], in0=bank[:, :].rearrange("p (k t) -> p k t", k=8),
                                                                         in1=sgT[:, :, cs], op=ALU.mult),
                  reads=[key] + [("sgT", b) for b in range(8)], writes=[("xbcT", 8 + b) for b in range(8)])
                STEP()
                for hb in range(2):
                    bank, key = PB.get()
                    for hh in range(2):
                        hd = hb * 2 + hh
                        A("pe", lambda e, bank=bank, hh=hh, hd=hd, c=c: e.matmul(bank[:, hh * 256:(hh + 1) * 256], lhsT=ktok[:, hd * 128:(hd + 1) * 128], rhs=vtok[:, c, hd * 256:(hd + 1) * 256], start=True, stop=True),
                          reads=["ktok", ("vtok", c)], writes=[key])
                    hs = slice(hb * 512, (hb + 1) * 512)
                    A("dve", lambda e, bank=bank, hs=hs: e.tensor_tensor(out=Sgla[l][:, hs], in0=Sgla[l][:, hs], in1=bank[:, :], op=ALU.add),
                      reads=[key, ("Sgla", l, hb)], writes=[("Sgla", l, hb)])
                for hd in range(4):
                    A("act", lambda e, hd=hd: e.activation(out=Gbf[l][:, hd * 256:(hd + 1) * 256], in_=Sgla[l][:, hd * 256:(hd + 1) * 256], func=AF.Copy, scale=egl[:, hd:hd + 1]),
                      reads=[("Sgla", l, hd // 2), "egl"], writes=[("Gbf", l)])
                for hd in range(4):
                    A("pool", lambda e, hd=hd: e.tensor_scalar(out=Sgla[l][:, hd * 256:(hd + 1) * 256], in0=Sgla[l][:, hd * 256:(hd + 1) * 256], scalar1=egl[:, hd:hd + 1], scalar2=0.0, op0=ALU.mult, op1=ALU.add),
                      reads=[("Sgla", l, hd // 2), "egl"], writes=[("Sgla", l, hd // 2)])
                STEP()
            REC[0] = None
            PB.ring = PBfull
            if OPT["s4"]:
                merge([sH, sI, sF], given)
                merge_branch(0)
            else:
                merge([sI, sJ], given)
            MARK("P4")
            merge_branch(1)
            MARK("OUT")
            if OPT["outc"]:
                wos = [next_w((l, "out", 0), pf=1), next_w((l, "out", 1), pf=0)]
                oorder = [(c, j2) for c in range(NCH) for j2 in range(2)]
            else:
                wos = [None, None]
                oorder = [(c, j2) for j2 in range(2) for c in range(NCH)]
            for c, j2 in oorder:
                if True:
                    if wos[j2] is None:
                        wos[j2] = next_w((l, "out", j2))
                    wo, wokey = wos[j2]
                    bank, key = PB.get()
                    for k in range(8):
                        A("pe", lambda e, k=k, c=c, wo=wo, bank=bank: e.matmul(bank[:, :], lhsT=qkT[:, k, c * 128:(c + 1) * 128], rhs=wo[:, k, :], start=(k == 0), stop=(k == 7)),
                          reads=[("qkT", k) for k in range(8)] + [wokey], writes=[key])
                    A("dve", lambda e, bank=bank, c=c, j2=j2: e.scalar_tensor_tensor(out=h[:, c, j2 * 512:(j2 + 1) * 512], in0=bank[:, :], scalar=0.5, in1=h[:, c, j2 * 512:(j2 + 1) * 512], op0=ALU.mult, op1=ALU.add),
                      reads=[key, ("h", c)], writes=[("h", c)])
            if first and not last_layer:
                A("pool", lambda e: e.memset(h[0:112, 0, :], 0.0), writes=[("h", 0)])

        for ti in range(ntiles):
            for c in range(NCH):
                gc = ti * NCH + c
                if gc == 0:
                    A("pool", lambda e: e.memset(h[:, 0, :], 0.0), writes=[("h", 0)])
                    A("sp", lambda e: e.dma_start(out=h[112:128, 0, :], in_=meta_d), writes=[("h", 0)], dma_sem=xsems[0])
                else:
                    r0 = (gc - 1) * 128
                    A("sp", lambda e, c=c, r0=r0: e.dma_start(out=h[:, c, :], in_=x_d[r0:r0 + 128, :]), writes=[("h", c)], dma_sem=xsems[c])
            for l in range(NL):
                tile_layer(ti, l)
            for c in range(NCH):
                gc = ti * NCH + c
                if gc == 0:
                    continue
                r0 = (gc - 1) * 128
                ub, ukey = UB.get()
                A("act", lambda e, c=c, ub=ub: e.activation(out=ub[:], in_=h[:, c, :], func=AF.Square, accum_out=ssq[:, c:c + 1]), reads=[("h", c)], writes=[("ssq", c), ukey])
                A("dve", lambda e, c=c: e.tensor_scalar(out=rs[:, c:c + 1], in0=ssq[:, c:c + 1], scalar1=1.0 / D, scalar2=EPS, op0=ALU.mult, op1=ALU.add),
                  reads=[("ssq", c)], writes=[("rs", c)])
                A("pool", lambda e, c=c: e.tensor_tensor(out=rs[:, c:c + 1], in0=rs[:, c:c + 1], in1=mhalf[:], op=ALU.pow), reads=[("rs", c), "mhalf"], writes=[("rs", c)])
                A("dve", lambda e, c=c: e.scalar_tensor_tensor(out=h[:, c, :], in0=h[:, c, :], scalar=rs[:, c:c + 1], in1=fnw_bc[:], op0=ALU.mult, op1=ALU.mult),
                  reads=[("h", c), ("rs", c), "fnw"], writes=[("h", c)])
                ok = ("out", gc)
                out_keys.append(ok)
                A("sp", lambda e, c=c, r0=r0: e.dma_start(out=out_d[r0:r0 + 128, :], in_=h[:, c, :]), reads=[("h", c)], writes=[ok], dma_sem=osems[c])
        A("sp", None, reads=out_keys)
        assert wstate["used"] == len(wlist)
        build.model_us = dict(SIM["free"])
        build.model_log = SIM["log"]
        build.mlog = SIM["mlog"]
        with nc.Block() as block:
            S.emit(block)
    return nc


def host_consts():
    i = np.arange(128)
    tri = (i[:, None] <= i[None, :]).astype(np.float32)
    strict = (i[:, None] > i[None, :]).astype(np.float32)
    mask0 = (i >= 112).astype(np.float32)[:, None]
    sel = np.zeros((128, 3, 128), np.float32)
    for r in range(3):
        sel[r, r, :] = 1.0
    return {"ident": np.eye(128, dtype=np.float32), "tri": tri, "strict": strict, "mask0": mask0, "sel": sel.reshape(128, 384)}


def make_in_maps(inputs, ncores, TR):
    f = lambda a: np.ascontiguousarray(np.asarray(a, dtype=np.float32))
    common = dict(host_consts())
    common["meta"] = f(inputs["meta_tokens"])
    common["w_in"] = f(inputs["w_in"])
    common["w_br"] = f(inputs["w_branch"])
    common["w_out"] = f(inputs["w_out"])
    common["normw"] = f(np.asarray(inputs["norm_w"]).reshape(2, 8, 128).transpose(2, 0, 1).reshape(128, 16))
    common["cw"] = f(np.asarray(inputs["conv_w"]).reshape(2, 4, 16, 128).transpose(3, 0, 2, 1).reshape(128, 128))
    common["cb"] = f(np.asarray(inputs["conv_b"]).reshape(1, 4096))
    common["dtb"] = f(inputs["dt_bias"])
    common["alog"] = f(inputs["a_log"])
    common["dskip"] = f(inputs["d_skip"])
    common["ssdnw"] = f(np.asarray(inputs["ssd_norm_w"]).reshape(2, 8, 128).transpose(2, 0, 1).reshape(128, 16))
    common["w2"] = f(np.asarray(inputs["gla_gate_w2"]).transpose(1, 0, 2).reshape(16, 1024))
    common["gb"] = f(np.asarray(inputs["gla_gate_b"]).reshape(1, 1024))
    common["gnw"] = f(np.asarray(inputs["gla_norm_w"]).reshape(2, 2, 128).transpose(2, 0, 1).reshape(128, 4))
    common["fnw"] = f(np.asarray(inputs["final_norm_w"]).reshape(1, 1024))
    x = np.asarray(inputs["x"], dtype=np.float32)
    maps = []
    for b in range(ncores):
        m = dict(common)
        m["x"] = np.ascontiguousarray(x[b, :TR])
        maps.append(m)
    return maps


_NC_CACHE = {}


def kernel(**inputs):
    x = np.asarray(inputs["x"])
    B, T, _ = x.shape
    NT = T // 128 + 1
    key = (NT,)
    if key not in _NC_CACHE:
        _NC_CACHE[key] = build(NT=NT, NCH=3, NL=2)
    nc = _NC_CACHE[key]
    maps = make_in_maps(inputs, B, T)
    res = run_bass_kernel_spmd(nc, maps, core_ids=list(range(B)))
    out = np.stack([np.asarray(r["out"], dtype=np.float32) for r in res.results], axis=0)
    return out
```

```python
import math
import numpy as np
import concourse.bass as bass
import concourse.mybir as mybir
from concourse.bass_utils import run_bass_kernel_spmd
from contextlib import ExitStack

F32 = mybir.dt.float32
BF16 = mybir.dt.bfloat16
AF = mybir.ActivationFunctionType
ALU = mybir.AluOpType

D = 1024
NIN = 8224
EPS = 1e-6
C_Z, C_XBC, C_DT, C_Q, C_K, C_V, C_G, C_ALR, C_M = 0, 1024, 3072, 3088, 3600, 4112, 5136, 6160, 6176


class Op:
    __slots__ = ("eng", "fn", "deps", "sig", "count", "sem", "is_dma", "inc", "fin")

    def __init__(self, eng, fn, is_dma=False):
        self.eng = eng
        self.fn = fn
        self.deps = set()
        self.sig = False
        self.count = 0
        self.sem = None
        self.is_dma = is_dma
        self.inc = 1
        self.fin = 0.0


class Sched:
    ENGS = ("pe", "act", "dve", "pool", "sp")

    def __init__(self, nc, stack):
        self.nc = nc
        self.stack = stack
        self.ops = []
        self.res = {}
        self.engsem = {e: stack.enter_context(nc.semaphore("s_" + e)) for e in self.ENGS}
        self.dma_counts = {}
        self.alias = {}

    def new_sem(self, name):
        return self.stack.enter_context(self.nc.semaphore(name))

    def uniq(self, key):
        lst = self.alias.setdefault(key, [])
        k = ("__u", key, len(lst))
        lst.append(k)
        return k

    def add(self, eng, fn, reads=(), writes=(), dma_sem=None, raw_keys=False):
        op = Op(eng, fn, is_dma=dma_sem is not None)
        if not raw_keys:
            reads = [rr for r in reads for rr in self.alias.get(r, [r])]
            writes = [ww for w in writes for ww in self.alias.get(w, [w])]
        if dma_sem is not None:
            op.sem = dma_sem
            op.sig = True
            op.inc = 16
            c = self.dma_counts.get(id(dma_sem), 0) + 16
            self.dma_counts[id(dma_sem)] = c
            op.count = c
        for r in reads:
            st = self.res.get(r)
            if st is None:
                st = [None, []]
                self.res[r] = st
            if st[0] is not None:
                op.deps.add(st[0])
            st[1].append(op)
        for w in writes:
            st = self.res.get(w)
            if st is None:
                st = [None, []]
                self.res[w] = st
            if st[0] is not None:
                op.deps.add(st[0])
            last = {}
            for rd in st[1]:
                if rd is op:
                    continue
                if rd.is_dma:
                    op.deps.add(rd)
                else:
                    last[rd.eng] = rd
            for rd in last.values():
                op.deps.add(rd)
            st[0] = op
            st[1] = []
        op.deps.discard(op)
        self.ops.append(op)
        return op

    def peek(self, reads, writes):
        reads = [rr for r in reads for rr in self.alias.get(r, [r])]
        writes = [ww for w in writes for ww in self.alias.get(w, [w])]
        deps = []
        for r in reads:
            st = self.res.get(r)
            if st is not None and st[0] is not None:
                deps.append(st[0])
        for w in writes:
            st = self.res.get(w)
            if st is None:
                continue
            if st[0] is not None:
                deps.append(st[0])
            last = {}
            for rd in st[1]:
                if rd.is_dma:
                    deps.append(rd)
                else:
                    last[rd.eng] = rd
            deps.extend(last.values())
        return deps

    def finalize(self):
        for op in self.ops:
            if op.eng == "pe" and not op.is_dma:
                op.deps = {d for d in op.deps if not (d.eng == "pe" and not d.is_dma)}
            for d in op.deps:
                d.sig = True
        cnt = {e: 0 for e in self.ENGS}
        for op in self.ops:
            if op.is_dma:
                continue
            if op.sig:
                cnt[op.eng] += 1
                op.count = cnt[op.eng]
                op.sem = self.engsem[op.eng]

    def emit(self, block):
        self.finalize()
        by_eng = {e: [o for o in self.ops if o.eng == e] for e in self.ENGS}

        def run(engine, ops):
            waited = {}
            for op in ops:
                need = {}
                for d in op.deps:
                    k = id(d.sem)
                    if need.get(k, (None, 0))[1] < d.count:
                        need[k] = (d.sem, d.count)
                for k, (sem, c) in need.items():
                    if waited.get(k, 0) < c:
                        engine.wait_ge(sem, c)
                        waited[k] = c
                if op.fn is None:
                    continue
                ins = op.fn(engine)
                if op.sig:
                    ins.then_inc(op.sem, op.inc)

        @block.tensor
        def _(e):
            run(e, by_eng["pe"])

        @block.scalar
        def _(e):
            run(e, by_eng["act"])

        @block.vector
        def _(e):
            run(e, by_eng["dve"])

        @block.gpsimd
        def _(e):
            run(e, by_eng["pool"])

        @block.sync
        def _(e):
            run(e, by_eng["sp"])


class Ring:
    def __init__(self, items, name, keys=None):
        self.items = items
        self.name = name
        self.keys = keys if keys is not None else [(name, j) for j in range(len(items))]
        self.i = 0

    def get(self):
        j = self.i % len(self.items)
        self.i += 1
        return self.items[j], self.keys[j]


class RingProxy:
    def __init__(self, ring):
        self.ring = ring

    def get(self):
        return self.ring.get()


import os
OPT = {"zs_eng": "pool", "diag_eng": "dve", "ub_eng": "dve", "greedy": 0, "outc": 1, "b2late": 0, "at_split": 1, "hb": 3, "s4": 0, "qkfirst": 0, "stmp_early": 0, "sbf_dve": 0, "hpipe": 1, "gprepB": 1, "at_bf16": 1}
for _k in list(OPT):
    if os.environ.get("KOPT_" + _k):
        OPT[_k] = type(OPT[_k])(os.environ["KOPT_" + _k])


def build(NT=33, NCH=3, NL=2, dbg=False):
    assert NT % NCH == 0
    N = NCH * 128
    ntiles = NT // NCH
    TR = (NT - 1) * 128
    nc = bass.Bass("TRN2", target_bir_lowering=False)

    def din(name, shape):
        return nc.dram_tensor(name, list(shape), F32, kind="ExternalInput").ap()

    x_d = din("x", [TR, D])
    meta_d = din("meta", [16, D])
    win_d = din("w_in", [2, D, NIN])
    wbr_d = din("w_br", [2, 2, D, D])
    wout_d = din("w_out", [2, D, D])
    normw_d = din("normw", [128, 16])
    cw_d = din("cw", [128, 128])
    cb_d = din("cb", [1, 2 * 2048])
    dtb_d = din("dtb", [2, 16])
    alog_d = din("alog", [2, 16])
    dskip_d = din("dskip", [2, 16])
    ssdnw_d = din("ssdnw", [128, 16])
    w2_d = din("w2", [16, 2 * 512])
    gb_d = din("gb", [1, 2 * 512])
    gnw_d = din("gnw", [128, 4])
    fnw_d = din("fnw", [1, D])
    ident_d = din("ident", [128, 128])
    tri_d = din("tri", [128, 128])
    strict_d = din("strict", [128, 128])
    mask0_d = din("mask0", [128, 1])
    sel_d = din("sel", [128, 3 * 128])
    out_d = nc.dram_tensor("out", [TR, D], F32, kind="ExternalOutput").ap()

    with ExitStack() as st:
        S = Sched(nc, st)

        def sb(name, shape, dt=F32):
            return st.enter_context(nc.sbuf_tensor("s_" + name, list(shape), dt))

        def ps(name, shape, dt=F32):
            return st.enter_context(nc.psum_tensor(name, list(shape), dt))

        REC = [None]

        class _FakeIns:
            def then_inc(self, *a, **k):
                return self

        class _FakeEng:
            def __init__(self):
                self.calls = []

            def __getattr__(self, name):
                def f(*a, **k):
                    self.calls.append((name, a, k))
                    return _FakeIns()
                return f

        def _fsz(ap):
            n = 1
            for d_ in ap.shape[1:]:
                n *= d_
            return n

        def _in_psum(ap):
            return "PSum" in type(ap.tensor).__name__

        def est_dur(eng, fn, is_dma):
            if fn is None:
                return 0.0
            if is_dma:
                return 3.0
            try:
                fe = _FakeEng()
                fn(fe)
                name, a, k = fe.calls[-1]
                if eng == "pe":
                    if name == "transpose":
                        return 0.13
                    rhs = k["rhs"]
                    n = _fsz(rhs)
                    return max(64, n) / 2400.0 * (4.0 if rhs.dtype == F32 else 1.0) + 0.01
                src = k.get("in_", k.get("in0"))
                n = _fsz(src) if src is not None else 256
                if eng == "act":
                    return 0.22 + n * 0.0009 + (0.09 if k.get("accum_out") is not None else 0.0)
                if eng == "dve":
                    t = 0.07 + n * 0.00105
                    if name == "tensor_tensor" and not _in_psum(k["in0"]) and not _in_psum(k["in1"]):
                        t = 0.07 + n * 0.0021
                    return t
                if eng == "pool":
                    if name == "tensor_tensor" and k.get("op") == ALU.pow:
                        return 0.8
                    if name == "tensor_scalar":
                        return 0.1 + n * 0.0015
                    return 0.3 + n * 0.0016
            except Exception:
                pass
            return 0.5

        SIM = {"free": {e: 0.0 for e in Sched.ENGS}, "pebusy": 0.0, "log": [], "dbg": dbg, "mlog": []}

        def MARK(name):
            SIM["log"].append((name, max(SIM["free"].values()), SIM["free"]["pe"], SIM["pebusy"]))

        def _lat(d, eng):
            return 0.35 if d.eng != eng or d.is_dma else 0.12

        def commit(eng, fn, reads, writes, dma_sem, kw, dur):
            op = S.add(eng, fn, reads, writes, dma_sem=dma_sem, **kw)
            t = SIM["free"][eng]
            for d in op.deps:
                if eng == "pe" and d.eng == "pe" and not d.is_dma:
                    continue
                t = max(t, d.fin + _lat(d, eng))
            if dma_sem is not None:
                SIM["free"][eng] = t + 0.1
                op.fin = t + dur
            else:
                op.fin = t + dur
                SIM["free"][eng] = op.fin
                if eng == "pe":
                    SIM["pebusy"] += dur
            return op

        def A(eng, fn, reads=(), writes=(), dma_sem=None, **kw):
            dur = est_dur(eng, fn, dma_sem is not None)
            if REC[0] is None:
                return commit(eng, fn, reads, writes, dma_sem, kw, dur)
            REC[0].append(("op", (eng, fn, list(reads), list(writes), dma_sem, kw, dur)))
            return None

        def STEP():
            if REC[0] is not None:
                REC[0].append(("step",))

        def NEED(tok):
            if REC[0] is not None:
                REC[0].append(("step",))
                REC[0].append(("need", tok))

        def GIVE(tok):
            if REC[0] is not None:
                REC[0].append(("give", tok))

        def step_stall(step):
            free = dict(SIM["free"])
            lw, lr = {}, {}
            stall = 0.0
            for it in step:
                if it[0] != "op":
                    continue
                eng, fn, r, w, ds, kw, dur = it[1]
                t = free[eng]
                t0 = t
                for d in S.peek(r, w):
                    if eng == "pe" and d.eng == "pe" and not d.is_dma:
                        continue
                    t = max(t, d.fin + _lat(d, eng))
                for key in r:
                    if key in lw:
                        t = max(t, lw[key] + 0.3)
                for key in w:
                    if key in lw:
                        t = max(t, lw[key] + 0.3)
                    if key in lr:
                        t = max(t, lr[key] + 0.3)
                stall += (t - t0) * (1.0 if eng == "pe" else 0.35)
                fin = t + dur
                if ds is not None:
                    free[eng] = t + 0.1
                else:
                    free[eng] = fin
                for key in r:
                    lr[key] = max(lr.get(key, 0.0), fin)
                for key in w:
                    lw[key] = fin
                    lr.pop(key, None)
            return stall

        def merge(streams, given):
            segs = []
            for stq in streams:
                steps = [[]]
                for it in stq:
                    if it[0] == "step":
                        if steps[-1]:
                            steps.append([])
                    else:
                        steps[-1].append(it)
                if not steps[-1]:
                    steps.pop()
                segs.append(steps)
            pos = [0] * len(segs)
            tot = [max(len(x), 1) for x in segs]
            while any(pos[i] < len(segs[i]) for i in range(len(segs))):
                cands = []
                for i in range(len(segs)):
                    if pos[i] >= len(segs[i]):
                        continue
                    step = segs[i][pos[i]]
                    if any(it[0] == "need" and it[1] not in given for it in step):
                        continue
                    cands.append((step_stall(step), i))
                assert cands, "merge deadlock"
                pick = None
                if OPT["greedy"]:
                    for st_, i in cands:
                        if st_ < 0.25:
                            pick = i
                            break
                    if pick is None:
                        pick = min(cands)[1]
                else:
                    pick = min(cands, key=lambda ci: pos[ci[1]] / tot[ci[1]])[1]
                for it in segs[pick][pos[pick]]:
                    if it[0] == "op":
                        eng, fn, r, w, ds, kw, dur = it[1]
                        commit(eng, fn, r, w, ds, kw, dur)
                    elif it[0] == "give":
                        given.add(it[1])
                if SIM.get("dbg"):
                    SIM["mlog"].append((pick, pos[pick], [round(c_[0], 2) for c_ in cands], dict(SIM["free"])))
                pos[pick] += 1
        h = sb("h", [128, NCH, D])
        ubs = [sb("ub%d" % i, [128, D], BF16) for i in range(2)]
        UB = Ring(ubs, "ub")
        uT = sb("uT", [128, 8, N], BF16)
        NWB = 3
        wbufs = [sb("wb%d" % i, [128, 8, 512], BF16) for i in range(NWB)]
        wsems = [S.new_sem("wsem%d" % i) for i in range(NWB)]
        wmisc = sb("wmisc", [128, 2, 8, 32], BF16)
        NDG = 4
        dgs = [sb("dg%d" % i, [128, 4, 128], BF16) for i in range(NDG)]
        DG = Ring(dgs, "dg")
        XW = N + 4
        xbcT = sb("xbcT", [128, 16, XW], BF16)
        hist = sb("hist", [128, 2, 16, 4], BF16)
        x2 = sb("x2", [128, NCH, D], BF16)
        B2tok = sb("B2tok", [128, NCH, 512], BF16)
        BCT = sb("BCT", [128, 8, N], BF16)
        zs = sb("zs", [128, NCH, D], BF16)
        qkT = sb("qkT", [128, 8, N], BF16)
        vtok = sb("vtok", [128, NCH, D], BF16)
        sgT = sb("sgT", [128, 8, N], BF16)
        alrT = sb("alrT", [128, N], BF16)
        ths = [sb("th%d" % i, [128, 512], BF16) for i in range(2)]
        TH = Ring(ths, "th")
        gws = [sb("gw%d" % i, [128, 512], BF16) for i in range(2)]
        GW = Ring(gws, "gw")
        ssq = sb("ssq", [128, NCH])
        rs = sb("rs", [128, NCH])
        dtr = sb("dtr", [128, NCH, 16])
        dte_x = sb("dte_x", [128, NCH, 16])
        dt_t = sb("dt_t", [128, NCH, 16])
        da_t = sb("da_t", [128, NCH, 16])
        ecumh = sb("ecumh", [128, NCH, 16])
        dte = sb("dte", [128, NCH, 16])
        cdb = sb("cdb", [128, NCH, 16])
        wq = sb("wq", [128, NCH, 16])
        dth = sb("dth", [128, NCH, 16])
        sq4 = sb("sq4", [128, 4])
        r4 = sb("r4", [128, 4])
        sq4g = sb("sq4g", [128, 4])
        r4g = sb("r4g", [128, 4])
        egl = sb("egl", [128, 4])
        At = sb("At", [128, 16, 128], BF16) if OPT["at_bf16"] else sb("At", [128, 8, 128])
        Lm = sb("Lm", [128, 16, 128], BF16)
        MT = sb("MT", [128, 16, 128], BF16)
        scm = sb("scm", [128, 4, 128], BF16)
        xdt = sb("xdt", [128, D], BF16)
        xw = sb("xw", [128, D], BF16)
        xd = sb("xd", [128, D], BF16)
        t1 = sb("t1", [128, D])
        yn = sb("yn", [128, D], BF16)
        stmp = sb("stmp", [128, D])
        e1 = sb("e1", [128, 512])
        eg = sb("eg", [128, 512])
        emg = sb("emg", [128, 512])
        on = sb("on", [128, D], BF16)
        qtl = sb("qtl", [128, 4, 128], BF16)
        ktl = sb("ktl", [128, 4, 128], BF16)
        ktok = sb("ktok", [128, 512], BF16)
        attm = sb("attm", [128, 4, 128], BF16)
        gth0 = sb("gth0", [128, 4, N], BF16)
        gth = [gth0, gth0]
        t0m = sb("t0m", [128, 8, N], BF16)
        t1ms = [sb("t1m%d" % i, [128, N]) for i in range(2)]
        T1M = Ring(t1ms, "t1m")
        Sssd = [sb("Sssd%d" % l, [128, D]) for l in range(2)]
        Sbf = [sb("Sbf%d" % l, [128, D], BF16) for l in range(2)]
        Sgla = [sb("Sgla%d" % l, [128, D]) for l in range(2)]
        Gbf = [sb("Gbf%d" % l, [128, D], BF16) for l in range(2)]
        identf = sb("identf", [128, 128])
        identb = sb("identb", [128, 128], BF16)
        tri = sb("tri", [128, 128])
        tri16 = sb("tri16", [128, 128])
        strict = sb("strict", [128, 128])
        maskb = sb("maskb", [128, 128], BF16)
        maskq = sb("maskq", [128, 128], BF16)
        onesf = sb("onesf", [128, 128])
        sel = sb("sel", [128, 3, 128], BF16)
        selN = sb("selN", [128, 2, N], BF16)
        l1b = sb("l1b", [128, NCH, 512], BF16)
        tri16b = sb("tri16b", [128, 128], BF16)
        rows = sb("rows", [128, 2048], BF16)
        gb8 = sb("gb8", [8, 128])
        gb8h = sb("gb8h", [8, 128], BF16)
        gb8hf = sb("gb8hf", [8, 128])
        gb8l = sb("gb8l", [8, 128], BF16)
        mhalf = sb("mhalf", [128, 1])
        mask0 = sb("mask0", [128, 1])
        normw = sb("normw", [128, 16])
        cw = sb("cw", [128, 128])
        dtb_bc = sb("dtb_bc", [128, 2, 16])
        a_bc = sb("a_bc", [128, 2, 16])
        dskh_bc = sb("dskh_bc", [128, 2, 16])
        ssdnw = sb("ssdnw", [128, 16])
        w2b = sb("w2b", [128, 1024], BF16)
        gnwh = sb("gnwh", [128, 4])
        fnw_bc = sb("fnw_bc", [128, D])
        pbs = [ps("pb%d" % i, [128, 512]) for i in range(6)]
        PBfull = Ring(pbs, "pb")
        PBh = Ring(pbs[0:3], "pb", keys=[("pb", j) for j in range(0, 3)])
        if OPT["s4"]:
            PBi = Ring(pbs[3:5], "pb", keys=[("pb", j) for j in range(3, 5)])
            PBf = Ring(pbs[5:6], "pb", keys=[("pb", 5)])
        elif OPT["hpipe"]:
            PBh = Ring(pbs[0:2], "pb", keys=[("pb", j) for j in range(0, 2)])
            YD = Ring(pbs[2:4], "pb", keys=[("pb", j) for j in range(2, 4)])
            PBi = PBh
            PBf = Ring(pbs[4:6], "pb", keys=[("pb", j) for j in range(4, 6)])
        else:
            PBi = PBh
            PBf = Ring(pbs[3:6], "pb", keys=[("pb", j) for j in range(3, 6)])
        PB = RingProxy(PBfull)
        pts = [ps("pt%d" % i, [128, 1024], BF16) for i in range(2)]
        PT = Ring(pts, "pt")

        csem = S.new_sem("csem")

        csemp = S.new_sem("csemp")
        cops = []
        copsp = []

        def cload(dst, src, key, eng="sp"):
            if eng == "sp":
                cops.append(A(eng, lambda e: e.dma_start(out=dst, in_=src), writes=[S.uniq(key)], dma_sem=csem, raw_keys=True))
            else:
                copsp.append(A(eng, lambda e: e.dma_start(out=dst, in_=src), writes=[S.uniq(key)], dma_sem=csemp, raw_keys=True))

        cload(identf[:], ident_d, "identf")
        cload(tri[:], tri_d, "tri")
        cload(strict[:], strict_d, "strict")
        cload(mask0[:], mask0_d, "mask0")
        cload(normw[:], normw_d, "normw")
        cload(cw[:], cw_d, "cw")
        cload(ssdnw[:], ssdnw_d, "ssdnw")
        cload(gnwh[:], gnw_d, "gnwh")
        cload(gb8[:], gb_d.rearrange("o (a b) -> (o a) b", a=8), "gb8")
        cload(fnw_bc[:], fnw_d[0:1, :].partition_broadcast(128), "fnw")
        for l in range(2):
            cload(dtb_bc[:, l, :], dtb_d[l:l + 1, :].partition_broadcast(128), "dtb")
            cload(a_bc[:, l, :], alog_d[l:l + 1, :].partition_broadcast(128), "a_bc")
            cload(dskh_bc[:, l, :], dskip_d[l:l + 1, :].partition_broadcast(128), "dskh")
        A("pool", lambda e: e.memset(rows[:], 0.0), writes=["rows0"])
        A("pool", lambda e: e.memset(alrT[:], 0.0), writes=["alrT"])
        A("pool", lambda e: e.memset(w2b[:], 0.0), writes=["w2b0"])
        copsp.append(A("pool", lambda e: e.dma_start(out=rows[0:1, :], in_=cb_d[:, 0:2048]), reads=["rows0"], writes=[S.uniq("rows")], dma_sem=csemp, raw_keys=True))
        copsp.append(A("pool", lambda e: e.dma_start(out=rows[1:2, :], in_=cb_d[:, 2048:4096]), reads=["rows0"], writes=[S.uniq("rows")], dma_sem=csemp, raw_keys=True))
        cload(sel[:].rearrange("p r m -> p (r m)"), sel_d, "sel", eng="pool")
        copsp.append(A("pool", lambda e: e.dma_start(out=w2b[0:16, :], in_=w2_d), reads=["w2b0"], writes=[S.uniq("w2b")], dma_sem=csemp, raw_keys=True))
        for l in range(2):
            cload(wmisc[:, l, :, 0:16], win_d[l, :, C_DT:C_DT + 16].rearrange("(k p) n -> p k n", p=128), "wmisc", eng="pool")
            cload(wmisc[:, l, :, 16:32], win_d[l, :, C_ALR:C_ALR + 16].rearrange("(k p) n -> p k n", p=128), "wmisc", eng="pool")

        for o in cops:
            o.count = 16 * len(cops)
        for o in copsp:
            o.count = 16 * len(copsp)
        A("dve", lambda e: e.tensor_copy(out=identb[:], in_=identf[:]), reads=["identf"], writes=["identb"])
        A("dve", lambda e: e.tensor_copy(out=maskb[:], in_=tri[:]), reads=["tri"], writes=["maskb"])
        A("dve", lambda e: e.tensor_scalar(out=maskq[:], in0=tri[:], scalar1=0.25, scalar2=None, op0=ALU.mult), reads=["tri"], writes=["maskq"])
        A("dve", lambda e: e.tensor_scalar(out=tri16[:], in0=tri[:], scalar1=-1.0 / 16.0, scalar2=None, op0=ALU.mult), reads=["tri"], writes=["tri16"])
        A("dve", lambda e: e.tensor_scalar(out=tri16b[:], in0=tri[:], scalar1=-1.0 / 16.0, scalar2=None, op0=ALU.mult), reads=["tri"], writes=["tri16b"])
        A("pool", lambda e: e.memset(onesf[:], 1.0), writes=["onesf"])
        A("pool", lambda e: e.memset(mhalf[:], -0.5), writes=["mhalf"])
        for r in range(2):
            A("pool", lambda e, r=r: e.tensor_copy(out=selN[:, r, :], in_=bc(sel[:, r, 0:1], [128, N])), reads=["sel"], writes=["selN"])
        A("pool", lambda e: e.memset(hist[:], 0.0), writes=["hist"])
        A("pool", lambda e: e.memset(xbcT[:], 0.0), writes=[("xbcT", b) for b in range(16)])
        for l in range(2):
            A("pool", lambda e, l=l: e.memset(Sssd[l][:], 0.0), writes=[("Sssd", l)])
            A("pool", lambda e, l=l: e.memset(Sbf[l][:], 0.0), writes=[("Sbf", l)])
            A("pool", lambda e, l=l: e.memset(Sgla[l][:], 0.0), writes=[("Sgla", l, 0), ("Sgla", l, 1)])
            A("pool", lambda e, l=l: e.memset(Gbf[l][:], 0.0), writes=[("Gbf", l)])
        A("act", lambda e: e.activation(out=a_bc[:], in_=a_bc[:], func=AF.Exp), reads=["a_bc"], writes=["a_bc"])
        A("dve", lambda e: e.tensor_scalar(out=a_bc[:], in0=a_bc[:], scalar1=-1.0, scalar2=None, op0=ALU.mult), reads=["a_bc"], writes=["a_bc"])
        A("dve", lambda e: e.tensor_scalar(out=dskh_bc[:], in0=dskh_bc[:], scalar1=0.5, scalar2=None, op0=ALU.mult), reads=["dskh"], writes=["dskh"])
        A("dve", lambda e: e.tensor_scalar(out=gnwh[:], in0=gnwh[:], scalar1=0.5, scalar2=None, op0=ALU.mult), reads=["gnwh"], writes=["gnwh"])
        A("dve", lambda e: e.tensor_copy(out=gb8h[:], in_=gb8[:]), reads=["gb8"], writes=["gb8h"])
        A("dve", lambda e: e.tensor_copy(out=gb8hf[:], in_=gb8h[:]), reads=["gb8h"], writes=["gb8hf"])
        A("dve", lambda e: e.tensor_tensor(out=gb8l[:], in0=gb8[:], in1=gb8hf[:], op=ALU.subtract), reads=["gb8", "gb8hf"], writes=["gb8l"])
        csem2 = S.new_sem("csem2")
        cops2 = []
        for a8 in range(8):
            cops2.append(A("sp", lambda e, a8=a8: e.dma_start(out=rows[2:3, a8 * 128:(a8 + 1) * 128], in_=gb8h[a8:a8 + 1, :]), reads=["gb8h", "rows0"], writes=[S.uniq("rows")], dma_sem=csem2))
            cops2.append(A("sp", lambda e, a8=a8: e.dma_start(out=rows[2:3, 1024 + a8 * 128:1024 + (a8 + 1) * 128], in_=gb8l[a8:a8 + 1, :]), reads=["gb8l", "rows0"], writes=[S.uniq("rows")], dma_sem=csem2))
        for o in cops2:
            o.count = 16 * len(cops2)

        def wsrc(l, kind, idx):
            if kind == "in":
                src = win_d[l, :, idx:idx + 512]
            elif kind == "br":
                b, j = idx
                src = wbr_d[l, b, :, j * 512:(j + 1) * 512]
            else:
                src = wout_d[l, :, idx * 512:(idx + 1) * 512]
            return src.rearrange("(k p) n -> p k n", p=128)

        def layer_groups(l):
            g = []
            for i in range(4):
                g.append((l, "in", C_XBC + 512 * i))
            if OPT["qkfirst"]:
                g.append((l, "in", C_Q))
                g.append((l, "in", C_K))
            for i in range(2):
                g.append((l, "in", C_Z + 512 * i))
            if not OPT["qkfirst"]:
                g.append((l, "in", C_Q))
                g.append((l, "in", C_K))
            for i in range(2):
                g.append((l, "in", C_V + 512 * i))
            for i in range(2):
                g.append((l, "in", C_G + 512 * i))
            for b in range(2):
                for j in range(2):
                    g.append((l, "in", C_M + b * 1024 + j * 512))
                    g.append((l, "br", (b, j)))
            for j in range(2):
                g.append((l, "out", j))
            return g

        wlist = []
        for ti in range(ntiles):
            for l in range(NL):
                wlist += layer_groups(l)
        wstate = {"issued": 0, "used": 0}
        PF = 2

        GP = len(layer_groups(0)) * NL
        wscr = nc.dram_tensor("wscr", [GP, 128, 8 * 512], BF16, kind="Internal").ap()
        wsemh = [S.new_sem("wsemh%d" % i) for i in range(NWB)]
        wsemst = [S.new_sem("wsemst%d" % i) for i in range(NWB)]

        def w_issue_upto(n):
            while wstate["issued"] < min(n, len(wlist)):
                i = wstate["issued"]
                l, kind, idx = wlist[i]
                slot = i % NWB
                if i < GP:
                    src = wsrc(l, kind, idx)
                    A("pool", lambda e, slot=slot, src=src: e.dma_start(out=wbufs[slot][:], in_=src),
                      writes=[("wb", slot)], dma_sem=wsems[slot])
                    if ntiles > 1:
                        A("sp", lambda e, slot=slot, i=i: e.dma_start(out=wscr[i], in_=wbufs[slot][:].rearrange("p k n -> p (k n)")),
                          reads=[("wb", slot)], writes=[("wscr", i)], dma_sem=wsemst[slot])
                else:
                    g = i % GP
                    A("sp", lambda e, slot=slot, g=g: e.dma_start(out=wbufs[slot][:].rearrange("p k n -> p (k n)"), in_=wscr[g]),
                      reads=[("wscr", g)], writes=[("wb", slot)], dma_sem=wsemh[slot])
                wstate["issued"] += 1

        def next_w(expect, pf=PF):
            i = wstate["used"]
            assert wlist[i] == expect, (wlist[i], expect)
            w_issue_upto(i + 1 + pf)
            wstate["used"] += 1
            return wbufs[i % NWB], ("wb", i % NWB)

        def bc(ap, shape):
            return ap.to_broadcast(list(shape))

        HALF_LN = math.log(0.5)
        QS_LN = math.log(128.0 ** -0.5)

        def silu2_evac(bank_ap, out_ap, key, wkey, ncols):
            th, tkey = TH.get()
            A("act", lambda e: e.activation(out=th[:, 0:ncols], in_=bank_ap, func=AF.Tanh, scale=0.5), reads=[key], writes=[tkey])
            A("dve", lambda e: e.scalar_tensor_tensor(out=out_ap, in0=th[:, 0:ncols], scalar=1.0, in1=bank_ap, op0=ALU.add, op1=ALU.mult),
              reads=[tkey, key], writes=[wkey])

        uT_all = [("uT", c) for c in range(NCH)]
        out_keys = []
        osems = [S.new_sem("osem%d" % i) for i in range(NCH)]
        xsems = [S.new_sem("xsem%d" % i) for i in range(NCH)]

        def tile_layer(ti, l):
            first = ti == 0
            last_layer = l == NL - 1
            MARK("A %d %d" % (ti, l))
            for c in range(NCH):
                ub, ukey = UB.get()
                A("act", lambda e, c=c, ub=ub: e.activation(out=ub[:], in_=h[:, c, :], func=AF.Square, accum_out=ssq[:, c:c + 1]),
                  reads=[("h", c)], writes=[("ssq", c), ukey])
                A("dve", lambda e, c=c: e.tensor_scalar(out=rs[:, c:c + 1], in0=ssq[:, c:c + 1], scalar1=1.0 / D, scalar2=EPS, op0=ALU.mult, op1=ALU.add),
                  reads=[("ssq", c)], writes=[("rs", c)])
                A("pool", lambda e, c=c: e.tensor_tensor(out=rs[:, c:c + 1], in0=rs[:, c:c + 1], in1=mhalf[:], op=ALU.pow),
                  reads=[("rs", c), "mhalf"], writes=[("rs", c)])
                if OPT["ub_eng"] == "act":
                    A("act", lambda e, c=c, ub=ub: e.activation(out=ub[:], in_=h[:, c, :], func=AF.Copy, scale=rs[:, c:c + 1]),
                      reads=[("h", c), ("rs", c)], writes=[ukey])
                else:
                    A("dve", lambda e, c=c, ub=ub: e.tensor_scalar(out=ub[:], in0=h[:, c, :], scalar1=rs[:, c:c + 1], scalar2=None, op0=ALU.mult),
                      reads=[("h", c), ("rs", c)], writes=[ukey])
                bank, key = PT.get()
                for k in range(8):
                    A("pe", lambda e, k=k, ub=ub, bank=bank: e.transpose(out=bank[:, k * 128:(k + 1) * 128], in_=ub[:, k * 128:(k + 1) * 128], identity=identb[:]),
                      reads=[ukey, "identb"], writes=[key])
                A("dve", lambda e, c=c, bank=bank: e.tensor_tensor(out=uT[:, :, c * 128:(c + 1) * 128], in0=bank[:, :].rearrange("p (k t) -> p k t", k=8),
                                                                    in1=bc(normw[:, l * 8:(l + 1) * 8].unsqueeze(2), [128, 8, 128]), op=ALU.mult),
                  reads=[key, "normw"], writes=[("uT", c)])
            MARK("B")
            for c in range(NCH):
                bank, key = PB.get()
                for k in range(8):
                    A("pe", lambda e, k=k, c=c, bank=bank: e.matmul(bank[:, 0:16], lhsT=uT[:, k, c * 128:(c + 1) * 128], rhs=wmisc[:, l, k, 0:16], start=(k == 0), stop=(k == 7)),
                      reads=[("uT", c), "wmisc"], writes=[key])
                A("dve", lambda e, c=c, bank=bank: e.tensor_tensor(out=dtr[:, c, :], in0=bank[:, 0:16], in1=dtb_bc[:, l, :], op=ALU.add),
                  reads=[key, "dtb"], writes=[("dtr", c)])
                A("act", lambda e, c=c: e.activation(out=dte_x[:, c, :], in_=dtr[:, c, :], func=AF.Exp), reads=[("dtr", c)], writes=[("dte_x", c)])
            for c in range(NCH):
                A("act", lambda e, c=c: e.activation(out=dt_t[:, c, :], in_=dte_x[:, c, :], func=AF.Ln, bias=1.0), reads=[("dte_x", c)], writes=[("dt", c)])
            for c in range(NCH):
                if first and c == 0:
                    A("dve", lambda e, c=c: e.tensor_scalar(out=dt_t[:, c, :], in0=dt_t[:, c, :], scalar1=mask0[:, 0:1], scalar2=None, op0=ALU.mult),
                      reads=[("dt", c), "mask0"], writes=[("dt", c)])
                A("dve", lambda e, c=c: e.tensor_tensor(out=da_t[:, c, :], in0=dt_t[:, c, :], in1=a_bc[:, l, :], op=ALU.mult),
                  reads=[("dt", c), "a_bc"], writes=[("da", c)])
            def stage_B2(c):
                bank2, key2 = PB.get()
                A("pe", lambda e, c=c, bank2=bank2: e.matmul(bank2[:, 0:16], lhsT=tri[:], rhs=da_t[:, c, :], start=True, stop=True), reads=[("da", c), "tri"], writes=[key2])
                A("pe", lambda e, c=c, bank2=bank2: e.matmul(bank2[:, 16:32], lhsT=onesf[:], rhs=da_t[:, c, :], start=True, stop=True), reads=[("da", c), "onesf"], writes=[key2])
                A("pe", lambda e, c=c, bank2=bank2: e.matmul(bank2[:, 32:48], lhsT=strict[:], rhs=da_t[:, c, :], start=True, stop=True), reads=[("da", c), "strict"], writes=[key2])
                A("act", lambda e, c=c, bank2=bank2: e.activation(out=ecumh[:, c, :], in_=bank2[:, 0:16], func=AF.Exp, bias=HALF_LN), reads=[key2], writes=[("ecumh", c)])
                A("act", lambda e, c=c, bank2=bank2: e.activation(out=cdb[:, c, :], in_=bank2[:, 16:32], func=AF.Exp), reads=[key2], writes=[("cdb", c)])
                A("act", lambda e, c=c, bank2=bank2: e.activation(out=dte[:, c, :], in_=bank2[:, 32:48], func=AF.Exp), reads=[key2], writes=[("dte", c)])
                A("dve", lambda e, c=c: e.scalar_tensor_tensor(out=wq[:, c, :], in0=dt_t[:, c, :], scalar=0.25, in1=dte[:, c, :], op0=ALU.mult, op1=ALU.mult),
                  reads=[("dt", c), ("dte", c)], writes=[("wq", c)])
                A("dve", lambda e, c=c: e.tensor_scalar(out=dth[:, c, :], in0=dt_t[:, c, :], scalar1=0.5, scalar2=None, op0=ALU.mult),
                  reads=[("dt", c)], writes=[("dth", c)])
            if not OPT["b2late"]:
                for c in range(NCH):
                    stage_B2(c)
            bank, key = PB.get()
            for k in range(8):
                A("pe", lambda e, k=k, bank=bank: e.matmul(bank[0:16, 0:N], lhsT=wmisc[:, l, k, 16:32], rhs=uT[:, k, :], start=(k == 0), stop=(k == 7)),
                  reads=uT_all + ["wmisc"], writes=[key])
            A("act", lambda e, bank=bank: e.activation(out=alrT[0:16, :], in_=bank[0:16, 0:N], func=AF.Copy), reads=[key], writes=["alrT"])
            if OPT["gprepB"]:
                for c in range(NCH):
                    cs = slice(c * 128, (c + 1) * 128)
                    bank, key = PB.get()
                    A("pe", lambda e, bank=bank, cs=cs: e.matmul(bank[:, :], lhsT=alrT[:, cs], rhs=w2b[:, l * 512:(l + 1) * 512], start=True, stop=False),
                      reads=["alrT", "w2b"], writes=[key])
                    A("pe", lambda e, bank=bank: e.matmul(bank[:, :], lhsT=sel[:, 2, :], rhs=rows[:, l * 512:(l + 1) * 512], start=False, stop=False),
                      reads=["sel", "rows"], writes=[key])
                    A("pe", lambda e, bank=bank: e.matmul(bank[:, :], lhsT=sel[:, 2, :], rhs=rows[:, 1024 + l * 512:1024 + (l + 1) * 512], start=False, stop=True),
                      reads=["sel", "rows"], writes=[key])
                    A("act", lambda e, bank=bank: e.activation(out=e1[:], in_=bank[:, :], func=AF.Exp, scale=-1.0), reads=[key], writes=["e1"])
                    A("act", lambda e, c=c: e.activation(out=l1b[:, c, :], in_=e1[:], func=AF.Ln, bias=1.0), reads=["e1"], writes=[("l1b", c)])
            MARK("C")
            for gi in range(4):
                wb, wkey = next_w((l, "in", C_XBC + 512 * gi))
                for j in range(4):
                    blk = gi * 4 + j
                    bank, key = PB.get()
                    for k in range(8):
                        A("pe", lambda e, k=k, j=j, wb=wb, bank=bank: e.matmul(bank[:, 0:N], lhsT=wb[:, k, j * 128:(j + 1) * 128], rhs=uT[:, k, :], start=(k == 0), stop=(k == 7)),
                          reads=uT_all + [wkey], writes=[key])
                    A("act", lambda e, blk=blk, bank=bank: e.activation(out=xbcT[:, blk, 4:4 + N], in_=bank[:, 0:N], func=AF.Copy), reads=[key], writes=[("xbcT", blk)])
            if OPT["b2late"]:
                for c in range(NCH):
                    stage_B2(c)
            allx = [("xbcT", b) for b in range(16)]
            A("pool", lambda e: e.tensor_copy(out=xbcT[:, :, 1:4], in_=hist[:, l, :, 1:4]), reads=[("hist", l)], writes=allx)
            A("pool", lambda e: e.tensor_copy(out=hist[:, l, :, 1:4], in_=xbcT[:, :, N + 1:N + 4]), reads=allx, writes=[("hist", l)])
            MARK("E")
            dgof = {}

            def gen_diag(blk):
                dg, dkey = DG.get()
                for tap in range(4):
                    col = (l * 16 + blk) * 4 + tap
                    A(OPT["diag_eng"], lambda e, dg=dg, tap=tap, col=col: e.tensor_scalar(out=dg[:, tap, :], in0=identb[:], scalar1=cw[:, col:col + 1], scalar2=0.0, op0=ALU.mult, op1=ALU.add),
                      reads=["identb", "cw"], writes=[dkey])
                dgof[blk] = (dg, dkey)

            def conv_tok(c, blks, out_ap, wkey):
                bank, key = PB.get()
                for j, blk in enumerate(blks):
                    dg, dkey = dgof[blk]
                    o = bank[:, j * 128:(j + 1) * 128]
                    A("pe", lambda e, o=o, blk=blk: e.matmul(o, lhsT=sel[:, l, :], rhs=rows[:, blk * 128:(blk + 1) * 128], start=True, stop=False),
                      reads=["sel", "rows"], writes=[key])
                    for tap in range(4):
                        A("pe", lambda e, o=o, blk=blk, tap=tap, dg=dg, c=c: e.matmul(o, lhsT=xbcT[:, blk, c * 128 + 1 + tap:c * 128 + 1 + tap + 128], rhs=dg[:, tap, :], start=False, stop=(tap == 3)),
                          reads=[("xbcT", blk), dkey], writes=[key])
                silu2_evac(bank[:, 0:128 * len(blks)], out_ap, key, wkey, 128 * len(blks))

            for half in range(2):
                blks = [half * 4 + j for j in range(4)]
                for b in blks:
                    gen_diag(b)
                for c in range(NCH):
                    conv_tok(c, blks, x2[:, c, half * 512:(half + 1) * 512], ("x2", c))
            for blk in range(8, 16):
                if blk % 4 == 0:
                    for b4 in range(blk, blk + 4):
                        gen_diag(b4)
                    if blk == 8:
                        for c in range(NCH):
                            conv_tok(c, [8, 9, 10, 11], B2tok[:, c, :], ("B2tok", c))
                dg, dkey = dgof[blk]
                bank, key = PB.get()
                A("pe", lambda e, bank=bank, blk=blk: e.matmul(bank[:, 0:N], lhsT=rows[:, blk * 128:(blk + 1) * 128], rhs=selN[:, l, :], start=True, stop=False),
                  reads=["selN", "rows"], writes=[key])
                for tap in range(4):
                    A("pe", lambda e, bank=bank, blk=blk, tap=tap, dg=dg: e.matmul(bank[:, 0:N], lhsT=dg[:, tap, :], rhs=xbcT[:, blk, 1 + tap:1 + tap + N], start=False, stop=(tap == 3)),
                      reads=[("xbcT", blk), dkey], writes=[key])
                silu2_evac(bank[:, 0:N], BCT[:, blk - 8, :], key, ("BCT", blk - 8), N)
            MARK("P2")
            sF = []
            REC[0] = sF
            PB.ring = PBf
            for part in (("qk", "z") if OPT["qkfirst"] else ("z", "qk")):
              if part == "z":
                for gi in range(2):
                    wb, wkey = next_w((l, "in", C_Z + 512 * gi))
                    for c in range(NCH):
                        bank, key = PB.get()
                        for k in range(8):
                            A("pe", lambda e, k=k, c=c, wb=wb, bank=bank: e.matmul(bank[:, :], lhsT=uT[:, k, c * 128:(c + 1) * 128], rhs=wb[:, k, :], start=(k == 0), stop=(k == 7)),
                              reads=[("uT", c), wkey], writes=[key])
                        silu2_evac(bank[:, :], zs[:, c, gi * 512:(gi + 1) * 512], key, ("zs", c), 512)
                        if gi == 1:
                            GIVE(("zs", c))
                        STEP()
              else:
                for qk in range(2):
                    wb, wkey = next_w((l, "in", C_Q if qk == 0 else C_K))
                    for j in range(4):
                        bank, key = PB.get()
                        for k in range(8):
                            A("pe", lambda e, k=k, j=j, wb=wb, bank=bank: e.matmul(bank[:, 0:N], lhsT=wb[:, k, j * 128:(j + 1) * 128], rhs=uT[:, k, :], start=(k == 0), stop=(k == 7)),
                              reads=uT_all + [wkey], writes=[key])
                        A("act", lambda e, bank=bank, qk=qk, j=j: e.activation(out=qkT[:, qk * 4 + j, :], in_=bank[:, 0:N], func=AF.Copy), reads=[key], writes=[("qkT", qk * 4 + j)])
                        if j == 3:
                            GIVE("q" if qk == 0 else "k")
                        STEP()
            for gi in range(2):
                wb, wkey = next_w((l, "in", C_V + 512 * gi))
                for c in range(NCH):
                    bank, key = PB.get()
                    for k in range(8):
                        A("pe", lambda e, k=k, c=c, wb=wb, bank=bank: e.matmul(bank[:, :], lhsT=uT[:, k, c * 128:(c + 1) * 128], rhs=wb[:, k, :], start=(k == 0), stop=(k == 7)),
                          reads=[("uT", c), wkey], writes=[key])
                    A("act", lambda e, bank=bank, c=c, gi=gi: e.activation(out=vtok[:, c, gi * 512:(gi + 1) * 512], in_=bank[:, :], func=AF.Copy), reads=[key], writes=[("vtok", c)])
                    if gi == 1:
                        GIVE(("v", c))
                    STEP()
            for gi in range(2):
                wb, wkey = next_w((l, "in", C_G + 512 * gi))
                for j in range(4):
                    blk = gi * 4 + j
                    bank, key = PB.get()
                    for k in range(8):
                        A("pe", lambda e, k=k, j=j, wb=wb, bank=bank: e.matmul(bank[:, 0:N], lhsT=wb[:, k, j * 128:(j + 1) * 128], rhs=uT[:, k, :], start=(k == 0), stop=(k == 7)),
                          reads=uT_all + [wkey], writes=[key])
                    gw, gkey = GW.get()
                    th, tkey = TH.get()
                    A("act", lambda e, bank=bank, gw=gw, blk=blk: e.activation(out=gw[:, 0:N], in_=bank[:, 0:N], func=AF.Copy, scale=gnwh[:, l * 2 + (blk % 2):l * 2 + (blk % 2) + 1]),
                      reads=[key, "gnwh"], writes=[gkey])
                    A("act", lambda e, bank=bank, th=th: e.activation(out=th[:, 0:N], in_=bank[:, 0:N], func=AF.Tanh, scale=0.5), reads=[key], writes=[tkey])
                    A("dve", lambda e, gw=gw, th=th, blk=blk: e.scalar_tensor_tensor(out=sgT[:, blk, :], in0=th[:, 0:N], scalar=1.0, in1=gw[:, 0:N], op0=ALU.add, op1=ALU.mult),
                      reads=[gkey, tkey], writes=[("sgT", blk)])
                    if blk == 7:
                        GIVE("g")
                        GIVE("Fdone")
                    STEP()
            sH = []
            REC[0] = sH
            PB.ring = PBh
            YDK = {}

            def H_front(c):
                cs = slice(c * 128, (c + 1) * 128)
                for hv in range(2):
                    for hh in range(8):
                        hd = hv * 8 + hh
                        ai = hd if OPT["at_bf16"] else hh
                        if OPT["at_split"] and hh % 2 == 1:
                            A("pool", lambda e, ai=ai, hd=hd, c=c: e.tensor_scalar(out=At[:, ai, :], in0=strict[:], scalar1=da_t[:, c, hd:hd + 1], scalar2=0.0, op0=ALU.mult, op1=ALU.add),
                              reads=["strict", ("da", c)], writes=[("At", ai)])
                        else:
                            A("dve", lambda e, ai=ai, hd=hd, c=c: e.tensor_scalar(out=At[:, ai, :], in0=strict[:], scalar1=da_t[:, c, hd:hd + 1], scalar2=None, op0=ALU.mult),
                              reads=["strict", ("da", c)], writes=[("At", ai)])
                    for qd in range(2):
                        bank, key = PB.get()
                        for hq in range(4):
                            hh = qd * 4 + hq
                            ai = (hv * 8 + hh) if OPT["at_bf16"] else hh
                            A("pe", lambda e, bank=bank, hq=hq, ai=ai: e.matmul(bank[:, hq * 128:(hq + 1) * 128], lhsT=At[:, ai, :], rhs=(maskb[:] if OPT["at_bf16"] else tri[:]), start=True, stop=True),
                              reads=[("At", ai), "tri", "maskb"], writes=[key])
                        h0 = hv * 8 + qd * 4
                        A("act", lambda e, bank=bank, h0=h0: e.activation(out=Lm[:, h0:h0 + 4, :], in_=bank[:, :].rearrange("p (a t) -> p a t", a=4), func=AF.Exp),
                          reads=[key], writes=[("Lm", h0 // 4)])
                        STEP()
                bank, key = PB.get()
                for g in range(4):
                    A("pe", lambda e, bank=bank, g=g, cs=cs: e.matmul(bank[:, g * 128:(g + 1) * 128], lhsT=BCT[:, g, cs], rhs=BCT[:, 4 + g, cs], start=True, stop=True),
                      reads=[("BCT", g), ("BCT", 4 + g)], writes=[key])
                A("dve", lambda e, bank=bank: e.tensor_tensor(out=scm[:], in0=bank[:, :].rearrange("p (g t) -> p g t", g=4), in1=bc(maskq[:, :].unsqueeze(1), [128, 4, 128]), op=ALU.mult),
                  reads=[key, "maskq"], writes=["scm"])
                A("dve", lambda e: e.tensor_tensor(out=MT[:].rearrange("p (g a) t -> p g a t", g=4), in0=Lm[:].rearrange("p (g a) t -> p g a t", g=4),
                                                   in1=bc(scm[:].unsqueeze(2), [128, 4, 4, 128]), op=ALU.mult),
                  reads=["scm"] + [("Lm", i) for i in range(4)], writes=["MT"])
                STEP()
                x3 = x2[:, c, :].rearrange("p (a d) -> p a d", a=16)
                A("pool", lambda e, x3=x3, c=c: e.tensor_tensor(out=xdt[:].rearrange("p (a d) -> p a d", a=16), in0=x3, in1=bc(dth[:, c, :].unsqueeze(2), [128, 16, 64]), op=ALU.mult),
                  reads=[("x2", c), ("dth", c)], writes=["xdt"])
                if not OPT["hpipe"]:
                    A("pool", lambda e, x3=x3, c=c: e.tensor_tensor(out=xw[:].rearrange("p (a d) -> p a d", a=16), in0=x3, in1=bc(wq[:, c, :].unsqueeze(2), [128, 16, 64]), op=ALU.mult),
                      reads=[("x2", c), ("wq", c)], writes=["xw"])
                A("pool", lambda e, x3=x3: e.tensor_tensor(out=xd[:].rearrange("p (a d) -> p a d", a=16), in0=x3, in1=bc(dskh_bc[:, l, :].unsqueeze(2), [128, 16, 64]), op=ALU.mult),
                  reads=[("x2", c), "dskh"], writes=["xd"])
                if OPT["stmp_early"]:
                    A("pool", lambda e, c=c: e.tensor_tensor(out=stmp[:].rearrange("p (a d) -> p a d", a=16), in0=Sssd[l][:].rearrange("p (a d) -> p a d", a=16),
                                                             in1=bc(cdb[:, c, :].unsqueeze(2), [128, 16, 64]), op=ALU.mult),
                      reads=[("Sssd", l), ("cdb", c)], writes=["stmp", "stmp2"])
                STEP()
                ydk = []
                YDK[c] = ydk
                for hb in range(2):
                    bank, key = YD.get() if OPT["hpipe"] else PB.get()
                    ydk.append((bank, key))
                    A("pe", lambda e, bank=bank, hb=hb: e.matmul(bank[:, :], lhsT=identb[:], rhs=xd[:, hb * 512:(hb + 1) * 512], start=True, stop=False),
                      reads=["identb", "xd"], writes=[key])
                    for h8 in range(8):
                        hd = hb * 8 + h8
                        A("pe", lambda e, bank=bank, h8=h8, hd=hd: e.matmul(bank[:, h8 * 64:(h8 + 1) * 64], lhsT=MT[:, hd, :], rhs=xdt[:, hd * 64:(hd + 1) * 64], start=False, stop=(h8 == 7)),
                          reads=["MT", "xdt"], writes=[key])
                    STEP()

            def H_mid(c):
                cs = slice(c * 128, (c + 1) * 128)
                ydk = YDK[c]
                for hb in range(2):
                    bank, key = PB.get()
                    for gg in range(2):
                        g = hb * 2 + gg
                        A("pe", lambda e, bank=bank, gg=gg, g=g, cs=cs: e.matmul(bank[:, gg * 256:(gg + 1) * 256], lhsT=BCT[:, 4 + g, cs], rhs=Sbf[l][:, g * 256:(g + 1) * 256], start=True, stop=True),
                          reads=[("BCT", 4 + g), ("Sbf", l)], writes=[key])
                    hs = slice(hb * 512, (hb + 1) * 512)
                    A("dve", lambda e, bank=bank, hb=hb, hs=hs, c=c: e.tensor_tensor(out=t1[:, hs].rearrange("p (a d) -> p a d", a=8), in0=bank[:, :].rearrange("p (a d) -> p a d", a=8),
                                                                                  in1=bc(ecumh[:, c, hb * 8:(hb + 1) * 8].unsqueeze(2), [128, 8, 64]), op=ALU.mult),
                      reads=[key, ("ecumh", c)], writes=[("t1", hb)])
                    ybank, ykey = ydk[hb]
                    A("dve", lambda e, ybank=ybank, hs=hs: e.tensor_tensor(out=t1[:, hs], in0=t1[:, hs], in1=ybank[:, :], op=ALU.add),
                      reads=[ykey, ("t1", hb)], writes=[("t1", hb)])
                    NEED(("zs", c))
                    A(OPT["zs_eng"], lambda e, hs=hs, c=c: e.tensor_tensor(out=t1[:, hs], in0=t1[:, hs], in1=zs[:, c, hs], op=ALU.mult),
                      reads=[("t1", hb), ("zs", c)], writes=[("t1", hb)])
                    STEP()
                for g in range(4):
                    A("act", lambda e, g=g: e.activation(out=yn[:, g * 256:(g + 1) * 256], in_=t1[:, g * 256:(g + 1) * 256], func=AF.Square, accum_out=sq4[:, g:g + 1]),
                      reads=[("t1", g // 2)], writes=[("sq4", g), ("yn", g // 2)])
                sq4k = [("sq4", g) for g in range(4)]
                A("dve", lambda e: e.tensor_scalar(out=r4[:], in0=sq4[:], scalar1=1.0 / 256, scalar2=4 * EPS, op0=ALU.mult, op1=ALU.add), reads=sq4k, writes=["r4"])
                A("pool", lambda e: e.tensor_tensor(out=r4[:], in0=r4[:], in1=bc(mhalf[:, 0:1], [128, 4]), op=ALU.pow), reads=["r4", "mhalf"], writes=["r4"])
                for g in range(4):
                    gs = slice(g * 256, (g + 1) * 256)
                    if g % 2 == 0:
                        A("dve", lambda e, g=g, gs=gs: e.tensor_scalar(out=yn[:, gs], in0=t1[:, gs], scalar1=r4[:, g:g + 1], scalar2=None, op0=ALU.mult),
                          reads=["r4", ("t1", g // 2)], writes=[("yn", g // 2)])
                    else:
                        A("act", lambda e, g=g, gs=gs: e.activation(out=yn[:, gs], in_=t1[:, gs], func=AF.Copy, scale=r4[:, g:g + 1]),
                          reads=["r4", ("t1", g // 2)], writes=[("yn", g // 2)])
                STEP()

            def H_tail(c):
                if OPT["hpipe"]:
                    x3 = x2[:, c, :].rearrange("p (a d) -> p a d", a=16)
                    A("pool", lambda e, x3=x3, c=c: e.tensor_tensor(out=xw[:].rearrange("p (a d) -> p a d", a=16), in0=x3, in1=bc(wq[:, c, :].unsqueeze(2), [128, 16, 64]), op=ALU.mult),
                      reads=[("x2", c), ("wq", c)], writes=["xw"])
                bank, key = PT.get()
                for k in range(8):
                    A("pe", lambda e, k=k, bank=bank: e.transpose(out=bank[:, k * 128:(k + 1) * 128], in_=yn[:, k * 128:(k + 1) * 128], identity=identb[:]),
                      reads=[("yn", 0), ("yn", 1), "identb"], writes=[key])
                A("dve", lambda e, bank=bank, c=c: e.tensor_tensor(out=xbcT[:, 0:8, 4 + c * 128:4 + (c + 1) * 128], in0=bank[:, :].rearrange("p (k t) -> p k t", k=8),
                                                                    in1=bc(ssdnw[:, l * 8:(l + 1) * 8].unsqueeze(2), [128, 8, 128]), op=ALU.mult),
                  reads=[key, "ssdnw"], writes=[("xbcT", b) for b in range(8)])
                if c == NCH - 1:
                    GIVE("ysT")
                STEP()
                if not OPT["stmp_early"]:
                    A("pool", lambda e, c=c: e.tensor_tensor(out=stmp[:].rearrange("p (a d) -> p a d", a=16), in0=Sssd[l][:].rearrange("p (a d) -> p a d", a=16),
                                                             in1=bc(cdb[:, c, :].unsqueeze(2), [128, 16, 64]), op=ALU.mult),
                      reads=[("Sssd", l), ("cdb", c)], writes=["stmp", "stmp2"])
                for hb in range(2):
                    bank, key = PB.get()
                    for gg in range(2):
                        g = hb * 2 + gg
                        A("pe", lambda e, bank=bank, gg=gg, g=g, c=c: e.matmul(bank[:, gg * 256:(gg + 1) * 256], lhsT=B2tok[:, c, g * 128:(g + 1) * 128], rhs=xw[:, g * 256:(g + 1) * 256], start=True, stop=True),
                          reads=[("B2tok", c), "xw"], writes=[key])
                    hs = slice(hb * 512, (hb + 1) * 512)
                    if OPT["sbf_dve"]:
                        A("dve", lambda e, bank=bank, hs=hs: e.tensor_tensor(out=Sbf[l][:, hs], in0=stmp[:, hs], in1=bank[:, :], op=ALU.add),
                          reads=[key, "stmp" if hb == 0 else "stmp2"], writes=[("Sbf", l)])
                    A("dve", lambda e, bank=bank, hs=hs: e.tensor_tensor(out=Sssd[l][:, hs], in0=stmp[:, hs], in1=bank[:, :], op=ALU.add),
                      reads=[key, "stmp" if hb == 0 else "stmp2"], writes=[("Sssd", l)])
                if not OPT["sbf_dve"]:
                    A("act", lambda e: e.activation(out=Sbf[l][:], in_=Sssd[l][:], func=AF.Copy), reads=[("Sssd", l)], writes=[("Sbf", l)])
                STEP()

            if OPT["hpipe"]:
                H_front(0)
                H_mid(0)
                for c in range(1, NCH):
                    H_front(c)
                    H_tail(c - 1)
                    H_mid(c)
                H_tail(NCH - 1)
            else:
                for c in range(NCH):
                    H_front(c)
                    H_mid(c)
                    H_tail(c)

            REC[0] = None
            PB.ring = PBfull
            given = set()
            if not OPT["s4"]:
                merge([sH, sF], given)
            MARK("P3")

            def merge_branch(b):
                for j in range(2):
                    wg, wgkey = next_w((l, "in", C_M + b * 1024 + j * 512))
                    for jj in range(4):
                        bank, key = PB.get()
                        for k in range(8):
                            A("pe", lambda e, k=k, jj=jj, wg=wg, bank=bank: e.matmul(bank[:, 0:N], lhsT=wg[:, k, jj * 128:(jj + 1) * 128], rhs=uT[:, k, :], start=(k == 0), stop=(k == 7)),
                              reads=uT_all + [wgkey], writes=[key])
                        A("act", lambda e, bank=bank, jj=jj: e.activation(out=gth[b][:, jj, :], in_=bank[:, 0:N], func=AF.Tanh, scale=0.5), reads=[key], writes=[("gth", jj)])
                        STEP()
                    wbr, wbkey = next_w((l, "br", (b, j)))
                    if b == 0:
                        NEED("ysT")
                    for jj in range(4):
                        bank, key = PB.get()
                        for k in range(8):
                            A("pe", lambda e, k=k, jj=jj, wbr=wbr, bank=bank: e.matmul(bank[:, 0:N], lhsT=wbr[:, k, jj * 128:(jj + 1) * 128], rhs=xbcT[:, b * 8 + k, 4:4 + N], start=(k == 0), stop=(k == 7)),
                              reads=[("xbcT", b * 8 + k) for k in range(8)] + [wbkey], writes=[key])
                        if b == 0:
                            A("dve", lambda e, bank=bank, jj=jj, j=j: e.scalar_tensor_tensor(out=t0m[:, j * 4 + jj, :], in0=gth[0][:, jj, :], scalar=1.0, in1=bank[:, 0:N], op0=ALU.add, op1=ALU.mult),
                              reads=[key, ("gth", jj)], writes=[("t0m", j * 4 + jj)])
                        else:
                            t1m, t1key = T1M.get()
                            A("dve", lambda e, bank=bank, jj=jj, t1m=t1m: e.scalar_tensor_tensor(out=t1m[:, :], in0=gth[1][:, jj, :], scalar=1.0, in1=bank[:, 0:N], op0=ALU.add, op1=ALU.mult),
                              reads=[key, ("gth", jj)], writes=[t1key])
                            A("pool", lambda e, jj=jj, j=j, t1m=t1m: e.tensor_tensor(out=qkT[:, j * 4 + jj, :], in0=t0m[:, j * 4 + jj, :], in1=t1m[:, :], op=ALU.add),
                              reads=[t1key, ("t0m", j * 4 + jj)], writes=[("qkT", j * 4 + jj)])
                        STEP()

            sJ = []
            if not OPT["s4"]:
                REC[0] = sJ
                PB.ring = PBf
                NEED("Fdone")
                merge_branch(0)
            sI = []
            REC[0] = sI
            PB.ring = PBi
            for c in range(NCH):
                cs = slice(c * 128, (c + 1) * 128)
                if not OPT["gprepB"]:
                    bank, key = PB.get()
                    A("pe", lambda e, bank=bank, cs=cs: e.matmul(bank[:, :], lhsT=alrT[:, cs], rhs=w2b[:, l * 512:(l + 1) * 512], start=True, stop=False),
                      reads=["alrT", "w2b"], writes=[key])
                    A("pe", lambda e, bank=bank: e.matmul(bank[:, :], lhsT=sel[:, 2, :], rhs=rows[:, l * 512:(l + 1) * 512], start=False, stop=False),
                      reads=["sel", "rows"], writes=[key])
                    A("pe", lambda e, bank=bank: e.matmul(bank[:, :], lhsT=sel[:, 2, :], rhs=rows[:, 1024 + l * 512:1024 + (l + 1) * 512], start=False, stop=True),
                      reads=["sel", "rows"], writes=[key])
                    A("act", lambda e, bank=bank: e.activation(out=e1[:], in_=bank[:, :], func=AF.Exp, scale=-1.0), reads=[key], writes=["e1"])
                    A("act", lambda e: e.activation(out=e1[:], in_=e1[:], func=AF.Ln, bias=1.0), reads=["e1"], writes=["e1"])
                bankg, keyg = PB.get()
                for j in range(4):
                    if OPT["gprepB"]:
                        A("pe", lambda e, bankg=bankg, j=j, c=c: e.matmul(bankg[:, j * 128:(j + 1) * 128], lhsT=l1b[:, c, j * 128:(j + 1) * 128], rhs=tri16b[:], start=True, stop=True),
                          reads=[("l1b", c), "tri16b"], writes=[keyg])
                    else:
                        A("pe", lambda e, bankg=bankg, j=j: e.matmul(bankg[:, j * 128:(j + 1) * 128], lhsT=e1[:, j * 128:(j + 1) * 128], rhs=tri16[:], start=True, stop=True),
                          reads=["e1", "tri16"], writes=[keyg])
                A("act", lambda e, bankg=bankg: e.activation(out=eg[:], in_=bankg[:, :], func=AF.Exp, bias=QS_LN), reads=[keyg], writes=["eg"])
                A("act", lambda e, bankg=bankg: e.activation(out=emg[:], in_=bankg[:, :], func=AF.Exp, scale=-1.0), reads=[keyg], writes=["emg"])
                A("act", lambda e, bankg=bankg: e.activation(out=egl[:, :], in_=bankg[:, :].rearrange("p (j t) -> p j t", j=4)[:, :, 127], func=AF.Exp), reads=[keyg], writes=["egl"])
                NEED("q")
                NEED("k")
                A("dve", lambda e, cs=cs: e.tensor_tensor(out=qtl[:], in0=qkT[:, 0:4, cs], in1=eg[:].rearrange("p (j t) -> p j t", j=4), op=ALU.mult),
                  reads=["eg"] + [("qkT", j) for j in range(4)], writes=["qtl"])
                A("dve", lambda e, cs=cs: e.tensor_tensor(out=ktl[:], in0=qkT[:, 4:8, cs], in1=emg[:].rearrange("p (j t) -> p j t", j=4), op=ALU.mult),
                  reads=["emg"] + [("qkT", 4 + j) for j in range(4)], writes=["ktl"])
                STEP()
                bank, key = PB.get()
                for hd in range(4):
                    A("pe", lambda e, bank=bank, hd=hd: e.matmul(bank[:, hd * 128:(hd + 1) * 128], lhsT=ktl[:, hd, :], rhs=qtl[:, hd, :], start=True, stop=True),
                      reads=["ktl", "qtl"], writes=[key])
                A("dve", lambda e, bank=bank: e.tensor_tensor(out=attm[:], in0=bank[:, :].rearrange("p (g t) -> p g t", g=4), in1=bc(maskb[:, :].unsqueeze(1), [128, 4, 128]), op=ALU.mult),
                  reads=[key, "maskb"], writes=["attm"])
                STEP()
                bankt, keyt = PT.get()
                for hd in range(4):
                    A("pe", lambda e, bankt=bankt, hd=hd: e.transpose(out=bankt[:, hd * 128:(hd + 1) * 128], in_=ktl[:, hd, :], identity=identb[:]),
                      reads=["ktl", "identb"], writes=[keyt])
                A("act", lambda e, bankt=bankt: e.activation(out=ktok[:], in_=bankt[:, 0:512], func=AF.Copy), reads=[keyt], writes=["ktok"])
                STEP()
                obk = []
                NEED(("v", c))
                for hb in range(2):
                    bank, key = PB.get()
                    obk.append((bank, key))
                    for hh in range(2):
                        hd = hb * 2 + hh
                        o = bank[:, hh * 256:(hh + 1) * 256]
                        A("pe", lambda e, o=o, hd=hd, c=c: e.matmul(o, lhsT=attm[:, hd, :], rhs=vtok[:, c, hd * 256:(hd + 1) * 256], start=True, stop=False),
                          reads=["attm", ("vtok", c)], writes=[key])
                        A("pe", lambda e, o=o, hd=hd: e.matmul(o, lhsT=qtl[:, hd, :], rhs=Gbf[l][:, hd * 256:(hd + 1) * 256], start=False, stop=True),
                          reads=["qtl", ("Gbf", l)], writes=[key])
                        A("act", lambda e, o=o, hd=hd: e.activation(out=on[:, hd * 256:(hd + 1) * 256], in_=o, func=AF.Square, accum_out=sq4g[:, hd:hd + 1]), reads=[key], writes=[("sq4g", hd), ("on", hd // 2)])
                    STEP()
                A("dve", lambda e: e.tensor_scalar(out=r4g[:], in0=sq4g[:], scalar1=1.0 / 256, scalar2=EPS, op0=ALU.mult, op1=ALU.add), reads=[("sq4g", i) for i in range(4)], writes=["r4g"])
                A("pool", lambda e: e.tensor_tensor(out=r4g[:], in0=r4g[:], in1=bc(mhalf[:, 0:1], [128, 4]), op=ALU.pow), reads=["r4g", "mhalf"], writes=["r4g"])
                for hb in range(2):
                    bank, key = obk[hb]
                    A("dve", lambda e, bank=bank, hb=hb: e.tensor_tensor(out=on[:, hb * 512:(hb + 1) * 512].rearrange("p (a d) -> p a d", a=2), in0=bank[:, :].rearrange("p (a d) -> p a d", a=2),
                                                                          in1=bc(r4g[:, hb * 2:(hb + 1) * 2].unsqueeze(2), [128, 2, 256]), op=ALU.mult),
                      reads=[key, "r4g"], writes=[("on", hb)])
                STEP()
                NEED("g")
                bank, key = PT.get()
                for k in range(8):
                    A("pe", lambda e, k=k, bank=bank: e.transpose(out=bank[:, k * 128:(k + 1) * 128], in_=on[:, k * 128:(k + 1) * 128], identity=identb[:]),
                      reads=[("on", 0), ("on", 1), "identb"], writes=[key])
                A("dve", lambda e, bank=bank, cs=cs, c=c: e.tensor_tensor(out=xbcT[:, 8:16, 4 + c * 128:4 + (c + 1) * 128], in0=bank[:, :].rearrange("p (k t) -> p k t", k=8),
                                                                         in1=sgT[:, :, cs], op=ALU.mult),
                  reads=[key] + [("sgT", b) for b in range(8)], writes=[("xbcT", 8 + b) for b in range(8)])
                STEP()
                for hb in range(2):
                    bank, key = PB.get()
                    for hh in range(2):
                        hd = hb * 2 + hh
                        A("pe", lambda e, bank=bank, hh=hh, hd=hd, c=c: e.matmul(bank[:, hh * 256:(hh + 1) * 256], lhsT=ktok[:, hd * 128:(hd + 1) * 128], rhs=vtok[:, c, hd * 256:(hd + 1) * 256], start=True, stop=True),
                          reads=["ktok", ("vtok", c)], writes=[key])
                    hs = slice(hb * 512, (hb + 1) * 512)
                    A("dve", lambda e, bank=bank, hs=hs: e.tensor_tensor(out=Sgla[l][:, hs], in0=Sgla[l][:, hs], in1=bank[:, :], op=ALU.add),
                      reads=[key, ("Sgla", l, hb)], writes=[("Sgla", l, hb)])
                for hd in range(4):
                    A("act", lambda e, hd=hd: e.activation(out=Gbf[l][:, hd * 256:(hd + 1) * 256], in_=Sgla[l][:, hd * 256:(hd + 1) * 256], func=AF.Copy, scale=egl[:, hd:hd + 1]),
                      reads=[("Sgla", l, hd // 2), "egl"], writes=[("Gbf", l)])
                for hd in range(4):
                    A("pool", lambda e, hd=hd: e.tensor_scalar(out=Sgla[l][:, hd * 256:(hd + 1) * 256], in0=Sgla[l][:, hd * 256:(hd + 1) * 256], scalar1=egl[:, hd:hd + 1], scalar2=0.0, op0=ALU.mult, op1=ALU.add),
                      reads=[("Sgla", l, hd // 2), "egl"], writes=[("Sgla", l, hd // 2)])
                STEP()
            REC[0] = None
            PB.ring = PBfull
            if OPT["s4"]:
                merge([sH, sI, sF], given)
                merge_branch(0)
            else:
                merge([sI, sJ], given)
            MARK("P4")
            merge_branch(1)
            MARK("OUT")
            if OPT["outc"]:
                wos = [next_w((l, "out", 0), pf=1), next_w((l, "out", 1), pf=0)]
                oorder = [(c, j2) for c in range(NCH) for j2 in range(2)]
            else:
                wos = [None, None]
                oorder = [(c, j2) for j2 in range(2) for c in range(NCH)]
            for c, j2 in oorder:
                if True:
                    if wos[j2] is None:
                        wos[j2] = next_w((l, "out", j2))
                    wo, wokey = wos[j2]
                    bank, key = PB.get()
                    for k in range(8):
                        A("pe", lambda e, k=k, c=c, wo=wo, bank=bank: e.matmul(bank[:, :], lhsT=qkT[:, k, c * 128:(c + 1) * 128], rhs=wo[:, k, :], start=(k == 0), stop=(k == 7)),
                          reads=[("qkT", k) for k in range(8)] + [wokey], writes=[key])
                    A("dve", lambda e, bank=bank, c=c, j2=j2: e.scalar_tensor_tensor(out=h[:, c, j2 * 512:(j2 + 1) * 512], in0=bank[:, :], scalar=0.5, in1=h[:, c, j2 * 512:(j2 + 1) * 512], op0=ALU.mult, op1=ALU.add),
                      reads=[key, ("h", c)], writes=[("h", c)])
            if first and not last_layer:
                A("pool", lambda e: e.memset(h[0:112, 0, :], 0.0), writes=[("h", 0)])

        for ti in range(ntiles):
            for c in range(NCH):
                gc = ti * NCH + c
                if gc == 0:
                    A("pool", lambda e: e.memset(h[:, 0, :], 0.0), writes=[("h", 0)])
                    A("sp", lambda e: e.dma_start(out=h[112:128, 0, :], in_=meta_d), writes=[("h", 0)], dma_sem=xsems[0])
                else:
                    r0 = (gc - 1) * 128
                    A("sp", lambda e, c=c, r0=r0: e.dma_start(out=h[:, c, :], in_=x_d[r0:r0 + 128, :]), writes=[("h", c)], dma_sem=xsems[c])
            for l in range(NL):
                tile_layer(ti, l)
            for c in range(NCH):
                gc = ti * NCH + c
                if gc == 0:
                    continue
                r0 = (gc - 1) * 128
                ub, ukey = UB.get()
                A("act", lambda e, c=c, ub=ub: e.activation(out=ub[:], in_=h[:, c, :], func=AF.Square, accum_out=ssq[:, c:c + 1]), reads=[("h", c)], writes=[("ssq", c), ukey])
                A("dve", lambda e, c=c: e.tensor_scalar(out=rs[:, c:c + 1], in0=ssq[:, c:c + 1], scalar1=1.0 / D, scalar2=EPS, op0=ALU.mult, op1=ALU.add),
                  reads=[("ssq", c)], writes=[("rs", c)])
                A("pool", lambda e, c=c: e.tensor_tensor(out=rs[:, c:c + 1], in0=rs[:, c:c + 1], in1=mhalf[:], op=ALU.pow), reads=[("rs", c), "mhalf"], writes=[("rs", c)])
                A("dve", lambda e, c=c: e.scalar_tensor_tensor(out=h[:, c, :], in0=h[:, c, :], scalar=rs[:, c:c + 1], in1=fnw_bc[:], op0=ALU.mult, op1=ALU.mult),
                  reads=[("h", c), ("rs", c), "fnw"], writes=[("h", c)])
                ok = ("out", gc)
                out_keys.append(ok)
                A("sp", lambda e, c=c, r0=r0: e.dma_start(out=out_d[r0:r0 + 128, :], in_=h[:, c, :]), reads=[("h", c)], writes=[ok], dma_sem=osems[c])
        A("sp", None, reads=out_keys)
        assert wstate["used"] == len(wlist)
        build.model_us = dict(SIM["free"])
        build.model_log = SIM["log"]
        build.mlog = SIM["mlog"]
        with nc.Block() as block:
            S.emit(block)
    return nc


def host_consts():
    i = np.arange(128)
    tri = (i[:, None] <= i[None, :]).astype(np.float32)
    strict = (i[:, None] > i[None, :]).astype(np.float32)
    mask0 = (i >= 112).astype(np.float32)[:, None]
    sel = np.zeros((128, 3, 128), np.float32)
    for r in range(3):
        sel[r, r, :] = 1.0
    return {"ident": np.eye(128, dtype=np.float32), "tri": tri, "strict": strict, "mask0": mask0, "sel": sel.reshape(128, 384)}


def make_in_maps(inputs, ncores, TR):
    f = lambda a: np.ascontiguousarray(np.asarray(a, dtype=np.float32))
    common = dict(host_consts())
    common["meta"] = f(inputs["meta_tokens"])
    common["w_in"] = f(inputs["w_in"])
    common["w_br"] = f(inputs["w_branch"])
    common["w_out"] = f(inputs["w_out"])
    common["normw"] = f(np.asarray(inputs["norm_w"]).reshape(2, 8, 128).transpose(2, 0, 1).reshape(128, 16))
    common["cw"] = f(np.asarray(inputs["conv_w"]).reshape(2, 4, 16, 128).transpose(3, 0, 2, 1).reshape(128, 128))
    common["cb"] = f(np.asarray(inputs["conv_b"]).reshape(1, 4096))
    common["dtb"] = f(inputs["dt_bias"])
    common["alog"] = f(inputs["a_log"])
    common["dskip"] = f(inputs["d_skip"])
    common["ssdnw"] = f(np.asarray(inputs["ssd_norm_w"]).reshape(2, 8, 128).transpose(2, 0, 1).reshape(128, 16))
    common["w2"] = f(np.asarray(inputs["gla_gate_w2"]).transpose(1, 0, 2).reshape(16, 1024))
    common["gb"] = f(np.asarray(inputs["gla_gate_b"]).reshape(1, 1024))
    common["gnw"] = f(np.asarray(inputs["gla_norm_w"]).reshape(2, 2, 128).transpose(2, 0, 1).reshape(128, 4))
    common["fnw"] = f(np.asarray(inputs["final_norm_w"]).reshape(1, 1024))
    x = np.asarray(inputs["x"], dtype=np.float32)
    maps = []
    for b in range(ncores):
        m = dict(common)
        m["x"] = np.ascontiguousarray(x[b, :TR])
        maps.append(m)
    return maps


_NC_CACHE = {}


def kernel(**inputs):
    x = np.asarray(inputs["x"])
    B, T, _ = x.shape
    NT = T // 128 + 1
    key = (NT,)
    if key not in _NC_CACHE:
        _NC_CACHE[key] = build(NT=NT, NCH=3, NL=2)
    nc = _NC_CACHE[key]
    maps = make_in_maps(inputs, B, T)
    res = run_bass_kernel_spmd(nc, maps, core_ids=list(range(B)))
    out = np.stack([np.asarray(r["out"], dtype=np.float32) for r in res.results], axis=0)
    return out
```

```python
import math
import numpy as np
import concourse.bass as bass
import concourse.mybir as mybir
from concourse.bass_utils import run_bass_kernel_spmd
from contextlib import ExitStack

F32 = mybir.dt.float32
BF16 = mybir.dt.bfloat16
AF = mybir.ActivationFunctionType
ALU = mybir.AluOpType

D = 1024
NIN = 8224
EPS = 1e-6
C_Z, C_XBC, C_DT, C_Q, C_K, C_V, C_G, C_ALR, C_M = 0, 1024, 3072, 3088, 3600, 4112, 5136, 6160, 6176


class Op:
    __slots__ = ("eng", "fn", "deps", "sig", "count", "sem", "is_dma", "inc", "fin")

    def __init__(self, eng, fn, is_dma=False):
        self.eng = eng
        self.fn = fn
        self.deps = set()
        self.sig = False
        self.count = 0
        self.sem = None
        self.is_dma = is_dma
        self.inc = 1
        self.fin = 0.0


class Sched:
    ENGS = ("pe", "act", "dve", "pool", "sp")

    def __init__(self, nc, stack):
        self.nc = nc
        self.stack = stack
        self.ops = []
        self.res = {}
        self.engsem = {e: stack.enter_context(nc.semaphore("s_" + e)) for e in self.ENGS}
        self.dma_counts = {}
        self.alias = {}

    def new_sem(self, name):
        return self.stack.enter_context(self.nc.semaphore(name))

    def uniq(self, key):
        lst = self.alias.setdefault(key, [])
        k = ("__u", key, len(lst))
        lst.append(k)
        return k

    def add(self, eng, fn, reads=(), writes=(), dma_sem=None, raw_keys=False):
        op = Op(eng, fn, is_dma=dma_sem is not None)
        if not raw_keys:
            reads = [rr for r in reads for rr in self.alias.get(r, [r])]
            writes = [ww for w in writes for ww in self.alias.get(w, [w])]
        if dma_sem is not None:
            op.sem = dma_sem
            op.sig = True
            op.inc = 16
            c = self.dma_counts.get(id(dma_sem), 0) + 16
            self.dma_counts[id(dma_sem)] = c
            op.count = c
        for r in reads:
            st = self.res.get(r)
            if st is None:
                st = [None, []]
                self.res[r] = st
            if st[0] is not None:
                op.deps.add(st[0])
            st[1].append(op)
        for w in writes:
            st = self.res.get(w)
            if st is None:
                st = [None, []]
                self.res[w] = st
            if st[0] is not None:
                op.deps.add(st[0])
            last = {}
            for rd in st[1]:
                if rd is op:
                    continue
                if rd.is_dma:
                    op.deps.add(rd)
                else:
                    last[rd.eng] = rd
            for rd in last.values():
                op.deps.add(rd)
            st[0] = op
            st[1] = []
        op.deps.discard(op)
        self.ops.append(op)
        return op

    def peek(self, reads, writes):
        reads = [rr for r in reads for rr in self.alias.get(r, [r])]
        writes = [ww for w in writes for ww in self.alias.get(w, [w])]
        deps = []
        for r in reads:
            st = self.res.get(r)
            if st is not None and st[0] is not None:
                deps.append(st[0])
        for w in writes:
            st = self.res.get(w)
            if st is None:
                continue
            if st[0] is not None:
                deps.append(st[0])
            last = {}
            for rd in st[1]:
                if rd.is_dma:
                    deps.append(rd)
                else:
                    last[rd.eng] = rd
            deps.extend(last.values())
        return deps

    def finalize(self):
        for op in self.ops:
            if op.eng == "pe" and not op.is_dma:
                op.deps = {d for d in op.deps if not (d.eng == "pe" and not d.is_dma)}
            for d in op.deps:
                d.sig = True
        cnt = {e: 0 for e in self.ENGS}
        for op in self.ops:
            if op.is_dma:
                continue
            if op.sig:
                cnt[op.eng] += 1
                op.count = cnt[op.eng]
                op.sem = self.engsem[op.eng]

    def emit(self, block):
        self.finalize()
        by_eng = {e: [o for o in self.ops if o.eng == e] for e in self.ENGS}

        def run(engine, ops):
            waited = {}
            for op in ops:
                need = {}
                for d in op.deps:
                    k = id(d.sem)
                    if need.get(k, (None, 0))[1] < d.count:
                        need[k] = (d.sem, d.count)
                for k, (sem, c) in need.items():
                    if waited.get(k, 0) < c:
                        engine.wait_ge(sem, c)
                        waited[k] = c
                if op.fn is None:
                    continue
                ins = op.fn(engine)
                if op.sig:
                    ins.then_inc(op.sem, op.inc)

        @block.tensor
        def _(e):
            run(e, by_eng["pe"])

        @block.scalar
        def _(e):
            run(e, by_eng["act"])

        @block.vector
        def _(e):
            run(e, by_eng["dve"])

        @block.gpsimd
        def _(e):
            run(e, by_eng["pool"])

        @block.sync
        def _(e):
            run(e, by_eng["sp"])


class Ring:
    def __init__(self, items, name, keys=None):
        self.items = items
        self.name = name
        self.keys = keys if keys is not None else [(name, j) for j in range(len(items))]
        self.i = 0

    def get(self):
        j = self.i % len(self.items)
        self.i += 1
        return self.items[j], self.keys[j]


class RingProxy:
    def __init__(self, ring):
        self.ring = ring

    def get(self):
        return self.ring.get()


import os
OPT = {"zs_eng": "pool", "diag_eng": "dve", "ub_eng": "dve", "greedy": 0, "outc": 1, "b2late": 0, "at_split": 1, "hb": 3, "s4": 0, "qkfirst": 0, "stmp_early": 0, "sbf_dve": 0, "hpipe": 1, "gprepB": 1, "at_bf16": 1, "ipipe": 0, "fbias": -0.08, "fbias2": -0.3}
for _k in list(OPT):
    if os.environ.get("KOPT_" + _k):
        OPT[_k] = type(OPT[_k])(os.environ["KOPT_" + _k])


def build(NT=33, NCH=3, NL=2, dbg=False):
    assert NT % NCH == 0
    N = NCH * 128
    ntiles = NT // NCH
    TR = (NT - 1) * 128
    nc = bass.Bass("TRN2", target_bir_lowering=False)

    def din(name, shape):
        return nc.dram_tensor(name, list(shape), F32, kind="ExternalInput").ap()

    x_d = din("x", [TR, D])
    meta_d = din("meta", [16, D])
    win_d = din("w_in", [2, D, NIN])
    wbr_d = din("w_br", [2, 2, D, D])
    wout_d = din("w_out", [2, D, D])
    normw_d = din("normw", [128, 16])
    cw_d = din("cw", [128, 128])
    cb_d = din("cb", [1, 2 * 2048])
    dtb_d = din("dtb", [2, 16])
    alog_d = din("alog", [2, 16])
    dskip_d = din("dskip", [2, 16])
    ssdnw_d = din("ssdnw", [128, 16])
    w2_d = din("w2", [16, 2 * 512])
    gb_d = din("gb", [1, 2 * 512])
    gnw_d = din("gnw", [128, 4])
    fnw_d = din("fnw", [1, D])
    ident_d = din("ident", [128, 128])
    tri_d = din("tri", [128, 128])
    strict_d = din("strict", [128, 128])
    mask0_d = din("mask0", [128, 1])
    sel_d = din("sel", [128, 3 * 128])
    out_d = nc.dram_tensor("out", [TR, D], F32, kind="ExternalOutput").ap()

    with ExitStack() as st:
        S = Sched(nc, st)

        def sb(name, shape, dt=F32):
            return st.enter_context(nc.sbuf_tensor("s_" + name, list(shape), dt))

        def ps(name, shape, dt=F32):
            return st.enter_context(nc.psum_tensor(name, list(shape), dt))

        REC = [None]

        class _FakeIns:
            def then_inc(self, *a, **k):
                return self

        class _FakeEng:
            def __init__(self):
                self.calls = []

            def __getattr__(self, name):
                def f(*a, **k):
                    self.calls.append((name, a, k))
                    return _FakeIns()
                return f

        def _fsz(ap):
            n = 1
            for d_ in ap.shape[1:]:
                n *= d_
            return n

        def _in_psum(ap):
            return "PSum" in type(ap.tensor).__name__

        def est_dur(eng, fn, is_dma):
            if fn is None:
                return 0.0
            if is_dma:
                return 3.0
            try:
                fe = _FakeEng()
                fn(fe)
                name, a, k = fe.calls[-1]
                if eng == "pe":
                    if name == "transpose":
                        return 0.13
                    rhs = k["rhs"]
                    n = _fsz(rhs)
                    return max(64, n) / 2400.0 * (4.0 if rhs.dtype == F32 else 1.0) + 0.01
                src = k.get("in_", k.get("in0"))
                n = _fsz(src) if src is not None else 256
                if eng == "act":
                    return 0.22 + n * 0.0009 + (0.09 if k.get("accum_out") is not None else 0.0)
                if eng == "dve":
                    t = 0.07 + n * 0.00105
                    if name == "tensor_tensor" and not _in_psum(k["in0"]) and not _in_psum(k["in1"]):
                        t = 0.07 + n * 0.0021
                    return t
                if eng == "pool":
                    if name == "tensor_tensor" and k.get("op") == ALU.pow:
                        return 0.8
                    if name == "tensor_scalar":
                        return 0.1 + n * 0.0015
                    return 0.3 + n * 0.0016
            except Exception:
                pass
            return 0.5

        SIM = {"free": {e: 0.0 for e in Sched.ENGS}, "pebusy": 0.0, "log": [], "dbg": dbg, "mlog": []}

        def MARK(name):
            SIM["log"].append((name, max(SIM["free"].values()), SIM["free"]["pe"], SIM["pebusy"]))

        def _lat(d, eng):
            return 0.35 if d.eng != eng or d.is_dma else 0.12

        def commit(eng, fn, reads, writes, dma_sem, kw, dur):
            op = S.add(eng, fn, reads, writes, dma_sem=dma_sem, **kw)
            t = SIM["free"][eng]
            for d in op.deps:
                if eng == "pe" and d.eng == "pe" and not d.is_dma:
                    continue
                t = max(t, d.fin + _lat(d, eng))
            if dma_sem is not None:
                SIM["free"][eng] = t + 0.1
                op.fin = t + dur
            else:
                op.fin = t + dur
                SIM["free"][eng] = op.fin
                if eng == "pe":
                    SIM["pebusy"] += dur
            return op

        def A(eng, fn, reads=(), writes=(), dma_sem=None, **kw):
            dur = est_dur(eng, fn, dma_sem is not None)
            if REC[0] is None:
                return commit(eng, fn, reads, writes, dma_sem, kw, dur)
            REC[0].append(("op", (eng, fn, list(reads), list(writes), dma_sem, kw, dur)))
            return None

        def STEP():
            if REC[0] is not None:
                REC[0].append(("step",))

        def NEED(tok):
            if REC[0] is not None:
                REC[0].append(("step",))
                REC[0].append(("need", tok))

        def GIVE(tok):
            if REC[0] is not None:
                REC[0].append(("give", tok))

        def step_stall(step):
            free = dict(SIM["free"])
            lw, lr = {}, {}
            stall = 0.0
            for it in step:
                if it[0] != "op":
                    continue
                eng, fn, r, w, ds, kw, dur = it[1]
                t = free[eng]
                t0 = t
                for d in S.peek(r, w):
                    if eng == "pe" and d.eng == "pe" and not d.is_dma:
                        continue
                    t = max(t, d.fin + _lat(d, eng))
                for key in r:
                    if key in lw:
                        t = max(t, lw[key] + 0.3)
                for key in w:
                    if key in lw:
                        t = max(t, lw[key] + 0.3)
                    if key in lr:
                        t = max(t, lr[key] + 0.3)
                stall += (t - t0) * (1.0 if eng == "pe" else 0.35)
                fin = t + dur
                if ds is not None:
                    free[eng] = t + 0.1
                else:
                    free[eng] = fin
                for key in r:
                    lr[key] = max(lr.get(key, 0.0), fin)
                for key in w:
                    lw[key] = fin
                    lr.pop(key, None)
            return stall

        def merge(streams, given, bias=0.0):
            segs = []
            for stq in streams:
                steps = [[]]
                for it in stq:
                    if it[0] == "step":
                        if steps[-1]:
                            steps.append([])
                    else:
                        steps[-1].append(it)
                if not steps[-1]:
                    steps.pop()
                segs.append(steps)
            pos = [0] * len(segs)
            tot = [max(len(x), 1) for x in segs]
            while any(pos[i] < len(segs[i]) for i in range(len(segs))):
                cands = []
                for i in range(len(segs)):
                    if pos[i] >= len(segs[i]):
                        continue
                    step = segs[i][pos[i]]
                    if any(it[0] == "need" and it[1] not in given for it in step):
                        continue
                    cands.append((step_stall(step), i))
                assert cands, "merge deadlock"
                pick = None
                if OPT["greedy"]:
                    for st_, i in cands:
                        if st_ < 0.25:
                            pick = i
                            break
                    if pick is None:
                        pick = min(cands)[1]
                else:
                    pick = min(cands, key=lambda ci: pos[ci[1]] / tot[ci[1]] - (bias if ci[1] > 0 else 0.0))[1]
                for it in segs[pick][pos[pick]]:
                    if it[0] == "op":
                        eng, fn, r, w, ds, kw, dur = it[1]
                        commit(eng, fn, r, w, ds, kw, dur)
                    elif it[0] == "give":
                        given.add(it[1])
                if SIM.get("dbg"):
                    SIM["mlog"].append((pick, pos[pick], [round(c_[0], 2) for c_ in cands], dict(SIM["free"])))
                pos[pick] += 1
        h = sb("h", [128, NCH, D])
        ubs = [sb("ub%d" % i, [128, D], BF16) for i in range(2)]
        UB = Ring(ubs, "ub")
        uT = sb("uT", [128, 8, N], BF16)
        NWB = 3
        wbufs = [sb("wb%d" % i, [128, 8, 512], BF16) for i in range(NWB)]
        wsems = [S.new_sem("wsem%d" % i) for i in range(NWB)]
        wmisc = sb("wmisc", [128, 2, 8, 32], BF16)
        NDG = 4
        dgs = [sb("dg%d" % i, [128, 4, 128], BF16) for i in range(NDG)]
        DG = Ring(dgs, "dg")
        XW = N + 4
        xbcT = sb("xbcT", [128, 16, XW], BF16)
        hist = sb("hist", [128, 2, 16, 4], BF16)
        x2 = sb("x2", [128, NCH, D], BF16)
        B2tok = sb("B2tok", [128, NCH, 512], BF16)
        BCT = sb("BCT", [128, 8, N], BF16)
        zs = sb("zs", [128, NCH, D], BF16)
        qkT = sb("qkT", [128, 8, N], BF16)
        vtok = sb("vtok", [128, NCH, D], BF16)
        sgT = sb("sgT", [128, 8, N], BF16)
        alrT = sb("alrT", [128, N], BF16)
        ths = [sb("th%d" % i, [128, 512], BF16) for i in range(2)]
        TH = Ring(ths, "th")
        gws = [sb("gw%d" % i, [128, 512], BF16) for i in range(2)]
        GW = Ring(gws, "gw")
        ssq = sb("ssq", [128, NCH])
        rs = sb("rs", [128, NCH])
        dtr = sb("dtr", [128, NCH, 16])
        dte_x = sb("dte_x", [128, NCH, 16])
        dt_t = sb("dt_t", [128, NCH, 16])
        da_t = sb("da_t", [128, NCH, 16])
        ecumh = sb("ecumh", [128, NCH, 16])
        dte = sb("dte", [128, NCH, 16])
        cdb = sb("cdb", [128, NCH, 16])
        wq = sb("wq", [128, NCH, 16])
        dth = sb("dth", [128, NCH, 16])
        sq4 = sb("sq4", [128, 4])
        r4 = sb("r4", [128, 4])
        sq4g = sb("sq4g", [128, 4])
        r4g = sb("r4g", [128, 4])
        egl = sb("egl", [128, NCH, 4])
        At = sb("At", [128, 16, 128], BF16) if OPT["at_bf16"] else sb("At", [128, 8, 128])
        Lm = sb("Lm", [128, 16, 128], BF16)
        MT = sb("MT", [128, 16, 128], BF16)
        scm = sb("scm", [128, 4, 128], BF16)
        xdt = sb("xdt", [128, D], BF16)
        xw = sb("xw", [128, D], BF16)
        xd = sb("xd", [128, D], BF16)
        t1 = sb("t1", [128, D])
        yn = sb("yn", [128, D], BF16)
        stmp = sb("stmp", [128, D])
        e1 = sb("e1", [128, 512])
        eg = sb("eg", [128, 512])
        emg = sb("emg", [128, 512])
        on = sb("on", [128, D], BF16)
        qtl = sb("qtl", [128, 4, 128], BF16)
        ktl = sb("ktl", [128, 4, 128], BF16)
        ktoks = [sb("ktok%d" % i, [128, 512], BF16) for i in range(2 if OPT["ipipe"] else 1)]
        attm = sb("attm", [128, 4, 128], BF16)
        gth0 = sb("gth0", [128, 4, N], BF16)
        gth = [gth0, gth0]
        t0m = sb("t0m", [128, 8, N], BF16)
        t1ms = [sb("t1m%d" % i, [128, N]) for i in range(1 if OPT["ipipe"] else 2)]
        T1M = Ring(t1ms, "t1m")
        Sssd = [sb("Sssd%d" % l, [128, D]) for l in range(2)]
        Sbf = [sb("Sbf%d" % l, [128, D], BF16) for l in range(2)]
        Sgla = [sb("Sgla%d" % l, [128, D]) for l in range(2)]
        Gbf = [sb("Gbf%d" % l, [128, D], BF16) for l in range(2)]
        identf = sb("identf", [128, 128])
        identb = sb("identb", [128, 128], BF16)
        tri = sb("tri", [128, 128])
        tri16 = sb("tri16", [128, 128])
        strict = sb("strict", [128, 128])
        maskb = sb("maskb", [128, 128], BF16)
        maskq = sb("maskq", [128, 128], BF16)
        onesf = sb("onesf", [128, 128])
        sel = sb("sel", [128, 3, 128], BF16)
        selN = sb("selN", [128, 2, N], BF16)
        l1b = sb("l1b", [128, NCH, 512], BF16)
        tri16b = sb("tri16b", [128, 128], BF16)
        rows = sb("rows", [128, 2048], BF16)
        gb8 = sb("gb8", [8, 128])
        gb8h = sb("gb8h", [8, 128], BF16)
        gb8hf = sb("gb8hf", [8, 128])
        gb8l = sb("gb8l", [8, 128], BF16)
        mhalf = sb("mhalf", [128, 1])
        mask0 = sb("mask0", [128, 1])
        normw = sb("normw", [128, 16])
        cw = sb("cw", [128, 128])
        dtb_bc = sb("dtb_bc", [128, 2, 16])
        a_bc = sb("a_bc", [128, 2, 16])
        dskh_bc = sb("dskh_bc", [128, 2, 16])
        ssdnw = sb("ssdnw", [128, 16])
        w2b = sb("w2b", [128, 1024], BF16)
        gnwh = sb("gnwh", [128, 4])
        fnw_bc = sb("fnw_bc", [128, D])
        pbs = [ps("pb%d" % i, [128, 512]) for i in range(6)]
        PBfull = Ring(pbs, "pb")
        PBh = Ring(pbs[0:3], "pb", keys=[("pb", j) for j in range(0, 3)])
        if OPT["s4"]:
            PBi = Ring(pbs[3:5], "pb", keys=[("pb", j) for j in range(3, 5)])
            PBf = Ring(pbs[5:6], "pb", keys=[("pb", 5)])
        elif OPT["hpipe"]:
            PBh = Ring(pbs[0:2], "pb", keys=[("pb", j) for j in range(0, 2)])
            YD = Ring(pbs[2:4], "pb", keys=[("pb", j) for j in range(2, 4)])
            PBi = PBh
            PBf = Ring(pbs[4:6], "pb", keys=[("pb", j) for j in range(4, 6)])
        else:
            PBi = PBh
            PBf = Ring(pbs[3:6], "pb", keys=[("pb", j) for j in range(3, 6)])
        PB = RingProxy(PBfull)
        pts = [ps("pt%d" % i, [128, 1024], BF16) for i in range(2)]
        PT = Ring(pts, "pt")

        csem = S.new_sem("csem")

        csemp = S.new_sem("csemp")
        cops = []
        copsp = []

        def cload(dst, src, key, eng="sp"):
            if eng == "sp":
                cops.append(A(eng, lambda e: e.dma_start(out=dst, in_=src), writes=[S.uniq(key)], dma_sem=csem, raw_keys=True))
            else:
                copsp.append(A(eng, lambda e: e.dma_start(out=dst, in_=src), writes=[S.uniq(key)], dma_sem=csemp, raw_keys=True))

        cload(identf[:], ident_d, "identf")
        cload(tri[:], tri_d, "tri")
        cload(strict[:], strict_d, "strict")
        cload(mask0[:], mask0_d, "mask0")
        cload(normw[:], normw_d, "normw")
        cload(cw[:], cw_d, "cw")
        cload(ssdnw[:], ssdnw_d, "ssdnw")
        cload(gnwh[:], gnw_d, "gnwh")
        cload(gb8[:], gb_d.rearrange("o (a b) -> (o a) b", a=8), "gb8")
        cload(fnw_bc[:], fnw_d[0:1, :].partition_broadcast(128), "fnw")
        for l in range(2):
            cload(dtb_bc[:, l, :], dtb_d[l:l + 1, :].partition_broadcast(128), "dtb")
            cload(a_bc[:, l, :], alog_d[l:l + 1, :].partition_broadcast(128), "a_bc")
            cload(dskh_bc[:, l, :], dskip_d[l:l + 1, :].partition_broadcast(128), "dskh")
        A("pool", lambda e: e.memset(rows[:], 0.0), writes=["rows0"])
        A("pool", lambda e: e.memset(alrT[:], 0.0), writes=["alrT"])
        A("pool", lambda e: e.memset(w2b[:], 0.0), writes=["w2b0"])
        copsp.append(A("pool", lambda e: e.dma_start(out=rows[0:1, :], in_=cb_d[:, 0:2048]), reads=["rows0"], writes=[S.uniq("rows")], dma_sem=csemp, raw_keys=True))
        copsp.append(A("pool", lambda e: e.dma_start(out=rows[1:2, :], in_=cb_d[:, 2048:4096]), reads=["rows0"], writes=[S.uniq("rows")], dma_sem=csemp, raw_keys=True))
        cload(sel[:].rearrange("p r m -> p (r m)"), sel_d, "sel", eng="pool")
        copsp.append(A("pool", lambda e: e.dma_start(out=w2b[0:16, :], in_=w2_d), reads=["w2b0"], writes=[S.uniq("w2b")], dma_sem=csemp, raw_keys=True))
        for l in range(2):
            cload(wmisc[:, l, :, 0:16], win_d[l, :, C_DT:C_DT + 16].rearrange("(k p) n -> p k n", p=128), "wmisc", eng="pool")
            cload(wmisc[:, l, :, 16:32], win_d[l, :, C_ALR:C_ALR + 16].rearrange("(k p) n -> p k n", p=128), "wmisc", eng="pool")

        for o in cops:
            o.count = 16 * len(cops)
        for o in copsp:
            o.count = 16 * len(copsp)
        A("dve", lambda e: e.tensor_copy(out=identb[:], in_=identf[:]), reads=["identf"], writes=["identb"])
        A("dve", lambda e: e.tensor_copy(out=maskb[:], in_=tri[:]), reads=["tri"], writes=["maskb"])
        A("dve", lambda e: e.tensor_scalar(out=maskq[:], in0=tri[:], scalar1=0.25, scalar2=None, op0=ALU.mult), reads=["tri"], writes=["maskq"])
        A("dve", lambda e: e.tensor_scalar(out=tri16[:], in0=tri[:], scalar1=-1.0 / 16.0, scalar2=None, op0=ALU.mult), reads=["tri"], writes=["tri16"])
        A("dve", lambda e: e.tensor_scalar(out=tri16b[:], in0=tri[:], scalar1=-1.0 / 16.0, scalar2=None, op0=ALU.mult), reads=["tri"], writes=["tri16b"])
        A("pool", lambda e: e.memset(onesf[:], 1.0), writes=["onesf"])
        A("pool", lambda e: e.memset(mhalf[:], -0.5), writes=["mhalf"])
        for r in range(2):
            A("pool", lambda e, r=r: e.tensor_copy(out=selN[:, r, :], in_=bc(sel[:, r, 0:1], [128, N])), reads=["sel"], writes=["selN"])
        A("pool", lambda e: e.memset(hist[:], 0.0), writes=["hist"])
        A("pool", lambda e: e.memset(xbcT[:], 0.0), writes=[("xbcT", b) for b in range(16)])
        for l in range(2):
            A("pool", lambda e, l=l: e.memset(Sssd[l][:], 0.0), writes=[("Sssd", l)])
            A("pool", lambda e, l=l: e.memset(Sbf[l][:], 0.0), writes=[("Sbf", l)])
            A("pool", lambda e, l=l: e.memset(Sgla[l][:], 0.0), writes=[("Sgla", l, 0), ("Sgla", l, 1)])
            A("pool", lambda e, l=l: e.memset(Gbf[l][:], 0.0), writes=[("Gbf", l)])
        A("act", lambda e: e.activation(out=a_bc[:], in_=a_bc[:], func=AF.Exp), reads=["a_bc"], writes=["a_bc"])
        A("dve", lambda e: e.tensor_scalar(out=a_bc[:], in0=a_bc[:], scalar1=-1.0, scalar2=None, op0=ALU.mult), reads=["a_bc"], writes=["a_bc"])
        A("dve", lambda e: e.tensor_scalar(out=dskh_bc[:], in0=dskh_bc[:], scalar1=0.5, scalar2=None, op0=ALU.mult), reads=["dskh"], writes=["dskh"])
        A("dve", lambda e: e.tensor_scalar(out=gnwh[:], in0=gnwh[:], scalar1=0.5, scalar2=None, op0=ALU.mult), reads=["gnwh"], writes=["gnwh"])
        A("dve", lambda e: e.tensor_copy(out=gb8h[:], in_=gb8[:]), reads=["gb8"], writes=["gb8h"])
        A("dve", lambda e: e.tensor_copy(out=gb8hf[:], in_=gb8h[:]), reads=["gb8h"], writes=["gb8hf"])
        A("dve", lambda e: e.tensor_tensor(out=gb8l[:], in0=gb8[:], in1=gb8hf[:], op=ALU.subtract), reads=["gb8", "gb8hf"], writes=["gb8l"])
        csem2 = S.new_sem("csem2")
        cops2 = []
        for a8 in range(8):
            cops2.append(A("sp", lambda e, a8=a8: e.dma_start(out=rows[2:3, a8 * 128:(a8 + 1) * 128], in_=gb8h[a8:a8 + 1, :]), reads=["gb8h", "rows0"], writes=[S.uniq("rows")], dma_sem=csem2))
            cops2.append(A("sp", lambda e, a8=a8: e.dma_start(out=rows[2:3, 1024 + a8 * 128:1024 + (a8 + 1) * 128], in_=gb8l[a8:a8 + 1, :]), reads=["gb8l", "rows0"], writes=[S.uniq("rows")], dma_sem=csem2))
        for o in cops2:
            o.count = 16 * len(cops2)

        def wsrc(l, kind, idx):
            if kind == "in":
                src = win_d[l, :, idx:idx + 512]
            elif kind == "br":
                b, j = idx
                src = wbr_d[l, b, :, j * 512:(j + 1) * 512]
            else:
                src = wout_d[l, :, idx * 512:(idx + 1) * 512]
            return src.rearrange("(k p) n -> p k n", p=128)

        def layer_groups(l):
            g = []
            for i in range(4):
                g.append((l, "in", C_XBC + 512 * i))
            if OPT["qkfirst"]:
                g.append((l, "in", C_Q))
                g.append((l, "in", C_K))
            for i in range(2):
                g.append((l, "in", C_Z + 512 * i))
            if not OPT["qkfirst"]:
                g.append((l, "in", C_Q))
                g.append((l, "in", C_K))
            for i in range(2):
                g.append((l, "in", C_V + 512 * i))
            for i in range(2):
                g.append((l, "in", C_G + 512 * i))
            for b in range(2):
                for j in range(2):
                    g.append((l, "in", C_M + b * 1024 + j * 512))
                    g.append((l, "br", (b, j)))
            for j in range(2):
                g.append((l, "out", j))
            return g

        wlist = []
        for ti in range(ntiles):
            for l in range(NL):
                wlist += layer_groups(l)
        wstate = {"issued": 0, "used": 0}
        PF = 2

        GP = len(layer_groups(0)) * NL
        wscr = nc.dram_tensor("wscr", [GP, 128, 8 * 512], BF16, kind="Internal").ap()
        wsemh = [S.new_sem("wsemh%d" % i) for i in range(NWB)]
        wsemst = [S.new_sem("wsemst%d" % i) for i in range(NWB)]

        def w_issue_upto(n):
            while wstate["issued"] < min(n, len(wlist)):
                i = wstate["issued"]
                l, kind, idx = wlist[i]
                slot = i % NWB
                if i < GP:
                    src = wsrc(l, kind, idx)
                    A("pool", lambda e, slot=slot, src=src: e.dma_start(out=wbufs[slot][:], in_=src),
                      writes=[("wb", slot)], dma_sem=wsems[slot])
                    if ntiles > 1:
                        A("sp", lambda e, slot=slot, i=i: e.dma_start(out=wscr[i], in_=wbufs[slot][:].rearrange("p k n -> p (k n)")),
                          reads=[("wb", slot)], writes=[("wscr", i)], dma_sem=wsemst[slot])
                else:
                    g = i % GP
                    A("sp", lambda e, slot=slot, g=g: e.dma_start(out=wbufs[slot][:].rearrange("p k n -> p (k n)"), in_=wscr[g]),
                      reads=[("wscr", g)], writes=[("wb", slot)], dma_sem=wsemh[slot])
                wstate["issued"] += 1

        def next_w(expect, pf=PF):
            i = wstate["used"]
            assert wlist[i] == expect, (wlist[i], expect)
            w_issue_upto(i + 1 + pf)
            wstate["used"] += 1
            return wbufs[i % NWB], ("wb", i % NWB)

        def bc(ap, shape):
            return ap.to_broadcast(list(shape))

        HALF_LN = math.log(0.5)
        QS_LN = math.log(128.0 ** -0.5)

        def silu2_evac(bank_ap, out_ap, key, wkey, ncols):
            th, tkey = TH.get()
            A("act", lambda e: e.activation(out=th[:, 0:ncols], in_=bank_ap, func=AF.Tanh, scale=0.5), reads=[key], writes=[tkey])
            A("dve", lambda e: e.scalar_tensor_tensor(out=out_ap, in0=th[:, 0:ncols], scalar=1.0, in1=bank_ap, op0=ALU.add, op1=ALU.mult),
              reads=[tkey, key], writes=[wkey])

        uT_all = [("uT", c) for c in range(NCH)]
        out_keys = []
        osems = [S.new_sem("osem%d" % i) for i in range(NCH)]
        xsems = [S.new_sem("xsem%d" % i) for i in range(NCH)]

        def tile_layer(ti, l):
            first = ti == 0
            last_layer = l == NL - 1
            MARK("A %d %d" % (ti, l))
            for c in range(NCH):
                ub, ukey = UB.get()
                A("act", lambda e, c=c, ub=ub: e.activation(out=ub[:], in_=h[:, c, :], func=AF.Square, accum_out=ssq[:, c:c + 1]),
                  reads=[("h", c)], writes=[("ssq", c), ukey])
                A("dve", lambda e, c=c: e.tensor_scalar(out=rs[:, c:c + 1], in0=ssq[:, c:c + 1], scalar1=1.0 / D, scalar2=EPS, op0=ALU.mult, op1=ALU.add),
                  reads=[("ssq", c)], writes=[("rs", c)])
                A("pool", lambda e, c=c: e.tensor_tensor(out=rs[:, c:c + 1], in0=rs[:, c:c + 1], in1=mhalf[:], op=ALU.pow),
                  reads=[("rs", c), "mhalf"], writes=[("rs", c)])
                if OPT["ub_eng"] == "act":
                    A("act", lambda e, c=c, ub=ub: e.activation(out=ub[:], in_=h[:, c, :], func=AF.Copy, scale=rs[:, c:c + 1]),
                      reads=[("h", c), ("rs", c)], writes=[ukey])
                else:
                    A("dve", lambda e, c=c, ub=ub: e.tensor_scalar(out=ub[:], in0=h[:, c, :], scalar1=rs[:, c:c + 1], scalar2=None, op0=ALU.mult),
                      reads=[("h", c), ("rs", c)], writes=[ukey])
                bank, key = PT.get()
                for k in range(8):
                    A("pe", lambda e, k=k, ub=ub, bank=bank: e.transpose(out=bank[:, k * 128:(k + 1) * 128], in_=ub[:, k * 128:(k + 1) * 128], identity=identb[:]),
                      reads=[ukey, "identb"], writes=[key])
                A("dve", lambda e, c=c, bank=bank: e.tensor_tensor(out=uT[:, :, c * 128:(c + 1) * 128], in0=bank[:, :].rearrange("p (k t) -> p k t", k=8),
                                                                    in1=bc(normw[:, l * 8:(l + 1) * 8].unsqueeze(2), [128, 8, 128]), op=ALU.mult),
                  reads=[key, "normw"], writes=[("uT", c)])
            MARK("B")
            for c in range(NCH):
                bank, key = PB.get()
                for k in range(8):
                    A("pe", lambda e, k=k, c=c, bank=bank: e.matmul(bank[:, 0:16], lhsT=uT[:, k, c * 128:(c + 1) * 128], rhs=wmisc[:, l, k, 0:16], start=(k == 0), stop=(k == 7)),
                      reads=[("uT", c), "wmisc"], writes=[key])
                A("dve", lambda e, c=c, bank=bank: e.tensor_tensor(out=dtr[:, c, :], in0=bank[:, 0:16], in1=dtb_bc[:, l, :], op=ALU.add),
                  reads=[key, "dtb"], writes=[("dtr", c)])
                A("act", lambda e, c=c: e.activation(out=dte_x[:, c, :], in_=dtr[:, c, :], func=AF.Exp), reads=[("dtr", c)], writes=[("dte_x", c)])
            for c in range(NCH):
                A("act", lambda e, c=c: e.activation(out=dt_t[:, c, :], in_=dte_x[:, c, :], func=AF.Ln, bias=1.0), reads=[("dte_x", c)], writes=[("dt", c)])
            for c in range(NCH):
                if first and c == 0:
                    A("dve", lambda e, c=c: e.tensor_scalar(out=dt_t[:, c, :], in0=dt_t[:, c, :], scalar1=mask0[:, 0:1], scalar2=None, op0=ALU.mult),
                      reads=[("dt", c), "mask0"], writes=[("dt", c)])
                A("dve", lambda e, c=c: e.tensor_tensor(out=da_t[:, c, :], in0=dt_t[:, c, :], in1=a_bc[:, l, :], op=ALU.mult),
                  reads=[("dt", c), "a_bc"], writes=[("da", c)])
            def stage_B2(c):
                bank2, key2 = PB.get()
                A("pe", lambda e, c=c, bank2=bank2: e.matmul(bank2[:, 0:16], lhsT=tri[:], rhs=da_t[:, c, :], start=True, stop=True), reads=[("da", c), "tri"], writes=[key2])
                A("pe", lambda e, c=c, bank2=bank2: e.matmul(bank2[:, 16:32], lhsT=onesf[:], rhs=da_t[:, c, :], start=True, stop=True), reads=[("da", c), "onesf"], writes=[key2])
                A("pe", lambda e, c=c, bank2=bank2: e.matmul(bank2[:, 32:48], lhsT=strict[:], rhs=da_t[:, c, :], start=True, stop=True), reads=[("da", c), "strict"], writes=[key2])
                A("act", lambda e, c=c, bank2=bank2: e.activation(out=ecumh[:, c, :], in_=bank2[:, 0:16], func=AF.Exp, bias=HALF_LN), reads=[key2], writes=[("ecumh", c)])
                A("act", lambda e, c=c, bank2=bank2: e.activation(out=cdb[:, c, :], in_=bank2[:, 16:32], func=AF.Exp), reads=[key2], writes=[("cdb", c)])
                A("act", lambda e, c=c, bank2=bank2: e.activation(out=dte[:, c, :], in_=bank2[:, 32:48], func=AF.Exp), reads=[key2], writes=[("dte", c)])
                A("dve", lambda e, c=c: e.scalar_tensor_tensor(out=wq[:, c, :], in0=dt_t[:, c, :], scalar=0.25, in1=dte[:, c, :], op0=ALU.mult, op1=ALU.mult),
                  reads=[("dt", c), ("dte", c)], writes=[("wq", c)])
                A("dve", lambda e, c=c: e.tensor_scalar(out=dth[:, c, :], in0=dt_t[:, c, :], scalar1=0.5, scalar2=None, op0=ALU.mult),
                  reads=[("dt", c)], writes=[("dth", c)])
            if not OPT["b2late"]:
                for c in range(NCH):
                    stage_B2(c)
            bank, key = PB.get()
            for k in range(8):
                A("pe", lambda e, k=k, bank=bank: e.matmul(bank[0:16, 0:N], lhsT=wmisc[:, l, k, 16:32], rhs=uT[:, k, :], start=(k == 0), stop=(k == 7)),
                  reads=uT_all + ["wmisc"], writes=[key])
            A("act", lambda e, bank=bank: e.activation(out=alrT[0:16, :], in_=bank[0:16, 0:N], func=AF.Copy), reads=[key], writes=["alrT"])
            if OPT["gprepB"]:
                for c in range(NCH):
                    cs = slice(c * 128, (c + 1) * 128)
                    bank, key = PB.get()
                    A("pe", lambda e, bank=bank, cs=cs: e.matmul(bank[:, :], lhsT=alrT[:, cs], rhs=w2b[:, l * 512:(l + 1) * 512], start=True, stop=False),
                      reads=["alrT", "w2b"], writes=[key])
                    A("pe", lambda e, bank=bank: e.matmul(bank[:, :], lhsT=sel[:, 2, :], rhs=rows[:, l * 512:(l + 1) * 512], start=False, stop=False),
                      reads=["sel", "rows"], writes=[key])
                    A("pe", lambda e, bank=bank: e.matmul(bank[:, :], lhsT=sel[:, 2, :], rhs=rows[:, 1024 + l * 512:1024 + (l + 1) * 512], start=False, stop=True),
                      reads=["sel", "rows"], writes=[key])
                    A("act", lambda e, bank=bank: e.activation(out=e1[:], in_=bank[:, :], func=AF.Exp, scale=-1.0), reads=[key], writes=["e1"])
                    A("act", lambda e, c=c: e.activation(out=l1b[:, c, :], in_=e1[:], func=AF.Ln, bias=1.0), reads=["e1"], writes=[("l1b", c)])
            MARK("C")
            for gi in range(4):
                wb, wkey = next_w((l, "in", C_XBC + 512 * gi))
                for j in range(4):
                    blk = gi * 4 + j
                    bank, key = PB.get()
                    for k in range(8):
                        A("pe", lambda e, k=k, j=j, wb=wb, bank=bank: e.matmul(bank[:, 0:N], lhsT=wb[:, k, j * 128:(j + 1) * 128], rhs=uT[:, k, :], start=(k == 0), stop=(k == 7)),
                          reads=uT_all + [wkey], writes=[key])
                    A("act", lambda e, blk=blk, bank=bank: e.activation(out=xbcT[:, blk, 4:4 + N], in_=bank[:, 0:N], func=AF.Copy), reads=[key], writes=[("xbcT", blk)])
            if OPT["b2late"]:
                for c in range(NCH):
                    stage_B2(c)
            allx = [("xbcT", b) for b in range(16)]
            A("pool", lambda e: e.tensor_copy(out=xbcT[:, :, 1:4], in_=hist[:, l, :, 1:4]), reads=[("hist", l)], writes=allx)
            A("pool", lambda e: e.tensor_copy(out=hist[:, l, :, 1:4], in_=xbcT[:, :, N + 1:N + 4]), reads=allx, writes=[("hist", l)])
            MARK("E")
            dgof = {}

            def gen_diag(blk):
                dg, dkey = DG.get()
                for tap in range(4):
                    col = (l * 16 + blk) * 4 + tap
                    A(OPT["diag_eng"], lambda e, dg=dg, tap=tap, col=col: e.tensor_scalar(out=dg[:, tap, :], in0=identb[:], scalar1=cw[:, col:col + 1], scalar2=0.0, op0=ALU.mult, op1=ALU.add),
                      reads=["identb", "cw"], writes=[dkey])
                dgof[blk] = (dg, dkey)

            def conv_tok(c, blks, out_ap, wkey):
                bank, key = PB.get()
                for j, blk in enumerate(blks):
                    dg, dkey = dgof[blk]
                    o = bank[:, j * 128:(j + 1) * 128]
                    A("pe", lambda e, o=o, blk=blk: e.matmul(o, lhsT=sel[:, l, :], rhs=rows[:, blk * 128:(blk + 1) * 128], start=True, stop=False),
                      reads=["sel", "rows"], writes=[key])
                    for tap in range(4):
                        A("pe", lambda e, o=o, blk=blk, tap=tap, dg=dg, c=c: e.matmul(o, lhsT=xbcT[:, blk, c * 128 + 1 + tap:c * 128 + 1 + tap + 128], rhs=dg[:, tap, :], start=False, stop=(tap == 3)),
                          reads=[("xbcT", blk), dkey], writes=[key])
                silu2_evac(bank[:, 0:128 * len(blks)], out_ap, key, wkey, 128 * len(blks))

            for half in range(2):
                blks = [half * 4 + j for j in range(4)]
                for b in blks:
                    gen_diag(b)
                for c in range(NCH):
                    conv_tok(c, blks, x2[:, c, half * 512:(half + 1) * 512], ("x2", c))
            for blk in range(8, 16):
                if blk % 4 == 0:
                    for b4 in range(blk, blk + 4):
                        gen_diag(b4)
                    if blk == 8:
                        for c in range(NCH):
                            conv_tok(c, [8, 9, 10, 11], B2tok[:, c, :], ("B2tok", c))
                dg, dkey = dgof[blk]
                bank, key = PB.get()
                A("pe", lambda e, bank=bank, blk=blk: e.matmul(bank[:, 0:N], lhsT=rows[:, blk * 128:(blk + 1) * 128], rhs=selN[:, l, :], start=True, stop=False),
                  reads=["selN", "rows"], writes=[key])
                for tap in range(4):
                    A("pe", lambda e, bank=bank, blk=blk, tap=tap, dg=dg: e.matmul(bank[:, 0:N], lhsT=dg[:, tap, :], rhs=xbcT[:, blk, 1 + tap:1 + tap + N], start=False, stop=(tap == 3)),
                      reads=[("xbcT", blk), dkey], writes=[key])
                silu2_evac(bank[:, 0:N], BCT[:, blk - 8, :], key, ("BCT", blk - 8), N)
            MARK("P2")
            sF = []
            REC[0] = sF
            PB.ring = PBf
            for part in (("qk", "z") if OPT["qkfirst"] else ("z", "qk")):
              if part == "z":
                for gi in range(2):
                    wb, wkey = next_w((l, "in", C_Z + 512 * gi))
                    for c in range(NCH):
                        bank, key = PB.get()
                        for k in range(8):
                            A("pe", lambda e, k=k, c=c, wb=wb, bank=bank: e.matmul(bank[:, :], lhsT=uT[:, k, c * 128:(c + 1) * 128], rhs=wb[:, k, :], start=(k == 0), stop=(k == 7)),
                              reads=[("uT", c), wkey], writes=[key])
                        silu2_evac(bank[:, :], zs[:, c, gi * 512:(gi + 1) * 512], key, ("zs", c), 512)
                        if gi == 1:
                            GIVE(("zs", c))
                        STEP()
              else:
                for qk in range(2):
                    wb, wkey = next_w((l, "in", C_Q if qk == 0 else C_K))
                    for j in range(4):
                        bank, key = PB.get()
                        for k in range(8):
                            A("pe", lambda e, k=k, j=j, wb=wb, bank=bank: e.matmul(bank[:, 0:N], lhsT=wb[:, k, j * 128:(j + 1) * 128], rhs=uT[:, k, :], start=(k == 0), stop=(k == 7)),
                              reads=uT_all + [wkey], writes=[key])
                        A("act", lambda e, bank=bank, qk=qk, j=j: e.activation(out=qkT[:, qk * 4 + j, :], in_=bank[:, 0:N], func=AF.Copy), reads=[key], writes=[("qkT", qk * 4 + j)])
                        if j == 3:
                            GIVE("q" if qk == 0 else "k")
                        STEP()
            for gi in range(2):
                wb, wkey = next_w((l, "in", C_V + 512 * gi))
                for c in range(NCH):
                    bank, key = PB.get()
                    for k in range(8):
                        A("pe", lambda e, k=k, c=c, wb=wb, bank=bank: e.matmul(bank[:, :], lhsT=uT[:, k, c * 128:(c + 1) * 128], rhs=wb[:, k, :], start=(k == 0), stop=(k == 7)),
                          reads=[("uT", c), wkey], writes=[key])
                    A("act", lambda e, bank=bank, c=c, gi=gi: e.activation(out=vtok[:, c, gi * 512:(gi + 1) * 512], in_=bank[:, :], func=AF.Copy), reads=[key], writes=[("vtok", c)])
                    if gi == 1:
                        GIVE(("v", c))
                    STEP()
            for gi in range(2):
                wb, wkey = next_w((l, "in", C_G + 512 * gi))
                for j in range(4):
                    blk = gi * 4 + j
                    bank, key = PB.get()
                    for k in range(8):
                        A("pe", lambda e, k=k, j=j, wb=wb, bank=bank: e.matmul(bank[:, 0:N], lhsT=wb[:, k, j * 128:(j + 1) * 128], rhs=uT[:, k, :], start=(k == 0), stop=(k == 7)),
                          reads=uT_all + [wkey], writes=[key])
                    gw, gkey = GW.get()
                    th, tkey = TH.get()
                    A("act", lambda e, bank=bank, gw=gw, blk=blk: e.activation(out=gw[:, 0:N], in_=bank[:, 0:N], func=AF.Copy, scale=gnwh[:, l * 2 + (blk % 2):l * 2 + (blk % 2) + 1]),
                      reads=[key, "gnwh"], writes=[gkey])
                    A("act", lambda e, bank=bank, th=th: e.activation(out=th[:, 0:N], in_=bank[:, 0:N], func=AF.Tanh, scale=0.5), reads=[key], writes=[tkey])
                    A("dve", lambda e, gw=gw, th=th, blk=blk: e.scalar_tensor_tensor(out=sgT[:, blk, :], in0=th[:, 0:N], scalar=1.0, in1=gw[:, 0:N], op0=ALU.add, op1=ALU.mult),
                      reads=[gkey, tkey], writes=[("sgT", blk)])
                    if blk == 7:
                        GIVE("g")
                        GIVE("Fdone")
                    STEP()
            sH = []
            REC[0] = sH
            PB.ring = PBh
            YDK = {}

            def H_front(c):
                cs = slice(c * 128, (c + 1) * 128)
                for hv in range(2):
                    for hh in range(8):
                        hd = hv * 8 + hh
                        ai = hd if OPT["at_bf16"] else hh
                        if OPT["at_split"] and hh % 2 == 1:
                            A("pool", lambda e, ai=ai, hd=hd, c=c: e.tensor_scalar(out=At[:, ai, :], in0=strict[:], scalar1=da_t[:, c, hd:hd + 1], scalar2=0.0, op0=ALU.mult, op1=ALU.add),
                              reads=["strict", ("da", c)], writes=[("At", ai)])
                        else:
                            A("dve", lambda e, ai=ai, hd=hd, c=c: e.tensor_scalar(out=At[:, ai, :], in0=strict[:], scalar1=da_t[:, c, hd:hd + 1], scalar2=None, op0=ALU.mult),
                              reads=["strict", ("da", c)], writes=[("At", ai)])
                    for qd in range(2):
                        bank, key = PB.get()
                        for hq in range(4):
                            hh = qd * 4 + hq
                            ai = (hv * 8 + hh) if OPT["at_bf16"] else hh
                            A("pe", lambda e, bank=bank, hq=hq, ai=ai: e.matmul(bank[:, hq * 128:(hq + 1) * 128], lhsT=At[:, ai, :], rhs=(maskb[:] if OPT["at_bf16"] else tri[:]), start=True, stop=True),
                              reads=[("At", ai), "tri", "maskb"], writes=[key])
                        h0 = hv * 8 + qd * 4
                        A("act", lambda e, bank=bank, h0=h0: e.activation(out=Lm[:, h0:h0 + 4, :], in_=bank[:, :].rearrange("p (a t) -> p a t", a=4), func=AF.Exp),
                          reads=[key], writes=[("Lm", h0 // 4)])
                        STEP()
                bank, key = PB.get()
                for g in range(4):
                    A("pe", lambda e, bank=bank, g=g, cs=cs: e.matmul(bank[:, g * 128:(g + 1) * 128], lhsT=BCT[:, g, cs], rhs=BCT[:, 4 + g, cs], start=True, stop=True),
                      reads=[("BCT", g), ("BCT", 4 + g)], writes=[key])
                A("dve", lambda e, bank=bank: e.tensor_tensor(out=scm[:], in0=bank[:, :].rearrange("p (g t) -> p g t", g=4), in1=bc(maskq[:, :].unsqueeze(1), [128, 4, 128]), op=ALU.mult),
                  reads=[key, "maskq"], writes=["scm"])
                A("dve", lambda e: e.tensor_tensor(out=MT[:].rearrange("p (g a) t -> p g a t", g=4), in0=Lm[:].rearrange("p (g a) t -> p g a t", g=4),
                                                   in1=bc(scm[:].unsqueeze(2), [128, 4, 4, 128]), op=ALU.mult),
                  reads=["scm"] + [("Lm", i) for i in range(4)], writes=["MT"])
                STEP()
                x3 = x2[:, c, :].rearrange("p (a d) -> p a d", a=16)
                A("pool", lambda e, x3=x3, c=c: e.tensor_tensor(out=xdt[:].rearrange("p (a d) -> p a d", a=16), in0=x3, in1=bc(dth[:, c, :].unsqueeze(2), [128, 16, 64]), op=ALU.mult),
                  reads=[("x2", c), ("dth", c)], writes=["xdt"])
                if not OPT["hpipe"]:
                    A("pool", lambda e, x3=x3, c=c: e.tensor_tensor(out=xw[:].rearrange("p (a d) -> p a d", a=16), in0=x3, in1=bc(wq[:, c, :].unsqueeze(2), [128, 16, 64]), op=ALU.mult),
                      reads=[("x2", c), ("wq", c)], writes=["xw"])
                A("pool", lambda e, x3=x3: e.tensor_tensor(out=xd[:].rearrange("p (a d) -> p a d", a=16), in0=x3, in1=bc(dskh_bc[:, l, :].unsqueeze(2), [128, 16, 64]), op=ALU.mult),
                  reads=[("x2", c), "dskh"], writes=["xd"])
                if OPT["stmp_early"]:
                    A("pool", lambda e, c=c: e.tensor_tensor(out=stmp[:].rearrange("p (a d) -> p a d", a=16), in0=Sssd[l][:].rearrange("p (a d) -> p a d", a=16),
                                                             in1=bc(cdb[:, c, :].unsqueeze(2), [128, 16, 64]), op=ALU.mult),
                      reads=[("Sssd", l), ("cdb", c)], writes=["stmp", "stmp2"])
                STEP()
                ydk = []
                YDK[c] = ydk
                for hb in range(2):
                    bank, key = YD.get() if OPT["hpipe"] else PB.get()
                    ydk.append((bank, key))
                    A("pe", lambda e, bank=bank, hb=hb: e.matmul(bank[:, :], lhsT=identb[:], rhs=xd[:, hb * 512:(hb + 1) * 512], start=True, stop=False),
                      reads=["identb", "xd"], writes=[key])
                    for h8 in range(8):
                        hd = hb * 8 + h8
                        A("pe", lambda e, bank=bank, h8=h8, hd=hd: e.matmul(bank[:, h8 * 64:(h8 + 1) * 64], lhsT=MT[:, hd, :], rhs=xdt[:, hd * 64:(hd + 1) * 64], start=False, stop=(h8 == 7)),
                          reads=["MT", "xdt"], writes=[key])
                    STEP()

            def H_mid(c):
                cs = slice(c * 128, (c + 1) * 128)
                ydk = YDK[c]
                for hb in range(2):
                    bank, key = PB.get()
                    for gg in range(2):
                        g = hb * 2 + gg
                        A("pe", lambda e, bank=bank, gg=gg, g=g, cs=cs: e.matmul(bank[:, gg * 256:(gg + 1) * 256], lhsT=BCT[:, 4 + g, cs], rhs=Sbf[l][:, g * 256:(g + 1) * 256], start=True, stop=True),
                          reads=[("BCT", 4 + g), ("Sbf", l)], writes=[key])
                    hs = slice(hb * 512, (hb + 1) * 512)
                    A("dve", lambda e, bank=bank, hb=hb, hs=hs, c=c: e.tensor_tensor(out=t1[:, hs].rearrange("p (a d) -> p a d", a=8), in0=bank[:, :].rearrange("p (a d) -> p a d", a=8),
                                                                                  in1=bc(ecumh[:, c, hb * 8:(hb + 1) * 8].unsqueeze(2), [128, 8, 64]), op=ALU.mult),
                      reads=[key, ("ecumh", c)], writes=[("t1", hb)])
                    ybank, ykey = ydk[hb]
                    A("dve", lambda e, ybank=ybank, hs=hs: e.tensor_tensor(out=t1[:, hs], in0=t1[:, hs], in1=ybank[:, :], op=ALU.add),
                      reads=[ykey, ("t1", hb)], writes=[("t1", hb)])
                    NEED(("zs", c))
                    A(OPT["zs_eng"], lambda e, hs=hs, c=c: e.tensor_tensor(out=t1[:, hs], in0=t1[:, hs], in1=zs[:, c, hs], op=ALU.mult),
                      reads=[("t1", hb), ("zs", c)], writes=[("t1", hb)])
                    STEP()
                for g in range(4):
                    A("act", lambda e, g=g: e.activation(out=yn[:, g * 256:(g + 1) * 256], in_=t1[:, g * 256:(g + 1) * 256], func=AF.Square, accum_out=sq4[:, g:g + 1]),
                      reads=[("t1", g // 2)], writes=[("sq4", g), ("yn", g // 2)])
                sq4k = [("sq4", g) for g in range(4)]
                A("dve", lambda e: e.tensor_scalar(out=r4[:], in0=sq4[:], scalar1=1.0 / 256, scalar2=4 * EPS, op0=ALU.mult, op1=ALU.add), reads=sq4k, writes=["r4"])
                A("pool", lambda e: e.tensor_tensor(out=r4[:], in0=r4[:], in1=bc(mhalf[:, 0:1], [128, 4]), op=ALU.pow), reads=["r4", "mhalf"], writes=["r4"])
                for g in range(4):
                    gs = slice(g * 256, (g + 1) * 256)
                    if g % 2 == 0:
                        A("dve", lambda e, g=g, gs=gs: e.tensor_scalar(out=yn[:, gs], in0=t1[:, gs], scalar1=r4[:, g:g + 1], scalar2=None, op0=ALU.mult),
                          reads=["r4", ("t1", g // 2)], writes=[("yn", g // 2)])
                    else:
                        A("act", lambda e, g=g, gs=gs: e.activation(out=yn[:, gs], in_=t1[:, gs], func=AF.Copy, scale=r4[:, g:g + 1]),
                          reads=["r4", ("t1", g // 2)], writes=[("yn", g // 2)])
                STEP()

            def H_tail(c):
                if OPT["hpipe"]:
                    x3 = x2[:, c, :].rearrange("p (a d) -> p a d", a=16)
                    A("pool", lambda e, x3=x3, c=c: e.tensor_tensor(out=xw[:].rearrange("p (a d) -> p a d", a=16), in0=x3, in1=bc(wq[:, c, :].unsqueeze(2), [128, 16, 64]), op=ALU.mult),
                      reads=[("x2", c), ("wq", c)], writes=["xw"])
                bank, key = PT.get()
                for k in range(8):
                    A("pe", lambda e, k=k, bank=bank: e.transpose(out=bank[:, k * 128:(k + 1) * 128], in_=yn[:, k * 128:(k + 1) * 128], identity=identb[:]),
                      reads=[("yn", 0), ("yn", 1), "identb"], writes=[key])
                A("dve", lambda e, bank=bank, c=c: e.tensor_tensor(out=xbcT[:, 0:8, 4 + c * 128:4 + (c + 1) * 128], in0=bank[:, :].rearrange("p (k t) -> p k t", k=8),
                                                                    in1=bc(ssdnw[:, l * 8:(l + 1) * 8].unsqueeze(2), [128, 8, 128]), op=ALU.mult),
                  reads=[key, "ssdnw"], writes=[("xbcT", b) for b in range(8)])
                if c == NCH - 1:
                    GIVE("ysT")
                STEP()
                if not OPT["stmp_early"]:
                    A("pool", lambda e, c=c: e.tensor_tensor(out=stmp[:].rearrange("p (a d) -> p a d", a=16), in0=Sssd[l][:].rearrange("p (a d) -> p a d", a=16),
                                                             in1=bc(cdb[:, c, :].unsqueeze(2), [128, 16, 64]), op=ALU.mult),
                      reads=[("Sssd", l), ("cdb", c)], writes=["stmp", "stmp2"])
                for hb in range(2):
                    bank, key = PB.get()
                    for gg in range(2):
                        g = hb * 2 + gg
                        A("pe", lambda e, bank=bank, gg=gg, g=g, c=c: e.matmul(bank[:, gg * 256:(gg + 1) * 256], lhsT=B2tok[:, c, g * 128:(g + 1) * 128], rhs=xw[:, g * 256:(g + 1) * 256], start=True, stop=True),
                          reads=[("B2tok", c), "xw"], writes=[key])
                    hs = slice(hb * 512, (hb + 1) * 512)
                    if OPT["sbf_dve"]:
                        A("dve", lambda e, bank=bank, hs=hs: e.tensor_tensor(out=Sbf[l][:, hs], in0=stmp[:, hs], in1=bank[:, :], op=ALU.add),
                          reads=[key, "stmp" if hb == 0 else "stmp2"], writes=[("Sbf", l)])
                    A("dve", lambda e, bank=bank, hs=hs: e.tensor_tensor(out=Sssd[l][:, hs], in0=stmp[:, hs], in1=bank[:, :], op=ALU.add),
                      reads=[key, "stmp" if hb == 0 else "stmp2"], writes=[("Sssd", l)])
                if not OPT["sbf_dve"]:
                    A("act", lambda e: e.activation(out=Sbf[l][:], in_=Sssd[l][:], func=AF.Copy), reads=[("Sssd", l)], writes=[("Sbf", l)])
                STEP()

            if OPT["hpipe"]:
                H_front(0)
                H_mid(0)
                for c in range(1, NCH):
                    H_front(c)
                    H_tail(c - 1)
                    H_mid(c)
                H_tail(NCH - 1)
            else:
                for c in range(NCH):
                    H_front(c)
                    H_mid(c)
                    H_tail(c)

            REC[0] = None
            PB.ring = PBfull
            given = set()
            if not OPT["s4"]:
                merge([sH, sF], given, OPT["fbias"])
            MARK("P3")

            def merge_branch(b):
                for j in range(2):
                    wg, wgkey = next_w((l, "in", C_M + b * 1024 + j * 512))
                    for jj in range(4):
                        bank, key = PB.get()
                        for k in range(8):
                            A("pe", lambda e, k=k, jj=jj, wg=wg, bank=bank: e.matmul(bank[:, 0:N], lhsT=wg[:, k, jj * 128:(jj + 1) * 128], rhs=uT[:, k, :], start=(k == 0), stop=(k == 7)),
                              reads=uT_all + [wgkey], writes=[key])
                        A("act", lambda e, bank=bank, jj=jj: e.activation(out=gth[b][:, jj, :], in_=bank[:, 0:N], func=AF.Tanh, scale=0.5), reads=[key], writes=[("gth", jj)])
                        STEP()
                    wbr, wbkey = next_w((l, "br", (b, j)))
                    if b == 0:
                        NEED("ysT")
                    for jj in range(4):
                        bank, key = PB.get()
                        for k in range(8):
                            A("pe", lambda e, k=k, jj=jj, wbr=wbr, bank=bank: e.matmul(bank[:, 0:N], lhsT=wbr[:, k, jj * 128:(jj + 1) * 128], rhs=xbcT[:, b * 8 + k, 4:4 + N], start=(k == 0), stop=(k == 7)),
                              reads=[("xbcT", b * 8 + k) for k in range(8)] + [wbkey], writes=[key])
                        if b == 0:
                            A("dve", lambda e, bank=bank, jj=jj, j=j: e.scalar_tensor_tensor(out=t0m[:, j * 4 + jj, :], in0=gth[0][:, jj, :], scalar=1.0, in1=bank[:, 0:N], op0=ALU.add, op1=ALU.mult),
                              reads=[key, ("gth", jj)], writes=[("t0m", j * 4 + jj)])
                        else:
                            t1m, t1key = T1M.get()
                            A("dve", lambda e, bank=bank, jj=jj, t1m=t1m: e.scalar_tensor_tensor(out=t1m[:, :], in0=gth[1][:, jj, :], scalar=1.0, in1=bank[:, 0:N], op0=ALU.add, op1=ALU.mult),
                              reads=[key, ("gth", jj)], writes=[t1key])
                            A("pool", lambda e, jj=jj, j=j, t1m=t1m: e.tensor_tensor(out=qkT[:, j * 4 + jj, :], in0=t0m[:, j * 4 + jj, :], in1=t1m[:, :], op=ALU.add),
                              reads=[t1key, ("t0m", j * 4 + jj)], writes=[("qkT", j * 4 + jj)])
                        STEP()

            sJ = []
            if not OPT["s4"]:
                REC[0] = sJ
                PB.ring = PBf
                NEED("Fdone")
                merge_branch(0)
            sI = []
            REC[0] = sI
            PB.ring = PBi

            def I_front(c):
                cs = slice(c * 128, (c + 1) * 128)
                ktok = ktoks[c % len(ktoks)]
                if not OPT["gprepB"]:
                    bank, key = PB.get()
                    A("pe", lambda e, bank=bank, cs=cs: e.matmul(bank[:, :], lhsT=alrT[:, cs], rhs=w2b[:, l * 512:(l + 1) * 512], start=True, stop=False),
                      reads=["alrT", "w2b"], writes=[key])
                    A("pe", lambda e, bank=bank: e.matmul(bank[:, :], lhsT=sel[:, 2, :], rhs=rows[:, l * 512:(l + 1) * 512], start=False, stop=False),
                      reads=["sel", "rows"], writes=[key])
                    A("pe", lambda e, bank=bank: e.matmul(bank[:, :], lhsT=sel[:, 2, :], rhs=rows[:, 1024 + l * 512:1024 + (l + 1) * 512], start=False, stop=True),
                      reads=["sel", "rows"], writes=[key])
                    A("act", lambda e, bank=bank: e.activation(out=e1[:], in_=bank[:, :], func=AF.Exp, scale=-1.0), reads=[key], writes=["e1"])
                    A("act", lambda e: e.activation(out=e1[:], in_=e1[:], func=AF.Ln, bias=1.0), reads=["e1"], writes=["e1"])
                bankg, keyg = PB.get()
                for j in range(4):
                    if OPT["gprepB"]:
                        A("pe", lambda e, bankg=bankg, j=j, c=c: e.matmul(bankg[:, j * 128:(j + 1) * 128], lhsT=l1b[:, c, j * 128:(j + 1) * 128], rhs=tri16b[:], start=True, stop=True),
                          reads=[("l1b", c), "tri16b"], writes=[keyg])
                    else:
                        A("pe", lambda e, bankg=bankg, j=j: e.matmul(bankg[:, j * 128:(j + 1) * 128], lhsT=e1[:, j * 128:(j + 1) * 128], rhs=tri16[:], start=True, stop=True),
                          reads=["e1", "tri16"], writes=[keyg])
                A("act", lambda e, bankg=bankg: e.activation(out=eg[:], in_=bankg[:, :], func=AF.Exp, bias=QS_LN), reads=[keyg], writes=["eg"])
                A("act", lambda e, bankg=bankg: e.activation(out=emg[:], in_=bankg[:, :], func=AF.Exp, scale=-1.0), reads=[keyg], writes=["emg"])
                A("act", lambda e, bankg=bankg, c=c: e.activation(out=egl[:, c, :], in_=bankg[:, :].rearrange("p (j t) -> p j t", j=4)[:, :, 127], func=AF.Exp), reads=[keyg], writes=[("egl", c)])
                NEED("q")
                NEED("k")
                A("dve", lambda e, cs=cs: e.tensor_tensor(out=qtl[:], in0=qkT[:, 0:4, cs], in1=eg[:].rearrange("p (j t) -> p j t", j=4), op=ALU.mult),
                  reads=["eg"] + [("qkT", j) for j in range(4)], writes=["qtl"])
                A("dve", lambda e, cs=cs: e.tensor_tensor(out=ktl[:], in0=qkT[:, 4:8, cs], in1=emg[:].rearrange("p (j t) -> p j t", j=4), op=ALU.mult),
                  reads=["emg"] + [("qkT", 4 + j) for j in range(4)], writes=["ktl"])
                STEP()
                bank, key = PB.get()
                for hd in range(4):
                    A("pe", lambda e, bank=bank, hd=hd: e.matmul(bank[:, hd * 128:(hd + 1) * 128], lhsT=ktl[:, hd, :], rhs=qtl[:, hd, :], start=True, stop=True),
                      reads=["ktl", "qtl"], writes=[key])
                A("dve", lambda e, bank=bank: e.tensor_tensor(out=attm[:], in0=bank[:, :].rearrange("p (g t) -> p g t", g=4), in1=bc(maskb[:, :].unsqueeze(1), [128, 4, 128]), op=ALU.mult),
                  reads=[key, "maskb"], writes=["attm"])
                STEP()
                bankt, keyt = PT.get()
                for hd in range(4):
                    A("pe", lambda e, bankt=bankt, hd=hd: e.transpose(out=bankt[:, hd * 128:(hd + 1) * 128], in_=ktl[:, hd, :], identity=identb[:]),
                      reads=["ktl", "identb"], writes=[keyt])
                A("act", lambda e, bankt=bankt, ktok=ktok: e.activation(out=ktok[:], in_=bankt[:, 0:512], func=AF.Copy), reads=[keyt], writes=[("ktok", c % len(ktoks))])
                STEP()

            def I_mid(c):
                cs = slice(c * 128, (c + 1) * 128)
                obk = []
                NEED(("v", c))
                for hb in range(2):
                    bank, key = PB.get()
                    obk.append((bank, key))
                    for hh in range(2):
                        hd = hb * 2 + hh
                        o = bank[:, hh * 256:(hh + 1) * 256]
                        A("pe", lambda e, o=o, hd=hd, c=c: e.matmul(o, lhsT=attm[:, hd, :], rhs=vtok[:, c, hd * 256:(hd + 1) * 256], start=True, stop=False),
                          reads=["attm", ("vtok", c)], writes=[key])
                        A("pe", lambda e, o=o, hd=hd: e.matmul(o, lhsT=qtl[:, hd, :], rhs=Gbf[l][:, hd * 256:(hd + 1) * 256], start=False, stop=True),
                          reads=["qtl", ("Gbf", l)], writes=[key])
                        A("act", lambda e, o=o, hd=hd: e.activation(out=on[:, hd * 256:(hd + 1) * 256], in_=o, func=AF.Square, accum_out=sq4g[:, hd:hd + 1]), reads=[key], writes=[("sq4g", hd), ("on", hd // 2)])
                    STEP()
                A("dve", lambda e: e.tensor_scalar(out=r4g[:], in0=sq4g[:], scalar1=1.0 / 256, scalar2=EPS, op0=ALU.mult, op1=ALU.add), reads=[("sq4g", i) for i in range(4)], writes=["r4g"])
                A("pool", lambda e: e.tensor_tensor(out=r4g[:], in0=r4g[:], in1=bc(mhalf[:, 0:1], [128, 4]), op=ALU.pow), reads=["r4g", "mhalf"], writes=["r4g"])
                for hb in range(2):
                    bank, key = obk[hb]
                    A("dve", lambda e, bank=bank, hb=hb: e.tensor_tensor(out=on[:, hb * 512:(hb + 1) * 512].rearrange("p (a d) -> p a d", a=2), in0=bank[:, :].rearrange("p (a d) -> p a d", a=2),
                                                                          in1=bc(r4g[:, hb * 2:(hb + 1) * 2].unsqueeze(2), [128, 2, 256]), op=ALU.mult),
                      reads=[key, "r4g"], writes=[("on", hb)])
                STEP()
                NEED("g")
                bank, key = PT.get()
                for k in range(8):
                    A("pe", lambda e, k=k, bank=bank: e.transpose(out=bank[:, k * 128:(k + 1) * 128], in_=on[:, k * 128:(k + 1) * 128], identity=identb[:]),
                      reads=[("on", 0), ("on", 1), "identb"], writes=[key])
                A("dve", lambda e, bank=bank, cs=cs, c=c: e.tensor_tensor(out=xbcT[:, 8:16, 4 + c * 128:4 + (c + 1) * 128], in0=bank[:, :].rearrange("p (k t) -> p k t", k=8),
                                                                         in1=sgT[:, :, cs], op=ALU.mult),
                  reads=[key] + [("sgT", b) for b in range(8)], writes=[("xbcT", 8 + b) for b in range(8)])
                STEP()

            def I_tail(c):
                ktok = ktoks[c % len(ktoks)]
                for hb in range(2):
                    bank, key = PB.get()
                    for hh in range(2):
                        hd = hb * 2 + hh
                        A("pe", lambda e, bank=bank, hh=hh, hd=hd, c=c, ktok=ktok: e.matmul(bank[:, hh * 256:(hh + 1) * 256], lhsT=ktok[:, hd * 128:(hd + 1) * 128], rhs=vtok[:, c, hd * 256:(hd + 1) * 256], start=True, stop=True),
                          reads=[("ktok", c % len(ktoks)), ("vtok", c)], writes=[key])
                    hs = slice(hb * 512, (hb + 1) * 512)
                    A("dve", lambda e, bank=bank, hs=hs: e.tensor_tensor(out=Sgla[l][:, hs], in0=Sgla[l][:, hs], in1=bank[:, :], op=ALU.add),
                      reads=[key, ("Sgla", l, hb)], writes=[("Sgla", l, hb)])
                for hd in range(4):
                    A("act", lambda e, hd=hd: e.activation(out=Gbf[l][:, hd * 256:(hd + 1) * 256], in_=Sgla[l][:, hd * 256:(hd + 1) * 256], func=AF.Copy, scale=egl[:, c, hd:hd + 1]),
                      reads=[("Sgla", l, hd // 2), ("egl", c)], writes=[("Gbf", l)])
                for hd in range(4):
                    A("pool", lambda e, hd=hd: e.tensor_scalar(out=Sgla[l][:, hd * 256:(hd + 1) * 256], in0=Sgla[l][:, hd * 256:(hd + 1) * 256], scalar1=egl[:, c, hd:hd + 1], scalar2=0.0, op0=ALU.mult, op1=ALU.add),
                      reads=[("Sgla", l, hd // 2), ("egl", c)], writes=[("Sgla", l, hd // 2)])
                STEP()

            if OPT["ipipe"]:
                I_front(0)
                I_mid(0)
                for c in range(1, NCH):
                    I_front(c)
                    I_tail(c - 1)
                    I_mid(c)
                I_tail(NCH - 1)
            else:
                for c in range(NCH):
                    I_front(c)
                    I_mid(c)
                    I_tail(c)
            REC[0] = None
            PB.ring = PBfull
            if OPT["s4"]:
                merge([sH, sI, sF], given)
                merge_branch(0)
            else:
                merge([sI, sJ], given, OPT["fbias2"])
            MARK("P4")
            merge_branch(1)
            MARK("OUT")
            if OPT["outc"]:
                wos = [next_w((l, "out", 0), pf=1), next_w((l, "out", 1), pf=0)]
                oorder = [(c, j2) for c in range(NCH) for j2 in range(2)]
            else:
                wos = [None, None]
                oorder = [(c, j2) for j2 in range(2) for c in range(NCH)]
            for c, j2 in oorder:
                if True:
                    if wos[j2] is None:
                        wos[j2] = next_w((l, "out", j2))
                    wo, wokey = wos[j2]
                    bank, key = PB.get()
                    for k in range(8):
                        A("pe", lambda e, k=k, c=c, wo=wo, bank=bank: e.matmul(bank[:, :], lhsT=qkT[:, k, c * 128:(c + 1) * 128], rhs=wo[:, k, :], start=(k == 0), stop=(k == 7)),
                          reads=[("qkT", k) for k in range(8)] + [wokey], writes=[key])
                    A("dve", lambda e, bank=bank, c=c, j2=j2: e.scalar_tensor_tensor(out=h[:, c, j2 * 512:(j2 + 1) * 512], in0=bank[:, :], scalar=0.5, in1=h[:, c, j2 * 512:(j2 + 1) * 512], op0=ALU.mult, op1=ALU.add),
                      reads=[key, ("h", c)], writes=[("h", c)])
            if first and not last_layer:
                A("pool", lambda e: e.memset(h[0:112, 0, :], 0.0), writes=[("h", 0)])

        for ti in range(ntiles):
            for c in range(NCH):
                gc = ti * NCH + c
                if gc == 0:
                    A("pool", lambda e: e.memset(h[:, 0, :], 0.0), writes=[("h", 0)])
                    A("sp", lambda e: e.dma_start(out=h[112:128, 0, :], in_=meta_d), writes=[("h", 0)], dma_sem=xsems[0])
                else:
                    r0 = (gc - 1) * 128
                    A("sp", lambda e, c=c, r0=r0: e.dma_start(out=h[:, c, :], in_=x_d[r0:r0 + 128, :]), writes=[("h", c)], dma_sem=xsems[c])
            for l in range(NL):
                tile_layer(ti, l)
            for c in range(NCH):
                gc = ti * NCH + c
                if gc == 0:
                    continue
                r0 = (gc - 1) * 128
                ub, ukey = UB.get()
                A("act", lambda e, c=c, ub=ub: e.activation(out=ub[:], in_=h[:, c, :], func=AF.Square, accum_out=ssq[:, c:c + 1]), reads=[("h", c)], writes=[("ssq", c), ukey])
                A("dve", lambda e, c=c: e.tensor_scalar(out=rs[:, c:c + 1], in0=ssq[:, c:c + 1], scalar1=1.0 / D, scalar2=EPS, op0=ALU.mult, op1=ALU.add),
                  reads=[("ssq", c)], writes=[("rs", c)])
                A("pool", lambda e, c=c: e.tensor_tensor(out=rs[:, c:c + 1], in0=rs[:, c:c + 1], in1=mhalf[:], op=ALU.pow), reads=[("rs", c), "mhalf"], writes=[("rs", c)])
                A("dve", lambda e, c=c: e.scalar_tensor_tensor(out=h[:, c, :], in0=h[:, c, :], scalar=rs[:, c:c + 1], in1=fnw_bc[:], op0=ALU.mult, op1=ALU.mult),
                  reads=[("h", c), ("rs", c), "fnw"], writes=[("h", c)])
                ok = ("out", gc)
                out_keys.append(ok)
                A("sp", lambda e, c=c, r0=r0: e.dma_start(out=out_d[r0:r0 + 128, :], in_=h[:, c, :]), reads=[("h", c)], writes=[ok], dma_sem=osems[c])
        A("sp", None, reads=out_keys)
        assert wstate["used"] == len(wlist)
        build.model_us = dict(SIM["free"])
        build.model_log = SIM["log"]
        build.mlog = SIM["mlog"]
        with nc.Block() as block:
            S.emit(block)
    return nc


def host_consts():
    i = np.arange(128)
    tri = (i[:, None] <= i[None, :]).astype(np.float32)
    strict = (i[:, None] > i[None, :]).astype(np.float32)
    mask0 = (i >= 112).astype(np.float32)[:, None]
    sel = np.zeros((128, 3, 128), np.float32)
    for r in range(3):
        sel[r, r, :] = 1.0
    return {"ident": np.eye(128, dtype=np.float32), "tri": tri, "strict": strict, "mask0": mask0, "sel": sel.reshape(128, 384)}


def make_in_maps(inputs, ncores, TR):
    f = lambda a: np.ascontiguousarray(np.asarray(a, dtype=np.float32))
    common = dict(host_consts())
    common["meta"] = f(inputs["meta_tokens"])
    common["w_in"] = f(inputs["w_in"])
    common["w_br"] = f(inputs["w_branch"])
    common["w_out"] = f(inputs["w_out"])
    common["normw"] = f(np.asarray(inputs["norm_w"]).reshape(2, 8, 128).transpose(2, 0, 1).reshape(128, 16))
    common["cw"] = f(np.asarray(inputs["conv_w"]).reshape(2, 4, 16, 128).transpose(3, 0, 2, 1).reshape(128, 128))
    common["cb"] = f(np.asarray(inputs["conv_b"]).reshape(1, 4096))
    common["dtb"] = f(inputs["dt_bias"])
    common["alog"] = f(inputs["a_log"])
    common["dskip"] = f(inputs["d_skip"])
    common["ssdnw"] = f(np.asarray(inputs["ssd_norm_w"]).reshape(2, 8, 128).transpose(2, 0, 1).reshape(128, 16))
    common["w2"] = f(np.asarray(inputs["gla_gate_w2"]).transpose(1, 0, 2).reshape(16, 1024))
    common["gb"] = f(np.asarray(inputs["gla_gate_b"]).reshape(1, 1024))
    common["gnw"] = f(np.asarray(inputs["gla_norm_w"]).reshape(2, 2, 128).transpose(2, 0, 1).reshape(128, 4))
    common["fnw"] = f(np.asarray(inputs["final_norm_w"]).reshape(1, 1024))
    x = np.asarray(inputs["x"], dtype=np.float32)
    maps = []
    for b in range(ncores):
        m = dict(common)
        m["x"] = np.ascontiguousarray(x[b, :TR])
        maps.append(m)
    return maps


_NC_CACHE = {}


def kernel(**inputs):
    x = np.asarray(inputs["x"])
    B, T, _ = x.shape
    NT = T // 128 + 1
    key = (NT,)
    if key not in _NC_CACHE:
        _NC_CACHE[key] = build(NT=NT, NCH=3, NL=2)
    nc = _NC_CACHE[key]
    maps = make_in_maps(inputs, B, T)
    res = run_bass_kernel_spmd(nc, maps, core_ids=list(range(B)))
    out = np.stack([np.asarray(r["out"], dtype=np.float32) for r in res.results], axis=0)
    return out
```

```python
import math
import numpy as np
import concourse.bass as bass
import concourse.mybir as mybir
from concourse.bass_utils import run_bass_kernel_spmd
from contextlib import ExitStack

F32 = mybir.dt.float32
BF16 = mybir.dt.bfloat16
AF = mybir.ActivationFunctionType
ALU = mybir.AluOpType

D = 1024
NIN = 8224
EPS = 1e-6
C_Z, C_XBC, C_DT, C_Q, C_K, C_V, C_G, C_ALR, C_M = 0, 1024, 3072, 3088, 3600, 4112, 5136, 6160, 6176


class Op:
    __slots__ = ("eng", "fn", "deps", "sig", "count", "sem", "is_dma", "inc", "fin")

    def __init__(self, eng, fn, is_dma=False):
        self.eng = eng
        self.fn = fn
        self.deps = set()
        self.sig = False
        self.count = 0
        self.sem = None
        self.is_dma = is_dma
        self.inc = 1
        self.fin = 0.0


class Sched:
    ENGS = ("pe", "act", "dve", "pool", "sp")

    def __init__(self, nc, stack):
        self.nc = nc
        self.stack = stack
        self.ops = []
        self.res = {}
        self.engsem = {e: stack.enter_context(nc.semaphore("s_" + e)) for e in self.ENGS}
        self.dma_counts = {}
        self.alias = {}

    def new_sem(self, name):
        return self.stack.enter_context(self.nc.semaphore(name))

    def uniq(self, key):
        lst = self.alias.setdefault(key, [])
        k = ("__u", key, len(lst))
        lst.append(k)
        return k

    def add(self, eng, fn, reads=(), writes=(), dma_sem=None, raw_keys=False):
        op = Op(eng, fn, is_dma=dma_sem is not None)
        if not raw_keys:
            reads = [rr for r in reads for rr in self.alias.get(r, [r])]
            writes = [ww for w in writes for ww in self.alias.get(w, [w])]
        if dma_sem is not None:
            op.sem = dma_sem
            op.sig = True
            op.inc = 16
            c = self.dma_counts.get(id(dma_sem), 0) + 16
            self.dma_counts[id(dma_sem)] = c
            op.count = c
        for r in reads:
            st = self.res.get(r)
            if st is None:
                st = [None, []]
                self.res[r] = st
            if st[0] is not None:
                op.deps.add(st[0])
            st[1].append(op)
        for w in writes:
            st = self.res.get(w)
            if st is None:
                st = [None, []]
                self.res[w] = st
            if st[0] is not None:
                op.deps.add(st[0])
            last = {}
            for rd in st[1]:
                if rd is op:
                    continue
                if rd.is_dma:
                    op.deps.add(rd)
                else:
                    last[rd.eng] = rd
            for rd in last.values():
                op.deps.add(rd)
            st[0] = op
            st[1] = []
        op.deps.discard(op)
        self.ops.append(op)
        return op

    def peek(self, reads, writes):
        reads = [rr for r in reads for rr in self.alias.get(r, [r])]
        writes = [ww for w in writes for ww in self.alias.get(w, [w])]
        deps = []
        for r in reads:
            st = self.res.get(r)
            if st is not None and st[0] is not None:
                deps.append(st[0])
        for w in writes:
            st = self.res.get(w)
            if st is None:
                continue
            if st[0] is not None:
                deps.append(st[0])
            last = {}
            for rd in st[1]:
                if rd.is_dma:
                    deps.append(rd)
                else:
                    last[rd.eng] = rd
            deps.extend(last.values())
        return deps

    def finalize(self):
        for op in self.ops:
            if op.eng == "pe" and not op.is_dma:
                op.deps = {d for d in op.deps if not (d.eng == "pe" and not d.is_dma)}
            for d in op.deps:
                d.sig = True
        cnt = {e: 0 for e in self.ENGS}
        for op in self.ops:
            if op.is_dma:
                continue
            if op.sig:
                cnt[op.eng] += 1
                op.count = cnt[op.eng]
                op.sem = self.engsem[op.eng]

    def emit(self, block):
        self.finalize()
        by_eng = {e: [o for o in self.ops if o.eng == e] for e in self.ENGS}

        def run(engine, ops):
            waited = {}
            for op in ops:
                need = {}
                for d in op.deps:
                    k = id(d.sem)
                    if need.get(k, (None, 0))[1] < d.count:
                        need[k] = (d.sem, d.count)
                for k, (sem, c) in need.items():
                    if waited.get(k, 0) < c:
                        engine.wait_ge(sem, c)
                        waited[k] = c
                if op.fn is None:
                    continue
                ins = op.fn(engine)
                if op.sig:
                    ins.then_inc(op.sem, op.inc)

        @block.tensor
        def _(e):
            run(e, by_eng["pe"])

        @block.scalar
        def _(e):
            run(e, by_eng["act"])

        @block.vector
        def _(e):
            run(e, by_eng["dve"])

        @block.gpsimd
        def _(e):
            run(e, by_eng["pool"])

        @block.sync
        def _(e):
            run(e, by_eng["sp"])


class Ring:
    def __init__(self, items, name, keys=None):
        self.items = items
        self.name = name
        self.keys = keys if keys is not None else [(name, j) for j in range(len(items))]
        self.i = 0

    def get(self):
        j = self.i % len(self.items)
        self.i += 1
        return self.items[j], self.keys[j]


class RingProxy:
    def __init__(self, ring):
        self.ring = ring

    def get(self):
        return self.ring.get()


import os
OPT = {"zs_eng": "dve", "diag_eng": "dve", "ub_eng": "dve", "greedy": 0, "outc": 1, "b2late": 0, "at_split": 1, "hb": 3, "s4": 0, "qkfirst": 0, "stmp_early": 0, "sbf_dve": 0, "hpipe": 1, "gprepB": 1, "at_bf16": 1, "ipipe": 0, "fbias": -0.08, "fbias2": -0.3, "pt1": 1}
for _k in list(OPT):
    if os.environ.get("KOPT_" + _k):
        OPT[_k] = type(OPT[_k])(os.environ["KOPT_" + _k])


def build(NT=33, NCH=3, NL=2, dbg=False):
    assert NT % NCH == 0
    N = NCH * 128
    ntiles = NT // NCH
    TR = (NT - 1) * 128
    nc = bass.Bass("TRN2", target_bir_lowering=False)

    def din(name, shape):
        return nc.dram_tensor(name, list(shape), F32, kind="ExternalInput").ap()

    x_d = din("x", [TR, D])
    meta_d = din("meta", [16, D])
    win_d = din("w_in", [2, D, NIN])
    wbr_d = din("w_br", [2, 2, D, D])
    wout_d = din("w_out", [2, D, D])
    normw_d = din("normw", [128, 16])
    cw_d = din("cw", [128, 128])
    cb_d = din("cb", [1, 2 * 2048])
    dtb_d = din("dtb", [2, 16])
    alog_d = din("alog", [2, 16])
    dskip_d = din("dskip", [2, 16])
    ssdnw_d = din("ssdnw", [128, 16])
    w2_d = din("w2", [16, 2 * 512])
    gb_d = din("gb", [1, 2 * 512])
    gnw_d = din("gnw", [128, 4])
    fnw_d = din("fnw", [1, D])
    ident_d = din("ident", [128, 128])
    tri_d = din("tri", [128, 128])
    strict_d = din("strict", [128, 128])
    mask0_d = din("mask0", [128, 1])
    sel_d = din("sel", [128, 3 * 128])
    out_d = nc.dram_tensor("out", [TR, D], F32, kind="ExternalOutput").ap()

    with ExitStack() as st:
        S = Sched(nc, st)

        def sb(name, shape, dt=F32):
            return st.enter_context(nc.sbuf_tensor("s_" + name, list(shape), dt))

        def ps(name, shape, dt=F32):
            return st.enter_context(nc.psum_tensor(name, list(shape), dt))

        REC = [None]

        class _FakeIns:
            def then_inc(self, *a, **k):
                return self

        class _FakeEng:
            def __init__(self):
                self.calls = []

            def __getattr__(self, name):
                def f(*a, **k):
                    self.calls.append((name, a, k))
                    return _FakeIns()
                return f

        def _fsz(ap):
            n = 1
            for d_ in ap.shape[1:]:
                n *= d_
            return n

        def _in_psum(ap):
            return "PSum" in type(ap.tensor).__name__

        def est_dur(eng, fn, is_dma):
            if fn is None:
                return 0.0
            if is_dma:
                return 3.0
            try:
                fe = _FakeEng()
                fn(fe)
                name, a, k = fe.calls[-1]
                if eng == "pe":
                    if name == "transpose":
                        return 0.13
                    rhs = k["rhs"]
                    n = _fsz(rhs)
                    return max(64, n) / 2400.0 * (4.0 if rhs.dtype == F32 else 1.0) + 0.01
                src = k.get("in_", k.get("in0"))
                n = _fsz(src) if src is not None else 256
                if eng == "act":
                    return 0.22 + n * 0.0009 + (0.09 if k.get("accum_out") is not None else 0.0)
                if eng == "dve":
                    t = 0.07 + n * 0.00105
                    if name == "tensor_tensor" and not _in_psum(k["in0"]) and not _in_psum(k["in1"]):
                        t = 0.07 + n * 0.0021
                    return t
                if eng == "pool":
                    if name == "tensor_tensor" and k.get("op") == ALU.pow:
                        return 0.8
                    if name == "tensor_scalar":
                        return 0.1 + n * 0.0015
                    return 0.3 + n * 0.0016
            except Exception:
                pass
            return 0.5

        SIM = {"free": {e: 0.0 for e in Sched.ENGS}, "pebusy": 0.0, "log": [], "dbg": dbg, "mlog": []}

        def MARK(name):
            SIM["log"].append((name, max(SIM["free"].values()), SIM["free"]["pe"], SIM["pebusy"]))

        def _lat(d, eng):
            return 0.35 if d.eng != eng or d.is_dma else 0.12

        def commit(eng, fn, reads, writes, dma_sem, kw, dur):
            op = S.add(eng, fn, reads, writes, dma_sem=dma_sem, **kw)
            t = SIM["free"][eng]
            for d in op.deps:
                if eng == "pe" and d.eng == "pe" and not d.is_dma:
                    continue
                t = max(t, d.fin + _lat(d, eng))
            if dma_sem is not None:
                SIM["free"][eng] = t + 0.1
                op.fin = t + dur
            else:
                op.fin = t + dur
                SIM["free"][eng] = op.fin
                if eng == "pe":
                    SIM["pebusy"] += dur
            return op

        def A(eng, fn, reads=(), writes=(), dma_sem=None, **kw):
            dur = est_dur(eng, fn, dma_sem is not None)
            if REC[0] is None:
                return commit(eng, fn, reads, writes, dma_sem, kw, dur)
            REC[0].append(("op", (eng, fn, list(reads), list(writes), dma_sem, kw, dur)))
            return None

        def STEP():
            if REC[0] is not None:
                REC[0].append(("step",))

        def NEED(tok):
            if REC[0] is not None:
                REC[0].append(("step",))
                REC[0].append(("need", tok))

        def GIVE(tok):
            if REC[0] is not None:
                REC[0].append(("give", tok))

        def step_stall(step):
            free = dict(SIM["free"])
            lw, lr = {}, {}
            stall = 0.0
            for it in step:
                if it[0] != "op":
                    continue
                eng, fn, r, w, ds, kw, dur = it[1]
                t = free[eng]
                t0 = t
                for d in S.peek(r, w):
                    if eng == "pe" and d.eng == "pe" and not d.is_dma:
                        continue
                    t = max(t, d.fin + _lat(d, eng))
                for key in r:
                    if key in lw:
                        t = max(t, lw[key] + 0.3)
                for key in w:
                    if key in lw:
                        t = max(t, lw[key] + 0.3)
                    if key in lr:
                        t = max(t, lr[key] + 0.3)
                stall += (t - t0) * (1.0 if eng == "pe" else 0.35)
                fin = t + dur
                if ds is not None:
                    free[eng] = t + 0.1
                else:
                    free[eng] = fin
                for key in r:
                    lr[key] = max(lr.get(key, 0.0), fin)
                for key in w:
                    lw[key] = fin
                    lr.pop(key, None)
            return stall

        def merge(streams, given, bias=0.0):
            segs = []
            for stq in streams:
                steps = [[]]
                for it in stq:
                    if it[0] == "step":
                        if steps[-1]:
                            steps.append([])
                    else:
                        steps[-1].append(it)
                if not steps[-1]:
                    steps.pop()
                segs.append(steps)
            pos = [0] * len(segs)
            tot = [max(len(x), 1) for x in segs]
            while any(pos[i] < len(segs[i]) for i in range(len(segs))):
                cands = []
                for i in range(len(segs)):
                    if pos[i] >= len(segs[i]):
                        continue
                    step = segs[i][pos[i]]
                    if any(it[0] == "need" and it[1] not in given for it in step):
                        continue
                    cands.append((step_stall(step), i))
                assert cands, "merge deadlock"
                pick = None
                if OPT["greedy"]:
                    for st_, i in cands:
                        if st_ < 0.25:
                            pick = i
                            break
                    if pick is None:
                        pick = min(cands)[1]
                else:
                    pick = min(cands, key=lambda ci: pos[ci[1]] / tot[ci[1]] - (bias if ci[1] > 0 else 0.0))[1]
                for it in segs[pick][pos[pick]]:
                    if it[0] == "op":
                        eng, fn, r, w, ds, kw, dur = it[1]
                        commit(eng, fn, r, w, ds, kw, dur)
                    elif it[0] == "give":
                        given.add(it[1])
                if SIM.get("dbg"):
                    SIM["mlog"].append((pick, pos[pick], [round(c_[0], 2) for c_ in cands], dict(SIM["free"])))
                pos[pick] += 1
        h = sb("h", [128, NCH, D])
        ubs = [sb("ub%d" % i, [128, D], BF16) for i in range(2)]
        UB = Ring(ubs, "ub")
        uT = sb("uT", [128, 8, N], BF16)
        NWB = 3
        wbufs = [sb("wb%d" % i, [128, 8, 512], BF16) for i in range(NWB)]
        wsems = [S.new_sem("wsem%d" % i) for i in range(NWB)]
        wmisc = sb("wmisc", [128, 2, 8, 32], BF16)
        NDG = 4
        dgs = [sb("dg%d" % i, [128, 4, 128], BF16) for i in range(NDG)]
        DG = Ring(dgs, "dg")
        XW = N + 4
        xbcT = sb("xbcT", [128, 16, XW], BF16)
        hist = sb("hist", [128, 2, 16, 4], BF16)
        x2 = sb("x2", [128, NCH, D], BF16)
        B2tok = sb("B2tok", [128, NCH, 512], BF16)
        BCT = sb("BCT", [128, 8, N], BF16)
        zs = sb("zs", [128, NCH, D], BF16)
        qkT = sb("qkT", [128, 8, N], BF16)
        vtok = sb("vtok", [128, NCH, D], BF16)
        sgT = sb("sgT", [128, 8, N], BF16)
        alrT = sb("alrT", [128, N], BF16)
        ths = [sb("th%d" % i, [128, 512], BF16) for i in range(2)]
        TH = Ring(ths, "th")
        gws = [sb("gw%d" % i, [128, 512], BF16) for i in range(2)]
        GW = Ring(gws, "gw")
        ssq = sb("ssq", [128, NCH])
        rs = sb("rs", [128, NCH])
        dtr = sb("dtr", [128, NCH, 16])
        dte_x = sb("dte_x", [128, NCH, 16])
        dt_t = sb("dt_t", [128, NCH, 16])
        da_t = sb("da_t", [128, NCH, 16])
        ecumh = sb("ecumh", [128, NCH, 16])
        dte = sb("dte", [128, NCH, 16])
        cdb = sb("cdb", [128, NCH, 16])
        wq = sb("wq", [128, NCH, 16])
        dth = sb("dth", [128, NCH, 16])
        sq4 = sb("sq4", [128, 4])
        r4 = sb("r4", [128, 4])
        sq4g = sb("sq4g", [128, 4])
        r4g = sb("r4g", [128, 4])
        egl = sb("egl", [128, NCH, 4])
        At = sb("At", [128, 16, 128], BF16) if OPT["at_bf16"] else sb("At", [128, 8, 128])
        Lm = sb("Lm", [128, 16, 128], BF16)
        MT = sb("MT", [128, 16, 128], BF16)
        scm = sb("scm", [128, 4, 128], BF16)
        xdt = sb("xdt", [128, D], BF16)
        xw = sb("xw", [128, D], BF16)
        xd = sb("xd", [128, D], BF16)
        t1 = sb("t1", [128, D])
        yn = sb("yn", [128, D], BF16)
        stmp = sb("stmp", [128, D])
        e1 = sb("e1", [128, 512])
        eg = sb("eg", [128, 512])
        emg = sb("emg", [128, 512])
        on = sb("on", [128, D], BF16)
        qtl = sb("qtl", [128, 4, 128], BF16)
        ktl = sb("ktl", [128, 4, 128], BF16)
        ktoks = [sb("ktok%d" % i, [128, 512], BF16) for i in range(2 if OPT["ipipe"] else 1)]
        attm = sb("attm", [128, 4, 128], BF16)
        gth0 = sb("gth0", [128, 4, N], BF16)
        gth = [gth0, gth0]
        t0m = sb("t0m", [128, 8, N], BF16)
        t1ms = [sb("t1m%d" % i, [128, N]) for i in range(1 if OPT["ipipe"] else 2)]
        T1M = Ring(t1ms, "t1m")
        Sssd = [sb("Sssd%d" % l, [128, D]) for l in range(2)]
        Sbf = [sb("Sbf%d" % l, [128, D], BF16) for l in range(2)]
        Sgla = [sb("Sgla%d" % l, [128, D]) for l in range(2)]
        Gbf = [sb("Gbf%d" % l, [128, D], BF16) for l in range(2)]
        identf = sb("identf", [128, 128])
        identb = sb("identb", [128, 128], BF16)
        tri = sb("tri", [128, 128])
        tri16 = sb("tri16", [128, 128])
        strict = sb("strict", [128, 128])
        maskb = sb("maskb", [128, 128], BF16)
        maskq = sb("maskq", [128, 128], BF16)
        onesf = sb("onesf", [128, 128])
        sel = sb("sel", [128, 3, 128], BF16)
        selN = sb("selN", [128, 2, N], BF16)
        l1b = sb("l1b", [128, NCH, 512], BF16)
        tri16b = sb("tri16b", [128, 128], BF16)
        rows = sb("rows", [128, 2048], BF16)
        gb8 = sb("gb8", [8, 128])
        gb8h = sb("gb8h", [8, 128], BF16)
        gb8hf = sb("gb8hf", [8, 128])
        gb8l = sb("gb8l", [8, 128], BF16)
        mhalf = sb("mhalf", [128, 1])
        mask0 = sb("mask0", [128, 1])
        normw = sb("normw", [128, 16])
        cw = sb("cw", [128, 128])
        dtb_bc = sb("dtb_bc", [128, 2, 16])
        a_bc = sb("a_bc", [128, 2, 16])
        dskh_bc = sb("dskh_bc", [128, 2, 16])
        ssdnw = sb("ssdnw", [128, 16])
        w2b = sb("w2b", [128, 1024], BF16)
        gnwh = sb("gnwh", [128, 4])
        fnw_bc = sb("fnw_bc", [128, D])
        NPB = 7 if OPT["pt1"] else 6
        pbs = [ps("pb%d" % i, [128, 512]) for i in range(NPB)]
        PBfull = Ring(pbs, "pb")
        PBh = Ring(pbs[0:3], "pb", keys=[("pb", j) for j in range(0, 3)])
        if OPT["s4"]:
            PBi = Ring(pbs[3:5], "pb", keys=[("pb", j) for j in range(3, 5)])
            PBf = Ring(pbs[5:6], "pb", keys=[("pb", 5)])
        elif OPT["hpipe"]:
            PBh = Ring(pbs[0:2], "pb", keys=[("pb", j) for j in range(0, 2)])
            YD = Ring(pbs[2:4], "pb", keys=[("pb", j) for j in range(2, 4)])
            PBi = PBh
            PBf = Ring(pbs[4:NPB], "pb", keys=[("pb", j) for j in range(4, NPB)])
        else:
            PBi = PBh
            PBf = Ring(pbs[3:6], "pb", keys=[("pb", j) for j in range(3, 6)])
        PB = RingProxy(PBfull)
        pts = [ps("pt%d" % i, [128, 1024], BF16) for i in range(1 if OPT["pt1"] else 2)]
        PT = Ring(pts, "pt")

        csem = S.new_sem("csem")

        csemp = S.new_sem("csemp")
        cops = []
        copsp = []

        def cload(dst, src, key, eng="sp"):
            if eng == "sp":
                cops.append(A(eng, lambda e: e.dma_start(out=dst, in_=src), writes=[S.uniq(key)], dma_sem=csem, raw_keys=True))
            else:
                copsp.append(A(eng, lambda e: e.dma_start(out=dst, in_=src), writes=[S.uniq(key)], dma_sem=csemp, raw_keys=True))

        cload(identf[:], ident_d, "identf")
        cload(tri[:], tri_d, "tri")
        cload(strict[:], strict_d, "strict")
        cload(mask0[:], mask0_d, "mask0")
        cload(normw[:], normw_d, "normw")
        cload(cw[:], cw_d, "cw")
        cload(ssdnw[:], ssdnw_d, "ssdnw")
        cload(gnwh[:], gnw_d, "gnwh")
        cload(gb8[:], gb_d.rearrange("o (a b) -> (o a) b", a=8), "gb8")
        cload(fnw_bc[:], fnw_d[0:1, :].partition_broadcast(128), "fnw")
        for l in range(2):
            cload(dtb_bc[:, l, :], dtb_d[l:l + 1, :].partition_broadcast(128), "dtb")
            cload(a_bc[:, l, :], alog_d[l:l + 1, :].partition_broadcast(128), "a_bc")
            cload(dskh_bc[:, l, :], dskip_d[l:l + 1, :].partition_broadcast(128), "dskh")
        A("pool", lambda e: e.memset(rows[:], 0.0), writes=["rows0"])
        A("pool", lambda e: e.memset(alrT[:], 0.0), writes=["alrT"])
        A("pool", lambda e: e.memset(w2b[:], 0.0), writes=["w2b0"])
        copsp.append(A("pool", lambda e: e.dma_start(out=rows[0:1, :], in_=cb_d[:, 0:2048]), reads=["rows0"], writes=[S.uniq("rows")], dma_sem=csemp, raw_keys=True))
        copsp.append(A("pool", lambda e: e.dma_start(out=rows[1:2, :], in_=cb_d[:, 2048:4096]), reads=["rows0"], writes=[S.uniq("rows")], dma_sem=csemp, raw_keys=True))
        cload(sel[:].rearrange("p r m -> p (r m)"), sel_d, "sel", eng="pool")
        copsp.append(A("pool", lambda e: e.dma_start(out=w2b[0:16, :], in_=w2_d), reads=["w2b0"], writes=[S.uniq("w2b")], dma_sem=csemp, raw_keys=True))
        for l in range(2):
            cload(wmisc[:, l, :, 0:16], win_d[l, :, C_DT:C_DT + 16].rearrange("(k p) n -> p k n", p=128), "wmisc", eng="pool")
            cload(wmisc[:, l, :, 16:32], win_d[l, :, C_ALR:C_ALR + 16].rearrange("(k p) n -> p k n", p=128), "wmisc", eng="pool")

        for o in cops:
            o.count = 16 * len(cops)
        for o in copsp:
            o.count = 16 * len(copsp)
        A("dve", lambda e: e.tensor_copy(out=identb[:], in_=identf[:]), reads=["identf"], writes=["identb"])
        A("dve", lambda e: e.tensor_copy(out=maskb[:], in_=tri[:]), reads=["tri"], writes=["maskb"])
        A("dve", lambda e: e.tensor_scalar(out=maskq[:], in0=tri[:], scalar1=0.25, scalar2=None, op0=ALU.mult), reads=["tri"], writes=["maskq"])
        A("dve", lambda e: e.tensor_scalar(out=tri16[:], in0=tri[:], scalar1=-1.0 / 16.0, scalar2=None, op0=ALU.mult), reads=["tri"], writes=["tri16"])
        A("dve", lambda e: e.tensor_scalar(out=tri16b[:], in0=tri[:], scalar1=-1.0 / 16.0, scalar2=None, op0=ALU.mult), reads=["tri"], writes=["tri16b"])
        A("pool", lambda e: e.memset(onesf[:], 1.0), writes=["onesf"])
        A("pool", lambda e: e.memset(mhalf[:], -0.5), writes=["mhalf"])
        for r in range(2):
            A("pool", lambda e, r=r: e.tensor_copy(out=selN[:, r, :], in_=bc(sel[:, r, 0:1], [128, N])), reads=["sel"], writes=["selN"])
        A("pool", lambda e: e.memset(hist[:], 0.0), writes=["hist"])
        A("pool", lambda e: e.memset(xbcT[:], 0.0), writes=[("xbcT", b) for b in range(16)])
        for l in range(2):
            A("pool", lambda e, l=l: e.memset(Sssd[l][:], 0.0), writes=[("Sssd", l)])
            A("pool", lambda e, l=l: e.memset(Sbf[l][:], 0.0), writes=[("Sbf", l)])
            A("pool", lambda e, l=l: e.memset(Sgla[l][:], 0.0), writes=[("Sgla", l, 0), ("Sgla", l, 1)])
            A("pool", lambda e, l=l: e.memset(Gbf[l][:], 0.0), writes=[("Gbf", l)])
        A("act", lambda e: e.activation(out=a_bc[:], in_=a_bc[:], func=AF.Exp), reads=["a_bc"], writes=["a_bc"])
        A("dve", lambda e: e.tensor_scalar(out=a_bc[:], in0=a_bc[:], scalar1=-1.0, scalar2=None, op0=ALU.mult), reads=["a_bc"], writes=["a_bc"])
        A("dve", lambda e: e.tensor_scalar(out=dskh_bc[:], in0=dskh_bc[:], scalar1=0.5, scalar2=None, op0=ALU.mult), reads=["dskh"], writes=["dskh"])
        A("dve", lambda e: e.tensor_scalar(out=gnwh[:], in0=gnwh[:], scalar1=0.5, scalar2=None, op0=ALU.mult), reads=["gnwh"], writes=["gnwh"])
        A("dve", lambda e: e.tensor_copy(out=gb8h[:], in_=gb8[:]), reads=["gb8"], writes=["gb8h"])
        A("dve", lambda e: e.tensor_copy(out=gb8hf[:], in_=gb8h[:]), reads=["gb8h"], writes=["gb8hf"])
        A("dve", lambda e: e.tensor_tensor(out=gb8l[:], in0=gb8[:], in1=gb8hf[:], op=ALU.subtract), reads=["gb8", "gb8hf"], writes=["gb8l"])
        csem2 = S.new_sem("csem2")
        cops2 = []
        for a8 in range(8):
            cops2.append(A("sp", lambda e, a8=a8: e.dma_start(out=rows[2:3, a8 * 128:(a8 + 1) * 128], in_=gb8h[a8:a8 + 1, :]), reads=["gb8h", "rows0"], writes=[S.uniq("rows")], dma_sem=csem2))
            cops2.append(A("sp", lambda e, a8=a8: e.dma_start(out=rows[2:3, 1024 + a8 * 128:1024 + (a8 + 1) * 128], in_=gb8l[a8:a8 + 1, :]), reads=["gb8l", "rows0"], writes=[S.uniq("rows")], dma_sem=csem2))
        for o in cops2:
            o.count = 16 * len(cops2)

        def wsrc(l, kind, idx):
            if kind == "in":
                src = win_d[l, :, idx:idx + 512]
            elif kind == "br":
                b, j = idx
                src = wbr_d[l, b, :, j * 512:(j + 1) * 512]
            else:
                src = wout_d[l, :, idx * 512:(idx + 1) * 512]
            return src.rearrange("(k p) n -> p k n", p=128)

        def layer_groups(l):
            g = []
            for i in range(4):
                g.append((l, "in", C_XBC + 512 * i))
            if OPT["qkfirst"]:
                g.append((l, "in", C_Q))
                g.append((l, "in", C_K))
            for i in range(2):
                g.append((l, "in", C_Z + 512 * i))
            if not OPT["qkfirst"]:
                g.append((l, "in", C_Q))
                g.append((l, "in", C_K))
            for i in range(2):
                g.append((l, "in", C_V + 512 * i))
            for i in range(2):
                g.append((l, "in", C_G + 512 * i))
            for b in range(2):
                for j in range(2):
                    g.append((l, "in", C_M + b * 1024 + j * 512))
                    g.append((l, "br", (b, j)))
            for j in range(2):
                g.append((l, "out", j))
            return g

        wlist = []
        for ti in range(ntiles):
            for l in range(NL):
                wlist += layer_groups(l)
        wstate = {"issued": 0, "used": 0}
        PF = 2

        GP = len(layer_groups(0)) * NL
        wscr = nc.dram_tensor("wscr", [GP, 128, 8 * 512], BF16, kind="Internal").ap()
        wsemh = [S.new_sem("wsemh%d" % i) for i in range(NWB)]
        wsemst = [S.new_sem("wsemst%d" % i) for i in range(NWB)]

        def w_issue_upto(n):
            while wstate["issued"] < min(n, len(wlist)):
                i = wstate["issued"]
                l, kind, idx = wlist[i]
                slot = i % NWB
                if i < GP:
                    src = wsrc(l, kind, idx)
                    A("pool", lambda e, slot=slot, src=src: e.dma_start(out=wbufs[slot][:], in_=src),
                      writes=[("wb", slot)], dma_sem=wsems[slot])
                    if ntiles > 1:
                        A("sp", lambda e, slot=slot, i=i: e.dma_start(out=wscr[i], in_=wbufs[slot][:].rearrange("p k n -> p (k n)")),
                          reads=[("wb", slot)], writes=[("wscr", i)], dma_sem=wsemst[slot])
                else:
                    g = i % GP
                    A("sp", lambda e, slot=slot, g=g: e.dma_start(out=wbufs[slot][:].rearrange("p k n -> p (k n)"), in_=wscr[g]),
                      reads=[("wscr", g)], writes=[("wb", slot)], dma_sem=wsemh[slot])
                wstate["issued"] += 1

        def next_w(expect, pf=PF):
            i = wstate["used"]
            assert wlist[i] == expect, (wlist[i], expect)
            w_issue_upto(i + 1 + pf)
            wstate["used"] += 1
            return wbufs[i % NWB], ("wb", i % NWB)

        def bc(ap, shape):
            return ap.to_broadcast(list(shape))

        HALF_LN = math.log(0.5)
        QS_LN = math.log(128.0 ** -0.5)

        def silu2_evac(bank_ap, out_ap, key, wkey, ncols):
            th, tkey = TH.get()
            A("act", lambda e: e.activation(out=th[:, 0:ncols], in_=bank_ap, func=AF.Tanh, scale=0.5), reads=[key], writes=[tkey])
            A("dve", lambda e: e.scalar_tensor_tensor(out=out_ap, in0=th[:, 0:ncols], scalar=1.0, in1=bank_ap, op0=ALU.add, op1=ALU.mult),
              reads=[tkey, key], writes=[wkey])

        uT_all = [("uT", c) for c in range(NCH)]
        out_keys = []
        osems = [S.new_sem("osem%d" % i) for i in range(NCH)]
        xsems = [S.new_sem("xsem%d" % i) for i in range(NCH)]

        def tile_layer(ti, l):
            first = ti == 0
            last_layer = l == NL - 1
            MARK("A %d %d" % (ti, l))
            for c in range(NCH):
                ub, ukey = UB.get()
                A("act", lambda e, c=c, ub=ub: e.activation(out=ub[:], in_=h[:, c, :], func=AF.Square, accum_out=ssq[:, c:c + 1]),
                  reads=[("h", c)], writes=[("ssq", c), ukey])
                A("dve", lambda e, c=c: e.tensor_scalar(out=rs[:, c:c + 1], in0=ssq[:, c:c + 1], scalar1=1.0 / D, scalar2=EPS, op0=ALU.mult, op1=ALU.add),
                  reads=[("ssq", c)], writes=[("rs", c)])
                A("pool", lambda e, c=c: e.tensor_tensor(out=rs[:, c:c + 1], in0=rs[:, c:c + 1], in1=mhalf[:], op=ALU.pow),
                  reads=[("rs", c), "mhalf"], writes=[("rs", c)])
                if OPT["ub_eng"] == "act":
                    A("act", lambda e, c=c, ub=ub: e.activation(out=ub[:], in_=h[:, c, :], func=AF.Copy, scale=rs[:, c:c + 1]),
                      reads=[("h", c), ("rs", c)], writes=[ukey])
                else:
                    A("dve", lambda e, c=c, ub=ub: e.tensor_scalar(out=ub[:], in0=h[:, c, :], scalar1=rs[:, c:c + 1], scalar2=None, op0=ALU.mult),
                      reads=[("h", c), ("rs", c)], writes=[ukey])
                bank, key = PT.get()
                for k in range(8):
                    A("pe", lambda e, k=k, ub=ub, bank=bank: e.transpose(out=bank[:, k * 128:(k + 1) * 128], in_=ub[:, k * 128:(k + 1) * 128], identity=identb[:]),
                      reads=[ukey, "identb"], writes=[key])
                A("dve", lambda e, c=c, bank=bank: e.tensor_tensor(out=uT[:, :, c * 128:(c + 1) * 128], in0=bank[:, :].rearrange("p (k t) -> p k t", k=8),
                                                                    in1=bc(normw[:, l * 8:(l + 1) * 8].unsqueeze(2), [128, 8, 128]), op=ALU.mult),
                  reads=[key, "normw"], writes=[("uT", c)])
            MARK("B")
            for c in range(NCH):
                bank, key = PB.get()
                for k in range(8):
                    A("pe", lambda e, k=k, c=c, bank=bank: e.matmul(bank[:, 0:16], lhsT=uT[:, k, c * 128:(c + 1) * 128], rhs=wmisc[:, l, k, 0:16], start=(k == 0), stop=(k == 7)),
                      reads=[("uT", c), "wmisc"], writes=[key])
                A("dve", lambda e, c=c, bank=bank: e.tensor_tensor(out=dtr[:, c, :], in0=bank[:, 0:16], in1=dtb_bc[:, l, :], op=ALU.add),
                  reads=[key, "dtb"], writes=[("dtr", c)])
                A("act", lambda e, c=c: e.activation(out=dte_x[:, c, :], in_=dtr[:, c, :], func=AF.Exp), reads=[("dtr", c)], writes=[("dte_x", c)])
            for c in range(NCH):
                A("act", lambda e, c=c: e.activation(out=dt_t[:, c, :], in_=dte_x[:, c, :], func=AF.Ln, bias=1.0), reads=[("dte_x", c)], writes=[("dt", c)])
            for c in range(NCH):
                if first and c == 0:
                    A("dve", lambda e, c=c: e.tensor_scalar(out=dt_t[:, c, :], in0=dt_t[:, c, :], scalar1=mask0[:, 0:1], scalar2=None, op0=ALU.mult),
                      reads=[("dt", c), "mask0"], writes=[("dt", c)])
                A("dve", lambda e, c=c: e.tensor_tensor(out=da_t[:, c, :], in0=dt_t[:, c, :], in1=a_bc[:, l, :], op=ALU.mult),
                  reads=[("dt", c), "a_bc"], writes=[("da", c)])
            def stage_B2(c):
                bank2, key2 = PB.get()
                A("pe", lambda e, c=c, bank2=bank2: e.matmul(bank2[:, 0:16], lhsT=tri[:], rhs=da_t[:, c, :], start=True, stop=True), reads=[("da", c), "tri"], writes=[key2])
                A("pe", lambda e, c=c, bank2=bank2: e.matmul(bank2[:, 16:32], lhsT=onesf[:], rhs=da_t[:, c, :], start=True, stop=True), reads=[("da", c), "onesf"], writes=[key2])
                A("pe", lambda e, c=c, bank2=bank2: e.matmul(bank2[:, 32:48], lhsT=strict[:], rhs=da_t[:, c, :], start=True, stop=True), reads=[("da", c), "strict"], writes=[key2])
                A("act", lambda e, c=c, bank2=bank2: e.activation(out=ecumh[:, c, :], in_=bank2[:, 0:16], func=AF.Exp, bias=HALF_LN), reads=[key2], writes=[("ecumh", c)])
                A("act", lambda e, c=c, bank2=bank2: e.activation(out=cdb[:, c, :], in_=bank2[:, 16:32], func=AF.Exp), reads=[key2], writes=[("cdb", c)])
                A("act", lambda e, c=c, bank2=bank2: e.activation(out=dte[:, c, :], in_=bank2[:, 32:48], func=AF.Exp), reads=[key2], writes=[("dte", c)])
                A("dve", lambda e, c=c: e.scalar_tensor_tensor(out=wq[:, c, :], in0=dt_t[:, c, :], scalar=0.25, in1=dte[:, c, :], op0=ALU.mult, op1=ALU.mult),
                  reads=[("dt", c), ("dte", c)], writes=[("wq", c)])
                A("dve", lambda e, c=c: e.tensor_scalar(out=dth[:, c, :], in0=dt_t[:, c, :], scalar1=0.5, scalar2=None, op0=ALU.mult),
                  reads=[("dt", c)], writes=[("dth", c)])
            if not OPT["b2late"]:
                for c in range(NCH):
                    stage_B2(c)
            bank, key = PB.get()
            for k in range(8):
                A("pe", lambda e, k=k, bank=bank: e.matmul(bank[0:16, 0:N], lhsT=wmisc[:, l, k, 16:32], rhs=uT[:, k, :], start=(k == 0), stop=(k == 7)),
                  reads=uT_all + ["wmisc"], writes=[key])
            A("act", lambda e, bank=bank: e.activation(out=alrT[0:16, :], in_=bank[0:16, 0:N], func=AF.Copy), reads=[key], writes=["alrT"])
            if OPT["gprepB"]:
                for c in range(NCH):
                    cs = slice(c * 128, (c + 1) * 128)
                    bank, key = PB.get()
                    A("pe", lambda e, bank=bank, cs=cs: e.matmul(bank[:, :], lhsT=alrT[:, cs], rhs=w2b[:, l * 512:(l + 1) * 512], start=True, stop=False),
                      reads=["alrT", "w2b"], writes=[key])
                    A("pe", lambda e, bank=bank: e.matmul(bank[:, :], lhsT=sel[:, 2, :], rhs=rows[:, l * 512:(l + 1) * 512], start=False, stop=False),
                      reads=["sel", "rows"], writes=[key])
                    A("pe", lambda e, bank=bank: e.matmul(bank[:, :], lhsT=sel[:, 2, :], rhs=rows[:, 1024 + l * 512:1024 + (l + 1) * 512], start=False, stop=True),
                      reads=["sel", "rows"], writes=[key])
                    A("act", lambda e, bank=bank: e.activation(out=e1[:], in_=bank[:, :], func=AF.Exp, scale=-1.0), reads=[key], writes=["e1"])
                    A("act", lambda e, c=c: e.activation(out=l1b[:, c, :], in_=e1[:], func=AF.Ln, bias=1.0), reads=["e1"], writes=[("l1b", c)])
            MARK("C")
            for gi in range(4):
                wb, wkey = next_w((l, "in", C_XBC + 512 * gi))
                for j in range(4):
                    blk = gi * 4 + j
                    bank, key = PB.get()
                    for k in range(8):
                        A("pe", lambda e, k=k, j=j, wb=wb, bank=bank: e.matmul(bank[:, 0:N], lhsT=wb[:, k, j * 128:(j + 1) * 128], rhs=uT[:, k, :], start=(k == 0), stop=(k == 7)),
                          reads=uT_all + [wkey], writes=[key])
                    A("act", lambda e, blk=blk, bank=bank: e.activation(out=xbcT[:, blk, 4:4 + N], in_=bank[:, 0:N], func=AF.Copy), reads=[key], writes=[("xbcT", blk)])
            if OPT["b2late"]:
                for c in range(NCH):
                    stage_B2(c)
            allx = [("xbcT", b) for b in range(16)]
            A("pool", lambda e: e.tensor_copy(out=xbcT[:, :, 1:4], in_=hist[:, l, :, 1:4]), reads=[("hist", l)], writes=allx)
            A("pool", lambda e: e.tensor_copy(out=hist[:, l, :, 1:4], in_=xbcT[:, :, N + 1:N + 4]), reads=allx, writes=[("hist", l)])
            MARK("E")
            dgof = {}

            def gen_diag(blk):
                dg, dkey = DG.get()
                for tap in range(4):
                    col = (l * 16 + blk) * 4 + tap
                    A(OPT["diag_eng"], lambda e, dg=dg, tap=tap, col=col: e.tensor_scalar(out=dg[:, tap, :], in0=identb[:], scalar1=cw[:, col:col + 1], scalar2=0.0, op0=ALU.mult, op1=ALU.add),
                      reads=["identb", "cw"], writes=[dkey])
                dgof[blk] = (dg, dkey)

            def conv_tok(c, blks, out_ap, wkey):
                bank, key = PB.get()
                for j, blk in enumerate(blks):
                    dg, dkey = dgof[blk]
                    o = bank[:, j * 128:(j + 1) * 128]
                    A("pe", lambda e, o=o, blk=blk: e.matmul(o, lhsT=sel[:, l, :], rhs=rows[:, blk * 128:(blk + 1) * 128], start=True, stop=False),
                      reads=["sel", "rows"], writes=[key])
                    for tap in range(4):
                        A("pe", lambda e, o=o, blk=blk, tap=tap, dg=dg, c=c: e.matmul(o, lhsT=xbcT[:, blk, c * 128 + 1 + tap:c * 128 + 1 + tap + 128], rhs=dg[:, tap, :], start=False, stop=(tap == 3)),
                          reads=[("xbcT", blk), dkey], writes=[key])
                silu2_evac(bank[:, 0:128 * len(blks)], out_ap, key, wkey, 128 * len(blks))

            for half in range(2):
                blks = [half * 4 + j for j in range(4)]
                for b in blks:
                    gen_diag(b)
                for c in range(NCH):
                    conv_tok(c, blks, x2[:, c, half * 512:(half + 1) * 512], ("x2", c))
            for blk in range(8, 16):
                if blk % 4 == 0:
                    for b4 in range(blk, blk + 4):
                        gen_diag(b4)
                    if blk == 8:
                        for c in range(NCH):
                            conv_tok(c, [8, 9, 10, 11], B2tok[:, c, :], ("B2tok", c))
                dg, dkey = dgof[blk]
                bank, key = PB.get()
                A("pe", lambda e, bank=bank, blk=blk: e.matmul(bank[:, 0:N], lhsT=rows[:, blk * 128:(blk + 1) * 128], rhs=selN[:, l, :], start=True, stop=False),
                  reads=["selN", "rows"], writes=[key])
                for tap in range(4):
                    A("pe", lambda e, bank=bank, blk=blk, tap=tap, dg=dg: e.matmul(bank[:, 0:N], lhsT=dg[:, tap, :], rhs=xbcT[:, blk, 1 + tap:1 + tap + N], start=False, stop=(tap == 3)),
                      reads=[("xbcT", blk), dkey], writes=[key])
                silu2_evac(bank[:, 0:N], BCT[:, blk - 8, :], key, ("BCT", blk - 8), N)
            MARK("P2")
            sF = []
            REC[0] = sF
            PB.ring = PBf
            for part in (("qk", "z") if OPT["qkfirst"] else ("z", "qk")):
              if part == "z":
                for gi in range(2):
                    wb, wkey = next_w((l, "in", C_Z + 512 * gi))
                    for c in range(NCH):
                        bank, key = PB.get()
                        for k in range(8):
                            A("pe", lambda e, k=k, c=c, wb=wb, bank=bank: e.matmul(bank[:, :], lhsT=uT[:, k, c * 128:(c + 1) * 128], rhs=wb[:, k, :], start=(k == 0), stop=(k == 7)),
                              reads=[("uT", c), wkey], writes=[key])
                        silu2_evac(bank[:, :], zs[:, c, gi * 512:(gi + 1) * 512], key, ("zs", c), 512)
                        if gi == 1:
                            GIVE(("zs", c))
                        STEP()
              else:
                for qk in range(2):
                    wb, wkey = next_w((l, "in", C_Q if qk == 0 else C_K))
                    for j in range(4):
                        bank, key = PB.get()
                        for k in range(8):
                            A("pe", lambda e, k=k, j=j, wb=wb, bank=bank: e.matmul(bank[:, 0:N], lhsT=wb[:, k, j * 128:(j + 1) * 128], rhs=uT[:, k, :], start=(k == 0), stop=(k == 7)),
                              reads=uT_all + [wkey], writes=[key])
                        A("act", lambda e, bank=bank, qk=qk, j=j: e.activation(out=qkT[:, qk * 4 + j, :], in_=bank[:, 0:N], func=AF.Copy), reads=[key], writes=[("qkT", qk * 4 + j)])
                        if j == 3:
                            GIVE("q" if qk == 0 else "k")
                        STEP()
            for gi in range(2):
                wb, wkey = next_w((l, "in", C_V + 512 * gi))
                for c in range(NCH):
                    bank, key = PB.get()
                    for k in range(8):
                        A("pe", lambda e, k=k, c=c, wb=wb, bank=bank: e.matmul(bank[:, :], lhsT=uT[:, k, c * 128:(c + 1) * 128], rhs=wb[:, k, :], start=(k == 0), stop=(k == 7)),
                          reads=[("uT", c), wkey], writes=[key])
                    A("act", lambda e, bank=bank, c=c, gi=gi: e.activation(out=vtok[:, c, gi * 512:(gi + 1) * 512], in_=bank[:, :], func=AF.Copy), reads=[key], writes=[("vtok", c)])
                    if gi == 1:
                        GIVE(("v", c))
                    STEP()
            for gi in range(2):
                wb, wkey = next_w((l, "in", C_G + 512 * gi))
                for j in range(4):
                    blk = gi * 4 + j
                    bank, key = PB.get()
                    for k in range(8):
                        A("pe", lambda e, k=k, j=j, wb=wb, bank=bank: e.matmul(bank[:, 0:N], lhsT=wb[:, k, j * 128:(j + 1) * 128], rhs=uT[:, k, :], start=(k == 0), stop=(k == 7)),
                          reads=uT_all + [wkey], writes=[key])
                    gw, gkey = GW.get()
                    th, tkey = TH.get()
                    A("act", lambda e, bank=bank, gw=gw, blk=blk: e.activation(out=gw[:, 0:N], in_=bank[:, 0:N], func=AF.Copy, scale=gnwh[:, l * 2 + (blk % 2):l * 2 + (blk % 2) + 1]),
                      reads=[key, "gnwh"], writes=[gkey])
                    A("act", lambda e, bank=bank, th=th: e.activation(out=th[:, 0:N], in_=bank[:, 0:N], func=AF.Tanh, scale=0.5), reads=[key], writes=[tkey])
                    A("dve", lambda e, gw=gw, th=th, blk=blk: e.scalar_tensor_tensor(out=sgT[:, blk, :], in0=th[:, 0:N], scalar=1.0, in1=gw[:, 0:N], op0=ALU.add, op1=ALU.mult),
                      reads=[gkey, tkey], writes=[("sgT", blk)])
                    if blk == 7:
                        GIVE("g")
                        GIVE("Fdone")
                    STEP()
            sH = []
            REC[0] = sH
            PB.ring = PBh
            YDK = {}

            def H_front(c):
                cs = slice(c * 128, (c + 1) * 128)
                for hv in range(2):
                    for hh in range(8):
                        hd = hv * 8 + hh
                        ai = hd if OPT["at_bf16"] else hh
                        if OPT["at_split"] and hh % 2 == 1:
                            A("pool", lambda e, ai=ai, hd=hd, c=c: e.tensor_scalar(out=At[:, ai, :], in0=strict[:], scalar1=da_t[:, c, hd:hd + 1], scalar2=0.0, op0=ALU.mult, op1=ALU.add),
                              reads=["strict", ("da", c)], writes=[("At", ai)])
                        else:
                            A("dve", lambda e, ai=ai, hd=hd, c=c: e.tensor_scalar(out=At[:, ai, :], in0=strict[:], scalar1=da_t[:, c, hd:hd + 1], scalar2=None, op0=ALU.mult),
                              reads=["strict", ("da", c)], writes=[("At", ai)])
                    for qd in range(2):
                        bank, key = PB.get()
                        for hq in range(4):
                            hh = qd * 4 + hq
                            ai = (hv * 8 + hh) if OPT["at_bf16"] else hh
                            A("pe", lambda e, bank=bank, hq=hq, ai=ai: e.matmul(bank[:, hq * 128:(hq + 1) * 128], lhsT=At[:, ai, :], rhs=(maskb[:] if OPT["at_bf16"] else tri[:]), start=True, stop=True),
                              reads=[("At", ai), "tri", "maskb"], writes=[key])
                        h0 = hv * 8 + qd * 4
                        A("act", lambda e, bank=bank, h0=h0: e.activation(out=Lm[:, h0:h0 + 4, :], in_=bank[:, :].rearrange("p (a t) -> p a t", a=4), func=AF.Exp),
                          reads=[key], writes=[("Lm", h0 // 4)])
                        STEP()
                bank, key = PB.get()
                for g in range(4):
                    A("pe", lambda e, bank=bank, g=g, cs=cs: e.matmul(bank[:, g * 128:(g + 1) * 128], lhsT=BCT[:, g, cs], rhs=BCT[:, 4 + g, cs], start=True, stop=True),
                      reads=[("BCT", g), ("BCT", 4 + g)], writes=[key])
                A("dve", lambda e, bank=bank: e.tensor_tensor(out=scm[:], in0=bank[:, :].rearrange("p (g t) -> p g t", g=4), in1=bc(maskq[:, :].unsqueeze(1), [128, 4, 128]), op=ALU.mult),
                  reads=[key, "maskq"], writes=["scm"])
                A("dve", lambda e: e.tensor_tensor(out=MT[:].rearrange("p (g a) t -> p g a t", g=4), in0=Lm[:].rearrange("p (g a) t -> p g a t", g=4),
                                                   in1=bc(scm[:].unsqueeze(2), [128, 4, 4, 128]), op=ALU.mult),
                  reads=["scm"] + [("Lm", i) for i in range(4)], writes=["MT"])
                STEP()
                x3 = x2[:, c, :].rearrange("p (a d) -> p a d", a=16)
                A("pool", lambda e, x3=x3, c=c: e.tensor_tensor(out=xdt[:].rearrange("p (a d) -> p a d", a=16), in0=x3, in1=bc(dth[:, c, :].unsqueeze(2), [128, 16, 64]), op=ALU.mult),
                  reads=[("x2", c), ("dth", c)], writes=["xdt"])
                if not OPT["hpipe"]:
                    A("pool", lambda e, x3=x3, c=c: e.tensor_tensor(out=xw[:].rearrange("p (a d) -> p a d", a=16), in0=x3, in1=bc(wq[:, c, :].unsqueeze(2), [128, 16, 64]), op=ALU.mult),
                      reads=[("x2", c), ("wq", c)], writes=["xw"])
                A("pool", lambda e, x3=x3: e.tensor_tensor(out=xd[:].rearrange("p (a d) -> p a d", a=16), in0=x3, in1=bc(dskh_bc[:, l, :].unsqueeze(2), [128, 16, 64]), op=ALU.mult),
                  reads=[("x2", c), "dskh"], writes=["xd"])
                if OPT["stmp_early"]:
                    A("pool", lambda e, c=c: e.tensor_tensor(out=stmp[:].rearrange("p (a d) -> p a d", a=16), in0=Sssd[l][:].rearrange("p (a d) -> p a d", a=16),
                                                             in1=bc(cdb[:, c, :].unsqueeze(2), [128, 16, 64]), op=ALU.mult),
                      reads=[("Sssd", l), ("cdb", c)], writes=["stmp", "stmp2"])
                STEP()
                ydk = []
                YDK[c] = ydk
                for hb in range(2):
                    bank, key = YD.get() if OPT["hpipe"] else PB.get()
                    ydk.append((bank, key))
                    A("pe", lambda e, bank=bank, hb=hb: e.matmul(bank[:, :], lhsT=identb[:], rhs=xd[:, hb * 512:(hb + 1) * 512], start=True, stop=False),
                      reads=["identb", "xd"], writes=[key])
                    for h8 in range(8):
                        hd = hb * 8 + h8
                        A("pe", lambda e, bank=bank, h8=h8, hd=hd: e.matmul(bank[:, h8 * 64:(h8 + 1) * 64], lhsT=MT[:, hd, :], rhs=xdt[:, hd * 64:(hd + 1) * 64], start=False, stop=(h8 == 7)),
                          reads=["MT", "xdt"], writes=[key])
                    STEP()

            def H_mid(c):
                cs = slice(c * 128, (c + 1) * 128)
                ydk = YDK[c]
                for hb in range(2):
                    bank, key = PB.get()
                    for gg in range(2):
                        g = hb * 2 + gg
                        A("pe", lambda e, bank=bank, gg=gg, g=g, cs=cs: e.matmul(bank[:, gg * 256:(gg + 1) * 256], lhsT=BCT[:, 4 + g, cs], rhs=Sbf[l][:, g * 256:(g + 1) * 256], start=True, stop=True),
                          reads=[("BCT", 4 + g), ("Sbf", l)], writes=[key])
                    hs = slice(hb * 512, (hb + 1) * 512)
                    A("dve", lambda e, bank=bank, hb=hb, hs=hs, c=c: e.tensor_tensor(out=t1[:, hs].rearrange("p (a d) -> p a d", a=8), in0=bank[:, :].rearrange("p (a d) -> p a d", a=8),
                                                                                  in1=bc(ecumh[:, c, hb * 8:(hb + 1) * 8].unsqueeze(2), [128, 8, 64]), op=ALU.mult),
                      reads=[key, ("ecumh", c)], writes=[("t1", hb)])
                    ybank, ykey = ydk[hb]
                    A("dve", lambda e, ybank=ybank, hs=hs: e.tensor_tensor(out=t1[:, hs], in0=t1[:, hs], in1=ybank[:, :], op=ALU.add),
                      reads=[ykey, ("t1", hb)], writes=[("t1", hb)])
                    NEED(("zs", c))
                    A(OPT["zs_eng"], lambda e, hs=hs, c=c: e.tensor_tensor(out=t1[:, hs], in0=t1[:, hs], in1=zs[:, c, hs], op=ALU.mult),
                      reads=[("t1", hb), ("zs", c)], writes=[("t1", hb)])
                    STEP()
                for g in range(4):
                    A("act", lambda e, g=g: e.activation(out=yn[:, g * 256:(g + 1) * 256], in_=t1[:, g * 256:(g + 1) * 256], func=AF.Square, accum_out=sq4[:, g:g + 1]),
                      reads=[("t1", g // 2)], writes=[("sq4", g), ("yn", g // 2)])
                sq4k = [("sq4", g) for g in range(4)]
                A("dve", lambda e: e.tensor_scalar(out=r4[:], in0=sq4[:], scalar1=1.0 / 256, scalar2=4 * EPS, op0=ALU.mult, op1=ALU.add), reads=sq4k, writes=["r4"])
                A("pool", lambda e: e.tensor_tensor(out=r4[:], in0=r4[:], in1=bc(mhalf[:, 0:1], [128, 4]), op=ALU.pow), reads=["r4", "mhalf"], writes=["r4"])
                for g in range(4):
                    gs = slice(g * 256, (g + 1) * 256)
                    if g % 2 == 0:
                        A("dve", lambda e, g=g, gs=gs: e.tensor_scalar(out=yn[:, gs], in0=t1[:, gs], scalar1=r4[:, g:g + 1], scalar2=None, op0=ALU.mult),
                          reads=["r4", ("t1", g // 2)], writes=[("yn", g // 2)])
                    else:
                        A("act", lambda e, g=g, gs=gs: e.activation(out=yn[:, gs], in_=t1[:, gs], func=AF.Copy, scale=r4[:, g:g + 1]),
                          reads=["r4", ("t1", g // 2)], writes=[("yn", g // 2)])
                STEP()

            def H_tail(c):
                if OPT["hpipe"]:
                    x3 = x2[:, c, :].rearrange("p (a d) -> p a d", a=16)
                    A("pool", lambda e, x3=x3, c=c: e.tensor_tensor(out=xw[:].rearrange("p (a d) -> p a d", a=16), in0=x3, in1=bc(wq[:, c, :].unsqueeze(2), [128, 16, 64]), op=ALU.mult),
                      reads=[("x2", c), ("wq", c)], writes=["xw"])
                bank, key = PT.get()
                for k in range(8):
                    A("pe", lambda e, k=k, bank=bank: e.transpose(out=bank[:, k * 128:(k + 1) * 128], in_=yn[:, k * 128:(k + 1) * 128], identity=identb[:]),
                      reads=[("yn", 0), ("yn", 1), "identb"], writes=[key])
                A("dve", lambda e, bank=bank, c=c: e.tensor_tensor(out=xbcT[:, 0:8, 4 + c * 128:4 + (c + 1) * 128], in0=bank[:, :].rearrange("p (k t) -> p k t", k=8),
                                                                    in1=bc(ssdnw[:, l * 8:(l + 1) * 8].unsqueeze(2), [128, 8, 128]), op=ALU.mult),
                  reads=[key, "ssdnw"], writes=[("xbcT", b) for b in range(8)])
                if c == NCH - 1:
                    GIVE("ysT")
                STEP()
                if not OPT["stmp_early"]:
                    A("pool", lambda e, c=c: e.tensor_tensor(out=stmp[:].rearrange("p (a d) -> p a d", a=16), in0=Sssd[l][:].rearrange("p (a d) -> p a d", a=16),
                                                             in1=bc(cdb[:, c, :].unsqueeze(2), [128, 16, 64]), op=ALU.mult),
                      reads=[("Sssd", l), ("cdb", c)], writes=["stmp", "stmp2"])
                for hb in range(2):
                    bank, key = PB.get()
                    for gg in range(2):
                        g = hb * 2 + gg
                        A("pe", lambda e, bank=bank, gg=gg, g=g, c=c: e.matmul(bank[:, gg * 256:(gg + 1) * 256], lhsT=B2tok[:, c, g * 128:(g + 1) * 128], rhs=xw[:, g * 256:(g + 1) * 256], start=True, stop=True),
                          reads=[("B2tok", c), "xw"], writes=[key])
                    hs = slice(hb * 512, (hb + 1) * 512)
                    if OPT["sbf_dve"]:
                        A("dve", lambda e, bank=bank, hs=hs: e.tensor_tensor(out=Sbf[l][:, hs], in0=stmp[:, hs], in1=bank[:, :], op=ALU.add),
                          reads=[key, "stmp" if hb == 0 else "stmp2"], writes=[("Sbf", l)])
                    A("dve", lambda e, bank=bank, hs=hs: e.tensor_tensor(out=Sssd[l][:, hs], in0=stmp[:, hs], in1=bank[:, :], op=ALU.add),
                      reads=[key, "stmp" if hb == 0 else "stmp2"], writes=[("Sssd", l)])
                if not OPT["sbf_dve"]:
                    A("act", lambda e: e.activation(out=Sbf[l][:], in_=Sssd[l][:], func=AF.Copy), reads=[("Sssd", l)], writes=[("Sbf", l)])
                STEP()

            if OPT["hpipe"]:
                H_front(0)
                H_mid(0)
                for c in range(1, NCH):
                    H_front(c)
                    H_tail(c - 1)
                    H_mid(c)
                H_tail(NCH - 1)
            else:
                for c in range(NCH):
                    H_front(c)
                    H_mid(c)
                    H_tail(c)

            REC[0] = None
            PB.ring = PBfull
            given = set()
            if not OPT["s4"]:
                merge([sH, sF], given, OPT["fbias"])
            MARK("P3")

            def merge_branch(b):
                for j in range(2):
                    wg, wgkey = next_w((l, "in", C_M + b * 1024 + j * 512))
                    for jj in range(4):
                        bank, key = PB.get()
                        for k in range(8):
                            A("pe", lambda e, k=k, jj=jj, wg=wg, bank=bank: e.matmul(bank[:, 0:N], lhsT=wg[:, k, jj * 128:(jj + 1) * 128], rhs=uT[:, k, :], start=(k == 0), stop=(k == 7)),
                              reads=uT_all + [wgkey], writes=[key])
                        A("act", lambda e, bank=bank, jj=jj: e.activation(out=gth[b][:, jj, :], in_=bank[:, 0:N], func=AF.Tanh, scale=0.5), reads=[key], writes=[("gth", jj)])
                        STEP()
                    wbr, wbkey = next_w((l, "br", (b, j)))
                    if b == 0:
                        NEED("ysT")
                    for jj in range(4):
                        bank, key = PB.get()
                        for k in range(8):
                            A("pe", lambda e, k=k, jj=jj, wbr=wbr, bank=bank: e.matmul(bank[:, 0:N], lhsT=wbr[:, k, jj * 128:(jj + 1) * 128], rhs=xbcT[:, b * 8 + k, 4:4 + N], start=(k == 0), stop=(k == 7)),
                              reads=[("xbcT", b * 8 + k) for k in range(8)] + [wbkey], writes=[key])
                        if b == 0:
                            A("dve", lambda e, bank=bank, jj=jj, j=j: e.scalar_tensor_tensor(out=t0m[:, j * 4 + jj, :], in0=gth[0][:, jj, :], scalar=1.0, in1=bank[:, 0:N], op0=ALU.add, op1=ALU.mult),
                              reads=[key, ("gth", jj)], writes=[("t0m", j * 4 + jj)])
                        else:
                            t1m, t1key = T1M.get()
                            A("dve", lambda e, bank=bank, jj=jj, t1m=t1m: e.scalar_tensor_tensor(out=t1m[:, :], in0=gth[1][:, jj, :], scalar=1.0, in1=bank[:, 0:N], op0=ALU.add, op1=ALU.mult),
                              reads=[key, ("gth", jj)], writes=[t1key])
                            A("pool", lambda e, jj=jj, j=j, t1m=t1m: e.tensor_tensor(out=qkT[:, j * 4 + jj, :], in0=t0m[:, j * 4 + jj, :], in1=t1m[:, :], op=ALU.add),
                              reads=[t1key, ("t0m", j * 4 + jj)], writes=[("qkT", j * 4 + jj)])
                        STEP()

            sJ = []
            if not OPT["s4"]:
                REC[0] = sJ
                PB.ring = PBf
                NEED("Fdone")
                merge_branch(0)
            sI = []
            REC[0] = sI
            PB.ring = PBi

            def I_front(c):
                cs = slice(c * 128, (c + 1) * 128)
                ktok = ktoks[c % len(ktoks)]
                if not OPT["gprepB"]:
                    bank, key = PB.get()
                    A("pe", lambda e, bank=bank, cs=cs: e.matmul(bank[:, :], lhsT=alrT[:, cs], rhs=w2b[:, l * 512:(l + 1) * 512], start=True, stop=False),
                      reads=["alrT", "w2b"], writes=[key])
                    A("pe", lambda e, bank=bank: e.matmul(bank[:, :], lhsT=sel[:, 2, :], rhs=rows[:, l * 512:(l + 1) * 512], start=False, stop=False),
                      reads=["sel", "rows"], writes=[key])
                    A("pe", lambda e, bank=bank: e.matmul(bank[:, :], lhsT=sel[:, 2, :], rhs=rows[:, 1024 + l * 512:1024 + (l + 1) * 512], start=False, stop=True),
                      reads=["sel", "rows"], writes=[key])
                    A("act", lambda e, bank=bank: e.activation(out=e1[:], in_=bank[:, :], func=AF.Exp, scale=-1.0), reads=[key], writes=["e1"])
                    A("act", lambda e: e.activation(out=e1[:], in_=e1[:], func=AF.Ln, bias=1.0), reads=["e1"], writes=["e1"])
                bankg, keyg = PB.get()
                for j in range(4):
                    if OPT["gprepB"]:
                        A("pe", lambda e, bankg=bankg, j=j, c=c: e.matmul(bankg[:, j * 128:(j + 1) * 128], lhsT=l1b[:, c, j * 128:(j + 1) * 128], rhs=tri16b[:], start=True, stop=True),
                          reads=[("l1b", c), "tri16b"], writes=[keyg])
                    else:
                        A("pe", lambda e, bankg=bankg, j=j: e.matmul(bankg[:, j * 128:(j + 1) * 128], lhsT=e1[:, j * 128:(j + 1) * 128], rhs=tri16[:], start=True, stop=True),
                          reads=["e1", "tri16"], writes=[keyg])
                A("act", lambda e, bankg=bankg: e.activation(out=eg[:], in_=bankg[:, :], func=AF.Exp, bias=QS_LN), reads=[keyg], writes=["eg"])
                A("act", lambda e, bankg=bankg: e.activation(out=emg[:], in_=bankg[:, :], func=AF.Exp, scale=-1.0), reads=[keyg], writes=["emg"])
                A("act", lambda e, bankg=bankg, c=c: e.activation(out=egl[:, c, :], in_=bankg[:, :].rearrange("p (j t) -> p j t", j=4)[:, :, 127], func=AF.Exp), reads=[keyg], writes=[("egl", c)])
                NEED("q")
                NEED("k")
                A("dve", lambda e, cs=cs: e.tensor_tensor(out=qtl[:], in0=qkT[:, 0:4, cs], in1=eg[:].rearrange("p (j t) -> p j t", j=4), op=ALU.mult),
                  reads=["eg"] + [("qkT", j) for j in range(4)], writes=["qtl"])
                A("dve", lambda e, cs=cs: e.tensor_tensor(out=ktl[:], in0=qkT[:, 4:8, cs], in1=emg[:].rearrange("p (j t) -> p j t", j=4), op=ALU.mult),
                  reads=["emg"] + [("qkT", 4 + j) for j in range(4)], writes=["ktl"])
                STEP()
                bank, key = PB.get()
                for hd in range(4):
                    A("pe", lambda e, bank=bank, hd=hd: e.matmul(bank[:, hd * 128:(hd + 1) * 128], lhsT=ktl[:, hd, :], rhs=qtl[:, hd, :], start=True, stop=True),
                      reads=["ktl", "qtl"], writes=[key])
                A("dve", lambda e, bank=bank: e.tensor_tensor(out=attm[:], in0=bank[:, :].rearrange("p (g t) -> p g t", g=4), in1=bc(maskb[:, :].unsqueeze(1), [128, 4, 128]), op=ALU.mult),
                  reads=[key, "maskb"], writes=["attm"])
                STEP()
                bankt, keyt = PT.get()
                for hd in range(4):
                    A("pe", lambda e, bankt=bankt, hd=hd: e.transpose(out=bankt[:, hd * 128:(hd + 1) * 128], in_=ktl[:, hd, :], identity=identb[:]),
                      reads=["ktl", "identb"], writes=[keyt])
                A("act", lambda e, bankt=bankt, ktok=ktok: e.activation(out=ktok[:], in_=bankt[:, 0:512], func=AF.Copy), reads=[keyt], writes=[("ktok", c % len(ktoks))])
                STEP()

            def I_mid(c):
                cs = slice(c * 128, (c + 1) * 128)
                obk = []
                NEED(("v", c))
                for hb in range(2):
                    bank, key = PB.get()
                    obk.append((bank, key))
                    for hh in range(2):
                        hd = hb * 2 + hh
                        o = bank[:, hh * 256:(hh + 1) * 256]
                        A("pe", lambda e, o=o, hd=hd, c=c: e.matmul(o, lhsT=attm[:, hd, :], rhs=vtok[:, c, hd * 256:(hd + 1) * 256], start=True, stop=False),
                          reads=["attm", ("vtok", c)], writes=[key])
                        A("pe", lambda e, o=o, hd=hd: e.matmul(o, lhsT=qtl[:, hd, :], rhs=Gbf[l][:, hd * 256:(hd + 1) * 256], start=False, stop=True),
                          reads=["qtl", ("Gbf", l)], writes=[key])
                        A("act", lambda e, o=o, hd=hd: e.activation(out=on[:, hd * 256:(hd + 1) * 256], in_=o, func=AF.Square, accum_out=sq4g[:, hd:hd + 1]), reads=[key], writes=[("sq4g", hd), ("on", hd // 2)])
                    STEP()
                A("dve", lambda e: e.tensor_scalar(out=r4g[:], in0=sq4g[:], scalar1=1.0 / 256, scalar2=EPS, op0=ALU.mult, op1=ALU.add), reads=[("sq4g", i) for i in range(4)], writes=["r4g"])
                A("pool", lambda e: e.tensor_tensor(out=r4g[:], in0=r4g[:], in1=bc(mhalf[:, 0:1], [128, 4]), op=ALU.pow), reads=["r4g", "mhalf"], writes=["r4g"])
                for hb in range(2):
                    bank, key = obk[hb]
                    A("dve", lambda e, bank=bank, hb=hb: e.tensor_tensor(out=on[:, hb * 512:(hb + 1) * 512].rearrange("p (a d) -> p a d", a=2), in0=bank[:, :].rearrange("p (a d) -> p a d", a=2),
                                                                          in1=bc(r4g[:, hb * 2:(hb + 1) * 2].unsqueeze(2), [128, 2, 256]), op=ALU.mult),
                      reads=[key, "r4g"], writes=[("on", hb)])
                STEP()
                NEED("g")
                bank, key = PT.get()
                for k in range(8):
                    A("pe", lambda e, k=k, bank=bank: e.transpose(out=bank[:, k * 128:(k + 1) * 128], in_=on[:, k * 128:(k + 1) * 128], identity=identb[:]),
                      reads=[("on", 0), ("on", 1), "identb"], writes=[key])
                A("dve", lambda e, bank=bank, cs=cs, c=c: e.tensor_tensor(out=xbcT[:, 8:16, 4 + c * 128:4 + (c + 1) * 128], in0=bank[:, :].rearrange("p (k t) -> p k t", k=8),
                                                                         in1=sgT[:, :, cs], op=ALU.mult),
                  reads=[key] + [("sgT", b) for b in range(8)], writes=[("xbcT", 8 + b) for b in range(8)])
                STEP()

            def I_tail(c):
                ktok = ktoks[c % len(ktoks)]
                for hb in range(2):
                    bank, key = PB.get()
                    for hh in range(2):
                        hd = hb * 2 + hh
                        A("pe", lambda e, bank=bank, hh=hh, hd=hd, c=c, ktok=ktok: e.matmul(bank[:, hh * 256:(hh + 1) * 256], lhsT=ktok[:, hd * 128:(hd + 1) * 128], rhs=vtok[:, c, hd * 256:(hd + 1) * 256], start=True, stop=True),
                          reads=[("ktok", c % len(ktoks)), ("vtok", c)], writes=[key])
                    hs = slice(hb * 512, (hb + 1) * 512)
                    A("dve", lambda e, bank=bank, hs=hs: e.tensor_tensor(out=Sgla[l][:, hs], in0=Sgla[l][:, hs], in1=bank[:, :], op=ALU.add),
                      reads=[key, ("Sgla", l, hb)], writes=[("Sgla", l, hb)])
                for hd in range(4):
                    A("act", lambda e, hd=hd: e.activation(out=Gbf[l][:, hd * 256:(hd + 1) * 256], in_=Sgla[l][:, hd * 256:(hd + 1) * 256], func=AF.Copy, scale=egl[:, c, hd:hd + 1]),
                      reads=[("Sgla", l, hd // 2), ("egl", c)], writes=[("Gbf", l)])
                for hd in range(4):
                    A("pool", lambda e, hd=hd: e.tensor_scalar(out=Sgla[l][:, hd * 256:(hd + 1) * 256], in0=Sgla[l][:, hd * 256:(hd + 1) * 256], scalar1=egl[:, c, hd:hd + 1], scalar2=0.0, op0=ALU.mult, op1=ALU.add),
                      reads=[("Sgla", l, hd // 2), ("egl", c)], writes=[("Sgla", l, hd // 2)])
                STEP()

            if OPT["ipipe"]:
                I_front(0)
                I_mid(0)
                for c in range(1, NCH):
                    I_front(c)
                    I_tail(c - 1)
                    I_mid(c)
                I_tail(NCH - 1)
            else:
                for c in range(NCH):
                    I_front(c)
                    I_mid(c)
                    I_tail(c)
            REC[0] = None
            PB.ring = PBfull
            if OPT["s4"]:
                merge([sH, sI, sF], given)
                merge_branch(0)
            else:
                merge([sI, sJ], given, OPT["fbias2"])
            MARK("P4")
            merge_branch(1)
            MARK("OUT")
            if OPT["outc"]:
                wos = [next_w((l, "out", 0), pf=1), next_w((l, "out", 1), pf=0)]
                oorder = [(c, j2) for c in range(NCH) for j2 in range(2)]
            else:
                wos = [None, None]
                oorder = [(c, j2) for j2 in range(2) for c in range(NCH)]
            for c, j2 in oorder:
                if True:
                    if wos[j2] is None:
                        wos[j2] = next_w((l, "out", j2))
                    wo, wokey = wos[j2]
                    bank, key = PB.get()
                    for k in range(8):
                        A("pe", lambda e, k=k, c=c, wo=wo, bank=bank: e.matmul(bank[:, :], lhsT=qkT[:, k, c * 128:(c + 1) * 128], rhs=wo[:, k, :], start=(k == 0), stop=(k == 7)),
                          reads=[("qkT", k) for k in range(8)] + [wokey], writes=[key])
                    A("dve", lambda e, bank=bank, c=c, j2=j2: e.scalar_tensor_tensor(out=h[:, c, j2 * 512:(j2 + 1) * 512], in0=bank[:, :], scalar=0.5, in1=h[:, c, j2 * 512:(j2 + 1) * 512], op0=ALU.mult, op1=ALU.add),
                      reads=[key, ("h", c)], writes=[("h", c)])
            if first and not last_layer:
                A("pool", lambda e: e.memset(h[0:112, 0, :], 0.0), writes=[("h", 0)])

        for ti in range(ntiles):
            for c in range(NCH):
                gc = ti * NCH + c
                if gc == 0:
                    A("pool", lambda e: e.memset(h[:, 0, :], 0.0), writes=[("h", 0)])
                    A("sp", lambda e: e.dma_start(out=h[112:128, 0, :], in_=meta_d), writes=[("h", 0)], dma_sem=xsems[0])
                else:
                    r0 = (gc - 1) * 128
                    A("sp", lambda e, c=c, r0=r0: e.dma_start(out=h[:, c, :], in_=x_d[r0:r0 + 128, :]), writes=[("h", c)], dma_sem=xsems[c])
            for l in range(NL):
                tile_layer(ti, l)
            for c in range(NCH):
                gc = ti * NCH + c
                if gc == 0:
                    continue
                r0 = (gc - 1) * 128
                ub, ukey = UB.get()
                A("act", lambda e, c=c, ub=ub: e.activation(out=ub[:], in_=h[:, c, :], func=AF.Square, accum_out=ssq[:, c:c + 1]), reads=[("h", c)], writes=[("ssq", c), ukey])
                A("dve", lambda e, c=c: e.tensor_scalar(out=rs[:, c:c + 1], in0=ssq[:, c:c + 1], scalar1=1.0 / D, scalar2=EPS, op0=ALU.mult, op1=ALU.add),
                  reads=[("ssq", c)], writes=[("rs", c)])
                A("pool", lambda e, c=c: e.tensor_tensor(out=rs[:, c:c + 1], in0=rs[:, c:c + 1], in1=mhalf[:], op=ALU.pow), reads=[("rs", c), "mhalf"], writes=[("rs", c)])
                A("dve", lambda e, c=c: e.scalar_tensor_tensor(out=h[:, c, :], in0=h[:, c, :], scalar=rs[:, c:c + 1], in1=fnw_bc[:], op0=ALU.mult, op1=ALU.mult),
                  reads=[("h", c), ("rs", c), "fnw"], writes=[("h", c)])
                ok = ("out", gc)
                out_keys.append(ok)
                A("sp", lambda e, c=c, r0=r0: e.dma_start(out=out_d[r0:r0 + 128, :], in_=h[:, c, :]), reads=[("h", c)], writes=[ok], dma_sem=osems[c])
        A("sp", None, reads=out_keys)
        assert wstate["used"] == len(wlist)
        build.model_us = dict(SIM["free"])
        build.model_log = SIM["log"]
        build.mlog = SIM["mlog"]
        with nc.Block() as block:
            S.emit(block)
    return nc


def host_consts():
    i = np.arange(128)
    tri = (i[:, None] <= i[None, :]).astype(np.float32)
    strict = (i[:, None] > i[None, :]).astype(np.float32)
    mask0 = (i >= 112).astype(np.float32)[:, None]
    sel = np.zeros((128, 3, 128), np.float32)
    for r in range(3):
        sel[r, r, :] = 1.0
    return {"ident": np.eye(128, dtype=np.float32), "tri": tri, "strict": strict, "mask0": mask0, "sel": sel.reshape(128, 384)}


def make_in_maps(inputs, ncores, TR):
    f = lambda a: np.ascontiguousarray(np.asarray(a, dtype=np.float32))
    common = dict(host_consts())
    common["meta"] = f(inputs["meta_tokens"])
    common["w_in"] = f(inputs["w_in"])
    common["w_br"] = f(inputs["w_branch"])
    common["w_out"] = f(inputs["w_out"])
    common["normw"] = f(np.asarray(inputs["norm_w"]).reshape(2, 8, 128).transpose(2, 0, 1).reshape(128, 16))
    common["cw"] = f(np.asarray(inputs["conv_w"]).reshape(2, 4, 16, 128).transpose(3, 0, 2, 1).reshape(128, 128))
    common["cb"] = f(np.asarray(inputs["conv_b"]).reshape(1, 4096))
    common["dtb"] = f(inputs["dt_bias"])
    common["alog"] = f(inputs["a_log"])
    common["dskip"] = f(inputs["d_skip"])
    common["ssdnw"] = f(np.asarray(inputs["ssd_norm_w"]).reshape(2, 8, 128).transpose(2, 0, 1).reshape(128, 16))
    common["w2"] = f(np.asarray(inputs["gla_gate_w2"]).transpose(1, 0, 2).reshape(16, 1024))
    common["gb"] = f(np.asarray(inputs["gla_gate_b"]).reshape(1, 1024))
    common["gnw"] = f(np.asarray(inputs["gla_norm_w"]).reshape(2, 2, 128).transpose(2, 0, 1).reshape(128, 4))
    common["fnw"] = f(np.asarray(inputs["final_norm_w"]).reshape(1, 1024))
    x = np.asarray(inputs["x"], dtype=np.float32)
    maps = []
    for b in range(ncores):
        m = dict(common)
        m["x"] = np.ascontiguousarray(x[b, :TR])
        maps.append(m)
    return maps


_NC_CACHE = {}


def kernel(**inputs):
    x = np.asarray(inputs["x"])
    B, T, _ = x.shape
    NT = T // 128 + 1
    key = (NT,)
    if key not in _NC_CACHE:
        _NC_CACHE[key] = build(NT=NT, NCH=3, NL=2)
    nc = _NC_CACHE[key]
    maps = make_in_maps(inputs, B, T)
    res = run_bass_kernel_spmd(nc, maps, core_ids=list(range(B)))
    out = np.stack([np.asarray(r["out"], dtype=np.float32) for r in res.results], axis=0)
    return out
```
